# Optimizing a Trainium2 kernel written in Bass

```python
import math
import jax, jax.numpy as jnp
from jax import lax
import numpy as np

D_MODEL = 1024
BATCH = 1
SEQ = 16384
DEPTH = 4

N_MIXERS = 3
HGRN_EXPAND = 128
HGRN_HEADS = D_MODEL // HGRN_EXPAND
HGRN_DV = D_MODEL // HGRN_HEADS
HGRN_CHUNK = 64
ATTN_HEADS = 8
HEAD_DIM = D_MODEL // ATTN_HEADS
MOBA_BLOCK = 256
MOBA_TOPK = 3
MOBA_QCHUNK = 64
ROPE_THETA = 500000.0
ROPE_DIM = HEAD_DIM // 4
CONV_WIDTH = 3
D_FF = 2816
FFN_CONV_WIDTH = 3
LN_EPS = 1e-5
RMS_EPS = 1e-6
DEEPNORM_ALPHA = (2.0 * DEPTH) ** 0.25
DEEPNORM_BETA = (8.0 * DEPTH) ** -0.25
MASK_NEG = -1e30

kernel_name = 'hybrid_hgrn2_moba_shortconv_deepnorm'


def layer_norm(x, g, b):
    xf = x.astype(jnp.float32)
    mu = jnp.mean(xf, axis=-1, keepdims=True)
    var = jnp.mean(jnp.square(xf - mu), axis=-1, keepdims=True)
    y = (xf - mu) * lax.rsqrt(var + LN_EPS) * g.astype(jnp.float32) + b.astype(jnp.float32)
    return y.astype(x.dtype)


def causal_dwconv(x, w):
    W = w.shape[0]
    S = x.shape[1]
    xp = jnp.pad(x, ((0, 0), (W - 1, 0), (0, 0)))
    return sum(xp[:, j:j + S, :] * w[j] for j in range(W))


def partial_rope(x, positions):
    half = ROPE_DIM // 2
    inv_freq = 1.0 / (ROPE_THETA ** (jnp.arange(0, ROPE_DIM, 2, dtype=jnp.float32) / ROPE_DIM))
    ang = positions.astype(jnp.float32)[..., None] * inv_freq
    cos = jnp.cos(ang)[:, :, None, :]
    sin = jnp.sin(ang)[:, :, None, :]
    xf = x.astype(jnp.float32)
    x1 = xf[..., :half]
    x2 = xf[..., half:ROPE_DIM]
    out = jnp.concatenate([x1 * cos - x2 * sin, x2 * cos + x1 * sin, xf[..., ROPE_DIM:]], axis=-1)
    return out.astype(x.dtype)


def hgrn_lower_bound(lower_bounds, layer):
    s = jax.nn.softmax(lower_bounds.astype(jnp.float32), axis=0)
    lb = jnp.cumsum(s, axis=0) - s[0]
    return lb[layer]


def hgrn2_mixer(x, lb, w_in, norm_g, w_out):
    B, S, _ = x.shape
    H, dk, dv, C = HGRN_HEADS, HGRN_EXPAND, HGRN_DV, HGRN_CHUNK
    NC = S // C
    proj = x @ w_in
    q, fz, i, gate = jnp.split(proj, [H * dk, 2 * H * dk, 2 * H * dk + H * dv], axis=-1)
    q = jax.nn.silu(q.astype(jnp.float32)).reshape(B, S, H, dk)
    fz = fz.astype(jnp.float32).reshape(B, S, H, dk)
    lb = lb.reshape(H, dk)
    sig = jax.nn.sigmoid(fz)
    k = (1.0 - lb) * (1.0 - sig)
    g = jnp.log(lb + (1.0 - lb) * sig)
    v = i.astype(jnp.float32).reshape(B, S, H, dv)

    def to_chunks(t):
        return t.reshape(B, NC, C, H, t.shape[-1]).transpose(1, 0, 3, 2, 4)

    qc, kc, vc, gc = to_chunks(q), to_chunks(k), to_chunks(v), to_chunks(g)
    bc = jnp.cumsum(gc, axis=3)
    causal = jnp.tril(jnp.ones((C, C), dtype=bool))[:, :, None]

    def step(state, inp):
        q_, k_, v_, b_ = inp
        o_inter = jnp.einsum('bhtd,bhdv->bhtv', q_ * jnp.exp(b_), state)
        diff = b_[:, :, :, None, :] - b_[:, :, None, :, :]
        decay = jnp.where(causal, jnp.exp(jnp.where(causal, diff, 0.0)), 0.0)
        a = jnp.einsum('bhtd,bhsd,bhtsd->bhts', q_, k_, decay)
        o = o_inter + jnp.einsum('bhts,bhsv->bhtv', a, v_)
        b_last = b_[:, :, -1, :]
        k_dec = k_ * jnp.exp(b_last[:, :, None, :] - b_)
        state = jnp.exp(b_last)[..., None] * state + jnp.einsum('bhsd,bhsv->bhdv', k_dec, v_)
        return state, o

    state0 = jnp.zeros((B, H, dk, dv), jnp.float32)
    _, oc = lax.scan(step, state0, (qc, kc, vc, bc))
    o = oc.transpose(1, 0, 3, 2, 4).reshape(B, S, H, dv)
    o = o * lax.rsqrt(jnp.mean(jnp.square(o), axis=-1, keepdims=True) + RMS_EPS)
    o = o.reshape(B, S, H * dv) * norm_g.astype(jnp.float32) * jax.nn.silu(gate.astype(jnp.float32))
    return o.astype(x.dtype) @ w_out


def moba_mixer(x, positions, w_in, w_out):
    B, S, _ = x.shape
    H, hd, L = ATTN_HEADS, HEAD_DIM, MOBA_BLOCK
    qkv = (x @ w_in).reshape(B, S, 3, H, hd)
    q = partial_rope(qkv[:, :, 0], positions)
    k = partial_rope(qkv[:, :, 1], positions)
    v = qkv[:, :, 2]
    Sp = -(-S // L) * L
    pad = Sp - S
    q, k, v = [jnp.pad(t, ((0, 0), (0, pad), (0, 0), (0, 0))) for t in (q, k, v)]
    NB = Sp // L
    K = min(MOBA_TOPK, NB)
    kb = k.reshape(B, NB, L, H, hd).transpose(0, 1, 3, 2, 4)
    vb = v.reshape(B, NB, L, H, hd).transpose(0, 1, 3, 2, 4)
    kmean = jnp.mean(kb.astype(jnp.float32), axis=3)
    gate = jnp.einsum('bshd,bnhd->bshn', q.astype(jnp.float32), kmean)
    qblk = jnp.arange(Sp) // L
    past = jnp.arange(NB)[None, :] < qblk[:, None]
    gate = jnp.where(past[None, :, None, :], gate, MASK_NEG)
    _, sel = lax.top_k(gate, K)

    QC = MOBA_QCHUNK
    NQ = Sp // QC
    q_chunks = q.reshape(B, NQ, QC, H, hd).transpose(1, 0, 2, 3, 4)
    sel_chunks = sel.reshape(B, NQ, QC, H, K).transpose(1, 0, 2, 3, 4)
    bidx = jnp.arange(B)[:, None, None, None]
    hidx = jnp.arange(H)[None, None, :, None]
    scale = 1.0 / math.sqrt(hd)

    def attend(args):
        qc, sc, ci = args
        start = ci * QC
        blk = start // L
        k_sel = kb[bidx, sc, hidx]
        v_sel = vb[bidx, sc, hidx]
        k_own = lax.dynamic_index_in_dim(kb, blk, axis=1, keepdims=False)
        v_own = lax.dynamic_index_in_dim(vb, blk, axis=1, keepdims=False)
        s_sel = jnp.einsum('bqhd,bqhjld->bqhjl', qc, k_sel).astype(jnp.float32) * scale
        slot_ok = jnp.arange(K) < blk
        s_sel = jnp.where(slot_ok[:, None], s_sel, MASK_NEG).reshape(B, QC, H, K * L)
        s_own = jnp.einsum('bqhd,bhld->bqhl', qc, k_own).astype(jnp.float32) * scale
        qpos = start + jnp.arange(QC)
        kpos = blk * L + jnp.arange(L)
        s_own = jnp.where((kpos[None, :] <= qpos[:, None])[None, :, None, :], s_own, MASK_NEG)
        p = jax.nn.softmax(jnp.concatenate([s_sel, s_own], axis=-1), axis=-1).astype(v.dtype)
        p_sel = p[..., :K * L].reshape(B, QC, H, K, L)
        p_own = p[..., K * L:]
        return (jnp.einsum('bqhjl,bqhjld->bqhd', p_sel, v_sel)
                + jnp.einsum('bqhl,bhld->bqhd', p_own, v_own))

    o = lax.map(attend, (q_chunks, sel_chunks, jnp.arange(NQ)))
    o = o.transpose(1, 0, 2, 3, 4).reshape(B, Sp, H * hd)[:, :S]
    return o @ w_out


def short_conv_mixer(x, w_in, conv_w, w_out):
    bg, cg, h = jnp.split(x @ w_in, 3, axis=-1)
    y = bg * causal_dwconv(cg * h, conv_w)
    return y @ w_out


def conv_ffn(x, w_up, conv_w, w_down):
    u = causal_dwconv(x @ w_up, conv_w)
    a, b = jnp.split(u, 2, axis=-1)
    return (jax.nn.silu(a) * b) @ w_down


def setup_inputs(seed: int = 0) -> dict:
    key = jax.random.key(seed)
    keys = iter(jax.random.split(key, 64))

    def nrm(shape, scale):
        return jax.random.normal(next(keys), shape, jnp.float32) * scale

    D = D_MODEL
    p = {}
    p['x'] = nrm((BATCH, SEQ, D), 1.0)
    p['positions'] = jnp.tile(jnp.arange(SEQ, dtype=jnp.int32)[None, :], (BATCH, 1))
    p['hgrn_lower_bounds'] = nrm((DEPTH, HGRN_HEADS * HGRN_EXPAND), 0.5)
    for i in range(DEPTH):
        kind = i % N_MIXERS
        if kind == 0:
            p[f'l{i}_mix_w_in'] = nrm((D, 2 * HGRN_HEADS * HGRN_EXPAND + 2 * HGRN_HEADS * HGRN_DV), D ** -0.5)
            p[f'l{i}_mix_norm_g'] = 1.0 + nrm((HGRN_HEADS * HGRN_DV,), 0.02)
            p[f'l{i}_mix_w_out'] = nrm((HGRN_HEADS * HGRN_DV, D), DEEPNORM_BETA * (HGRN_HEADS * HGRN_DV) ** -0.5)
        elif kind == 1:
            p[f'l{i}_mix_w_in'] = nrm((D, 3 * ATTN_HEADS * HEAD_DIM), D ** -0.5)
            p[f'l{i}_mix_w_out'] = nrm((ATTN_HEADS * HEAD_DIM, D), DEEPNORM_BETA * (ATTN_HEADS * HEAD_DIM) ** -0.5)
        else:
            p[f'l{i}_mix_w_in'] = nrm((D, 3 * D), D ** -0.5)
            p[f'l{i}_mix_conv'] = nrm((CONV_WIDTH, D), CONV_WIDTH ** -0.5)
            p[f'l{i}_mix_w_out'] = nrm((D, D), DEEPNORM_BETA * D ** -0.5)
        p[f'l{i}_ln1_g'] = 1.0 + nrm((D,), 0.02)
        p[f'l{i}_ln1_b'] = nrm((D,), 0.02)
        p[f'l{i}_ffn_w_up'] = nrm((D, 2 * D_FF), D ** -0.5)
        p[f'l{i}_ffn_conv'] = nrm((FFN_CONV_WIDTH, 2 * D_FF), FFN_CONV_WIDTH ** -0.5)
        p[f'l{i}_ffn_w_down'] = nrm((D_FF, D), DEEPNORM_BETA * D_FF ** -0.5)
        p[f'l{i}_ln2_g'] = 1.0 + nrm((D,), 0.02)
        p[f'l{i}_ln2_b'] = nrm((D,), 0.02)
    return p


def reference(x, positions, hgrn_lower_bounds,
              l0_mix_w_in, l0_mix_norm_g, l0_mix_w_out, l0_ln1_g, l0_ln1_b,
              l0_ffn_w_up, l0_ffn_conv, l0_ffn_w_down, l0_ln2_g, l0_ln2_b,
              l1_mix_w_in, l1_mix_w_out, l1_ln1_g, l1_ln1_b,
              l1_ffn_w_up, l1_ffn_conv, l1_ffn_w_down, l1_ln2_g, l1_ln2_b,
              l2_mix_w_in, l2_mix_conv, l2_mix_w_out, l2_ln1_g, l2_ln1_b,
              l2_ffn_w_up, l2_ffn_conv, l2_ffn_w_down, l2_ln2_g, l2_ln2_b,
              l3_mix_w_in, l3_mix_norm_g, l3_mix_w_out, l3_ln1_g, l3_ln1_b,
              l3_ffn_w_up, l3_ffn_conv, l3_ffn_w_down, l3_ln2_g, l3_ln2_b):
    layers = [
        ((l0_mix_w_in, l0_mix_norm_g, l0_mix_w_out), l0_ln1_g, l0_ln1_b,
         (l0_ffn_w_up, l0_ffn_conv, l0_ffn_w_down), l0_ln2_g, l0_ln2_b),
        ((l1_mix_w_in, l1_mix_w_out), l1_ln1_g, l1_ln1_b,
         (l1_ffn_w_up, l1_ffn_conv, l1_ffn_w_down), l1_ln2_g, l1_ln2_b),
        ((l2_mix_w_in, l2_mix_conv, l2_mix_w_out), l2_ln1_g, l2_ln1_b,
         (l2_ffn_w_up, l2_ffn_conv, l2_ffn_w_down), l2_ln2_g, l2_ln2_b),
        ((l3_mix_w_in, l3_mix_norm_g, l3_mix_w_out), l3_ln1_g, l3_ln1_b,
         (l3_ffn_w_up, l3_ffn_conv, l3_ffn_w_down), l3_ln2_g, l3_ln2_b),
    ]
    for i in range(DEPTH):
        mix_p, ln1_g, ln1_b, ffn_p, ln2_g, ln2_b = layers[i]
        kind = i % N_MIXERS
        if kind == 0:
            m = hgrn2_mixer(x, hgrn_lower_bound(hgrn_lower_bounds, i), *mix_p)
        elif kind == 1:
            m = moba_mixer(x, positions, *mix_p)
        else:
            m = short_conv_mixer(x, *mix_p)
        x = layer_norm(DEEPNORM_ALPHA * x + m, ln1_g, ln1_b)
        x = layer_norm(DEEPNORM_ALPHA * x + conv_ffn(x, *ffn_p), ln2_g, ln2_b)
    return x
```

```python
import math
import numpy as np
from contextlib import ExitStack
import concourse.bass as bass
import concourse.mybir as mybir
from concourse.bass_utils import run_bass_kernel_spmd

F32 = mybir.dt.float32
BF16 = mybir.dt.bfloat16
I32 = mybir.dt.int32
AF = mybir.ActivationFunctionType
ALU = mybir.AluOpType
AX = mybir.AxisListType

NCORE = 8
T = 2048
D = 1024
DC = 8
DFF = 2816
FC = 22
DEPTH = 4
ALPHA = (2.0 * DEPTH) ** 0.25
LN_EPS = 1e-5
RMS_EPS = 1e-6
NT = T // 512


class Buf:
    __slots__ = ("name", "w", "r")

    def __init__(self, name="b"):
        self.name = name
        self.w = None
        self.r = {}


class DSem:
    def __init__(self, key, h):
        self.key = key
        self.h = h
        self.cnt = 0


class Prog:
    ENG = ("pe", "act", "dve", "pool", "sp")

    def __init__(self, nc, es):
        self.nc = nc
        self.es = es
        self.eng = dict(pe=nc.tensor, act=nc.scalar, dve=nc.vector, pool=nc.gpsimd, sp=nc.sync)
        self.semh = {}
        self.cnt = {}
        self.known = {k: {} for k in self.ENG}
        for k in self.ENG:
            self.semh[k] = es.enter_context(nc.semaphore("s_" + k))
            self.cnt[k] = 0
        self.dsems = []
        self.nwait = 0
        self.nins = 0

    def dsem(self, name):
        h = self.es.enter_context(self.nc.semaphore("d_" + name))
        d = DSem("d_" + name, h)
        self.semh[d.key] = h
        self.dsems.append(d)
        return d

    def _wait(self, e, key, val):
        if val <= 0:
            return
        if self.known[e].get(key, 0) >= val:
            return
        if key == e:
            if e == "pe":
                return
            if val > self.cnt[e]:
                return
        self.eng[e].wait_ge(self.semh[key], val)
        self.nwait += 1
        self.known[e][key] = val

    def deps(self, e, reads, writes):
        need = {}
        for b in reads:
            if b.w is not None:
                k, v = b.w
                if need.get(k, 0) < v:
                    need[k] = v
        for b in writes:
            if b.w is not None:
                k, v = b.w
                if need.get(k, 0) < v:
                    need[k] = v
            for k, v in b.r.items():
                if need.get(k, 0) < v:
                    need[k] = v
        for k, v in need.items():
            self._wait(e, k, v)

    def op(self, e, fn, reads=(), writes=(), signal=True):
        self.deps(e, reads, writes)
        ins = fn(self.eng[e])
        self.nins += 1
        if signal:
            self.cnt[e] += 1
            ins.then_inc(self.semh[e], 1)
            mark = (e, self.cnt[e])
        else:
            mark = (e, self.cnt[e] + 1)
        for b in writes:
            b.w = mark
            b.r = {}
        for b in reads:
            if b.r.get(e, 0) < mark[1]:
                b.r[e] = mark[1]
        return ins

    def _mark_async(self, ds, reads, writes):
        mark = (ds.key, ds.cnt)
        for b in writes:
            b.w = mark
            b.r = {}
        for b in reads:
            b.r[ds.key] = ds.cnt

    def dma(self, q, out, in_, ds, reads=(), writes=(), **kw):
        self.deps(q, reads, writes)
        ds.cnt += 16
        ins = self.eng[q].dma_start(out=out, in_=in_, **kw)
        ins.then_inc(ds.h, 16)
        self._mark_async(ds, reads, writes)
        return ins

    def allgather(self, in_ap, out_ap, ds, reads=(), writes=()):
        q = "pool"
        self.deps(q, reads, writes)
        ds.cnt += 1
        ins = self.nc.gpsimd.collective_compute(
            "AllGather", ALU.bypass, replica_groups=[list(range(NCORE))],
            ins=[in_ap.opt()], outs=[out_ap.opt()])
        ins.then_inc(ds.h, 1)
        self._mark_async(ds, reads, writes)
        return ins

    def barrier(self, engines=None):
        engines = engines or self.ENG
        for e in engines:
            for k in self.ENG:
                if k != e:
                    self._wait(e, k, self.cnt[k])
            for d in self.dsems:
                self._wait(e, d.key, d.cnt)


def tile_w(W):
    Din, Fo = W.shape
    Wt = W.reshape(Din // 128, 128, Fo // 128, 128).transpose(2, 1, 0, 3)
    return np.ascontiguousarray(Wt).reshape(Fo // 128 * 128, Din)


def vec_cols(v):
    return np.ascontiguousarray(v.reshape(-1, 128).T)


V_LNG = 0
V_LNB = 8
V_CONV = 16
V_NORMG = 148
V_LBRAW = 156
NVEC = 188
C_HMASK = 0
C_LMASK = 8
C_MASK2 = 12
C_IDENT = 140
C_INVF = 268
C_SGN = 269
NCST = 270


def conv_cols(cw):
    return np.concatenate([vec_cols(cw[k]) for k in range(cw.shape[0])], axis=1)


def make_vec(ln_g, ln_b, conv=None, norm_g=None, lbraw=None):
    v = np.zeros((128, NVEC), np.float32)
    v[:, V_LNG:V_LNG + 8] = vec_cols(ln_g)
    v[:, V_LNB:V_LNB + 8] = vec_cols(ln_b)
    if conv is not None:
        c = conv_cols(conv)
        v[:, V_CONV:V_CONV + c.shape[1]] = c
    if norm_g is not None:
        v[:, V_NORMG:V_NORMG + 8] = vec_cols(norm_g)
    if lbraw is not None:
        v[:, V_LBRAW:V_LBRAW + 32] = np.concatenate([vec_cols(lbraw[l]) for l in range(DEPTH)], axis=1)
    return v


def make_cst(core, layer=0):
    c = np.zeros((128, NCST), np.float32)
    for r in range(NCORE):
        c[:, C_HMASK + r] = 1.0 if r < core else 0.0
    for l in range(DEPTH):
        c[:, C_LMASK + l] = 1.0 if 1 <= l <= layer else 0.0
    s_ = np.arange(128)[:, None]
    t_ = np.arange(128)[None, :]
    c[:, C_MASK2:C_MASK2 + 128] = ((s_ // 64 == t_ // 64) & (s_ <= t_)).astype(np.float32)
    c[:, C_IDENT:C_IDENT + 128] = np.eye(128, dtype=np.float32)
    invf = (1.0 / (500000.0 ** (np.arange(0, 32, 2, dtype=np.float32) / np.float32(32.0)))).astype(np.float32)
    c[0:32, C_INVF] = np.concatenate([invf, invf])
    c[0:16, C_SGN] = -1.0
    c[16:32, C_SGN] = 1.0
    return c


class Builder:
    def __init__(self, kind):
        self.kind = kind
        self.nc = bass.Bass("TRN2", target_bir_lowering=False)

    def dram(self, name, shape, dt, kind="Internal"):
        return self.nc.dram_tensor(name, list(shape), dt, kind=kind).ap()

    def sb(self, es, name, shape, dt):
        self._uid = getattr(self, "_uid", 0) + 1
        return es.enter_context(self.nc.sbuf_tensor(f"{name}_{self._uid}", list(shape), dt))

    def build(self):
        nc = self.nc
        kind = self.kind
        with ExitStack() as es:
            self.es = es
            P = self.P = Prog(nc, es)
            self.xT = self.dram("xT", [D, T + 2], F32, "ExternalInput")
            self.vec_d = self.dram("vec", [128, NVEC], F32, "ExternalInput")
            self.cst_d = self.dram("cst", [128, NCST], F32, "ExternalInput")
            self.wd = {}
            if kind == "ff":
                self.wd["wup"] = self.dram("wup", [2 * DFF, D], F32, "ExternalInput")
                self.wd["wdn"] = self.dram("wdn", [D, DFF], F32, "ExternalInput")
            elif kind == "cv":
                self.wd["win"] = self.dram("win", [3072, D], F32, "ExternalInput")
                self.wd["wout"] = self.dram("wout", [D, D], F32, "ExternalInput")
            elif kind == "hg":
                self.wd["win"] = self.dram("win", [4096, D], F32, "ExternalInput")
                self.wd["wout"] = self.dram("wout", [D, D], F32, "ExternalInput")
            elif kind == "hg1":
                self.wd["win"] = self.dram("win", [4096, D], F32, "ExternalInput")
            elif kind == "mo1":
                self.wd["win"] = self.dram("win", [5120, D], F32, "ExternalInput")
            elif kind == "mo2":
                self.wd["wout"] = self.dram("wout", [D, D], F32, "ExternalInput")
            if kind not in ("mo1", "hg1"):
                self.outT = self.dram("outT", [D, T], F32, "ExternalOutput")
            self.XR = self.sb(es, "XR", [128, DC, T], F32)
            self.XB = self.sb(es, "XB", [128, DC, T + 2], BF16)
            self.bXR = [Buf(f"XR{n}") for n in range(NT)]
            self.bXB = [Buf(f"XB{n}") for n in range(NT)]
            self.bXBh = Buf("XBh")
            self.vec = self.sb(es, "vecs", [128, NVEC], F32)
            self.cst = self.sb(es, "csts", [128, NCST], F32)
            self.bvec = Buf("vec")
            self.ones_ln = self.sb(es, "ones_ln", [128, 128], BF16)
            self.bconst = Buf("const")
            NW = 2 if kind == "mo2" else 8
            self.WS = [self.sb(es, f"ws{k}", [128, DC, 128], BF16) for k in range(NW)]
            self.bWS = [Buf(f"ws{k}") for k in range(NW)]
            self.dWS = [P.dsem(f"ws{k}") for k in range(NW)]
            self.wsi = 0
            self.PS = [es.enter_context(nc.psum_tensor(f"ps{k}", [128, 512], F32)) for k in range(8)]
            self.bPS = [Buf(f"ps{k}") for k in range(8)]
            self.psi = 0
            self.d_out = P.dsem("out")
            self.d_misc = P.dsem("misc")
            self.d_misc2 = P.dsem("misc2")
            self.bout = Buf("out")

            P.dma("sp", self.vec[:], self.vec_d, self.d_misc, writes=[self.bvec])
            P.dma("sp", self.cst[:], self.cst_d, self.d_misc2, writes=[self.bvec])
            xTv = self.xT.rearrange("(c p) t -> p c t", p=128)
            d_ins = [P.dsem(f"in{n}") for n in range(NT + 1)]
            for n in range(NT):
                P.dma("sp", self.XR[:, :, n * 512:(n + 1) * 512], xTv[:, :, 2 + n * 512:2 + (n + 1) * 512],
                      d_ins[n], writes=[self.bXR[n]])
            self.hal32 = self.sb(es, "hal32", [128, DC, 2], F32)
            self.bhal32 = Buf("hal32")
            P.dma("sp", self.hal32[:], xTv[:, :, 0:2], d_ins[NT], writes=[self.bhal32])
            P.op("dve", lambda e: e.memset(self.ones_ln[:], 1.0 / D), writes=[self.bconst])
            if kind != "mo2":
                for n in range(NT):
                    P.op("act", lambda e: e.activation(out=self.XB[:, :, 2 + n * 512:2 + (n + 1) * 512],
                                                       in_=self.XR[:, :, n * 512:(n + 1) * 512], func=AF.Copy),
                         reads=[self.bXR[n]], writes=[self.bXB[n]])
                P.op("act", lambda e: e.activation(out=self.XB[:, :, 0:2], in_=self.hal32[:], func=AF.Copy),
                     reads=[self.bhal32], writes=[self.bXBh])

            if kind == "ff":
                self.ffn()
            elif kind == "cv":
                self.conv_mixer()
            elif kind in ("hg", "hg1"):
                self.hgrn_mixer()
            elif kind == "mo1":
                self.moba_qkv()
            elif kind == "mo2":
                self.moba_attn()
            if kind not in ("mo1", "hg1"):
                self.layernorm(V_LNG, V_LNB)
                oTv = self.outT.rearrange("(c p) t -> p c t", p=128)
                for n in range(NT):
                    P.dma("sp", oTv[:, :, n * 512:(n + 1) * 512], self.XR[:, :, n * 512:(n + 1) * 512],
                          self.d_out, reads=[self.bXR[n]], writes=[self.bout])
            P.barrier()
        return nc

    def bank(self):
        rb = getattr(self, "rot_banks", (0, 1, 2, 3, 4, 5, 6, 7))
        self.psi = (self.psi + 1) % len(rb)
        k = rb[self.psi]
        return self.PS[k], self.bPS[k]

    def load_w(self, name, j, c0=0, nch=DC):
        P = self.P
        k = self.wsi
        self.wsi = (self.wsi + 1) % len(self.WS)
        slot, b, ds = self.WS[k], self.bWS[k], self.dWS[k]
        src = self.wd[name][j * 128:(j + 1) * 128, c0 * 128:(c0 + nch) * 128]
        dst = slot[:, 0:nch, :].rearrange("p c i -> p (c i)")
        P.dma("pool", dst, src, ds, writes=[b])
        return slot, b

    def xb_bufs(self, col0, n):
        bs = []
        if col0 < 2:
            bs.append(self.bXBh)
        lo = max(col0 - 2, 0)
        hi = col0 + n - 2
        for t in range(NT):
            if lo < (t + 1) * 512 and hi > t * 512:
                bs.append(self.bXB[t])
        return bs

    def proj(self, w, bw, col0, n, ps, bps, m0=0, m=128, src=None, srcbufs=None, nch=DC):
        P = self.P
        src = self.XB if src is None else src
        srcbufs = self.xb_bufs(col0, n) if srcbufs is None else srcbufs
        for c in range(nch):
            P.op("pe", lambda e: e.matmul(ps[0:m, 0:n], w[:, c, m0:m0 + m], src[:, c, col0:col0 + n],
                                          start=(c == 0), stop=(c == nch - 1)),
                 reads=[bw] + srcbufs, writes=[bps], signal=(c == nch - 1))

    def layernorm(self, goff, boff):
        P = self.P
        with ExitStack() as es:
            zb = [self.sb(es, f"ln_zb{k}", [128, DC, 512], BF16) for k in range(2)]
            zq = [self.sb(es, f"ln_zq{k}", [128, DC, 512], BF16) for k in range(2)]
            bzb = [Buf() for _ in range(2)]
            bzq = [Buf() for _ in range(2)]
            mean = [self.sb(es, f"ln_mean{k}", [128, 512], F32) for k in range(2)]
            rstd = [self.sb(es, f"ln_rstd{k}", [128, 512], F32) for k in range(2)]
            tmp = [self.sb(es, f"ln_tmp{k}", [128, 512], F32) for k in range(2)]
            bmean = [Buf() for _ in range(2)]
            brstd = [Buf() for _ in range(2)]
            btmp = [Buf() for _ in range(2)]
            for n in range(NT):
                k = n % 2
                cs = slice(n * 512, (n + 1) * 512)
                P.op("act", lambda e: e.activation(out=zb[k][:], in_=self.XR[:, :, cs], func=AF.Copy),
                     reads=[self.bXR[n]], writes=[bzb[k]])
                P.op("act", lambda e: e.activation(out=zq[k][:], in_=self.XR[:, :, cs], func=AF.Square),
                     reads=[self.bXR[n]], writes=[bzq[k]])
                pm, bpm = self.bank()
                for c in range(DC):
                    P.op("pe", lambda e: e.matmul(pm[:, :], self.ones_ln[:], zb[k][:, c, :], start=(c == 0), stop=(c == DC - 1)),
                         reads=[self.bconst, bzb[k]], writes=[bpm], signal=(c == DC - 1))
                pq, bpq = self.bank()
                for c in range(DC):
                    P.op("pe", lambda e: e.matmul(pq[:, :], self.ones_ln[:], zq[k][:, c, :], start=(c == 0), stop=(c == DC - 1)),
                         reads=[self.bconst, bzq[k]], writes=[bpq], signal=(c == DC - 1))
                P.op("dve", lambda e: e.tensor_copy(out=mean[k][:], in_=pm[:, :]), reads=[bpm], writes=[bmean[k]])
                P.op("dve", lambda e: e.tensor_tensor(out=tmp[k][:], in0=mean[k][:], in1=mean[k][:], op=ALU.mult),
                     reads=[bmean[k]], writes=[btmp[k]])
                P.op("dve", lambda e: e.tensor_tensor(out=rstd[k][:], in0=pq[:, :], in1=tmp[k][:], op=ALU.subtract),
                     reads=[bpq, btmp[k]], writes=[brstd[k]])
                P.op("dve", lambda e: e.tensor_scalar(out=rstd[k][:], in0=rstd[k][:], scalar1=LN_EPS, scalar2=None,
                                                      op0=ALU.add),
                     reads=[brstd[k]], writes=[brstd[k]])
                P.op("act", lambda e: e.activation(out=rstd[k][:], in_=rstd[k][:], func=AF.Sqrt),
                     reads=[brstd[k]], writes=[brstd[k]])
                P.op("dve", lambda e: e.reciprocal(out=rstd[k][:], in_=rstd[k][:]),
                     reads=[brstd[k]], writes=[brstd[k]])
                for c in range(DC):
                    P.op("dve", lambda e: e.tensor_tensor(out=tmp[k][:], in0=self.XR[:, c, cs], in1=mean[k][:], op=ALU.subtract),
                         reads=[self.bXR[n], bmean[k]], writes=[btmp[k]])
                    P.op("dve", lambda e: e.tensor_tensor(out=tmp[k][:], in0=tmp[k][:], in1=rstd[k][:], op=ALU.mult),
                         reads=[btmp[k], brstd[k]], writes=[btmp[k]])
                    P.op("act", lambda e: e.activation(out=self.XR[:, c, cs], in_=tmp[k][:], func=AF.Identity,
                                                       bias=self.vec[:, boff + c:boff + c + 1],
                                                       scale=self.vec[:, goff + c:goff + c + 1]),
                         reads=[btmp[k], self.bvec], writes=[self.bXR[n]])
                P.op("act", lambda e: e.activation(out=self.XB[:, :, 2 + n * 512:2 + (n + 1) * 512],
                                                   in_=self.XR[:, :, cs], func=AF.Copy),
                     reads=[self.bXR[n]], writes=[self.bXB[n]])
            P.barrier()

    def out_proj_residual(self, wname, Y, bY, nch, c0=0, first=True, wtile_c0=0, coff=0):
        P = self.P
        for f in range(DC):
            w, bw = self.load_w(wname, f, c0=wtile_c0, nch=nch)
            for n in range(NT):
                ps, bps = self.bank()
                self.proj(w, bw, coff + n * 512, 512, ps, bps, src=Y, srcbufs=bY if isinstance(bY, list) else [bY], nch=nch)
                cs = slice(n * 512, (n + 1) * 512)
                if first:
                    P.op("dve", lambda e: e.scalar_tensor_tensor(out=self.XR[:, f, cs], in0=self.XR[:, f, cs], scalar=ALPHA,
                                                                 in1=ps[:, :], op0=ALU.mult, op1=ALU.add),
                         reads=[bps, self.bXR[n]], writes=[self.bXR[n]])
                else:
                    P.op("dve", lambda e: e.tensor_tensor(out=self.XR[:, f, cs], in0=self.XR[:, f, cs], in1=ps[:, :], op=ALU.add),
                         reads=[bps, self.bXR[n]], writes=[self.bXR[n]])

    def ffn(self):
        P = self.P
        cvo = V_CONV
        groups = [(0, 8), (8, 15), (15, 22)]
        with ExitStack() as es:
            GH = 8
            H = self.sb(es, "ffn_H", [128, GH, T], BF16)
            bH = [Buf() for _ in range(GH)]
            U = [[self.sb(es, f"ffn_U{ab}{k}", [128, 514], F32) for k in range(3)] for ab in range(2)]
            bU = [[Buf() for _ in range(3)] for _ in range(2)]
            Y = [[self.sb(es, f"ffn_Y{ab}{k}", [128, 512], F32) for k in range(2)] for ab in range(2)]
            bY = [[Buf() for _ in range(2)] for _ in range(2)]
            TT = [[self.sb(es, f"ffn_T{ab}{k}", [128, 512], F32) for k in range(2)] for ab in range(2)]
            bTT = [[Buf() for _ in range(2)] for _ in range(2)]
            SA = [self.sb(es, f"ffn_SA{k}", [128, 512], F32) for k in range(2)]
            bSA = [Buf() for _ in range(2)]
            ui = 0
            yi = 0
            for gi, (j0, j1) in enumerate(groups):
                for j in range(j0, j1):
                    ws = [self.load_w("wup", j), self.load_w("wup", FC + j)]
                    for n in range(NT):
                        uk = ui % 3
                        up = (ui - 1) % 3
                        ui += 1
                        yk = yi % 2
                        yi += 1
                        for ab in range(2):
                            w, bw = ws[ab]
                            u, bu = U[ab][uk], bU[ab][uk]
                            cj = j + ab * FC
                            if n == 0:
                                ph, bph = self.bank()
                                self.proj(w, bw, 0, 2, ph, bph)
                                P.op("act", lambda e: e.activation(out=u[:, 0:2], in_=ph[:, 0:2], func=AF.Copy),
                                     reads=[bph], writes=[bu])
                            else:
                                P.op("act", lambda e: e.activation(out=u[:, 0:2], in_=U[ab][up][:, 512:514], func=AF.Copy),
                                     reads=[bU[ab][up]], writes=[bu])
                            ps, bps = self.bank()
                            self.proj(w, bw, 2 + n * 512, 512, ps, bps)
                            P.op("act", lambda e: e.activation(out=u[:, 2:514], in_=ps[:, :], func=AF.Copy),
                                 reads=[bps], writes=[bu])
                            t, bt = TT[ab][yk], bTT[ab][yk]
                            y, by = Y[ab][yk], bY[ab][yk]
                            w0 = self.vec[:, cvo + cj:cvo + cj + 1]
                            w1 = self.vec[:, cvo + 44 + cj:cvo + 44 + cj + 1]
                            w2 = self.vec[:, cvo + 88 + cj:cvo + 88 + cj + 1]
                            P.op("act", lambda e: e.activation(out=t[:], in_=u[:, 0:512], func=AF.Copy, scale=w0),
                                 reads=[bu, self.bvec], writes=[bt])
                            P.op("dve", lambda e: e.scalar_tensor_tensor(out=t[:], in0=u[:, 1:513], scalar=w1, in1=t[:],
                                                                         op0=ALU.mult, op1=ALU.add),
                                 reads=[bu, bt, self.bvec], writes=[bt])
                            P.op("dve", lambda e: e.scalar_tensor_tensor(out=y[:], in0=u[:, 2:514], scalar=w2, in1=t[:],
                                                                         op0=ALU.mult, op1=ALU.add),
                                 reads=[bu, bt, self.bvec], writes=[by])
                        sa, bsa = SA[yk], bSA[yk]
                        P.op("act", lambda e: e.activation(out=sa[:], in_=Y[0][yk][:], func=AF.Silu),
                             reads=[bY[0][yk]], writes=[bsa])
                        P.op("dve", lambda e: e.tensor_tensor(out=H[:, j - j0, n * 512:(n + 1) * 512], in0=sa[:], in1=Y[1][yk][:],
                                                              op=ALU.mult),
                             reads=[bsa, bY[1][yk]], writes=[bH[j - j0]])
                self.out_proj_residual("wdn", H, bH[:j1 - j0], j1 - j0, first=(gi == 0), wtile_c0=j0)
            P.barrier()

    def conv_mixer(self):
        P = self.P
        cvo = V_CONV
        with ExitStack() as es:
            Yo = self.sb(es, "cm_Y", [128, DC, T], BF16)
            bYo = [Buf() for _ in range(DC)]
            PR = [self.sb(es, f"cm_P{k}", [128, 514], F32) for k in range(3)]
            bPR = [Buf() for _ in range(3)]
            CG = [self.sb(es, f"cm_CG{k}", [128, 514], F32) for k in range(2)]
            bCG = [Buf() for _ in range(2)]
            TT = [self.sb(es, f"cm_T{k}", [128, 512], F32) for k in range(2)]
            bTT = [Buf() for _ in range(2)]
            ui = 0
            for c in range(DC):
                wbg = self.load_w("win", c)
                wcg = self.load_w("win", DC + c)
                wh = self.load_w("win", 2 * DC + c)
                w0 = self.vec[:, cvo + c:cvo + c + 1]
                w1 = self.vec[:, cvo + 8 + c:cvo + 8 + c + 1]
                w2 = self.vec[:, cvo + 16 + c:cvo + 16 + c + 1]
                for n in range(NT):
                    uk = ui % 3
                    up = (ui - 1) % 3
                    k2 = ui % 2
                    ui += 1
                    pr, bpr = PR[uk], bPR[uk]
                    cg, bcg = CG[k2], bCG[k2]
                    t, bt = TT[k2], bTT[k2]
                    if n == 0:
                        p1, bp1 = self.bank()
                        self.proj(wcg[0], wcg[1], 0, 2, p1, bp1)
                        P.op("act", lambda e: e.activation(out=cg[:, 0:2], in_=p1[:, 0:2], func=AF.Copy), reads=[bp1], writes=[bcg])
                        p2, bp2 = self.bank()
                        self.proj(wh[0], wh[1], 0, 2, p2, bp2)
                        P.op("dve", lambda e: e.tensor_tensor(out=pr[:, 0:2], in0=cg[:, 0:2], in1=p2[:, 0:2], op=ALU.mult),
                             reads=[bcg, bp2], writes=[bpr])
                    else:
                        P.op("act", lambda e: e.activation(out=pr[:, 0:2], in_=PR[up][:, 512:514], func=AF.Copy),
                             reads=[bPR[up]], writes=[bpr])
                    p1, bp1 = self.bank()
                    self.proj(wcg[0], wcg[1], 2 + n * 512, 512, p1, bp1)
                    P.op("act", lambda e: e.activation(out=cg[:, 2:514], in_=p1[:, :], func=AF.Copy), reads=[bp1], writes=[bcg])
                    p2, bp2 = self.bank()
                    self.proj(wh[0], wh[1], 2 + n * 512, 512, p2, bp2)
                    P.op("dve", lambda e: e.tensor_tensor(out=pr[:, 2:514], in0=cg[:, 2:514], in1=p2[:, :], op=ALU.mult),
                         reads=[bcg, bp2], writes=[bpr])
                    p3, bp3 = self.bank()
                    self.proj(wbg[0], wbg[1], 2 + n * 512, 512, p3, bp3)
                    P.op("act", lambda e: e.activation(out=t[:], in_=pr[:, 0:512], func=AF.Copy, scale=w0),
                         reads=[bpr, self.bvec], writes=[bt])
                    P.op("dve", lambda e: e.scalar_tensor_tensor(out=t[:], in0=pr[:, 1:513], scalar=w1, in1=t[:],
                                                                 op0=ALU.mult, op1=ALU.add),
                         reads=[bpr, bt, self.bvec], writes=[bt])
                    P.op("dve", lambda e: e.scalar_tensor_tensor(out=t[:], in0=pr[:, 2:514], scalar=w2, in1=t[:],
                                                                 op0=ALU.mult, op1=ALU.add),
                         reads=[bpr, bt, self.bvec], writes=[bt])
                    P.op("dve", lambda e: e.tensor_tensor(out=Yo[:, c, n * 512:(n + 1) * 512], in0=t[:], in1=p3[:, :], op=ALU.mult),
                         reads=[bt, bp3], writes=[bYo[c]])
            self.out_proj_residual("wout", Yo, bYo, DC, first=True)
            P.barrier()

    def hgrn_mixer(self):
        P = self.P
        nc = self.nc
        H = 8
        lite = (self.kind == "hg1")
        with ExitStack() as es:
            self.Aall = self.dram("Aall", [128, NCORE * H], F32, "ExternalInput")
            self.Ball = self.dram("Ball", [NCORE * 128, H * 128], F32, "ExternalInput")
            self.Aout = self.dram("Aout", [128, H], F32, "ExternalOutput")
            self.Bout = self.dram("Bout", [128, H * 128], F32, "ExternalOutput")
            G = [self.sb(es, f"hg_G{k}", [128, 512], BF16) for k in range(2)]
            bG = [Buf() for _ in range(2)]
            Y = self.sb(es, "hg_Y", [128, H, T], BF16)
            bY = [Buf() for _ in range(H)]
            S = self.sb(es, "hg_S", [128, H, 128], F32)
            bS = [Buf() for _ in range(H)]
            Aall = self.sb(es, "hg_Aall", [128, NCORE, H], F32)
            bAall = Buf()
            ap_ = self.sb(es, "hg_ap", [128, H], F32)
            bap = Buf()
            om = self.sb(es, "hg_om", [128, NCORE], F32)
            bom = Buf()
            Bm = [self.sb(es, f"hg_Bm{k}", [128, H, 128], F32) for k in range(1)]
            bBm = [Buf() for _ in range(1)]
            dBm = [P.dsem(f"bm{k}") for k in range(1)]
            d_a = P.dsem("aall")
            lbe = self.sb(es, "hg_lbe", [128, 4, H], F32)
            lb = self.sb(es, "hg_lb", [128, H], F32)
            oml = self.sb(es, "hg_oml", [128, H], F32)
            lbm1 = self.sb(es, "hg_lbm1", [128, H], F32)
            lsum = self.sb(es, "hg_lsum", [128, H], F32)
            blb = Buf()
            rmask = self.sb(es, "hg_rmask", [128, 512], F32)
            ident = self.sb(es, "hg_ident", [128, 128], BF16)
            ones128 = self.sb(es, "hg_ones", [128, 128], BF16)
            bcn = Buf()
            blsum = self.sb(es, "hg_blsum", [128, H], F32)
            bblsum = Buf()
            P.op("dve", lambda e: e.memset(rmask[:], 1.0), writes=[bcn])
            P.op("dve", lambda e: e.memset(rmask[:].rearrange("p (c t) -> p c t", t=64)[:, :, 0:1], 0.0), writes=[bcn])
            P.op("dve", lambda e: e.memset(ones128[:], 1.0 / 128.0), writes=[bcn])
            P.op("dve", lambda e: e.memset(blsum[:], 0.0), writes=[bblsum])
            P.op("act", lambda e: e.activation(out=ident[:], in_=self.cst[:, C_IDENT:C_IDENT + 128], func=AF.Copy),
                 reads=[self.bvec], writes=[bcn])
            P.op("act", lambda e: e.activation(out=lbe[:].rearrange("p l h -> p (l h)"), in_=self.vec[:, V_LBRAW:V_LBRAW + 32], func=AF.Exp),
                 reads=[self.bvec], writes=[blb])
            P.op("dve", lambda e: e.tensor_tensor(out=lsum[:], in0=lbe[:, 0, :], in1=lbe[:, 1, :], op=ALU.add), reads=[blb], writes=[blb])
            P.op("dve", lambda e: e.tensor_tensor(out=lsum[:], in0=lsum[:], in1=lbe[:, 2, :], op=ALU.add), reads=[blb], writes=[blb])
            P.op("dve", lambda e: e.tensor_tensor(out=lsum[:], in0=lsum[:], in1=lbe[:, 3, :], op=ALU.add), reads=[blb], writes=[blb])
            P.op("dve", lambda e: e.reciprocal(out=lsum[:], in_=lsum[:]), reads=[blb], writes=[blb])
            P.op("dve", lambda e: e.tensor_scalar(out=lb[:], in0=lbe[:, 1, :], scalar1=self.cst[:, C_LMASK + 1:C_LMASK + 2], scalar2=None, op0=ALU.mult),
                 reads=[blb, self.bvec], writes=[blb])
            for l in (2, 3):
                P.op("dve", lambda e: e.scalar_tensor_tensor(out=lb[:], in0=lbe[:, l, :], scalar=self.cst[:, C_LMASK + l:C_LMASK + l + 1],
                                                             in1=lb[:], op0=ALU.mult, op1=ALU.add),
                     reads=[blb, self.bvec], writes=[blb])
            P.op("dve", lambda e: e.tensor_tensor(out=lb[:], in0=lb[:], in1=lsum[:], op=ALU.mult), reads=[blb], writes=[blb])
            P.op("dve", lambda e: e.tensor_scalar(out=lbm1[:], in0=lb[:], scalar1=-1.0, scalar2=None, op0=ALU.add), reads=[blb], writes=[blb])
            P.op("dve", lambda e: e.tensor_scalar(out=oml[:], in0=lbm1[:], scalar1=-1.0, scalar2=None, op0=ALU.mult), reads=[blb], writes=[blb])
            P.dma("sp", Aall[:].rearrange("p r h -> p (r h)"), self.Aall, d_a, writes=[bAall])
            P.op("dve", lambda e: e.memset(S[:], 0.0), writes=bS)
            P.op("dve", lambda e: e.tensor_scalar(out=om[:], in0=self.cst[:, C_HMASK:C_HMASK + NCORE], scalar1=-1.0, scalar2=1.0,
                                                  op0=ALU.mult, op1=ALU.add), reads=[self.bvec], writes=[bom])
            for r in range(NCORE - 1):
                k = 0
                P.dma("sp", Bm[k][:].rearrange("p h v -> p (h v)"), self.Ball[r * 128:(r + 1) * 128, :], dBm[k], writes=[bBm[k]])
                mr = self.cst[:, C_HMASK + r:C_HMASK + r + 1]
                P.op("dve", lambda e: e.tensor_scalar(out=ap_[:], in0=Aall[:, r, :], scalar1=mr, scalar2=om[:, r:r + 1], op0=ALU.mult, op1=ALU.add),
                     reads=[bAall, self.bvec, bom], writes=[bap])
                P.op("dve", lambda e: e.tensor_scalar(out=Bm[k][:], in0=Bm[k][:], scalar1=mr, scalar2=None, op0=ALU.mult),
                     reads=[bBm[k], self.bvec], writes=[bBm[k]])
                for h in range(H):
                    P.op("dve", lambda e: e.scalar_tensor_tensor(out=S[:, h, :], in0=S[:, h, :], scalar=ap_[:, h:h + 1], in1=Bm[k][:, h, :],
                                                                 op0=ALU.mult, op1=ALU.add),
                         reads=[bS[h], bap, bBm[k]], writes=[bS[h]])
            R2 = lambda nm, shp, dt: [self.sb(es, f"{nm}{k}", shp, dt) for k in range(2)]
            sig, bsig = R2("hg_sig", [128, 512], F32), [Buf(), Buf()]
            qs, bqs = R2("hg_qs", [128, 512], F32), [Buf(), Buf()]
            R1 = lambda nm, shp, dt: [self.sb(es, nm, shp, dt)] * 2
            B1 = lambda: [Buf()] * 2
            kk, bkk = R1("hg_k", [128, 512], F32), B1()
            gg, bgg = R1("hg_g", [128, 512], F32), B1()
            bb, bbb = R1("hg_b", [128, 512], F32), B1()
            e1, be1 = R2("hg_e1", [128, 512], F32), [Buf(), Buf()]
            e2, be2 = R1("hg_e2", [128, 512], F32), B1()
            ebl, bebl = R2("hg_ebl", [128, 8], F32), [Buf(), Buf()]
            qp, bqp = R2("hg_qp", [128, 512], BF16), [Buf(), Buf()]
            kp, bkp = R2("hg_kp", [128, 512], BF16), [Buf(), Buf()]
            ktok, bktok = R2("hg_ktok", [128, 4, 128], BF16), [Buf(), Buf()]
            vtok, bvtok = R2("hg_vtok", [128, 4, 128], BF16), [Buf(), Buf()]
            am, bam = R2("hg_am", [128, 128], BF16), [Buf(), Buf()]
            sdb, bsdb = R2("hg_sdb", [128, 128], BF16), [Buf(), Buf()]
            osq, bosq = R2("hg_osq", [128, 512], BF16), [Buf(), Buf()]
            rr, brr = R2("hg_rr", [128, 512], F32), [Buf(), Buf()]
            yy, byy = R2("hg_yy", [128, 512], F32), [Buf(), Buf()]
            it = 0
            sdi = 0
            ami = 0
            self.rot_banks = (2, 3, 4, 5, 6, 7)
            for h in range(H):
                wq = self.load_w("win", h)
                wf = self.load_w("win", H + h)
                wi = self.load_w("win", 2 * H + h)
                wg = self.load_w("win", 3 * H + h)
                lbh = lb[:, h:h + 1]
                omlh = oml[:, h:h + 1]
                lbm1h = lbm1[:, h:h + 1]
                for n in range(NT):
                    k = it % 2
                    it += 1
                    c0 = 2 + n * 512
                    cs = slice(n * 512, (n + 1) * 512)
                    if not lite:
                        pq, bpq = self.bank()
                        self.proj(wq[0], wq[1], c0, 512, pq, bpq)
                    pf, bpf = self.bank()
                    self.proj(wf[0], wf[1], c0, 512, pf, bpf)
                    if not lite:
                        pg, bpg = self.bank()
                        self.proj(wg[0], wg[1], c0, 512, pg, bpg)
                    pv, bpv = self.bank()
                    for s4 in range(4):
                        for c in range(DC):
                            P.op("pe", lambda e: e.matmul(pv[:, s4 * 128:(s4 + 1) * 128], self.XB[:, c, c0 + s4 * 128:c0 + (s4 + 1) * 128],
                                                          wi[0][:, c, :], start=(c == 0), stop=(c == DC - 1)),
                                 reads=[wi[1], self.bXB[n]], writes=[bpv], signal=(c == DC - 1 and s4 == 3))
                    P.op("act", lambda e: e.activation(out=sig[k][:], in_=pf[:, :], func=AF.Sigmoid), reads=[bpf], writes=[bsig[k]])
                    if not lite:
                        P.op("act", lambda e: e.activation(out=qs[k][:], in_=pq[:, :], func=AF.Silu), reads=[bpq], writes=[bqs[k]])
                        P.op("act", lambda e: e.activation(out=G[k][:], in_=pg[:, :], func=AF.Silu), reads=[bpg], writes=[bG[k]])
                    P.op("act", lambda e: e.activation(out=vtok[k][:].rearrange("p s v -> p (s v)"), in_=pv[:, :], func=AF.Copy),
                         reads=[bpv], writes=[bvtok[k]])
                    P.op("dve", lambda e: e.tensor_scalar(out=kk[k][:], in0=sig[k][:], scalar1=lbm1h, scalar2=omlh, op0=ALU.mult, op1=ALU.add),
                         reads=[bsig[k], blb], writes=[bkk[k]])
                    P.op("act", lambda e: e.activation(out=gg[k][:], in_=sig[k][:], func=AF.Ln, bias=lbh, scale=omlh),
                         reads=[bsig[k], blb], writes=[bgg[k]])
                    P.op("dve", lambda e: e.tensor_tensor_scan(out=bb[k][:], data0=rmask[:], data1=gg[k][:], initial=0.0, op0=ALU.mult, op1=ALU.add),
                         reads=[bgg[k], bcn], writes=[bbb[k]])
                    b3 = bb[k][:].rearrange("p (c t) -> p c t", t=64)
                    P.op("dve", lambda e: e.tensor_tensor(out=e1[k][:].rearrange("p (c t) -> p c t", t=64), in0=b3,
                                                          in1=b3[:, :, 63:64].to_broadcast([128, 8, 64]), op=ALU.subtract),
                         reads=[bbb[k]], writes=[be1[k]])
                    P.op("act", lambda e: e.activation(out=e2[k][:], in_=e1[k][:], func=AF.Exp, scale=-1.0), reads=[be1[k]], writes=[be2[k]])
                    if not lite:
                        P.op("act", lambda e: e.activation(out=e1[k][:], in_=e1[k][:], func=AF.Exp), reads=[be1[k], be2[k]], writes=[be1[k]])
                    P.op("act", lambda e: e.activation(out=ebl[k][:], in_=b3[:, :, 63], func=AF.Exp), reads=[bbb[k]], writes=[bebl[k]])
                    P.op("dve", lambda e: e.tensor_reduce(out=rr[k][:, 0:1], in_=b3[:, :, 63], axis=AX.X, op=ALU.add), reads=[bbb[k]], writes=[brr[k]])
                    P.op("dve", lambda e: e.tensor_tensor(out=blsum[:, h:h + 1], in0=blsum[:, h:h + 1], in1=rr[k][:, 0:1], op=ALU.add),
                         reads=[brr[k], bblsum], writes=[bblsum])
                    if not lite:
                        P.op("dve", lambda e: e.tensor_tensor(out=qp[k][:], in0=qs[k][:], in1=e1[k][:], op=ALU.mult), reads=[bqs[k], be1[k]], writes=[bqp[k]])
                    P.op("dve", lambda e: e.tensor_tensor(out=kp[k][:], in0=kk[k][:], in1=e2[k][:], op=ALU.mult), reads=[bkk[k], be2[k]], writes=[bkp[k]])
                    pt, bpt = self.bank()
                    ptb = pt[:].bitcast(BF16)
                    for s4 in range(4):
                        P.op("pe", lambda e: e.transpose(ptb[:, s4 * 128:(s4 + 1) * 128], kp[k][:, s4 * 128:(s4 + 1) * 128], ident[:]),
                             reads=[bkp[k], bcn], writes=[bpt], signal=(s4 == 3))
                    P.op("act", lambda e: e.activation(out=ktok[k][:].rearrange("p s v -> p (s v)"), in_=ptb[:, 0:512], func=AF.Copy),
                         reads=[bpt], writes=[bktok[k]])
                    po, bpo = self.PS[k], self.bPS[k]
                    for s4 in range(4):
                        sl = slice(s4 * 128, (s4 + 1) * 128)
                        if not lite:
                            pa, bpa = self.bank()
                            P.op("pe", lambda e: e.matmul(pa[:, 0:128], kp[k][:, sl], qp[k][:, sl], start=True, stop=True),
                                 reads=[bkp[k], bqp[k]], writes=[bpa])
                            a_ = ami % 2
                            ami += 1
                            P.op("dve", lambda e: e.tensor_tensor(out=am[a_][:], in0=pa[:, 0:128], in1=self.cst[:, C_MASK2:C_MASK2 + 128], op=ALU.mult),
                                 reads=[bpa, self.bvec], writes=[bam[a_]])
                            P.op("pe", lambda e: e.matmul(po[:, sl], vtok[k][:, s4, :], am[a_][:], start=True, stop=False),
                                 reads=[bvtok[k], bam[a_]], writes=[bpo], signal=False)
                        for half in range(2):
                            ci = s4 * 2 + half
                            d_ = sdi % 2
                            sdi += 1
                            eb = ebl[k][:, ci:ci + 1]
                            if not lite:
                                P.op("dve", lambda e: e.tensor_scalar(out=sdb[d_][:], in0=S[:, h, :], scalar1=eb, scalar2=None, op0=ALU.mult),
                                     reads=[bS[h], bebl[k]], writes=[bsdb[d_]])
                                hs = slice(s4 * 128 + half * 64, s4 * 128 + half * 64 + 64)
                                P.op("pe", lambda e: e.matmul(po[:, hs], sdb[d_][:], qp[k][:, hs], start=False, stop=(half == 1)),
                                     reads=[bsdb[d_], bqp[k]], writes=[bpo], signal=True)
                            pu, bpu = self.bank()
                            rows = slice(half * 64, half * 64 + 64)
                            P.op("pe", lambda e: e.matmul(pu[:, 0:128], ktok[k][rows, s4, :], vtok[k][rows, s4, :], start=True, stop=True),
                                 reads=[bktok[k], bvtok[k]], writes=[bpu])
                            P.op("dve", lambda e: e.scalar_tensor_tensor(out=S[:, h, :], in0=S[:, h, :], scalar=eb, in1=pu[:, 0:128],
                                                                         op0=ALU.mult, op1=ALU.add),
                                 reads=[bS[h], bebl[k], bpu], writes=[bS[h]])
                    if lite:
                        continue
                    P.op("act", lambda e: e.activation(out=osq[k][:], in_=po[:, :], func=AF.Square), reads=[bpo], writes=[bosq[k]])
                    pm, bpm = self.bank()
                    P.op("pe", lambda e: e.matmul(pm[:, :], ones128[:], osq[k][:], start=True, stop=True), reads=[bcn, bosq[k]], writes=[bpm])
                    P.op("dve", lambda e: e.tensor_scalar(out=rr[k][:], in0=pm[:, :], scalar1=RMS_EPS, scalar2=None, op0=ALU.add),
                         reads=[bpm], writes=[brr[k]])
                    P.op("act", lambda e: e.activation(out=rr[k][:], in_=rr[k][:], func=AF.Sqrt), reads=[brr[k]], writes=[brr[k]])
                    P.op("dve", lambda e: e.reciprocal(out=rr[k][:], in_=rr[k][:]), reads=[brr[k]], writes=[brr[k]])
                    P.op("dve", lambda e: e.tensor_tensor(out=yy[k][:], in0=po[:, :], in1=rr[k][:], op=ALU.mult), reads=[bpo, brr[k]], writes=[byy[k]])
                    P.op("dve", lambda e: e.scalar_tensor_tensor(out=Y[:, h, cs], in0=yy[k][:], scalar=self.vec[:, V_NORMG + h:V_NORMG + h + 1],
                                                                 in1=G[k][:], op0=ALU.mult, op1=ALU.mult),
                         reads=[byy[k], self.bvec, bG[k]], writes=[bY[h]])
            P.op("act", lambda e: e.activation(out=blsum[:], in_=blsum[:], func=AF.Exp), reads=[bblsum], writes=[bblsum])
            d_o = P.dsem("hgo")
            P.dma("sp", self.Aout, blsum[:], d_o, reads=[bblsum], writes=[self.bout])
            P.dma("sp", self.Bout, S[:].rearrange("p h v -> p (h v)"), d_o, reads=bS, writes=[self.bout])
            self.rot_banks = (0, 1, 2, 3, 4, 5, 6, 7)
            if not lite:
                self.out_proj_residual("wout", Y, bY, DC, first=True)
            P.barrier()

    def moba_qkv(self):
        P = self.P
        H = 8
        with ExitStack() as es:
            self.pos_d = self.dram("pos", [1, T], I32, "ExternalInput")
            self.QTo = self.dram("QT", [128, H * T], BF16, "ExternalOutput")
            self.KTo = self.dram("KT", [128, H * T], BF16, "ExternalOutput")
            self.Vo = self.dram("V", [128, 16 * 1024], BF16, "ExternalOutput")
            self.KMo = self.dram("KM", [128, H * 8], F32, "ExternalOutput")
            posi = self.sb(es, "mq_posi", [32, T], I32)
            ang = self.sb(es, "mq_ang", [32, T], F32)
            tmp = self.sb(es, "mq_tmp", [32, T], F32)
            Ct = self.sb(es, "mq_C", [32, T], F32)
            St = self.sb(es, "mq_S", [32, T], F32)
            npi = self.sb(es, "mq_npi", [32, 1], F32)
            km = self.sb(es, "mq_km", [128, H, 8], F32)
            btab, bkm = Buf(), Buf()
            d_p = P.dsem("pos")
            P.dma("sp", posi[:], self.pos_d.partition_broadcast(32), d_p, writes=[btab])
            P.op("dve", lambda e: e.tensor_copy(out=ang[:], in_=posi[:]), reads=[btab], writes=[btab])
            P.op("dve", lambda e: e.memset(npi[:], -math.pi), writes=[btab])
            P.op("dve", lambda e: e.tensor_scalar(out=ang[:], in0=ang[:], scalar1=self.cst[0:32, C_INVF:C_INVF + 1], scalar2=None, op0=ALU.mult),
                 reads=[btab, self.bvec], writes=[btab])
            ki = self.sb(es, "mq_ki", [32, T], I32)
            kfl = self.sb(es, "mq_kfl", [32, T], F32)
            for (dst_, off_) in ((St, 0.5), (Ct, 0.75)):
                P.op("dve", lambda e: e.tensor_scalar(out=tmp[:], in0=ang[:], scalar1=1.0 / (2 * math.pi), scalar2=off_, op0=ALU.mult, op1=ALU.add),
                     reads=[btab], writes=[btab])
                P.op("dve", lambda e: e.tensor_copy(out=ki[:], in_=tmp[:]), reads=[btab], writes=[btab])
                P.op("dve", lambda e: e.tensor_copy(out=kfl[:], in_=ki[:]), reads=[btab], writes=[btab])
                P.op("dve", lambda e: e.tensor_tensor(out=tmp[:], in0=tmp[:], in1=kfl[:], op=ALU.subtract), reads=[btab], writes=[btab])
                P.op("dve", lambda e: e.tensor_scalar(out=kfl[:], in0=tmp[:], scalar1=0.0, scalar2=None, op0=ALU.is_lt), reads=[btab], writes=[btab])
                P.op("dve", lambda e: e.tensor_tensor(out=tmp[:], in0=tmp[:], in1=kfl[:], op=ALU.add), reads=[btab], writes=[btab])
                P.op("act", lambda e: e.activation(out=dst_[:], in_=tmp[:], func=AF.Sin, bias=npi[:, 0:1], scale=2 * math.pi), reads=[btab], writes=[btab])
            P.op("dve", lambda e: e.tensor_scalar(out=St[:], in0=St[:], scalar1=self.cst[0:32, C_SGN:C_SGN + 1], scalar2=None, op0=ALU.mult),
                 reads=[btab, self.bvec], writes=[btab])
            R2 = lambda nm, shp, dt: [self.sb(es, f"{nm}{k}", shp, dt) for k in range(2)]
            t1, bt1 = R2("mq_t1", [32, 512], F32), [Buf(), Buf()]
            t2, bt2 = R2("mq_t2", [32, 512], F32), [Buf(), Buf()]
            kf, bkf = R2("mq_kf", [128, 512], F32), [Buf(), Buf()]
            ob, bob = R2("mq_ob", [128, 512], BF16), [Buf(), Buf()]
            vb, bvb = R2("mq_vb", [128, 4, 128], BF16), [Buf(), Buf()]
            dob = [P.dsem(f"ob{k}") for k in range(2)]
            dvb = [P.dsem(f"vb{k}") for k in range(2)]
            Vov = self.Vo.rearrange("p (t f) -> p t f", f=1024)
            it = 0
            for h in range(H):
                for qk in range(2):
                    w = self.load_w("win", qk * H + h)
                    wp = self.load_w("win", 3 * H + qk * H + h)
                    dst = self.QTo if qk == 0 else self.KTo
                    for n in range(NT):
                        k = it % 2
                        it += 1
                        c0 = 2 + n * 512
                        cs = slice(n * 512, (n + 1) * 512)
                        pa, bpa = self.bank()
                        self.proj(w[0], w[1], c0, 512, pa, bpa)
                        pb, bpb = self.bank()
                        self.proj(wp[0], wp[1], c0, 512, pb, bpb, m=32)
                        P.op("dve", lambda e: e.tensor_tensor(out=t1[k][:], in0=pa[0:32, :], in1=Ct[:, cs], op=ALU.mult), reads=[bpa, btab], writes=[bt1[k]])
                        P.op("dve", lambda e: e.tensor_tensor(out=t2[k][:], in0=pb[0:32, :], in1=St[:, cs], op=ALU.mult), reads=[bpb, btab], writes=[bt2[k]])
                        P.op("dve", lambda e: e.tensor_tensor(out=kf[k][0:32, :], in0=t1[k][:], in1=t2[k][:], op=ALU.add), reads=[bt1[k], bt2[k]], writes=[bkf[k]])
                        P.op("act", lambda e: e.activation(out=kf[k][32:64, :], in_=pa[32:64, :], func=AF.Copy), reads=[bpa], writes=[bkf[k]])
                        P.op("act", lambda e: e.activation(out=kf[k][64:128, :], in_=pa[64:128, :], func=AF.Copy), reads=[bpa], writes=[bkf[k]])
                        P.op("act", lambda e: e.activation(out=ob[k][:], in_=kf[k][:], func=AF.Copy), reads=[bkf[k]], writes=[bob[k]])
                        if qk == 1:
                            P.op("dve", lambda e: e.tensor_reduce(out=km[:, h, 2 * n:2 * n + 2], in_=kf[k][:].rearrange("p (b t) -> p b t", t=256),
                                                                  axis=AX.X, op=ALU.add), reads=[bkf[k]], writes=[bkm])
                        P.dma("sp", dst[:, h * T + n * 512:h * T + (n + 1) * 512], ob[k][:], dob[k], reads=[bob[k]], writes=[self.bout])
                wv = self.load_w("win", 2 * H + h)
                for n in range(NT):
                    k = it % 2
                    it += 1
                    c0 = 2 + n * 512
                    pv, bpv = self.bank()
                    for s4 in range(4):
                        for c in range(DC):
                            P.op("pe", lambda e: e.matmul(pv[:, s4 * 128:(s4 + 1) * 128], self.XB[:, c, c0 + s4 * 128:c0 + (s4 + 1) * 128],
                                                          wv[0][:, c, :], start=(c == 0), stop=(c == DC - 1)),
                                 reads=[wv[1], self.bXB[n]], writes=[bpv], signal=(c == DC - 1 and s4 == 3))
                    P.op("act", lambda e: e.activation(out=vb[k][:].rearrange("p s v -> p (s v)"), in_=pv[:, :], func=AF.Copy), reads=[bpv], writes=[bvb[k]])
                    P.dma("sp", Vov[:, n * 4:(n + 1) * 4, h * 128:(h + 1) * 128], vb[k][:], dvb[k], reads=[bvb[k]], writes=[self.bout])
            P.op("dve", lambda e: e.tensor_scalar(out=km[:], in0=km[:], scalar1=1.0 / 256.0, scalar2=None, op0=ALU.mult), reads=[bkm], writes=[bkm])
            d_k = P.dsem("kmo")
            P.dma("sp", self.KMo, km[:].rearrange("p h b -> p (h b)"), d_k, reads=[bkm], writes=[self.bout])
            P.barrier()

    def moba_attn(self):
        P = self.P
        H = 8
        NS = 72
        SCALE = 1.0 / math.sqrt(128.0)
        with ExitStack() as es:
            self.QTi = self.dram("QT", [128, H * T], BF16, "ExternalInput")
            self.Kall = self.dram("Kall", [H * 128, NS * 256], BF16, "ExternalInput")
            self.Vall = self.dram("Vall", [H * 128, NS * 256], BF16, "ExternalInput")
            self.KMall = self.dram("KMall", [128, H * NS], F32, "ExternalInput")
            self.mcst_d = self.dram("mcst", [128, 3 * 8 * NS + 4 * 512], F32, "ExternalInput")
            QT = self.sb(es, "ma_QT", [128, H, T], BF16)
            bQT = Buf()
            d_q = P.dsem("qt")
            P.dma("sp", QT[:].rearrange("p h t -> p (h t)"), self.QTi, d_q, writes=[bQT])
            mc = self.sb(es, "ma_mc", [128, 3, 8, NS], F32)
            caus32 = self.sb(es, "ma_c32", [128, 512], F32)
            caus = self.sb(es, "ma_caus", [128, 4, 512], BF16)
            bmc = Buf()
            d_m = P.dsem("mc")
            d_m2 = P.dsem("mc2")
            P.dma("sp", mc[:].rearrange("p a l s -> p (a l s)"), self.mcst_d[:, 0:3 * 8 * NS], d_m, writes=[bmc])
            for v in range(4):
                P.dma("sp", caus32[:], self.mcst_d[:, 3 * 8 * NS + v * 512:3 * 8 * NS + (v + 1) * 512], d_m2, writes=[bmc])
                P.op("act", lambda e: e.activation(out=caus[:, v, :], in_=caus32[:], func=AF.Copy), reads=[bmc], writes=[bmc])
            kmf = self.sb(es, "ma_kmf", [128, H, NS], F32)
            kmb = self.sb(es, "ma_kmb", [128, H, NS], BF16)
            d_km = P.dsem("km")
            bkm = Buf()
            P.dma("sp", kmf[:].rearrange("p h s -> p (h s)"), self.KMall, d_km, writes=[bkm])
            P.op("act", lambda e: e.activation(out=kmb[:], in_=kmf[:], func=AF.Copy), reads=[bkm], writes=[bkm])
            ident = self.sb(es, "ma_ident", [128, 128], BF16)
            ones = self.sb(es, "ma_ones", [128, 128], BF16)
            Esel = self.sb(es, "ma_Esel", [NS, NS, 128], BF16)
            bcn = Buf()
            P.op("act", lambda e: e.activation(out=ident[:], in_=self.cst[:, C_IDENT:C_IDENT + 128], func=AF.Copy), reads=[self.bvec], writes=[bcn])
            P.op("dve", lambda e: e.memset(ones[:], 1.0), writes=[bcn])
            P.op("dve", lambda e: e.tensor_copy(out=Esel[:], in_=self.cst[0:NS, C_IDENT:C_IDENT + NS].unsqueeze(2).to_broadcast([NS, NS, 128])),
                 reads=[self.bvec], writes=[bcn])
            maskT = [self.sb(es, f"ma_maskT{k}", [NS, T], BF16) for k in range(2)]
            bmaskT = [Buf(), Buf()]
            R2 = lambda nm, shp, dt: [self.sb(es, f"{nm}{k}", shp, dt) for k in range(2)]
            gm, bgm = R2("ma_gm", [128, NS], F32), [Buf(), Buf()]
            t8, bt8 = R2("ma_t8", [128, 8], F32), [Buf(), Buf()]
            al, bal = R2("ma_al", [128, NS], F32), [Buf(), Buf()]
            alb, balb = R2("ma_alb", [128, NS], BF16), [Buf(), Buf()]
            NKS = 3
            KS = [self.sb(es, f"ma_KS{k}", [128, 1024], BF16) for k in range(NKS)]
            VS = [self.sb(es, f"ma_VS{k}", [128, 8, 128], BF16) for k in range(NKS)]
            bKS = [Buf() for _ in range(NKS)]
            bVS = [Buf() for _ in range(NKS)]
            dKS = [P.dsem(f"ks{k}") for k in range(NKS)]
            dVS = [P.dsem(f"vs{k}") for k in range(NKS)]
            NPT = 4
            PT = [self.sb(es, f"ma_PT{k}", [128, 512], BF16) for k in range(NPT)]
            bPT = [Buf() for _ in range(NPT)]
            rden, brden = R2("ma_rden", [128, 512], F32), [Buf(), Buf()]
            Y = self.XB
            bYh = [Buf() for _ in range(H)]
            self.rot_banks = (4, 5, 6, 7)
            gi = 0
            ksi = 0
            pti = 0
            for h in range(H):
                mk = h % 2
                for s16 in range(16):
                    g_ = gi % 2
                    gi += 1
                    lb_ = s16 // 2
                    qs_ = slice(s16 * 128, (s16 + 1) * 128)
                    pg, bpg = self.bank()
                    P.op("pe", lambda e: e.matmul(pg[:, 0:NS], QT[:, h, qs_], kmb[:, h, :], start=True, stop=True), reads=[bQT, bkm], writes=[bpg])
                    P.op("dve", lambda e: e.tensor_tensor(out=gm[g_][:], in0=pg[:, 0:NS], in1=mc[:, 0, lb_, :], op=ALU.add), reads=[bpg, bmc], writes=[bgm[g_]])
                    P.op("dve", lambda e: e.max(out=t8[g_][:], in_=gm[g_][:]), reads=[bgm[g_]], writes=[bt8[g_]])
                    P.op("dve", lambda e: e.tensor_scalar(out=al[g_][:], in0=gm[g_][:], scalar1=t8[g_][:, 2:3], scalar2=None, op0=ALU.is_ge),
                         reads=[bgm[g_], bt8[g_]], writes=[bal[g_]])
                    P.op("dve", lambda e: e.tensor_tensor(out=al[g_][:], in0=al[g_][:], in1=mc[:, 1, lb_, :], op=ALU.mult), reads=[bal[g_], bmc], writes=[bal[g_]])
                    P.op("dve", lambda e: e.tensor_tensor(out=al[g_][:], in0=al[g_][:], in1=mc[:, 2, lb_, :], op=ALU.add), reads=[bal[g_], bmc], writes=[bal[g_]])
                    P.op("dve", lambda e: e.tensor_scalar(out=alb[g_][:], in0=al[g_][:], scalar1=-1.0, scalar2=30000.0, op0=ALU.add, op1=ALU.mult),
                         reads=[bal[g_]], writes=[balb[g_]])
                    pt_, bpt_ = self.bank()
                    ptb = pt_[:].bitcast(BF16)
                    P.op("pe", lambda e: e.transpose(ptb[0:NS, 0:128], alb[g_][:], ident[:]), reads=[balb[g_], bcn], writes=[bpt_])
                    P.op("act", lambda e: e.activation(out=maskT[mk][:, qs_], in_=ptb[0:NS, 0:128], func=AF.Copy), reads=[bpt_], writes=[bmaskT[mk]])
                for qt in range(2):
                    pend = []
                    LA = 2
                    O = [self.PS[0], self.PS[1]]
                    bO = [self.bPS[0], self.bPS[1]]
                    DN = [self.PS[2], self.PS[3]]
                    bDN = [self.bPS[2], self.bPS[3]]
                    def need(j, hf):
                        if j < 8:
                            return j in (qt * 4 + hf * 2, qt * 4 + hf * 2 + 1)
                        return (j - 8) < 32 * qt + 16 * hf + 16
                    ulist = {hf: [(j, kt2) for j in range(NS) for kt2 in range(2) if need(j, hf)] for hf in range(2)}
                    for g4 in range(NS // 4):
                        if not any(need(g4 * 4 + j4, hf) for j4 in range(4) for hf in range(2)):
                            continue
                        ks = ksi % NKS
                        ksi += 1
                        P.dma("sp", KS[ks][:], self.Kall[h * 128:(h + 1) * 128, g4 * 1024:(g4 + 1) * 1024], dKS[ks], writes=[bKS[ks]])
                        P.dma("sp", VS[ks][:].rearrange("p t v -> p (t v)"), self.Vall[h * 128:(h + 1) * 128, g4 * 1024:(g4 + 1) * 1024], dVS[ks], writes=[bVS[ks]])
                        for j4 in range(4):
                            j = g4 * 4 + j4
                            for kt2 in range(2):
                                kti = j4 * 2 + kt2
                                for hf in range(2):
                                    if not need(j, hf):
                                        continue
                                    first = ((j, kt2) == ulist[hf][0])
                                    last = ((j, kt2) == ulist[hf][-1])
                                    q0 = qt * 1024 + hf * 512
                                    ps, bps = self.bank()
                                    lbs = (qt * 4 + hf * 2, qt * 4 + hf * 2 + 1)
                                    diag = j in lbs
                                    P.op("pe", lambda e: e.matmul(ps[:, :], KS[ks][:, kti * 128:(kti + 1) * 128], QT[:, h, q0:q0 + 512], start=True, stop=False),
                                         reads=[bKS[ks], bQT], writes=[bps], signal=False)
                                    P.op("pe", lambda e: e.matmul(ps[:, :], Esel[:, j, :], maskT[mk][:, q0:q0 + 512], start=False, stop=(not diag)),
                                         reads=[bcn, bmaskT[mk]], writes=[bps], signal=(not diag))
                                    if diag:
                                        v = kt2 * 2 + (j - lbs[0])
                                        P.op("pe", lambda e: e.matmul(ps[:, :], ident[:], caus[:, v, :], start=False, stop=True),
                                             reads=[bcn, bmc], writes=[bps])
                                    p_ = pti % NPT
                                    pti += 1
                                    P.op("act", lambda e: e.activation(out=PT[p_][:], in_=ps[:, :], func=AF.Exp, scale=SCALE), reads=[bps], writes=[bPT[p_]])
                                    def _pv(ks=ks, kti=kti, p_=p_, hf=hf, first=first, last=last):
                                        P.op("pe", lambda e: e.matmul(O[hf][:, :], VS[ks][:, kti, :], PT[p_][:], start=first, stop=last),
                                             reads=[bVS[ks], bPT[p_]], writes=[bO[hf]], signal=last)
                                        P.op("pe", lambda e: e.matmul(DN[hf][:, :], ones[:], PT[p_][:], start=first, stop=last),
                                             reads=[bcn, bPT[p_]], writes=[bDN[hf]], signal=True)
                                    pend.append(_pv)
                                    if len(pend) > LA:
                                        pend.pop(0)()
                    while pend:
                        pend.pop(0)()
                    for hf in range(2):
                        q0 = qt * 1024 + hf * 512
                        r_ = hf
                        P.op("dve", lambda e: e.reciprocal(out=rden[r_][:], in_=DN[hf][:, :]), reads=[bDN[hf]], writes=[brden[r_]])
                        P.op("dve", lambda e: e.tensor_tensor(out=Y[:, h, 2 + q0:2 + q0 + 512], in0=O[hf][:, :], in1=rden[r_][:], op=ALU.mult),
                             reads=[bO[hf], brden[r_]], writes=[bYh[h]])
            self.rot_banks = (0, 1, 2, 3, 4, 5, 6, 7)
            self.out_proj_residual("wout", Y, bYh, DC, first=True, coff=2)
            P.barrier()


_PROGS = {}


def prog(kind):
    if kind not in _PROGS:
        _PROGS[kind] = Builder(kind).build()
    return _PROGS[kind]


def shard_xT(x_full):
    xT = np.ascontiguousarray(x_full.T)
    xTp = np.concatenate([np.zeros((D, 2), np.float32), xT], axis=1)
    return [np.ascontiguousarray(xTp[:, c * T:c * T + T + 2]) for c in range(NCORE)]


def launch(kind, maps):
    res = run_bass_kernel_spmd(prog(kind), maps, core_ids=list(range(NCORE)))
    return res.results


def gather_x(results):
    return np.ascontiguousarray(np.concatenate([r["outT"] for r in results], axis=1).T)


def run_ffn(inp, i, x):
    xs = shard_xT(x)
    vec = make_vec(inp[f"l{i}_ln2_g"], inp[f"l{i}_ln2_b"], conv=inp[f"l{i}_ffn_conv"])
    wup = tile_w(inp[f"l{i}_ffn_w_up"])
    wdn = tile_w(inp[f"l{i}_ffn_w_down"])
    maps = [dict(xT=xs[c], vec=vec, cst=make_cst(c), wup=wup, wdn=wdn) for c in range(NCORE)]
    return gather_x(launch("ff", maps))


def run_conv(inp, i, x):
    xs = shard_xT(x)
    vec = make_vec(inp[f"l{i}_ln1_g"], inp[f"l{i}_ln1_b"], conv=inp[f"l{i}_mix_conv"])
    win = tile_w(inp[f"l{i}_mix_w_in"])
    wout = tile_w(inp[f"l{i}_mix_w_out"])
    maps = [dict(xT=xs[c], vec=vec, cst=make_cst(c), win=win, wout=wout) for c in range(NCORE)]
    return gather_x(launch("cv", maps))


def run_hgrn(inp, i, x):
    xs = shard_xT(x)
    vec = make_vec(inp[f"l{i}_ln1_g"], inp[f"l{i}_ln1_b"], norm_g=inp[f"l{i}_mix_norm_g"], lbraw=inp["hgrn_lower_bounds"])
    win = tile_w(inp[f"l{i}_mix_w_in"])
    wout = tile_w(inp[f"l{i}_mix_w_out"])
    Aall = np.zeros((128, NCORE * 8), np.float32)
    Ball = np.zeros((NCORE * 128, 1024), np.float32)
    out = None
    for phase in range(2):
        maps = [dict(xT=xs[c], vec=vec, cst=make_cst(c, layer=i), win=win, Aall=Aall, Ball=Ball) for c in range(NCORE)]
        if phase == 1:
            for m in maps:
                m["wout"] = wout
        res = launch("hg1" if phase == 0 else "hg", maps)
        if phase == 0:
            Aall = np.ascontiguousarray(np.concatenate([r["Aout"] for r in res], axis=1))
            Ball = np.ascontiguousarray(np.concatenate([r["Bout"] for r in res], axis=0))
        else:
            out = gather_x(res)
    return out


def zz_block(c, s_):
    return 8 * s_ + (c if s_ % 2 == 0 else 7 - c)


def run_moba(inp, i, x):
    H = 8
    NSL = 72
    xs = shard_xT(x)
    vec = make_vec(inp[f"l{i}_ln1_g"], inp[f"l{i}_ln1_b"])
    W = inp[f"l{i}_mix_w_in"]
    perm = []
    for qk in range(2):
        for h in range(H):
            base = qk * 1024 + h * 128
            Wp = np.zeros((D, 128), np.float32)
            Wp[:, 0:16] = W[:, base + 16:base + 32]
            Wp[:, 16:32] = W[:, base:base + 16]
            perm.append(Wp)
    win = tile_w(np.concatenate([W] + perm, axis=1))
    pos = inp["positions"].astype(np.int32)
    maps = [dict(xT=xs[c], vec=vec, cst=make_cst(c), win=win, pos=np.ascontiguousarray(pos[:, c * T:(c + 1) * T])) for c in range(NCORE)]
    res = launch("mo1", maps)
    Qg = np.concatenate([np.asarray(r["QT"]).reshape(128, H, 8, 256) for r in res], axis=2)
    Kg = np.concatenate([np.asarray(r["KT"]).reshape(128, H, 8, 256).transpose(1, 0, 2, 3) for r in res], axis=2)
    Vg = np.concatenate([np.asarray(r["V"]).reshape(128, 8, 2, H, 128).transpose(3, 0, 1, 2, 4) for r in res], axis=2)
    KMg = np.concatenate([np.asarray(r["KM"]).reshape(128, H, 8) for r in res], axis=2)
    xTfull = np.ascontiguousarray(x.T).reshape(D, 64, 256)
    wout = tile_w(inp[f"l{i}_mix_w_out"])
    caus = np.zeros((4, 128, 512), np.float32)
    p_ = np.arange(128)[:, None]
    qi = np.arange(256)[None, :]
    for kt2 in range(2):
        for posb in range(2):
            caus[kt2 * 2 + posb][:, posb * 256:(posb + 1) * 256] = np.where(kt2 * 128 + p_ > qi, -30000.0, 0.0)
    maps = []
    own_all = []
    for c in range(NCORE):
        own = [zz_block(c, s_) for s_ in range(8)]
        own_all.append(own)
        pc = np.array(own + list(range(64)))
        Kall = np.ascontiguousarray(Kg[:, :, pc, :]).reshape(H * 128, NSL * 256)
        Vall = np.ascontiguousarray(Vg[:, :, pc]).reshape(H * 128, NSL * 256)
        KMall = np.ascontiguousarray(KMg[:, :, pc]).reshape(128, H * NSL)
        QT = np.ascontiguousarray(Qg[:, :, np.array(own), :]).reshape(128, H * T)
        xT = np.concatenate([np.zeros((D, 2), np.float32), xTfull[:, np.array(own), :].reshape(D, T)], axis=1)
        past = np.zeros((8, NSL), np.float32)
        ownm = np.zeros((8, NSL), np.float32)
        for s_ in range(8):
            past[s_, 8:] = (np.arange(64) < own[s_]).astype(np.float32)
            ownm[s_, s_] = 1.0
        pastbias = np.where(past > 0, 0.0, -1e30).astype(np.float32)
        row = np.concatenate([pastbias.reshape(-1), past.reshape(-1), ownm.reshape(-1)])
        mcst = np.concatenate([np.broadcast_to(row[None, :], (128, 3 * 8 * NSL)), caus.transpose(1, 0, 2).reshape(128, 2048)], axis=1).astype(np.float32)
        maps.append(dict(xT=np.ascontiguousarray(xT), vec=vec, cst=make_cst(c), QT=QT, Kall=Kall, Vall=Vall, KMall=KMall,
                         mcst=np.ascontiguousarray(mcst), wout=wout))
    res2 = launch("mo2", maps)
    out = np.zeros((64, 256, D), np.float32)
    for c in range(NCORE):
        oc = np.asarray(res2[c]["outT"]).T.reshape(8, 256, D)
        for s_ in range(8):
            out[own_all[c][s_]] = oc[s_]
    return np.ascontiguousarray(out.reshape(64 * 256, D))


def kernel(**inputs):
    inp = {k: np.asarray(v) for k, v in inputs.items()}
    x = np.ascontiguousarray(inp["x"][0])
    for i in range(DEPTH):
        kind = i % 3
        if kind == 0:
            x = run_hgrn(inp, i, x)
        elif kind == 1:
            x = run_moba(inp, i, x)
        else:
            x = run_conv(inp, i, x)
        x = run_ffn(inp, i, x)
    return x[None].astype(np.float32)
```

```python
import math
import numpy as np
from contextlib import ExitStack
import concourse.bass as bass
import concourse.mybir as mybir
from concourse.bass_utils import run_bass_kernel_spmd

F32 = mybir.dt.float32
BF16 = mybir.dt.bfloat16
I32 = mybir.dt.int32
AF = mybir.ActivationFunctionType
ALU = mybir.AluOpType
AX = mybir.AxisListType

NCORE = 8
T = 2048
D = 1024
DC = 8
DFF = 2816
FC = 22
DEPTH = 4
ALPHA = (2.0 * DEPTH) ** 0.25
LN_EPS = 1e-5
RMS_EPS = 1e-6
NT = T // 512


class Buf:
    __slots__ = ("name", "w", "r")

    def __init__(self, name="b"):
        self.name = name
        self.w = None
        self.r = {}


class DSem:
    def __init__(self, key, h):
        self.key = key
        self.h = h
        self.cnt = 0


class Prog:
    ENG = ("pe", "act", "dve", "pool", "sp")

    def __init__(self, nc, es):
        self.nc = nc
        self.es = es
        self.eng = dict(pe=nc.tensor, act=nc.scalar, dve=nc.vector, pool=nc.gpsimd, sp=nc.sync)
        self.semh = {}
        self.cnt = {}
        self.known = {k: {} for k in self.ENG}
        for k in self.ENG:
            self.semh[k] = es.enter_context(nc.semaphore("s_" + k))
            self.cnt[k] = 0
        self.dsems = []
        self.nwait = 0
        self.nins = 0

    def dsem(self, name):
        h = self.es.enter_context(self.nc.semaphore("d_" + name))
        d = DSem("d_" + name, h)
        self.semh[d.key] = h
        self.dsems.append(d)
        return d

    def _wait(self, e, key, val):
        if val <= 0:
            return
        if self.known[e].get(key, 0) >= val:
            return
        if key == e:
            if e == "pe":
                return
            if val > self.cnt[e]:
                return
        self.eng[e].wait_ge(self.semh[key], val)
        self.nwait += 1
        self.known[e][key] = val

    def deps(self, e, reads, writes):
        need = {}
        for b in reads:
            if b.w is not None:
                k, v = b.w
                if need.get(k, 0) < v:
                    need[k] = v
        for b in writes:
            if b.w is not None:
                k, v = b.w
                if need.get(k, 0) < v:
                    need[k] = v
            for k, v in b.r.items():
                if need.get(k, 0) < v:
                    need[k] = v
        for k, v in need.items():
            self._wait(e, k, v)

    def op(self, e, fn, reads=(), writes=(), signal=True):
        self.deps(e, reads, writes)
        ins = fn(self.eng[e])
        self.nins += 1
        if signal:
            self.cnt[e] += 1
            ins.then_inc(self.semh[e], 1)
            mark = (e, self.cnt[e])
        else:
            mark = (e, self.cnt[e] + 1)
        for b in writes:
            b.w = mark
            b.r = {}
        for b in reads:
            if b.r.get(e, 0) < mark[1]:
                b.r[e] = mark[1]
        return ins

    def _mark_async(self, ds, reads, writes):
        mark = (ds.key, ds.cnt)
        for b in writes:
            b.w = mark
            b.r = {}
        for b in reads:
            b.r[ds.key] = ds.cnt

    def dma(self, q, out, in_, ds, reads=(), writes=(), **kw):
        self.deps(q, reads, writes)
        ds.cnt += 16
        ins = self.eng[q].dma_start(out=out, in_=in_, **kw)
        ins.then_inc(ds.h, 16)
        self._mark_async(ds, reads, writes)
        return ins

    def allgather(self, in_ap, out_ap, ds, reads=(), writes=()):
        q = "pool"
        self.deps(q, reads, writes)
        ds.cnt += 1
        ins = self.nc.gpsimd.collective_compute(
            "AllGather", ALU.bypass, replica_groups=[list(range(NCORE))],
            ins=[in_ap.opt()], outs=[out_ap.opt()])
        ins.then_inc(ds.h, 1)
        self._mark_async(ds, reads, writes)
        return ins

    def barrier(self, engines=None):
        engines = engines or self.ENG
        for e in engines:
            for k in self.ENG:
                if k != e:
                    self._wait(e, k, self.cnt[k])
            for d in self.dsems:
                self._wait(e, d.key, d.cnt)


def tile_w(W):
    Din, Fo = W.shape
    Wt = W.reshape(Din // 128, 128, Fo // 128, 128).transpose(2, 1, 0, 3)
    return np.ascontiguousarray(Wt).reshape(Fo // 128 * 128, Din)


def vec_cols(v):
    return np.ascontiguousarray(v.reshape(-1, 128).T)


V_LNG = 0
V_LNB = 8
V_CONV = 16
V_NORMG = 148
V_LBRAW = 156
NVEC = 188
C_HMASK = 0
C_LMASK = 8
C_MASK2 = 12
C_IDENT = 140
C_INVF = 268
C_SGN = 269
NCST = 270


def conv_cols(cw):
    return np.concatenate([vec_cols(cw[k]) for k in range(cw.shape[0])], axis=1)


def make_vec(ln_g, ln_b, conv=None, norm_g=None, lbraw=None):
    v = np.zeros((128, NVEC), np.float32)
    v[:, V_LNG:V_LNG + 8] = vec_cols(ln_g)
    v[:, V_LNB:V_LNB + 8] = vec_cols(ln_b)
    if conv is not None:
        c = conv_cols(conv)
        v[:, V_CONV:V_CONV + c.shape[1]] = c
    if norm_g is not None:
        v[:, V_NORMG:V_NORMG + 8] = vec_cols(norm_g)
    if lbraw is not None:
        v[:, V_LBRAW:V_LBRAW + 32] = np.concatenate([vec_cols(lbraw[l]) for l in range(DEPTH)], axis=1)
    return v


def make_cst(core, layer=0):
    c = np.zeros((128, NCST), np.float32)
    for r in range(NCORE):
        c[:, C_HMASK + r] = 1.0 if r < core else 0.0
    for l in range(DEPTH):
        c[:, C_LMASK + l] = 1.0 if 1 <= l <= layer else 0.0
    s_ = np.arange(128)[:, None]
    t_ = np.arange(128)[None, :]
    c[:, C_MASK2:C_MASK2 + 128] = ((s_ // 64 == t_ // 64) & (s_ <= t_)).astype(np.float32)
    c[:, C_IDENT:C_IDENT + 128] = np.eye(128, dtype=np.float32)
    invf = (1.0 / (500000.0 ** (np.arange(0, 32, 2, dtype=np.float32) / np.float32(32.0)))).astype(np.float32)
    c[0:32, C_INVF] = np.concatenate([invf, invf])
    c[0:16, C_SGN] = -1.0
    c[16:32, C_SGN] = 1.0
    return c


class Builder:
    def __init__(self, kind):
        self.kind = kind
        self.nc = bass.Bass("TRN2", target_bir_lowering=False)

    def dram(self, name, shape, dt, kind="Internal"):
        return self.nc.dram_tensor(name, list(shape), dt, kind=kind).ap()

    def sb(self, es, name, shape, dt):
        self._uid = getattr(self, "_uid", 0) + 1
        return es.enter_context(self.nc.sbuf_tensor(f"{name}_{self._uid}", list(shape), dt))

    def build(self):
        nc = self.nc
        kind = self.kind
        with ExitStack() as es:
            self.es = es
            P = self.P = Prog(nc, es)
            self.xT = self.dram("xT", [D, T + 2], F32, "ExternalInput")
            self.vec_d = self.dram("vec", [128, NVEC], F32, "ExternalInput")
            self.cst_d = self.dram("cst", [128, NCST], F32, "ExternalInput")
            self.wd = {}
            if kind == "ff":
                self.wd["wup"] = self.dram("wup", [2 * DFF, D], F32, "ExternalInput")
                self.wd["wdn"] = self.dram("wdn", [D, DFF], F32, "ExternalInput")
            elif kind == "cv":
                self.wd["win"] = self.dram("win", [3072, D], F32, "ExternalInput")
                self.wd["wout"] = self.dram("wout", [D, D], F32, "ExternalInput")
            elif kind == "hg":
                self.wd["win"] = self.dram("win", [4096, D], F32, "ExternalInput")
                self.wd["wout"] = self.dram("wout", [D, D], F32, "ExternalInput")
            elif kind == "hg1":
                self.wd["win"] = self.dram("win", [4096, D], F32, "ExternalInput")
            elif kind == "mo1":
                self.wd["win"] = self.dram("win", [5120, D], F32, "ExternalInput")
            elif kind == "mo2":
                self.wd["wout"] = self.dram("wout", [D, D], F32, "ExternalInput")
            if kind not in ("mo1", "hg1"):
                self.outT = self.dram("outT", [D, T], F32, "ExternalOutput")
            self.XR = self.sb(es, "XR", [128, DC, T], F32)
            self.XB = self.sb(es, "XB", [128, DC, T + 2], BF16)
            self.bXR = [Buf(f"XR{n}") for n in range(NT)]
            self.bXB = [Buf(f"XB{n}") for n in range(NT)]
            self.bXBh = Buf("XBh")
            self.vec = self.sb(es, "vecs", [128, NVEC], F32)
            self.cst = self.sb(es, "csts", [128, NCST], F32)
            self.bvec = Buf("vec")
            self.ones_ln = self.sb(es, "ones_ln", [128, 128], BF16)
            self.bconst = Buf("const")
            NW = 2 if kind == "mo2" else 8
            self.WS = [self.sb(es, f"ws{k}", [128, DC, 128], BF16) for k in range(NW)]
            self.bWS = [Buf(f"ws{k}") for k in range(NW)]
            self.dWS = [P.dsem(f"ws{k}") for k in range(NW)]
            self.wsi = 0
            self.PS = [es.enter_context(nc.psum_tensor(f"ps{k}", [128, 512], F32)) for k in range(8)]
            self.bPS = [Buf(f"ps{k}") for k in range(8)]
            self.psi = 0
            self.d_out = P.dsem("out")
            self.d_misc = P.dsem("misc")
            self.d_misc2 = P.dsem("misc2")
            self.bout = Buf("out")

            P.dma("sp", self.vec[:], self.vec_d, self.d_misc, writes=[self.bvec])
            P.dma("sp", self.cst[:], self.cst_d, self.d_misc2, writes=[self.bvec])
            xTv = self.xT.rearrange("(c p) t -> p c t", p=128)
            d_ins = [P.dsem(f"in{n}") for n in range(NT + 1)]
            for n in range(NT):
                P.dma("sp", self.XR[:, :, n * 512:(n + 1) * 512], xTv[:, :, 2 + n * 512:2 + (n + 1) * 512],
                      d_ins[n], writes=[self.bXR[n]])
            self.hal32 = self.sb(es, "hal32", [128, DC, 2], F32)
            self.bhal32 = Buf("hal32")
            P.dma("sp", self.hal32[:], xTv[:, :, 0:2], d_ins[NT], writes=[self.bhal32])
            P.op("dve", lambda e: e.memset(self.ones_ln[:], 1.0 / D), writes=[self.bconst])
            if kind != "mo2":
                for n in range(NT):
                    P.op("act", lambda e: e.activation(out=self.XB[:, :, 2 + n * 512:2 + (n + 1) * 512],
                                                       in_=self.XR[:, :, n * 512:(n + 1) * 512], func=AF.Copy),
                         reads=[self.bXR[n]], writes=[self.bXB[n]])
                P.op("act", lambda e: e.activation(out=self.XB[:, :, 0:2], in_=self.hal32[:], func=AF.Copy),
                     reads=[self.bhal32], writes=[self.bXBh])

            if kind == "ff":
                self.ffn()
            elif kind == "cv":
                self.conv_mixer()
            elif kind in ("hg", "hg1"):
                self.hgrn_mixer()
            elif kind == "mo1":
                self.moba_qkv()
            elif kind == "mo2":
                self.moba_attn()
            if kind not in ("mo1", "hg1"):
                self.layernorm(V_LNG, V_LNB)
                oTv = self.outT.rearrange("(c p) t -> p c t", p=128)
                for n in range(NT):
                    P.dma("sp", oTv[:, :, n * 512:(n + 1) * 512], self.XR[:, :, n * 512:(n + 1) * 512],
                          self.d_out, reads=[self.bXR[n]], writes=[self.bout])
            P.barrier()
        return nc

    def bank(self):
        rb = getattr(self, "rot_banks", (0, 1, 2, 3, 4, 5, 6, 7))
        self.psi = (self.psi + 1) % len(rb)
        k = rb[self.psi]
        return self.PS[k], self.bPS[k]

    def load_w(self, name, j, c0=0, nch=DC):
        P = self.P
        k = self.wsi
        self.wsi = (self.wsi + 1) % len(self.WS)
        slot, b, ds = self.WS[k], self.bWS[k], self.dWS[k]
        src = self.wd[name][j * 128:(j + 1) * 128, c0 * 128:(c0 + nch) * 128]
        dst = slot[:, 0:nch, :].rearrange("p c i -> p (c i)")
        P.dma("pool", dst, src, ds, writes=[b])
        return slot, b

    def xb_bufs(self, col0, n):
        bs = []
        if col0 < 2:
            bs.append(self.bXBh)
        lo = max(col0 - 2, 0)
        hi = col0 + n - 2
        for t in range(NT):
            if lo < (t + 1) * 512 and hi > t * 512:
                bs.append(self.bXB[t])
        return bs

    def proj(self, w, bw, col0, n, ps, bps, m0=0, m=128, src=None, srcbufs=None, nch=DC):
        P = self.P
        src = self.XB if src is None else src
        srcbufs = self.xb_bufs(col0, n) if srcbufs is None else srcbufs
        for c in range(nch):
            P.op("pe", lambda e: e.matmul(ps[0:m, 0:n], w[:, c, m0:m0 + m], src[:, c, col0:col0 + n],
                                          start=(c == 0), stop=(c == nch - 1)),
                 reads=[bw] + srcbufs, writes=[bps], signal=(c == nch - 1))

    def layernorm(self, goff, boff):
        P = self.P
        with ExitStack() as es:
            zb = [self.sb(es, f"ln_zb{k}", [128, DC, 512], BF16) for k in range(2)]
            zq = [self.sb(es, f"ln_zq{k}", [128, DC, 512], BF16) for k in range(2)]
            bzb = [Buf() for _ in range(2)]
            bzq = [Buf() for _ in range(2)]
            mean = [self.sb(es, f"ln_mean{k}", [128, 512], F32) for k in range(2)]
            rstd = [self.sb(es, f"ln_rstd{k}", [128, 512], F32) for k in range(2)]
            tmp = [self.sb(es, f"ln_tmp{k}", [128, 512], F32) for k in range(2)]
            bmean = [Buf() for _ in range(2)]
            brstd = [Buf() for _ in range(2)]
            btmp = [Buf() for _ in range(2)]
            for n in range(NT):
                k = n % 2
                cs = slice(n * 512, (n + 1) * 512)
                P.op("act", lambda e: e.activation(out=zb[k][:], in_=self.XR[:, :, cs], func=AF.Copy),
                     reads=[self.bXR[n]], writes=[bzb[k]])
                P.op("act", lambda e: e.activation(out=zq[k][:], in_=self.XR[:, :, cs], func=AF.Square),
                     reads=[self.bXR[n]], writes=[bzq[k]])
                pm, bpm = self.bank()
                for c in range(DC):
                    P.op("pe", lambda e: e.matmul(pm[:, :], self.ones_ln[:], zb[k][:, c, :], start=(c == 0), stop=(c == DC - 1)),
                         reads=[self.bconst, bzb[k]], writes=[bpm], signal=(c == DC - 1))
                pq, bpq = self.bank()
                for c in range(DC):
                    P.op("pe", lambda e: e.matmul(pq[:, :], self.ones_ln[:], zq[k][:, c, :], start=(c == 0), stop=(c == DC - 1)),
                         reads=[self.bconst, bzq[k]], writes=[bpq], signal=(c == DC - 1))
                P.op("dve", lambda e: e.tensor_copy(out=mean[k][:], in_=pm[:, :]), reads=[bpm], writes=[bmean[k]])
                P.op("dve", lambda e: e.tensor_tensor(out=tmp[k][:], in0=mean[k][:], in1=mean[k][:], op=ALU.mult),
                     reads=[bmean[k]], writes=[btmp[k]])
                P.op("dve", lambda e: e.tensor_tensor(out=rstd[k][:], in0=pq[:, :], in1=tmp[k][:], op=ALU.subtract),
                     reads=[bpq, btmp[k]], writes=[brstd[k]])
                P.op("dve", lambda e: e.tensor_scalar(out=rstd[k][:], in0=rstd[k][:], scalar1=LN_EPS, scalar2=None,
                                                      op0=ALU.add),
                     reads=[brstd[k]], writes=[brstd[k]])
                P.op("act", lambda e: e.activation(out=rstd[k][:], in_=rstd[k][:], func=AF.Sqrt),
                     reads=[brstd[k]], writes=[brstd[k]])
                P.op("dve", lambda e: e.reciprocal(out=rstd[k][:], in_=rstd[k][:]),
                     reads=[brstd[k]], writes=[brstd[k]])
                for c in range(DC):
                    P.op("dve", lambda e: e.tensor_tensor(out=tmp[k][:], in0=self.XR[:, c, cs], in1=mean[k][:], op=ALU.subtract),
                         reads=[self.bXR[n], bmean[k]], writes=[btmp[k]])
                    P.op("dve", lambda e: e.tensor_tensor(out=tmp[k][:], in0=tmp[k][:], in1=rstd[k][:], op=ALU.mult),
                         reads=[btmp[k], brstd[k]], writes=[btmp[k]])
                    P.op("act", lambda e: e.activation(out=self.XR[:, c, cs], in_=tmp[k][:], func=AF.Identity,
                                                       bias=self.vec[:, boff + c:boff + c + 1],
                                                       scale=self.vec[:, goff + c:goff + c + 1]),
                         reads=[btmp[k], self.bvec], writes=[self.bXR[n]])
                P.op("act", lambda e: e.activation(out=self.XB[:, :, 2 + n * 512:2 + (n + 1) * 512],
                                                   in_=self.XR[:, :, cs], func=AF.Copy),
                     reads=[self.bXR[n]], writes=[self.bXB[n]])
            P.barrier()

    def out_proj_residual(self, wname, Y, bY, nch, c0=0, first=True, wtile_c0=0, coff=0):
        P = self.P
        for f in range(DC):
            w, bw = self.load_w(wname, f, c0=wtile_c0, nch=nch)
            for n in range(NT):
                ps, bps = self.bank()
                self.proj(w, bw, coff + n * 512, 512, ps, bps, src=Y, srcbufs=bY if isinstance(bY, list) else [bY], nch=nch)
                cs = slice(n * 512, (n + 1) * 512)
                if first:
                    P.op("dve", lambda e: e.scalar_tensor_tensor(out=self.XR[:, f, cs], in0=self.XR[:, f, cs], scalar=ALPHA,
                                                                 in1=ps[:, :], op0=ALU.mult, op1=ALU.add),
                         reads=[bps, self.bXR[n]], writes=[self.bXR[n]])
                else:
                    P.op("dve", lambda e: e.tensor_tensor(out=self.XR[:, f, cs], in0=self.XR[:, f, cs], in1=ps[:, :], op=ALU.add),
                         reads=[bps, self.bXR[n]], writes=[self.bXR[n]])

    def ffn(self):
        P = self.P
        cvo = V_CONV
        groups = [(0, 8), (8, 15), (15, 22)]
        with ExitStack() as es:
            GH = 8
            H = self.sb(es, "ffn_H", [128, GH, T], BF16)
            bH = [Buf() for _ in range(GH)]
            U = [[self.sb(es, f"ffn_U{ab}{k}", [128, 514], F32) for k in range(3)] for ab in range(2)]
            bU = [[Buf() for _ in range(3)] for _ in range(2)]
            Y = [[self.sb(es, f"ffn_Y{ab}{k}", [128, 512], F32) for k in range(2)] for ab in range(2)]
            bY = [[Buf() for _ in range(2)] for _ in range(2)]
            TT = [[self.sb(es, f"ffn_T{ab}{k}", [128, 512], F32) for k in range(2)] for ab in range(2)]
            bTT = [[Buf() for _ in range(2)] for _ in range(2)]
            SA = [self.sb(es, f"ffn_SA{k}", [128, 512], F32) for k in range(2)]
            bSA = [Buf() for _ in range(2)]
            ui = 0
            yi = 0
            for gi, (j0, j1) in enumerate(groups):
                for j in range(j0, j1):
                    ws = [self.load_w("wup", j), self.load_w("wup", FC + j)]
                    for n in range(NT):
                        uk = ui % 3
                        up = (ui - 1) % 3
                        ui += 1
                        yk = yi % 2
                        yi += 1
                        for ab in range(2):
                            w, bw = ws[ab]
                            u, bu = U[ab][uk], bU[ab][uk]
                            cj = j + ab * FC
                            if n == 0:
                                ph, bph = self.bank()
                                self.proj(w, bw, 0, 2, ph, bph)
                                P.op("act", lambda e: e.activation(out=u[:, 0:2], in_=ph[:, 0:2], func=AF.Copy),
                                     reads=[bph], writes=[bu])
                            else:
                                P.op("act", lambda e: e.activation(out=u[:, 0:2], in_=U[ab][up][:, 512:514], func=AF.Copy),
                                     reads=[bU[ab][up]], writes=[bu])
                            ps, bps = self.bank()
                            self.proj(w, bw, 2 + n * 512, 512, ps, bps)
                            P.op("act", lambda e: e.activation(out=u[:, 2:514], in_=ps[:, :], func=AF.Copy),
                                 reads=[bps], writes=[bu])
                            t, bt = TT[ab][yk], bTT[ab][yk]
                            y, by = Y[ab][yk], bY[ab][yk]
                            w0 = self.vec[:, cvo + cj:cvo + cj + 1]
                            w1 = self.vec[:, cvo + 44 + cj:cvo + 44 + cj + 1]
                            w2 = self.vec[:, cvo + 88 + cj:cvo + 88 + cj + 1]
                            P.op("act", lambda e: e.activation(out=t[:], in_=u[:, 0:512], func=AF.Copy, scale=w0),
                                 reads=[bu, self.bvec], writes=[bt])
                            P.op("dve", lambda e: e.scalar_tensor_tensor(out=t[:], in0=u[:, 1:513], scalar=w1, in1=t[:],
                                                                         op0=ALU.mult, op1=ALU.add),
                                 reads=[bu, bt, self.bvec], writes=[bt])
                            P.op("dve", lambda e: e.scalar_tensor_tensor(out=y[:], in0=u[:, 2:514], scalar=w2, in1=t[:],
                                                                         op0=ALU.mult, op1=ALU.add),
                                 reads=[bu, bt, self.bvec], writes=[by])
                        sa, bsa = SA[yk], bSA[yk]
                        P.op("act", lambda e: e.activation(out=sa[:], in_=Y[0][yk][:], func=AF.Silu),
                             reads=[bY[0][yk]], writes=[bsa])
                        P.op("dve", lambda e: e.tensor_tensor(out=H[:, j - j0, n * 512:(n + 1) * 512], in0=sa[:], in1=Y[1][yk][:],
                                                              op=ALU.mult),
                             reads=[bsa, bY[1][yk]], writes=[bH[j - j0]])
                self.out_proj_residual("wdn", H, bH[:j1 - j0], j1 - j0, first=(gi == 0), wtile_c0=j0)
            P.barrier()

    def conv_mixer(self):
        P = self.P
        cvo = V_CONV
        with ExitStack() as es:
            Yo = self.sb(es, "cm_Y", [128, DC, T], BF16)
            bYo = [Buf() for _ in range(DC)]
            PR = [self.sb(es, f"cm_P{k}", [128, 514], F32) for k in range(3)]
            bPR = [Buf() for _ in range(3)]
            CG = [self.sb(es, f"cm_CG{k}", [128, 514], F32) for k in range(2)]
            bCG = [Buf() for _ in range(2)]
            TT = [self.sb(es, f"cm_T{k}", [128, 512], F32) for k in range(2)]
            bTT = [Buf() for _ in range(2)]
            ui = 0
            for c in range(DC):
                wbg = self.load_w("win", c)
                wcg = self.load_w("win", DC + c)
                wh = self.load_w("win", 2 * DC + c)
                w0 = self.vec[:, cvo + c:cvo + c + 1]
                w1 = self.vec[:, cvo + 8 + c:cvo + 8 + c + 1]
                w2 = self.vec[:, cvo + 16 + c:cvo + 16 + c + 1]
                for n in range(NT):
                    uk = ui % 3
                    up = (ui - 1) % 3
                    k2 = ui % 2
                    ui += 1
                    pr, bpr = PR[uk], bPR[uk]
                    cg, bcg = CG[k2], bCG[k2]
                    t, bt = TT[k2], bTT[k2]
                    if n == 0:
                        p1, bp1 = self.bank()
                        self.proj(wcg[0], wcg[1], 0, 2, p1, bp1)
                        P.op("act", lambda e: e.activation(out=cg[:, 0:2], in_=p1[:, 0:2], func=AF.Copy), reads=[bp1], writes=[bcg])
                        p2, bp2 = self.bank()
                        self.proj(wh[0], wh[1], 0, 2, p2, bp2)
                        P.op("dve", lambda e: e.tensor_tensor(out=pr[:, 0:2], in0=cg[:, 0:2], in1=p2[:, 0:2], op=ALU.mult),
                             reads=[bcg, bp2], writes=[bpr])
                    else:
                        P.op("act", lambda e: e.activation(out=pr[:, 0:2], in_=PR[up][:, 512:514], func=AF.Copy),
                             reads=[bPR[up]], writes=[bpr])
                    p1, bp1 = self.bank()
                    self.proj(wcg[0], wcg[1], 2 + n * 512, 512, p1, bp1)
                    P.op("act", lambda e: e.activation(out=cg[:, 2:514], in_=p1[:, :], func=AF.Copy), reads=[bp1], writes=[bcg])
                    p2, bp2 = self.bank()
                    self.proj(wh[0], wh[1], 2 + n * 512, 512, p2, bp2)
                    P.op("dve", lambda e: e.tensor_tensor(out=pr[:, 2:514], in0=cg[:, 2:514], in1=p2[:, :], op=ALU.mult),
                         reads=[bcg, bp2], writes=[bpr])
                    p3, bp3 = self.bank()
                    self.proj(wbg[0], wbg[1], 2 + n * 512, 512, p3, bp3)
                    P.op("act", lambda e: e.activation(out=t[:], in_=pr[:, 0:512], func=AF.Copy, scale=w0),
                         reads=[bpr, self.bvec], writes=[bt])
                    P.op("dve", lambda e: e.scalar_tensor_tensor(out=t[:], in0=pr[:, 1:513], scalar=w1, in1=t[:],
                                                                 op0=ALU.mult, op1=ALU.add),
                         reads=[bpr, bt, self.bvec], writes=[bt])
                    P.op("dve", lambda e: e.scalar_tensor_tensor(out=t[:], in0=pr[:, 2:514], scalar=w2, in1=t[:],
                                                                 op0=ALU.mult, op1=ALU.add),
                         reads=[bpr, bt, self.bvec], writes=[bt])
                    P.op("dve", lambda e: e.tensor_tensor(out=Yo[:, c, n * 512:(n + 1) * 512], in0=t[:], in1=p3[:, :], op=ALU.mult),
                         reads=[bt, bp3], writes=[bYo[c]])
            self.out_proj_residual("wout", Yo, bYo, DC, first=True)
            P.barrier()

    def hgrn_mixer(self):
        P = self.P
        nc = self.nc
        H = 8
        lite = (self.kind == "hg1")
        with ExitStack() as es:
            self.Aall = self.dram("Aall", [128, NCORE * H], F32, "ExternalInput")
            self.Ball = self.dram("Ball", [NCORE * 128, H * 128], F32, "ExternalInput")
            self.Aout = self.dram("Aout", [128, H], F32, "ExternalOutput")
            self.Bout = self.dram("Bout", [128, H * 128], F32, "ExternalOutput")
            G = [self.sb(es, f"hg_G{k}", [128, 512], BF16) for k in range(2)]
            bG = [Buf() for _ in range(2)]
            Y = self.sb(es, "hg_Y", [128, H, T], BF16)
            bY = [Buf() for _ in range(H)]
            S = self.sb(es, "hg_S", [128, H, 128], F32)
            bS = [Buf() for _ in range(H)]
            Aall = self.sb(es, "hg_Aall", [128, NCORE, H], F32)
            bAall = Buf()
            ap_ = self.sb(es, "hg_ap", [128, H], F32)
            bap = Buf()
            om = self.sb(es, "hg_om", [128, NCORE], F32)
            bom = Buf()
            Bm = [self.sb(es, f"hg_Bm{k}", [128, H, 128], F32) for k in range(1)]
            bBm = [Buf() for _ in range(1)]
            dBm = [P.dsem(f"bm{k}") for k in range(1)]
            d_a = P.dsem("aall")
            lbe = self.sb(es, "hg_lbe", [128, 4, H], F32)
            lb = self.sb(es, "hg_lb", [128, H], F32)
            oml = self.sb(es, "hg_oml", [128, H], F32)
            lbm1 = self.sb(es, "hg_lbm1", [128, H], F32)
            lsum = self.sb(es, "hg_lsum", [128, H], F32)
            blb = Buf()
            rmask = self.sb(es, "hg_rmask", [128, 512], F32)
            ident = self.sb(es, "hg_ident", [128, 128], BF16)
            ones128 = self.sb(es, "hg_ones", [128, 128], BF16)
            bcn = Buf()
            blsum = self.sb(es, "hg_blsum", [128, H], F32)
            bblsum = Buf()
            P.op("dve", lambda e: e.memset(rmask[:], 1.0), writes=[bcn])
            P.op("dve", lambda e: e.memset(rmask[:].rearrange("p (c t) -> p c t", t=64)[:, :, 0:1], 0.0), writes=[bcn])
            P.op("dve", lambda e: e.memset(ones128[:], 1.0 / 128.0), writes=[bcn])
            P.op("dve", lambda e: e.memset(blsum[:], 0.0), writes=[bblsum])
            P.op("act", lambda e: e.activation(out=ident[:], in_=self.cst[:, C_IDENT:C_IDENT + 128], func=AF.Copy),
                 reads=[self.bvec], writes=[bcn])
            P.op("act", lambda e: e.activation(out=lbe[:].rearrange("p l h -> p (l h)"), in_=self.vec[:, V_LBRAW:V_LBRAW + 32], func=AF.Exp),
                 reads=[self.bvec], writes=[blb])
            P.op("dve", lambda e: e.tensor_tensor(out=lsum[:], in0=lbe[:, 0, :], in1=lbe[:, 1, :], op=ALU.add), reads=[blb], writes=[blb])
            P.op("dve", lambda e: e.tensor_tensor(out=lsum[:], in0=lsum[:], in1=lbe[:, 2, :], op=ALU.add), reads=[blb], writes=[blb])
            P.op("dve", lambda e: e.tensor_tensor(out=lsum[:], in0=lsum[:], in1=lbe[:, 3, :], op=ALU.add), reads=[blb], writes=[blb])
            P.op("dve", lambda e: e.reciprocal(out=lsum[:], in_=lsum[:]), reads=[blb], writes=[blb])
            P.op("dve", lambda e: e.tensor_scalar(out=lb[:], in0=lbe[:, 1, :], scalar1=self.cst[:, C_LMASK + 1:C_LMASK + 2], scalar2=None, op0=ALU.mult),
                 reads=[blb, self.bvec], writes=[blb])
            for l in (2, 3):
                P.op("dve", lambda e: e.scalar_tensor_tensor(out=lb[:], in0=lbe[:, l, :], scalar=self.cst[:, C_LMASK + l:C_LMASK + l + 1],
                                                             in1=lb[:], op0=ALU.mult, op1=ALU.add),
                     reads=[blb, self.bvec], writes=[blb])
            P.op("dve", lambda e: e.tensor_tensor(out=lb[:], in0=lb[:], in1=lsum[:], op=ALU.mult), reads=[blb], writes=[blb])
            P.op("dve", lambda e: e.tensor_scalar(out=lbm1[:], in0=lb[:], scalar1=-1.0, scalar2=None, op0=ALU.add), reads=[blb], writes=[blb])
            P.op("dve", lambda e: e.tensor_scalar(out=oml[:], in0=lbm1[:], scalar1=-1.0, scalar2=None, op0=ALU.mult), reads=[blb], writes=[blb])
            P.dma("sp", Aall[:].rearrange("p r h -> p (r h)"), self.Aall, d_a, writes=[bAall])
            P.op("dve", lambda e: e.memset(S[:], 0.0), writes=bS)
            P.op("dve", lambda e: e.tensor_scalar(out=om[:], in0=self.cst[:, C_HMASK:C_HMASK + NCORE], scalar1=-1.0, scalar2=1.0,
                                                  op0=ALU.mult, op1=ALU.add), reads=[self.bvec], writes=[bom])
            for r in range(NCORE - 1):
                k = 0
                P.dma("sp", Bm[k][:].rearrange("p h v -> p (h v)"), self.Ball[r * 128:(r + 1) * 128, :], dBm[k], writes=[bBm[k]])
                mr = self.cst[:, C_HMASK + r:C_HMASK + r + 1]
                P.op("dve", lambda e: e.tensor_scalar(out=ap_[:], in0=Aall[:, r, :], scalar1=mr, scalar2=om[:, r:r + 1], op0=ALU.mult, op1=ALU.add),
                     reads=[bAall, self.bvec, bom], writes=[bap])
                P.op("dve", lambda e: e.tensor_scalar(out=Bm[k][:], in0=Bm[k][:], scalar1=mr, scalar2=None, op0=ALU.mult),
                     reads=[bBm[k], self.bvec], writes=[bBm[k]])
                for h in range(H):
                    P.op("dve", lambda e: e.scalar_tensor_tensor(out=S[:, h, :], in0=S[:, h, :], scalar=ap_[:, h:h + 1], in1=Bm[k][:, h, :],
                                                                 op0=ALU.mult, op1=ALU.add),
                         reads=[bS[h], bap, bBm[k]], writes=[bS[h]])
            R2 = lambda nm, shp, dt: [self.sb(es, f"{nm}{k}", shp, dt) for k in range(2)]
            sig, bsig = R2("hg_sig", [128, 512], F32), [Buf(), Buf()]
            qs, bqs = R2("hg_qs", [128, 512], F32), [Buf(), Buf()]
            R1 = lambda nm, shp, dt: [self.sb(es, nm, shp, dt)] * 2
            B1 = lambda: [Buf()] * 2
            kk, bkk = R1("hg_k", [128, 512], F32), B1()
            gg, bgg = R1("hg_g", [128, 512], F32), B1()
            bb, bbb = R1("hg_b", [128, 512], F32), B1()
            e1, be1 = R2("hg_e1", [128, 512], F32), [Buf(), Buf()]
            e2, be2 = R1("hg_e2", [128, 512], F32), B1()
            ebl, bebl = R2("hg_ebl", [128, 8], F32), [Buf(), Buf()]
            qp, bqp = R2("hg_qp", [128, 512], BF16), [Buf(), Buf()]
            kp, bkp = R2("hg_kp", [128, 512], BF16), [Buf(), Buf()]
            ktok, bktok = R2("hg_ktok", [128, 4, 128], BF16), [Buf(), Buf()]
            vtok, bvtok = R2("hg_vtok", [128, 4, 128], BF16), [Buf(), Buf()]
            am, bam = R2("hg_am", [128, 128], BF16), [Buf(), Buf()]
            sdb = [self.sb(es, f"hg_sdb{c_}", [128, 128], BF16) for c_ in range(8)]
            bsdb = [Buf() for _ in range(8)]
            osq, bosq = R2("hg_osq", [128, 512], BF16), [Buf(), Buf()]
            rr, brr = R2("hg_rr", [128, 512], F32), [Buf(), Buf()]
            yy, byy = R2("hg_yy", [128, 512], F32), [Buf(), Buf()]
            it = 0
            sdi = 0
            ami = 0
            self.rot_banks = (4, 5, 6, 7)
            for h in range(H):
                wq = self.load_w("win", h)
                wf = self.load_w("win", H + h)
                wi = self.load_w("win", 2 * H + h)
                wg = self.load_w("win", 3 * H + h)
                lbh = lb[:, h:h + 1]
                omlh = oml[:, h:h + 1]
                lbm1h = lbm1[:, h:h + 1]
                for n in range(NT):
                    k = it % 2
                    it += 1
                    c0 = 2 + n * 512
                    cs = slice(n * 512, (n + 1) * 512)
                    if not lite:
                        pq, bpq = self.bank()
                        self.proj(wq[0], wq[1], c0, 512, pq, bpq)
                    pf, bpf = self.bank()
                    self.proj(wf[0], wf[1], c0, 512, pf, bpf)
                    if not lite:
                        pg, bpg = self.bank()
                        self.proj(wg[0], wg[1], c0, 512, pg, bpg)
                    pv, bpv = self.bank()
                    for s4 in range(4):
                        for c in range(DC):
                            P.op("pe", lambda e: e.matmul(pv[:, s4 * 128:(s4 + 1) * 128], self.XB[:, c, c0 + s4 * 128:c0 + (s4 + 1) * 128],
                                                          wi[0][:, c, :], start=(c == 0), stop=(c == DC - 1)),
                                 reads=[wi[1], self.bXB[n]], writes=[bpv], signal=(c == DC - 1 and s4 == 3))
                    P.op("act", lambda e: e.activation(out=sig[k][:], in_=pf[:, :], func=AF.Sigmoid), reads=[bpf], writes=[bsig[k]])
                    if not lite:
                        P.op("act", lambda e: e.activation(out=qs[k][:], in_=pq[:, :], func=AF.Silu), reads=[bpq], writes=[bqs[k]])
                        P.op("act", lambda e: e.activation(out=G[k][:], in_=pg[:, :], func=AF.Silu), reads=[bpg], writes=[bG[k]])
                    P.op("act", lambda e: e.activation(out=vtok[k][:].rearrange("p s v -> p (s v)"), in_=pv[:, :], func=AF.Copy),
                         reads=[bpv], writes=[bvtok[k]])
                    P.op("dve", lambda e: e.tensor_scalar(out=kk[k][:], in0=sig[k][:], scalar1=lbm1h, scalar2=omlh, op0=ALU.mult, op1=ALU.add),
                         reads=[bsig[k], blb], writes=[bkk[k]])
                    P.op("act", lambda e: e.activation(out=gg[k][:], in_=sig[k][:], func=AF.Ln, bias=lbh, scale=omlh),
                         reads=[bsig[k], blb], writes=[bgg[k]])
                    P.op("dve", lambda e: e.tensor_tensor_scan(out=bb[k][:], data0=rmask[:], data1=gg[k][:], initial=0.0, op0=ALU.mult, op1=ALU.add),
                         reads=[bgg[k], bcn], writes=[bbb[k]])
                    b3 = bb[k][:].rearrange("p (c t) -> p c t", t=64)
                    P.op("dve", lambda e: e.tensor_tensor(out=e1[k][:].rearrange("p (c t) -> p c t", t=64), in0=b3,
                                                          in1=b3[:, :, 63:64].to_broadcast([128, 8, 64]), op=ALU.subtract),
                         reads=[bbb[k]], writes=[be1[k]])
                    P.op("act", lambda e: e.activation(out=e2[k][:], in_=e1[k][:], func=AF.Exp, scale=-1.0), reads=[be1[k]], writes=[be2[k]])
                    if not lite:
                        P.op("act", lambda e: e.activation(out=e1[k][:], in_=e1[k][:], func=AF.Exp), reads=[be1[k], be2[k]], writes=[be1[k]])
                    P.op("act", lambda e: e.activation(out=ebl[k][:], in_=b3[:, :, 63], func=AF.Exp), reads=[bbb[k]], writes=[bebl[k]])
                    P.op("dve", lambda e: e.tensor_reduce(out=rr[k][:, 0:1], in_=b3[:, :, 63], axis=AX.X, op=ALU.add), reads=[bbb[k]], writes=[brr[k]])
                    P.op("dve", lambda e: e.tensor_tensor(out=blsum[:, h:h + 1], in0=blsum[:, h:h + 1], in1=rr[k][:, 0:1], op=ALU.add),
                         reads=[brr[k], bblsum], writes=[bblsum])
                    if not lite:
                        P.op("dve", lambda e: e.tensor_tensor(out=qp[k][:], in0=qs[k][:], in1=e1[k][:], op=ALU.mult), reads=[bqs[k], be1[k]], writes=[bqp[k]])
                    P.op("dve", lambda e: e.tensor_tensor(out=kp[k][:], in0=kk[k][:], in1=e2[k][:], op=ALU.mult), reads=[bkk[k], be2[k]], writes=[bkp[k]])
                    pt, bpt = self.bank()
                    ptb = pt[:].bitcast(BF16)
                    for s4 in range(4):
                        P.op("pe", lambda e: e.transpose(ptb[:, s4 * 128:(s4 + 1) * 128], kp[k][:, s4 * 128:(s4 + 1) * 128], ident[:]),
                             reads=[bkp[k], bcn], writes=[bpt], signal=(s4 == 3))
                    P.op("act", lambda e: e.activation(out=ktok[k][:].rearrange("p s v -> p (s v)"), in_=ptb[:, 0:512], func=AF.Copy),
                         reads=[bpt], writes=[bktok[k]])
                    po, bpo = self.PS[k], self.bPS[k]
                    for ci in range(8):
                        s4, half = ci // 2, ci % 2
                        rows = slice(half * 64, half * 64 + 64)
                        pu, bpu = self.PS[2 + half], self.bPS[2 + half]
                        P.op("pe", lambda e: e.matmul(pu[:, s4 * 128:(s4 + 1) * 128], ktok[k][rows, s4, :], vtok[k][rows, s4, :],
                                                      start=True, stop=True),
                             reads=[bktok[k], bvtok[k]], writes=[bpu], signal=(ci >= 6))
                    for ci in range(8):
                        s4, half = ci // 2, ci % 2
                        eb = ebl[k][:, ci:ci + 1]
                        pu, bpu = self.PS[2 + half], self.bPS[2 + half]
                        if not lite:
                            P.op("dve", lambda e: e.tensor_scalar(out=sdb[ci][:], in0=S[:, h, :], scalar1=eb, scalar2=None, op0=ALU.mult),
                                 reads=[bS[h], bebl[k]], writes=[bsdb[ci]])
                        P.op("dve", lambda e: e.scalar_tensor_tensor(out=S[:, h, :], in0=S[:, h, :], scalar=eb, in1=pu[:, s4 * 128:(s4 + 1) * 128],
                                                                     op0=ALU.mult, op1=ALU.add),
                             reads=[bS[h], bebl[k], bpu], writes=[bS[h]])
                    for s4 in range(0 if lite else 4):
                        sl = slice(s4 * 128, (s4 + 1) * 128)
                        pa, bpa = self.bank()
                        P.op("pe", lambda e: e.matmul(pa[:, 0:128], kp[k][:, sl], qp[k][:, sl], start=True, stop=True),
                             reads=[bkp[k], bqp[k]], writes=[bpa])
                        a_ = ami % 2
                        ami += 1
                        P.op("dve", lambda e: e.tensor_tensor(out=am[a_][:], in0=pa[:, 0:128], in1=self.cst[:, C_MASK2:C_MASK2 + 128], op=ALU.mult),
                             reads=[bpa, self.bvec], writes=[bam[a_]])
                        P.op("pe", lambda e: e.matmul(po[:, sl], vtok[k][:, s4, :], am[a_][:], start=True, stop=False),
                             reads=[bvtok[k], bam[a_]], writes=[bpo], signal=False)
                        for half in range(2):
                            ci = s4 * 2 + half
                            hs = slice(s4 * 128 + half * 64, s4 * 128 + half * 64 + 64)
                            P.op("pe", lambda e: e.matmul(po[:, hs], sdb[ci][:], qp[k][:, hs], start=False, stop=(half == 1)),
                                 reads=[bsdb[ci], bqp[k]], writes=[bpo], signal=True)
                    if lite:
                        continue
                    P.op("act", lambda e: e.activation(out=osq[k][:], in_=po[:, :], func=AF.Square), reads=[bpo], writes=[bosq[k]])
                    pm, bpm = self.bank()
                    P.op("pe", lambda e: e.matmul(pm[:, :], ones128[:], osq[k][:], start=True, stop=True), reads=[bcn, bosq[k]], writes=[bpm])
                    P.op("dve", lambda e: e.tensor_scalar(out=rr[k][:], in0=pm[:, :], scalar1=RMS_EPS, scalar2=None, op0=ALU.add),
                         reads=[bpm], writes=[brr[k]])
                    P.op("act", lambda e: e.activation(out=rr[k][:], in_=rr[k][:], func=AF.Sqrt), reads=[brr[k]], writes=[brr[k]])
                    P.op("dve", lambda e: e.reciprocal(out=rr[k][:], in_=rr[k][:]), reads=[brr[k]], writes=[brr[k]])
                    P.op("dve", lambda e: e.tensor_tensor(out=yy[k][:], in0=po[:, :], in1=rr[k][:], op=ALU.mult), reads=[bpo, brr[k]], writes=[byy[k]])
                    P.op("dve", lambda e: e.scalar_tensor_tensor(out=Y[:, h, cs], in0=yy[k][:], scalar=self.vec[:, V_NORMG + h:V_NORMG + h + 1],
                                                                 in1=G[k][:], op0=ALU.mult, op1=ALU.mult),
                         reads=[byy[k], self.bvec, bG[k]], writes=[bY[h]])
            P.op("act", lambda e: e.activation(out=blsum[:], in_=blsum[:], func=AF.Exp), reads=[bblsum], writes=[bblsum])
            d_o = P.dsem("hgo")
            P.dma("sp", self.Aout, blsum[:], d_o, reads=[bblsum], writes=[self.bout])
            P.dma("sp", self.Bout, S[:].rearrange("p h v -> p (h v)"), d_o, reads=bS, writes=[self.bout])
            self.rot_banks = (0, 1, 2, 3, 4, 5, 6, 7)
            if not lite:
                self.out_proj_residual("wout", Y, bY, DC, first=True)
            P.barrier()

    def moba_qkv(self):
        P = self.P
        H = 8
        with ExitStack() as es:
            self.pos_d = self.dram("pos", [1, T], I32, "ExternalInput")
            self.QTo = self.dram("QT", [128, H * T], BF16, "ExternalOutput")
            self.KTo = self.dram("KT", [128, H * T], BF16, "ExternalOutput")
            self.Vo = self.dram("V", [128, 16 * 1024], BF16, "ExternalOutput")
            self.KMo = self.dram("KM", [128, H * 8], F32, "ExternalOutput")
            posi = self.sb(es, "mq_posi", [32, T], I32)
            ang = self.sb(es, "mq_ang", [32, T], F32)
            tmp = self.sb(es, "mq_tmp", [32, T], F32)
            Ct = self.sb(es, "mq_C", [32, T], F32)
            St = self.sb(es, "mq_S", [32, T], F32)
            npi = self.sb(es, "mq_npi", [32, 1], F32)
            km = self.sb(es, "mq_km", [128, H, 8], F32)
            btab, bkm = Buf(), Buf()
            d_p = P.dsem("pos")
            P.dma("sp", posi[:], self.pos_d.partition_broadcast(32), d_p, writes=[btab])
            P.op("dve", lambda e: e.tensor_copy(out=ang[:], in_=posi[:]), reads=[btab], writes=[btab])
            P.op("dve", lambda e: e.memset(npi[:], -math.pi), writes=[btab])
            P.op("dve", lambda e: e.tensor_scalar(out=ang[:], in0=ang[:], scalar1=self.cst[0:32, C_INVF:C_INVF + 1], scalar2=None, op0=ALU.mult),
                 reads=[btab, self.bvec], writes=[btab])
            ki = self.sb(es, "mq_ki", [32, T], I32)
            kfl = self.sb(es, "mq_kfl", [32, T], F32)
            for (dst_, off_) in ((St, 0.5), (Ct, 0.75)):
                P.op("dve", lambda e: e.tensor_scalar(out=tmp[:], in0=ang[:], scalar1=1.0 / (2 * math.pi), scalar2=off_, op0=ALU.mult, op1=ALU.add),
                     reads=[btab], writes=[btab])
                P.op("dve", lambda e: e.tensor_copy(out=ki[:], in_=tmp[:]), reads=[btab], writes=[btab])
                P.op("dve", lambda e: e.tensor_copy(out=kfl[:], in_=ki[:]), reads=[btab], writes=[btab])
                P.op("dve", lambda e: e.tensor_tensor(out=tmp[:], in0=tmp[:], in1=kfl[:], op=ALU.subtract), reads=[btab], writes=[btab])
                P.op("dve", lambda e: e.tensor_scalar(out=kfl[:], in0=tmp[:], scalar1=0.0, scalar2=None, op0=ALU.is_lt), reads=[btab], writes=[btab])
                P.op("dve", lambda e: e.tensor_tensor(out=tmp[:], in0=tmp[:], in1=kfl[:], op=ALU.add), reads=[btab], writes=[btab])
                P.op("act", lambda e: e.activation(out=dst_[:], in_=tmp[:], func=AF.Sin, bias=npi[:, 0:1], scale=2 * math.pi), reads=[btab], writes=[btab])
            P.op("dve", lambda e: e.tensor_scalar(out=St[:], in0=St[:], scalar1=self.cst[0:32, C_SGN:C_SGN + 1], scalar2=None, op0=ALU.mult),
                 reads=[btab, self.bvec], writes=[btab])
            R2 = lambda nm, shp, dt: [self.sb(es, f"{nm}{k}", shp, dt) for k in range(2)]
            t1, bt1 = R2("mq_t1", [32, 512], F32), [Buf(), Buf()]
            t2, bt2 = R2("mq_t2", [32, 512], F32), [Buf(), Buf()]
            kf, bkf = R2("mq_kf", [128, 512], F32), [Buf(), Buf()]
            ob, bob = R2("mq_ob", [128, 512], BF16), [Buf(), Buf()]
            vb, bvb = R2("mq_vb", [128, 4, 128], BF16), [Buf(), Buf()]
            dob = [P.dsem(f"ob{k}") for k in range(2)]
            dvb = [P.dsem(f"vb{k}") for k in range(2)]
            Vov = self.Vo.rearrange("p (t f) -> p t f", f=1024)
            it = 0
            for h in range(H):
                for qk in range(2):
                    w = self.load_w("win", qk * H + h)
                    wp = self.load_w("win", 3 * H + qk * H + h)
                    dst = self.QTo if qk == 0 else self.KTo
                    for n in range(NT):
                        k = it % 2
                        it += 1
                        c0 = 2 + n * 512
                        cs = slice(n * 512, (n + 1) * 512)
                        pa, bpa = self.bank()
                        self.proj(w[0], w[1], c0, 512, pa, bpa)
                        pb, bpb = self.bank()
                        self.proj(wp[0], wp[1], c0, 512, pb, bpb, m=32)
                        P.op("dve", lambda e: e.tensor_tensor(out=t1[k][:], in0=pa[0:32, :], in1=Ct[:, cs], op=ALU.mult), reads=[bpa, btab], writes=[bt1[k]])
                        P.op("dve", lambda e: e.tensor_tensor(out=t2[k][:], in0=pb[0:32, :], in1=St[:, cs], op=ALU.mult), reads=[bpb, btab], writes=[bt2[k]])
                        P.op("dve", lambda e: e.tensor_tensor(out=kf[k][0:32, :], in0=t1[k][:], in1=t2[k][:], op=ALU.add), reads=[bt1[k], bt2[k]], writes=[bkf[k]])
                        P.op("act", lambda e: e.activation(out=kf[k][32:64, :], in_=pa[32:64, :], func=AF.Copy), reads=[bpa], writes=[bkf[k]])
                        P.op("act", lambda e: e.activation(out=kf[k][64:128, :], in_=pa[64:128, :], func=AF.Copy), reads=[bpa], writes=[bkf[k]])
                        P.op("act", lambda e: e.activation(out=ob[k][:], in_=kf[k][:], func=AF.Copy), reads=[bkf[k]], writes=[bob[k]])
                        if qk == 1:
                            P.op("dve", lambda e: e.tensor_reduce(out=km[:, h, 2 * n:2 * n + 2], in_=kf[k][:].rearrange("p (b t) -> p b t", t=256),
                                                                  axis=AX.X, op=ALU.add), reads=[bkf[k]], writes=[bkm])
                        P.dma("sp", dst[:, h * T + n * 512:h * T + (n + 1) * 512], ob[k][:], dob[k], reads=[bob[k]], writes=[self.bout])
                wv = self.load_w("win", 2 * H + h)
                for n in range(NT):
                    k = it % 2
                    it += 1
                    c0 = 2 + n * 512
                    pv, bpv = self.bank()
                    for s4 in range(4):
                        for c in range(DC):
                            P.op("pe", lambda e: e.matmul(pv[:, s4 * 128:(s4 + 1) * 128], self.XB[:, c, c0 + s4 * 128:c0 + (s4 + 1) * 128],
                                                          wv[0][:, c, :], start=(c == 0), stop=(c == DC - 1)),
                                 reads=[wv[1], self.bXB[n]], writes=[bpv], signal=(c == DC - 1 and s4 == 3))
                    P.op("act", lambda e: e.activation(out=vb[k][:].rearrange("p s v -> p (s v)"), in_=pv[:, :], func=AF.Copy), reads=[bpv], writes=[bvb[k]])
                    P.dma("sp", Vov[:, n * 4:(n + 1) * 4, h * 128:(h + 1) * 128], vb[k][:], dvb[k], reads=[bvb[k]], writes=[self.bout])
            P.op("dve", lambda e: e.tensor_scalar(out=km[:], in0=km[:], scalar1=1.0 / 256.0, scalar2=None, op0=ALU.mult), reads=[bkm], writes=[bkm])
            d_k = P.dsem("kmo")
            P.dma("sp", self.KMo, km[:].rearrange("p h b -> p (h b)"), d_k, reads=[bkm], writes=[self.bout])
            P.barrier()

    def moba_attn(self):
        P = self.P
        H = 8
        NS = 72
        SCALE = 1.0 / math.sqrt(128.0)
        with ExitStack() as es:
            self.QTi = self.dram("QT", [128, H * T], BF16, "ExternalInput")
            self.Kall = self.dram("Kall", [H * 128, NS * 256], BF16, "ExternalInput")
            self.Vall = self.dram("Vall", [H * 128, NS * 256], BF16, "ExternalInput")
            self.KMall = self.dram("KMall", [128, H * NS], F32, "ExternalInput")
            self.mcst_d = self.dram("mcst", [128, 3 * 8 * NS + 4 * 512], F32, "ExternalInput")
            QT = self.sb(es, "ma_QT", [128, H, T], BF16)
            bQT = Buf()
            d_q = P.dsem("qt")
            P.dma("sp", QT[:].rearrange("p h t -> p (h t)"), self.QTi, d_q, writes=[bQT])
            mc = self.sb(es, "ma_mc", [128, 3, 8, NS], F32)
            caus32 = self.sb(es, "ma_c32", [128, 512], F32)
            caus = self.sb(es, "ma_caus", [128, 4, 512], BF16)
            bmc = Buf()
            d_m = P.dsem("mc")
            d_m2 = P.dsem("mc2")
            P.dma("sp", mc[:].rearrange("p a l s -> p (a l s)"), self.mcst_d[:, 0:3 * 8 * NS], d_m, writes=[bmc])
            for v in range(4):
                P.dma("sp", caus32[:], self.mcst_d[:, 3 * 8 * NS + v * 512:3 * 8 * NS + (v + 1) * 512], d_m2, writes=[bmc])
                P.op("act", lambda e: e.activation(out=caus[:, v, :], in_=caus32[:], func=AF.Copy), reads=[bmc], writes=[bmc])
            kmf = self.sb(es, "ma_kmf", [128, H, NS], F32)
            kmb = self.sb(es, "ma_kmb", [128, H, NS], BF16)
            d_km = P.dsem("km")
            bkm = Buf()
            P.dma("sp", kmf[:].rearrange("p h s -> p (h s)"), self.KMall, d_km, writes=[bkm])
            P.op("act", lambda e: e.activation(out=kmb[:], in_=kmf[:], func=AF.Copy), reads=[bkm], writes=[bkm])
            ident = self.sb(es, "ma_ident", [128, 128], BF16)
            ones = self.sb(es, "ma_ones", [128, 128], BF16)
            Esel = self.sb(es, "ma_Esel", [NS, NS, 128], BF16)
            bcn = Buf()
            P.op("act", lambda e: e.activation(out=ident[:], in_=self.cst[:, C_IDENT:C_IDENT + 128], func=AF.Copy), reads=[self.bvec], writes=[bcn])
            P.op("dve", lambda e: e.memset(ones[:], 1.0), writes=[bcn])
            P.op("dve", lambda e: e.tensor_copy(out=Esel[:], in_=self.cst[0:NS, C_IDENT:C_IDENT + NS].unsqueeze(2).to_broadcast([NS, NS, 128])),
                 reads=[self.bvec], writes=[bcn])
            maskT = [self.sb(es, f"ma_maskT{k}", [NS, T], BF16) for k in range(2)]
            bmaskT = [Buf(), Buf()]
            R2 = lambda nm, shp, dt: [self.sb(es, f"{nm}{k}", shp, dt) for k in range(2)]
            gm, bgm = R2("ma_gm", [128, NS], F32), [Buf(), Buf()]
            t8, bt8 = R2("ma_t8", [128, 8], F32), [Buf(), Buf()]
            al, bal = R2("ma_al", [128, NS], F32), [Buf(), Buf()]
            alb, balb = R2("ma_alb", [128, NS], BF16), [Buf(), Buf()]
            NKS = 3
            KS = [self.sb(es, f"ma_KS{k}", [128, 1024], BF16) for k in range(NKS)]
            VS = [self.sb(es, f"ma_VS{k}", [128, 8, 128], BF16) for k in range(NKS)]
            bKS = [Buf() for _ in range(NKS)]
            bVS = [Buf() for _ in range(NKS)]
            dKS = [P.dsem(f"ks{k}") for k in range(NKS)]
            dVS = [P.dsem(f"vs{k}") for k in range(NKS)]
            NPT = 4
            PT = [self.sb(es, f"ma_PT{k}", [128, 512], BF16) for k in range(NPT)]
            bPT = [Buf() for _ in range(NPT)]
            rden, brden = R2("ma_rden", [128, 512], F32), [Buf(), Buf()]
            Y = self.XB
            bYh = [Buf() for _ in range(H)]
            self.rot_banks = (4, 5, 6, 7)
            gi = 0
            ksi = 0
            pti = 0
            for h in range(H):
                mk = h % 2
                for s16 in range(16):
                    g_ = gi % 2
                    gi += 1
                    lb_ = s16 // 2
                    qs_ = slice(s16 * 128, (s16 + 1) * 128)
                    pg, bpg = self.bank()
                    P.op("pe", lambda e: e.matmul(pg[:, 0:NS], QT[:, h, qs_], kmb[:, h, :], start=True, stop=True), reads=[bQT, bkm], writes=[bpg])
                    P.op("dve", lambda e: e.tensor_tensor(out=gm[g_][:], in0=pg[:, 0:NS], in1=mc[:, 0, lb_, :], op=ALU.add), reads=[bpg, bmc], writes=[bgm[g_]])
                    P.op("dve", lambda e: e.max(out=t8[g_][:], in_=gm[g_][:]), reads=[bgm[g_]], writes=[bt8[g_]])
                    P.op("dve", lambda e: e.tensor_scalar(out=al[g_][:], in0=gm[g_][:], scalar1=t8[g_][:, 2:3], scalar2=None, op0=ALU.is_ge),
                         reads=[bgm[g_], bt8[g_]], writes=[bal[g_]])
                    P.op("dve", lambda e: e.tensor_tensor(out=al[g_][:], in0=al[g_][:], in1=mc[:, 1, lb_, :], op=ALU.mult), reads=[bal[g_], bmc], writes=[bal[g_]])
                    P.op("dve", lambda e: e.tensor_tensor(out=al[g_][:], in0=al[g_][:], in1=mc[:, 2, lb_, :], op=ALU.add), reads=[bal[g_], bmc], writes=[bal[g_]])
                    P.op("dve", lambda e: e.tensor_scalar(out=alb[g_][:], in0=al[g_][:], scalar1=-1.0, scalar2=30000.0, op0=ALU.add, op1=ALU.mult),
                         reads=[bal[g_]], writes=[balb[g_]])
                    pt_, bpt_ = self.bank()
                    ptb = pt_[:].bitcast(BF16)
                    P.op("pe", lambda e: e.transpose(ptb[0:NS, 0:128], alb[g_][:], ident[:]), reads=[balb[g_], bcn], writes=[bpt_])
                    P.op("act", lambda e: e.activation(out=maskT[mk][:, qs_], in_=ptb[0:NS, 0:128], func=AF.Copy), reads=[bpt_], writes=[bmaskT[mk]])
                for qt in range(2):
                    pend = []
                    LA = 2
                    O = [self.PS[0], self.PS[1]]
                    bO = [self.bPS[0], self.bPS[1]]
                    DN = [self.PS[2], self.PS[3]]
                    bDN = [self.bPS[2], self.bPS[3]]
                    def need(j, hf):
                        if j < 8:
                            return j in (qt * 4 + hf * 2, qt * 4 + hf * 2 + 1)
                        return (j - 8) < 32 * qt + 16 * hf + 16
                    ulist = {hf: [(j, kt2) for j in range(NS) for kt2 in range(2) if need(j, hf)] for hf in range(2)}
                    for g4 in range(NS // 4):
                        if not any(need(g4 * 4 + j4, hf) for j4 in range(4) for hf in range(2)):
                            continue
                        ks = ksi % NKS
                        ksi += 1
                        P.dma("sp", KS[ks][:], self.Kall[h * 128:(h + 1) * 128, g4 * 1024:(g4 + 1) * 1024], dKS[ks], writes=[bKS[ks]])
                        P.dma("sp", VS[ks][:].rearrange("p t v -> p (t v)"), self.Vall[h * 128:(h + 1) * 128, g4 * 1024:(g4 + 1) * 1024], dVS[ks], writes=[bVS[ks]])
                        for j4 in range(4):
                            j = g4 * 4 + j4
                            for kt2 in range(2):
                                kti = j4 * 2 + kt2
                                for hf in range(2):
                                    if not need(j, hf):
                                        continue
                                    first = ((j, kt2) == ulist[hf][0])
                                    last = ((j, kt2) == ulist[hf][-1])
                                    q0 = qt * 1024 + hf * 512
                                    ps, bps = self.bank()
                                    lbs = (qt * 4 + hf * 2, qt * 4 + hf * 2 + 1)
                                    diag = j in lbs
                                    P.op("pe", lambda e: e.matmul(ps[:, :], KS[ks][:, kti * 128:(kti + 1) * 128], QT[:, h, q0:q0 + 512], start=True, stop=False),
                                         reads=[bKS[ks], bQT], writes=[bps], signal=False)
                                    P.op("pe", lambda e: e.matmul(ps[:, :], Esel[:, j, :], maskT[mk][:, q0:q0 + 512], start=False, stop=(not diag)),
                                         reads=[bcn, bmaskT[mk]], writes=[bps], signal=(not diag))
                                    if diag:
                                        v = kt2 * 2 + (j - lbs[0])
                                        P.op("pe", lambda e: e.matmul(ps[:, :], ident[:], caus[:, v, :], start=False, stop=True),
                                             reads=[bcn, bmc], writes=[bps])
                                    p_ = pti % NPT
                                    pti += 1
                                    P.op("act", lambda e: e.activation(out=PT[p_][:], in_=ps[:, :], func=AF.Exp, scale=SCALE), reads=[bps], writes=[bPT[p_]])
                                    def _pv(ks=ks, kti=kti, p_=p_, hf=hf, first=first, last=last):
                                        P.op("pe", lambda e: e.matmul(O[hf][:, :], VS[ks][:, kti, :], PT[p_][:], start=first, stop=last),
                                             reads=[bVS[ks], bPT[p_]], writes=[bO[hf]], signal=last)
                                        P.op("pe", lambda e: e.matmul(DN[hf][:, :], ones[:], PT[p_][:], start=first, stop=last),
                                             reads=[bcn, bPT[p_]], writes=[bDN[hf]], signal=True)
                                    pend.append(_pv)
                                    if len(pend) > LA:
                                        pend.pop(0)()
                    while pend:
                        pend.pop(0)()
                    for hf in range(2):
                        q0 = qt * 1024 + hf * 512
                        r_ = hf
                        P.op("dve", lambda e: e.reciprocal(out=rden[r_][:], in_=DN[hf][:, :]), reads=[bDN[hf]], writes=[brden[r_]])
                        P.op("dve", lambda e: e.tensor_tensor(out=Y[:, h, 2 + q0:2 + q0 + 512], in0=O[hf][:, :], in1=rden[r_][:], op=ALU.mult),
                             reads=[bO[hf], brden[r_]], writes=[bYh[h]])
            self.rot_banks = (0, 1, 2, 3, 4, 5, 6, 7)
            self.out_proj_residual("wout", Y, bYh, DC, first=True, coff=2)
            P.barrier()


_PROGS = {}


def prog(kind):
    if kind not in _PROGS:
        _PROGS[kind] = Builder(kind).build()
    return _PROGS[kind]


def shard_xT(x_full):
    xT = np.ascontiguousarray(x_full.T)
    xTp = np.concatenate([np.zeros((D, 2), np.float32), xT], axis=1)
    return [np.ascontiguousarray(xTp[:, c * T:c * T + T + 2]) for c in range(NCORE)]


def launch(kind, maps):
    res = run_bass_kernel_spmd(prog(kind), maps, core_ids=list(range(NCORE)))
    return res.results


def gather_x(results):
    return np.ascontiguousarray(np.concatenate([r["outT"] for r in results], axis=1).T)


def run_ffn(inp, i, x):
    xs = shard_xT(x)
    vec = make_vec(inp[f"l{i}_ln2_g"], inp[f"l{i}_ln2_b"], conv=inp[f"l{i}_ffn_conv"])
    wup = tile_w(inp[f"l{i}_ffn_w_up"])
    wdn = tile_w(inp[f"l{i}_ffn_w_down"])
    maps = [dict(xT=xs[c], vec=vec, cst=make_cst(c), wup=wup, wdn=wdn) for c in range(NCORE)]
    return gather_x(launch("ff", maps))


def run_conv(inp, i, x):
    xs = shard_xT(x)
    vec = make_vec(inp[f"l{i}_ln1_g"], inp[f"l{i}_ln1_b"], conv=inp[f"l{i}_mix_conv"])
    win = tile_w(inp[f"l{i}_mix_w_in"])
    wout = tile_w(inp[f"l{i}_mix_w_out"])
    maps = [dict(xT=xs[c], vec=vec, cst=make_cst(c), win=win, wout=wout) for c in range(NCORE)]
    return gather_x(launch("cv", maps))


def run_hgrn(inp, i, x):
    xs = shard_xT(x)
    vec = make_vec(inp[f"l{i}_ln1_g"], inp[f"l{i}_ln1_b"], norm_g=inp[f"l{i}_mix_norm_g"], lbraw=inp["hgrn_lower_bounds"])
    win = tile_w(inp[f"l{i}_mix_w_in"])
    wout = tile_w(inp[f"l{i}_mix_w_out"])
    Aall = np.zeros((128, NCORE * 8), np.float32)
    Ball = np.zeros((NCORE * 128, 1024), np.float32)
    out = None
    for phase in range(2):
        maps = [dict(xT=xs[c], vec=vec, cst=make_cst(c, layer=i), win=win, Aall=Aall, Ball=Ball) for c in range(NCORE)]
        if phase == 1:
            for m in maps:
                m["wout"] = wout
        res = launch("hg1" if phase == 0 else "hg", maps)
        if phase == 0:
            Aall = np.ascontiguousarray(np.concatenate([r["Aout"] for r in res], axis=1))
            Ball = np.ascontiguousarray(np.concatenate([r["Bout"] for r in res], axis=0))
        else:
            out = gather_x(res)
    return out


def zz_block(c, s_):
    return 8 * s_ + (c if s_ % 2 == 0 else 7 - c)


def run_moba(inp, i, x):
    H = 8
    NSL = 72
    xs = shard_xT(x)
    vec = make_vec(inp[f"l{i}_ln1_g"], inp[f"l{i}_ln1_b"])
    W = inp[f"l{i}_mix_w_in"]
    perm = []
    for qk in range(2):
        for h in range(H):
            base = qk * 1024 + h * 128
            Wp = np.zeros((D, 128), np.float32)
            Wp[:, 0:16] = W[:, base + 16:base + 32]
            Wp[:, 16:32] = W[:, base:base + 16]
            perm.append(Wp)
    win = tile_w(np.concatenate([W] + perm, axis=1))
    pos = inp["positions"].astype(np.int32)
    maps = [dict(xT=xs[c], vec=vec, cst=make_cst(c), win=win, pos=np.ascontiguousarray(pos[:, c * T:(c + 1) * T])) for c in range(NCORE)]
    res = launch("mo1", maps)
    Qg = np.concatenate([np.asarray(r["QT"]).reshape(128, H, 8, 256) for r in res], axis=2)
    Kg = np.concatenate([np.asarray(r["KT"]).reshape(128, H, 8, 256).transpose(1, 0, 2, 3) for r in res], axis=2)
    Vg = np.concatenate([np.asarray(r["V"]).reshape(128, 8, 2, H, 128).transpose(3, 0, 1, 2, 4) for r in res], axis=2)
    KMg = np.concatenate([np.asarray(r["KM"]).reshape(128, H, 8) for r in res], axis=2)
    xTfull = np.ascontiguousarray(x.T).reshape(D, 64, 256)
    wout = tile_w(inp[f"l{i}_mix_w_out"])
    caus = np.zeros((4, 128, 512), np.float32)
    p_ = np.arange(128)[:, None]
    qi = np.arange(256)[None, :]
    for kt2 in range(2):
        for posb in range(2):
            caus[kt2 * 2 + posb][:, posb * 256:(posb + 1) * 256] = np.where(kt2 * 128 + p_ > qi, -30000.0, 0.0)
    maps = []
    own_all = []
    for c in range(NCORE):
        own = [zz_block(c, s_) for s_ in range(8)]
        own_all.append(own)
        pc = np.array(own + list(range(64)))
        Kall = np.ascontiguousarray(Kg[:, :, pc, :]).reshape(H * 128, NSL * 256)
        Vall = np.ascontiguousarray(Vg[:, :, pc]).reshape(H * 128, NSL * 256)
        KMall = np.ascontiguousarray(KMg[:, :, pc]).reshape(128, H * NSL)
        QT = np.ascontiguousarray(Qg[:, :, np.array(own), :]).reshape(128, H * T)
        xT = np.concatenate([np.zeros((D, 2), np.float32), xTfull[:, np.array(own), :].reshape(D, T)], axis=1)
        past = np.zeros((8, NSL), np.float32)
        ownm = np.zeros((8, NSL), np.float32)
        for s_ in range(8):
            past[s_, 8:] = (np.arange(64) < own[s_]).astype(np.float32)
            ownm[s_, s_] = 1.0
        pastbias = np.where(past > 0, 0.0, -1e30).astype(np.float32)
        row = np.concatenate([pastbias.reshape(-1), past.reshape(-1), ownm.reshape(-1)])
        mcst = np.concatenate([np.broadcast_to(row[None, :], (128, 3 * 8 * NSL)), caus.transpose(1, 0, 2).reshape(128, 2048)], axis=1).astype(np.float32)
        maps.append(dict(xT=np.ascontiguousarray(xT), vec=vec, cst=make_cst(c), QT=QT, Kall=Kall, Vall=Vall, KMall=KMall,
                         mcst=np.ascontiguousarray(mcst), wout=wout))
    res2 = launch("mo2", maps)
    out = np.zeros((64, 256, D), np.float32)
    for c in range(NCORE):
        oc = np.asarray(res2[c]["outT"]).T.reshape(8, 256, D)
        for s_ in range(8):
            out[own_all[c][s_]] = oc[s_]
    return np.ascontiguousarray(out.reshape(64 * 256, D))


def kernel(**inputs):
    inp = {k: np.asarray(v) for k, v in inputs.items()}
    x = np.ascontiguousarray(inp["x"][0])
    for i in range(DEPTH):
        kind = i % 3
        if kind == 0:
            x = run_hgrn(inp, i, x)
        elif kind == 1:
            x = run_moba(inp, i, x)
        else:
            x = run_conv(inp, i, x)
        x = run_ffn(inp, i, x)
    return x[None].astype(np.float32)
```

```python
import math
import numpy as np
from contextlib import ExitStack
import concourse.bass as bass
import concourse.mybir as mybir
from concourse.bass_utils import run_bass_kernel_spmd

F32 = mybir.dt.float32
BF16 = mybir.dt.bfloat16
I32 = mybir.dt.int32
AF = mybir.ActivationFunctionType
ALU = mybir.AluOpType
AX = mybir.AxisListType

NCORE = 8
T = 2048
D = 1024
DC = 8
DFF = 2816
FC = 22
DEPTH = 4
ALPHA = (2.0 * DEPTH) ** 0.25
LN_EPS = 1e-5
RMS_EPS = 1e-6
NT = T // 512


class Buf:
    __slots__ = ("name", "w", "r")

    def __init__(self, name="b"):
        self.name = name
        self.w = None
        self.r = {}


class DSem:
    def __init__(self, key, h):
        self.key = key
        self.h = h
        self.cnt = 0


class Prog:
    ENG = ("pe", "act", "dve", "pool", "sp")

    def __init__(self, nc, es):
        self.nc = nc
        self.es = es
        self.eng = dict(pe=nc.tensor, act=nc.scalar, dve=nc.vector, pool=nc.gpsimd, sp=nc.sync)
        self.semh = {}
        self.cnt = {}
        self.known = {k: {} for k in self.ENG}
        for k in self.ENG:
            self.semh[k] = es.enter_context(nc.semaphore("s_" + k))
            self.cnt[k] = 0
        self.dsems = []
        self.nwait = 0
        self.nins = 0

    def dsem(self, name):
        h = self.es.enter_context(self.nc.semaphore("d_" + name))
        d = DSem("d_" + name, h)
        self.semh[d.key] = h
        self.dsems.append(d)
        return d

    def _wait(self, e, key, val):
        if val <= 0:
            return
        if self.known[e].get(key, 0) >= val:
            return
        if key == e:
            if e == "pe":
                return
            if val > self.cnt[e]:
                return
        self.eng[e].wait_ge(self.semh[key], val)
        self.nwait += 1
        self.known[e][key] = val

    def deps(self, e, reads, writes):
        need = {}
        for b in reads:
            if b.w is not None:
                k, v = b.w
                if need.get(k, 0) < v:
                    need[k] = v
        for b in writes:
            if b.w is not None:
                k, v = b.w
                if need.get(k, 0) < v:
                    need[k] = v
            for k, v in b.r.items():
                if need.get(k, 0) < v:
                    need[k] = v
        for k, v in need.items():
            self._wait(e, k, v)

    def op(self, e, fn, reads=(), writes=(), signal=True):
        self.deps(e, reads, writes)
        ins = fn(self.eng[e])
        self.nins += 1
        if signal:
            self.cnt[e] += 1
            ins.then_inc(self.semh[e], 1)
            mark = (e, self.cnt[e])
        else:
            mark = (e, self.cnt[e] + 1)
        for b in writes:
            b.w = mark
            b.r = {}
        for b in reads:
            if b.r.get(e, 0) < mark[1]:
                b.r[e] = mark[1]
        return ins

    def _mark_async(self, ds, reads, writes):
        mark = (ds.key, ds.cnt)
        for b in writes:
            b.w = mark
            b.r = {}
        for b in reads:
            b.r[ds.key] = ds.cnt

    def dma(self, q, out, in_, ds, reads=(), writes=(), **kw):
        self.deps(q, reads, writes)
        ds.cnt += 16
        ins = self.eng[q].dma_start(out=out, in_=in_, **kw)
        ins.then_inc(ds.h, 16)
        self._mark_async(ds, reads, writes)
        return ins

    def allgather(self, in_ap, out_ap, ds, reads=(), writes=()):
        q = "pool"
        self.deps(q, reads, writes)
        ds.cnt += 1
        ins = self.nc.gpsimd.collective_compute(
            "AllGather", ALU.bypass, replica_groups=[list(range(NCORE))],
            ins=[in_ap.opt()], outs=[out_ap.opt()])
        ins.then_inc(ds.h, 1)
        self._mark_async(ds, reads, writes)
        return ins

    def barrier(self, engines=None):
        engines = engines or self.ENG
        for e in engines:
            for k in self.ENG:
                if k != e:
                    self._wait(e, k, self.cnt[k])
            for d in self.dsems:
                self._wait(e, d.key, d.cnt)


def tile_w(W):
    Din, Fo = W.shape
    Wt = W.reshape(Din // 128, 128, Fo // 128, 128).transpose(2, 1, 0, 3)
    return np.ascontiguousarray(Wt).reshape(Fo // 128 * 128, Din)


def vec_cols(v):
    return np.ascontiguousarray(v.reshape(-1, 128).T)


V_LNG = 0
V_LNB = 8
V_CONV = 16
V_NORMG = 148
V_LBRAW = 156
NVEC = 188
C_HMASK = 0
C_LMASK = 8
C_MASK2 = 12
C_IDENT = 140
C_INVF = 268
C_SGN = 269
NCST = 270


def conv_cols(cw):
    return np.concatenate([vec_cols(cw[k]) for k in range(cw.shape[0])], axis=1)


def make_vec(ln_g, ln_b, conv=None, norm_g=None, lbraw=None):
    v = np.zeros((128, NVEC), np.float32)
    v[:, V_LNG:V_LNG + 8] = vec_cols(ln_g)
    v[:, V_LNB:V_LNB + 8] = vec_cols(ln_b)
    if conv is not None:
        c = conv_cols(conv)
        v[:, V_CONV:V_CONV + c.shape[1]] = c
    if norm_g is not None:
        v[:, V_NORMG:V_NORMG + 8] = vec_cols(norm_g)
    if lbraw is not None:
        v[:, V_LBRAW:V_LBRAW + 32] = np.concatenate([vec_cols(lbraw[l]) for l in range(DEPTH)], axis=1)
    return v


def make_cst(core, layer=0):
    c = np.zeros((128, NCST), np.float32)
    for r in range(NCORE):
        c[:, C_HMASK + r] = 1.0 if r < core else 0.0
    for l in range(DEPTH):
        c[:, C_LMASK + l] = 1.0 if 1 <= l <= layer else 0.0
    s_ = np.arange(128)[:, None]
    t_ = np.arange(128)[None, :]
    c[:, C_MASK2:C_MASK2 + 128] = ((s_ // 64 == t_ // 64) & (s_ <= t_)).astype(np.float32)
    c[:, C_IDENT:C_IDENT + 128] = np.eye(128, dtype=np.float32)
    invf = (1.0 / (500000.0 ** (np.arange(0, 32, 2, dtype=np.float32) / np.float32(32.0)))).astype(np.float32)
    c[0:32, C_INVF] = np.concatenate([invf, invf])
    c[0:16, C_SGN] = -1.0
    c[16:32, C_SGN] = 1.0
    return c


class Builder:
    def __init__(self, kind):
        self.kind = kind
        self.nc = bass.Bass("TRN2", target_bir_lowering=False)

    def dram(self, name, shape, dt, kind="Internal"):
        return self.nc.dram_tensor(name, list(shape), dt, kind=kind).ap()

    def sb(self, es, name, shape, dt):
        self._uid = getattr(self, "_uid", 0) + 1
        return es.enter_context(self.nc.sbuf_tensor(f"{name}_{self._uid}", list(shape), dt))

    def build(self):
        nc = self.nc
        kind = self.kind
        with ExitStack() as es:
            self.es = es
            P = self.P = Prog(nc, es)
            self.xT = self.dram("xT", [D, T + 2], F32, "ExternalInput")
            self.vec_d = self.dram("vec", [128, NVEC], F32, "ExternalInput")
            self.cst_d = self.dram("cst", [128, NCST], F32, "ExternalInput")
            self.wd = {}
            if kind == "ff":
                self.wd["wup"] = self.dram("wup", [2 * DFF, D], F32, "ExternalInput")
                self.wd["wdn"] = self.dram("wdn", [D, DFF], F32, "ExternalInput")
            elif kind == "cv":
                self.wd["win"] = self.dram("win", [3072, D], F32, "ExternalInput")
                self.wd["wout"] = self.dram("wout", [D, D], F32, "ExternalInput")
            elif kind == "hg":
                self.wd["win"] = self.dram("win", [4096, D], F32, "ExternalInput")
                self.wd["wout"] = self.dram("wout", [D, D], F32, "ExternalInput")
            elif kind == "hg1":
                self.wd["win"] = self.dram("win", [4096, D], F32, "ExternalInput")
            elif kind == "mo1":
                self.wd["win"] = self.dram("win", [5120, D], F32, "ExternalInput")
            elif kind == "mo2":
                self.wd["wout"] = self.dram("wout", [D, D], F32, "ExternalInput")
            if kind not in ("mo1", "hg1"):
                self.outT = self.dram("outT", [D, T], F32, "ExternalOutput")
            self.XR = self.sb(es, "XR", [128, DC, T], F32)
            self.XB = self.sb(es, "XB", [128, DC, T + 2], BF16)
            self.bXR = [Buf(f"XR{n}") for n in range(NT)]
            self.bXB = [Buf(f"XB{n}") for n in range(NT)]
            self.bXBh = Buf("XBh")
            self.vec = self.sb(es, "vecs", [128, NVEC], F32)
            self.cst = self.sb(es, "csts", [128, NCST], F32)
            self.bvec = Buf("vec")
            self.ones_ln = self.sb(es, "ones_ln", [128, 128], BF16)
            self.bconst = Buf("const")
            NW = 2 if kind == "mo2" else 8
            self.WS = [self.sb(es, f"ws{k}", [128, DC, 128], BF16) for k in range(NW)]
            self.bWS = [Buf(f"ws{k}") for k in range(NW)]
            self.dWS = [P.dsem(f"ws{k}") for k in range(NW)]
            self.wsi = 0
            self.PS = [es.enter_context(nc.psum_tensor(f"ps{k}", [128, 512], F32)) for k in range(8)]
            self.bPS = [Buf(f"ps{k}") for k in range(8)]
            self.psi = 0
            self.d_out = P.dsem("out")
            self.d_misc = P.dsem("misc")
            self.d_misc2 = P.dsem("misc2")
            self.bout = Buf("out")

            P.dma("sp", self.vec[:], self.vec_d, self.d_misc, writes=[self.bvec])
            P.dma("sp", self.cst[:], self.cst_d, self.d_misc2, writes=[self.bvec])
            xTv = self.xT.rearrange("(c p) t -> p c t", p=128)
            d_ins = [P.dsem(f"in{n}") for n in range(NT + 1)]
            for n in range(NT):
                P.dma("sp", self.XR[:, :, n * 512:(n + 1) * 512], xTv[:, :, 2 + n * 512:2 + (n + 1) * 512],
                      d_ins[n], writes=[self.bXR[n]])
            self.hal32 = self.sb(es, "hal32", [128, DC, 2], F32)
            self.bhal32 = Buf("hal32")
            P.dma("sp", self.hal32[:], xTv[:, :, 0:2], d_ins[NT], writes=[self.bhal32])
            P.op("dve", lambda e: e.memset(self.ones_ln[:], 1.0 / D), writes=[self.bconst])
            if kind != "mo2":
                for n in range(NT):
                    P.op("act", lambda e: e.activation(out=self.XB[:, :, 2 + n * 512:2 + (n + 1) * 512],
                                                       in_=self.XR[:, :, n * 512:(n + 1) * 512], func=AF.Copy),
                         reads=[self.bXR[n]], writes=[self.bXB[n]])
                P.op("act", lambda e: e.activation(out=self.XB[:, :, 0:2], in_=self.hal32[:], func=AF.Copy),
                     reads=[self.bhal32], writes=[self.bXBh])

            if kind == "ff":
                self.ffn()
            elif kind == "cv":
                self.conv_mixer()
            elif kind in ("hg", "hg1"):
                self.hgrn_mixer()
            elif kind == "mo1":
                self.moba_qkv()
            elif kind == "mo2":
                self.moba_attn()
            if kind not in ("mo1", "hg1"):
                self.layernorm(V_LNG, V_LNB)
                oTv = self.outT.rearrange("(c p) t -> p c t", p=128)
                for n in range(NT):
                    P.dma("sp", oTv[:, :, n * 512:(n + 1) * 512], self.XR[:, :, n * 512:(n + 1) * 512],
                          self.d_out, reads=[self.bXR[n]], writes=[self.bout])
            P.barrier()
        return nc

    def bank(self):
        rb = getattr(self, "rot_banks", (0, 1, 2, 3, 4, 5, 6, 7))
        self.psi = (self.psi + 1) % len(rb)
        k = rb[self.psi]
        return self.PS[k], self.bPS[k]

    def load_w(self, name, j, c0=0, nch=DC):
        P = self.P
        k = self.wsi
        self.wsi = (self.wsi + 1) % len(self.WS)
        slot, b, ds = self.WS[k], self.bWS[k], self.dWS[k]
        src = self.wd[name][j * 128:(j + 1) * 128, c0 * 128:(c0 + nch) * 128]
        dst = slot[:, 0:nch, :].rearrange("p c i -> p (c i)")
        P.dma("pool", dst, src, ds, writes=[b])
        return slot, b

    def xb_bufs(self, col0, n):
        bs = []
        if col0 < 2:
            bs.append(self.bXBh)
        lo = max(col0 - 2, 0)
        hi = col0 + n - 2
        for t in range(NT):
            if lo < (t + 1) * 512 and hi > t * 512:
                bs.append(self.bXB[t])
        return bs

    def proj(self, w, bw, col0, n, ps, bps, m0=0, m=128, src=None, srcbufs=None, nch=DC):
        P = self.P
        src = self.XB if src is None else src
        srcbufs = self.xb_bufs(col0, n) if srcbufs is None else srcbufs
        for c in range(nch):
            P.op("pe", lambda e: e.matmul(ps[0:m, 0:n], w[:, c, m0:m0 + m], src[:, c, col0:col0 + n],
                                          start=(c == 0), stop=(c == nch - 1)),
                 reads=[bw] + srcbufs, writes=[bps], signal=(c == nch - 1))

    def layernorm(self, goff, boff):
        P = self.P
        with ExitStack() as es:
            zb = [self.sb(es, f"ln_zb{k}", [128, DC, 512], BF16) for k in range(2)]
            zq = [self.sb(es, f"ln_zq{k}", [128, DC, 512], BF16) for k in range(2)]
            bzb = [Buf() for _ in range(2)]
            bzq = [Buf() for _ in range(2)]
            mean = [self.sb(es, f"ln_mean{k}", [128, 512], F32) for k in range(2)]
            rstd = [self.sb(es, f"ln_rstd{k}", [128, 512], F32) for k in range(2)]
            tmp = [self.sb(es, f"ln_tmp{k}", [128, 512], F32) for k in range(2)]
            bmean = [Buf() for _ in range(2)]
            brstd = [Buf() for _ in range(2)]
            btmp = [Buf() for _ in range(2)]
            for n in range(NT):
                k = n % 2
                cs = slice(n * 512, (n + 1) * 512)
                P.op("act", lambda e: e.activation(out=zb[k][:], in_=self.XR[:, :, cs], func=AF.Copy),
                     reads=[self.bXR[n]], writes=[bzb[k]])
                P.op("act", lambda e: e.activation(out=zq[k][:], in_=self.XR[:, :, cs], func=AF.Square),
                     reads=[self.bXR[n]], writes=[bzq[k]])
                pm, bpm = self.bank()
                for c in range(DC):
                    P.op("pe", lambda e: e.matmul(pm[:, :], self.ones_ln[:], zb[k][:, c, :], start=(c == 0), stop=(c == DC - 1)),
                         reads=[self.bconst, bzb[k]], writes=[bpm], signal=(c == DC - 1))
                pq, bpq = self.bank()
                for c in range(DC):
                    P.op("pe", lambda e: e.matmul(pq[:, :], self.ones_ln[:], zq[k][:, c, :], start=(c == 0), stop=(c == DC - 1)),
                         reads=[self.bconst, bzq[k]], writes=[bpq], signal=(c == DC - 1))
                P.op("dve", lambda e: e.tensor_copy(out=mean[k][:], in_=pm[:, :]), reads=[bpm], writes=[bmean[k]])
                P.op("dve", lambda e: e.tensor_tensor(out=tmp[k][:], in0=mean[k][:], in1=mean[k][:], op=ALU.mult),
                     reads=[bmean[k]], writes=[btmp[k]])
                P.op("dve", lambda e: e.tensor_tensor(out=rstd[k][:], in0=pq[:, :], in1=tmp[k][:], op=ALU.subtract),
                     reads=[bpq, btmp[k]], writes=[brstd[k]])
                P.op("dve", lambda e: e.tensor_scalar(out=rstd[k][:], in0=rstd[k][:], scalar1=LN_EPS, scalar2=None,
                                                      op0=ALU.add),
                     reads=[brstd[k]], writes=[brstd[k]])
                P.op("act", lambda e: e.activation(out=rstd[k][:], in_=rstd[k][:], func=AF.Sqrt),
                     reads=[brstd[k]], writes=[brstd[k]])
                P.op("dve", lambda e: e.reciprocal(out=rstd[k][:], in_=rstd[k][:]),
                     reads=[brstd[k]], writes=[brstd[k]])
                for c in range(DC):
                    P.op("dve", lambda e: e.tensor_tensor(out=tmp[k][:], in0=self.XR[:, c, cs], in1=mean[k][:], op=ALU.subtract),
                         reads=[self.bXR[n], bmean[k]], writes=[btmp[k]])
                    P.op("dve", lambda e: e.tensor_tensor(out=tmp[k][:], in0=tmp[k][:], in1=rstd[k][:], op=ALU.mult),
                         reads=[btmp[k], brstd[k]], writes=[btmp[k]])
                    P.op("act", lambda e: e.activation(out=self.XR[:, c, cs], in_=tmp[k][:], func=AF.Identity,
                                                       bias=self.vec[:, boff + c:boff + c + 1],
                                                       scale=self.vec[:, goff + c:goff + c + 1]),
                         reads=[btmp[k], self.bvec], writes=[self.bXR[n]])
                P.op("act", lambda e: e.activation(out=self.XB[:, :, 2 + n * 512:2 + (n + 1) * 512],
                                                   in_=self.XR[:, :, cs], func=AF.Copy),
                     reads=[self.bXR[n]], writes=[self.bXB[n]])
            P.barrier()

    def out_proj_residual(self, wname, Y, bY, nch, c0=0, first=True, wtile_c0=0, coff=0):
        P = self.P
        for f in range(DC):
            w, bw = self.load_w(wname, f, c0=wtile_c0, nch=nch)
            for n in range(NT):
                ps, bps = self.bank()
                self.proj(w, bw, coff + n * 512, 512, ps, bps, src=Y, srcbufs=bY if isinstance(bY, list) else [bY], nch=nch)
                cs = slice(n * 512, (n + 1) * 512)
                if first:
                    P.op("dve", lambda e: e.scalar_tensor_tensor(out=self.XR[:, f, cs], in0=self.XR[:, f, cs], scalar=ALPHA,
                                                                 in1=ps[:, :], op0=ALU.mult, op1=ALU.add),
                         reads=[bps, self.bXR[n]], writes=[self.bXR[n]])
                else:
                    P.op("dve", lambda e: e.tensor_tensor(out=self.XR[:, f, cs], in0=self.XR[:, f, cs], in1=ps[:, :], op=ALU.add),
                         reads=[bps, self.bXR[n]], writes=[self.bXR[n]])

    def ffn(self):
        P = self.P
        cvo = V_CONV
        groups = [(0, 8), (8, 15), (15, 22)]
        with ExitStack() as es:
            GH = 8
            H = self.sb(es, "ffn_H", [128, GH, T], BF16)
            bH = [Buf() for _ in range(GH)]
            U = [[self.sb(es, f"ffn_U{ab}{k}", [128, 514], F32) for k in range(3)] for ab in range(2)]
            bU = [[Buf() for _ in range(3)] for _ in range(2)]
            Y = [[self.sb(es, f"ffn_Y{ab}{k}", [128, 512], F32) for k in range(2)] for ab in range(2)]
            bY = [[Buf() for _ in range(2)] for _ in range(2)]
            TT = [[self.sb(es, f"ffn_T{ab}{k}", [128, 512], F32) for k in range(2)] for ab in range(2)]
            bTT = [[Buf() for _ in range(2)] for _ in range(2)]
            SA = [self.sb(es, f"ffn_SA{k}", [128, 512], F32) for k in range(2)]
            bSA = [Buf() for _ in range(2)]
            ui = 0
            yi = 0
            for gi, (j0, j1) in enumerate(groups):
                for j in range(j0, j1):
                    ws = [self.load_w("wup", j), self.load_w("wup", FC + j)]
                    for n in range(NT):
                        uk = ui % 3
                        up = (ui - 1) % 3
                        ui += 1
                        yk = yi % 2
                        yi += 1
                        for ab in range(2):
                            w, bw = ws[ab]
                            u, bu = U[ab][uk], bU[ab][uk]
                            cj = j + ab * FC
                            if n == 0:
                                ph, bph = self.bank()
                                self.proj(w, bw, 0, 2, ph, bph)
                                P.op("act", lambda e: e.activation(out=u[:, 0:2], in_=ph[:, 0:2], func=AF.Copy),
                                     reads=[bph], writes=[bu])
                            else:
                                P.op("act", lambda e: e.activation(out=u[:, 0:2], in_=U[ab][up][:, 512:514], func=AF.Copy),
                                     reads=[bU[ab][up]], writes=[bu])
                            ps, bps = self.bank()
                            self.proj(w, bw, 2 + n * 512, 512, ps, bps)
                            P.op("act", lambda e: e.activation(out=u[:, 2:514], in_=ps[:, :], func=AF.Copy),
                                 reads=[bps], writes=[bu])
                            t, bt = TT[ab][yk], bTT[ab][yk]
                            y, by = Y[ab][yk], bY[ab][yk]
                            w0 = self.vec[:, cvo + cj:cvo + cj + 1]
                            w1 = self.vec[:, cvo + 44 + cj:cvo + 44 + cj + 1]
                            w2 = self.vec[:, cvo + 88 + cj:cvo + 88 + cj + 1]
                            P.op("act", lambda e: e.activation(out=t[:], in_=u[:, 0:512], func=AF.Copy, scale=w0),
                                 reads=[bu, self.bvec], writes=[bt])
                            P.op("dve", lambda e: e.scalar_tensor_tensor(out=t[:], in0=u[:, 1:513], scalar=w1, in1=t[:],
                                                                         op0=ALU.mult, op1=ALU.add),
                                 reads=[bu, bt, self.bvec], writes=[bt])
                            P.op("dve", lambda e: e.scalar_tensor_tensor(out=y[:], in0=u[:, 2:514], scalar=w2, in1=t[:],
                                                                         op0=ALU.mult, op1=ALU.add),
                                 reads=[bu, bt, self.bvec], writes=[by])
                        sa, bsa = SA[yk], bSA[yk]
                        P.op("act", lambda e: e.activation(out=sa[:], in_=Y[0][yk][:], func=AF.Silu),
                             reads=[bY[0][yk]], writes=[bsa])
                        P.op("dve", lambda e: e.tensor_tensor(out=H[:, j - j0, n * 512:(n + 1) * 512], in0=sa[:], in1=Y[1][yk][:],
                                                              op=ALU.mult),
                             reads=[bsa, bY[1][yk]], writes=[bH[j - j0]])
                self.out_proj_residual("wdn", H, bH[:j1 - j0], j1 - j0, first=(gi == 0), wtile_c0=j0)
            P.barrier()

    def conv_mixer(self):
        P = self.P
        cvo = V_CONV
        with ExitStack() as es:
            Yo = self.sb(es, "cm_Y", [128, DC, T], BF16)
            bYo = [Buf() for _ in range(DC)]
            PR = [self.sb(es, f"cm_P{k}", [128, 514], F32) for k in range(3)]
            bPR = [Buf() for _ in range(3)]
            CG = [self.sb(es, f"cm_CG{k}", [128, 514], F32) for k in range(2)]
            bCG = [Buf() for _ in range(2)]
            TT = [self.sb(es, f"cm_T{k}", [128, 512], F32) for k in range(2)]
            bTT = [Buf() for _ in range(2)]
            ui = 0
            for c in range(DC):
                wbg = self.load_w("win", c)
                wcg = self.load_w("win", DC + c)
                wh = self.load_w("win", 2 * DC + c)
                w0 = self.vec[:, cvo + c:cvo + c + 1]
                w1 = self.vec[:, cvo + 8 + c:cvo + 8 + c + 1]
                w2 = self.vec[:, cvo + 16 + c:cvo + 16 + c + 1]
                for n in range(NT):
                    uk = ui % 3
                    up = (ui - 1) % 3
                    k2 = ui % 2
                    ui += 1
                    pr, bpr = PR[uk], bPR[uk]
                    cg, bcg = CG[k2], bCG[k2]
                    t, bt = TT[k2], bTT[k2]
                    if n == 0:
                        p1, bp1 = self.bank()
                        self.proj(wcg[0], wcg[1], 0, 2, p1, bp1)
                        P.op("act", lambda e: e.activation(out=cg[:, 0:2], in_=p1[:, 0:2], func=AF.Copy), reads=[bp1], writes=[bcg])
                        p2, bp2 = self.bank()
                        self.proj(wh[0], wh[1], 0, 2, p2, bp2)
                        P.op("dve", lambda e: e.tensor_tensor(out=pr[:, 0:2], in0=cg[:, 0:2], in1=p2[:, 0:2], op=ALU.mult),
                             reads=[bcg, bp2], writes=[bpr])
                    else:
                        P.op("act", lambda e: e.activation(out=pr[:, 0:2], in_=PR[up][:, 512:514], func=AF.Copy),
                             reads=[bPR[up]], writes=[bpr])
                    p1, bp1 = self.bank()
                    self.proj(wcg[0], wcg[1], 2 + n * 512, 512, p1, bp1)
                    P.op("act", lambda e: e.activation(out=cg[:, 2:514], in_=p1[:, :], func=AF.Copy), reads=[bp1], writes=[bcg])
                    p2, bp2 = self.bank()
                    self.proj(wh[0], wh[1], 2 + n * 512, 512, p2, bp2)
                    P.op("dve", lambda e: e.tensor_tensor(out=pr[:, 2:514], in0=cg[:, 2:514], in1=p2[:, :], op=ALU.mult),
                         reads=[bcg, bp2], writes=[bpr])
                    p3, bp3 = self.bank()
                    self.proj(wbg[0], wbg[1], 2 + n * 512, 512, p3, bp3)
                    P.op("act", lambda e: e.activation(out=t[:], in_=pr[:, 0:512], func=AF.Copy, scale=w0),
                         reads=[bpr, self.bvec], writes=[bt])
                    P.op("dve", lambda e: e.scalar_tensor_tensor(out=t[:], in0=pr[:, 1:513], scalar=w1, in1=t[:],
                                                                 op0=ALU.mult, op1=ALU.add),
                         reads=[bpr, bt, self.bvec], writes=[bt])
                    P.op("dve", lambda e: e.scalar_tensor_tensor(out=t[:], in0=pr[:, 2:514], scalar=w2, in1=t[:],
                                                                 op0=ALU.mult, op1=ALU.add),
                         reads=[bpr, bt, self.bvec], writes=[bt])
                    P.op("dve", lambda e: e.tensor_tensor(out=Yo[:, c, n * 512:(n + 1) * 512], in0=t[:], in1=p3[:, :], op=ALU.mult),
                         reads=[bt, bp3], writes=[bYo[c]])
            self.out_proj_residual("wout", Yo, bYo, DC, first=True)
            P.barrier()

    def hgrn_mixer(self):
        P = self.P
        nc = self.nc
        H = 8
        lite = (self.kind == "hg1")
        with ExitStack() as es:
            self.Aall = self.dram("Aall", [128, NCORE * H], F32, "ExternalInput")
            self.Ball = self.dram("Ball", [NCORE * 128, H * 128], F32, "ExternalInput")
            self.Aout = self.dram("Aout", [128, H], F32, "ExternalOutput")
            self.Bout = self.dram("Bout", [128, H * 128], F32, "ExternalOutput")
            G = [self.sb(es, f"hg_G{k}", [128, 512], BF16) for k in range(2)]
            bG = [Buf() for _ in range(2)]
            Y = self.sb(es, "hg_Y", [128, H, T], BF16)
            bY = [Buf() for _ in range(H)]
            S = self.sb(es, "hg_S", [128, H, 128], F32)
            bS = [Buf() for _ in range(H)]
            Aall = self.sb(es, "hg_Aall", [128, NCORE, H], F32)
            bAall = Buf()
            ap_ = self.sb(es, "hg_ap", [128, H], F32)
            bap = Buf()
            om = self.sb(es, "hg_om", [128, NCORE], F32)
            bom = Buf()
            Bm = [self.sb(es, f"hg_Bm{k}", [128, H, 128], F32) for k in range(1)]
            bBm = [Buf() for _ in range(1)]
            dBm = [P.dsem(f"bm{k}") for k in range(1)]
            d_a = P.dsem("aall")
            lbe = self.sb(es, "hg_lbe", [128, 4, H], F32)
            lb = self.sb(es, "hg_lb", [128, H], F32)
            oml = self.sb(es, "hg_oml", [128, H], F32)
            lbm1 = self.sb(es, "hg_lbm1", [128, H], F32)
            lsum = self.sb(es, "hg_lsum", [128, H], F32)
            blb = Buf()
            rmask = self.sb(es, "hg_rmask", [128, 512], F32)
            ident = self.sb(es, "hg_ident", [128, 128], BF16)
            ones128 = self.sb(es, "hg_ones", [128, 128], BF16)
            bcn = Buf()
            blsum = self.sb(es, "hg_blsum", [128, H], F32)
            bblsum = Buf()
            P.op("dve", lambda e: e.memset(rmask[:], 1.0), writes=[bcn])
            P.op("dve", lambda e: e.memset(rmask[:].rearrange("p (c t) -> p c t", t=64)[:, :, 0:1], 0.0), writes=[bcn])
            P.op("dve", lambda e: e.memset(ones128[:], 1.0 / 128.0), writes=[bcn])
            P.op("dve", lambda e: e.memset(blsum[:], 0.0), writes=[bblsum])
            P.op("act", lambda e: e.activation(out=ident[:], in_=self.cst[:, C_IDENT:C_IDENT + 128], func=AF.Copy),
                 reads=[self.bvec], writes=[bcn])
            P.op("act", lambda e: e.activation(out=lbe[:].rearrange("p l h -> p (l h)"), in_=self.vec[:, V_LBRAW:V_LBRAW + 32], func=AF.Exp),
                 reads=[self.bvec], writes=[blb])
            P.op("dve", lambda e: e.tensor_tensor(out=lsum[:], in0=lbe[:, 0, :], in1=lbe[:, 1, :], op=ALU.add), reads=[blb], writes=[blb])
            P.op("dve", lambda e: e.tensor_tensor(out=lsum[:], in0=lsum[:], in1=lbe[:, 2, :], op=ALU.add), reads=[blb], writes=[blb])
            P.op("dve", lambda e: e.tensor_tensor(out=lsum[:], in0=lsum[:], in1=lbe[:, 3, :], op=ALU.add), reads=[blb], writes=[blb])
            P.op("dve", lambda e: e.reciprocal(out=lsum[:], in_=lsum[:]), reads=[blb], writes=[blb])
            P.op("dve", lambda e: e.tensor_scalar(out=lb[:], in0=lbe[:, 1, :], scalar1=self.cst[:, C_LMASK + 1:C_LMASK + 2], scalar2=None, op0=ALU.mult),
                 reads=[blb, self.bvec], writes=[blb])
            for l in (2, 3):
                P.op("dve", lambda e: e.scalar_tensor_tensor(out=lb[:], in0=lbe[:, l, :], scalar=self.cst[:, C_LMASK + l:C_LMASK + l + 1],
                                                             in1=lb[:], op0=ALU.mult, op1=ALU.add),
                     reads=[blb, self.bvec], writes=[blb])
            P.op("dve", lambda e: e.tensor_tensor(out=lb[:], in0=lb[:], in1=lsum[:], op=ALU.mult), reads=[blb], writes=[blb])
            P.op("dve", lambda e: e.tensor_scalar(out=lbm1[:], in0=lb[:], scalar1=-1.0, scalar2=None, op0=ALU.add), reads=[blb], writes=[blb])
            P.op("dve", lambda e: e.tensor_scalar(out=oml[:], in0=lbm1[:], scalar1=-1.0, scalar2=None, op0=ALU.mult), reads=[blb], writes=[blb])
            P.dma("sp", Aall[:].rearrange("p r h -> p (r h)"), self.Aall, d_a, writes=[bAall])
            P.op("dve", lambda e: e.memset(S[:], 0.0), writes=bS)
            P.op("dve", lambda e: e.tensor_scalar(out=om[:], in0=self.cst[:, C_HMASK:C_HMASK + NCORE], scalar1=-1.0, scalar2=1.0,
                                                  op0=ALU.mult, op1=ALU.add), reads=[self.bvec], writes=[bom])
            for r in range(NCORE - 1):
                k = 0
                P.dma("sp", Bm[k][:].rearrange("p h v -> p (h v)"), self.Ball[r * 128:(r + 1) * 128, :], dBm[k], writes=[bBm[k]])
                mr = self.cst[:, C_HMASK + r:C_HMASK + r + 1]
                P.op("dve", lambda e: e.tensor_scalar(out=ap_[:], in0=Aall[:, r, :], scalar1=mr, scalar2=om[:, r:r + 1], op0=ALU.mult, op1=ALU.add),
                     reads=[bAall, self.bvec, bom], writes=[bap])
                P.op("dve", lambda e: e.tensor_scalar(out=Bm[k][:], in0=Bm[k][:], scalar1=mr, scalar2=None, op0=ALU.mult),
                     reads=[bBm[k], self.bvec], writes=[bBm[k]])
                for h in range(H):
                    P.op("dve", lambda e: e.scalar_tensor_tensor(out=S[:, h, :], in0=S[:, h, :], scalar=ap_[:, h:h + 1], in1=Bm[k][:, h, :],
                                                                 op0=ALU.mult, op1=ALU.add),
                         reads=[bS[h], bap, bBm[k]], writes=[bS[h]])
            R2 = lambda nm, shp, dt: [self.sb(es, f"{nm}{k}", shp, dt) for k in range(2)]
            sig, bsig = R2("hg_sig", [128, 512], F32), [Buf(), Buf()]
            qs, bqs = R2("hg_qs", [128, 512], F32), [Buf(), Buf()]
            R1 = lambda nm, shp, dt: [self.sb(es, nm, shp, dt)] * 2
            B1 = lambda: [Buf()] * 2
            kk, bkk = R1("hg_k", [128, 512], F32), B1()
            gg, bgg = R1("hg_g", [128, 512], F32), B1()
            bb, bbb = R1("hg_b", [128, 512], F32), B1()
            e1, be1 = R2("hg_e1", [128, 512], F32), [Buf(), Buf()]
            e2, be2 = R1("hg_e2", [128, 512], F32), B1()
            ebl, bebl = R2("hg_ebl", [128, 8], F32), [Buf(), Buf()]
            qp, bqp = R2("hg_qp", [128, 512], BF16), [Buf(), Buf()]
            kp, bkp = R2("hg_kp", [128, 512], BF16), [Buf(), Buf()]
            ktok, bktok = R2("hg_ktok", [128, 4, 128], BF16), [Buf(), Buf()]
            vtok, bvtok = R2("hg_vtok", [128, 4, 128], BF16), [Buf(), Buf()]
            am, bam = R2("hg_am", [128, 128], BF16), [Buf(), Buf()]
            sdb = [self.sb(es, f"hg_sdb{c_}", [128, 128], BF16) for c_ in range(8)]
            bsdb = [Buf() for _ in range(8)]
            osq, bosq = R2("hg_osq", [128, 512], BF16), [Buf(), Buf()]
            rr, brr = R2("hg_rr", [128, 512], F32), [Buf(), Buf()]
            yy, byy = R2("hg_yy", [128, 512], F32), [Buf(), Buf()]
            it = 0
            sdi = 0
            ami = 0
            self.rot_banks = (4, 5, 6, 7)
            cnts = {"ami": 0}

            def stage1(h, n, k, Wh):
                wq, wf, wi, wg, lbh, omlh, lbm1h = Wh
                c0 = 2 + n * 512
                cs = slice(n * 512, (n + 1) * 512)
                if not lite:
                    pq, bpq = self.bank()
                    self.proj(wq[0], wq[1], c0, 512, pq, bpq)
                pf, bpf = self.bank()
                self.proj(wf[0], wf[1], c0, 512, pf, bpf)
                if not lite:
                    pg, bpg = self.bank()
                    self.proj(wg[0], wg[1], c0, 512, pg, bpg)
                pv, bpv = self.bank()
                for s4 in range(4):
                    for c in range(DC):
                        P.op("pe", lambda e: e.matmul(pv[:, s4 * 128:(s4 + 1) * 128], self.XB[:, c, c0 + s4 * 128:c0 + (s4 + 1) * 128],
                                                      wi[0][:, c, :], start=(c == 0), stop=(c == DC - 1)),
                             reads=[wi[1], self.bXB[n]], writes=[bpv], signal=(c == DC - 1 and s4 == 3))
                P.op("act", lambda e: e.activation(out=sig[k][:], in_=pf[:, :], func=AF.Sigmoid), reads=[bpf], writes=[bsig[k]])
                if not lite:
                    P.op("act", lambda e: e.activation(out=qs[k][:], in_=pq[:, :], func=AF.Silu), reads=[bpq], writes=[bqs[k]])
                    P.op("act", lambda e: e.activation(out=G[k][:], in_=pg[:, :], func=AF.Silu), reads=[bpg], writes=[bG[k]])
                P.op("act", lambda e: e.activation(out=vtok[k][:].rearrange("p s v -> p (s v)"), in_=pv[:, :], func=AF.Copy),
                     reads=[bpv], writes=[bvtok[k]])
                P.op("dve", lambda e: e.tensor_scalar(out=kk[k][:], in0=sig[k][:], scalar1=lbm1h, scalar2=omlh, op0=ALU.mult, op1=ALU.add),
                     reads=[bsig[k], blb], writes=[bkk[k]])
                P.op("act", lambda e: e.activation(out=gg[k][:], in_=sig[k][:], func=AF.Ln, bias=lbh, scale=omlh),
                     reads=[bsig[k], blb], writes=[bgg[k]])
                P.op("dve", lambda e: e.tensor_tensor_scan(out=bb[k][:], data0=rmask[:], data1=gg[k][:], initial=0.0, op0=ALU.mult, op1=ALU.add),
                     reads=[bgg[k], bcn], writes=[bbb[k]])
                b3 = bb[k][:].rearrange("p (c t) -> p c t", t=64)
                P.op("dve", lambda e: e.tensor_tensor(out=e1[k][:].rearrange("p (c t) -> p c t", t=64), in0=b3,
                                                      in1=b3[:, :, 63:64].to_broadcast([128, 8, 64]), op=ALU.subtract),
                     reads=[bbb[k]], writes=[be1[k]])
                P.op("act", lambda e: e.activation(out=e2[k][:], in_=e1[k][:], func=AF.Exp, scale=-1.0), reads=[be1[k]], writes=[be2[k]])
                if not lite:
                    P.op("act", lambda e: e.activation(out=e1[k][:], in_=e1[k][:], func=AF.Exp), reads=[be1[k], be2[k]], writes=[be1[k]])
                P.op("act", lambda e: e.activation(out=ebl[k][:], in_=b3[:, :, 63], func=AF.Exp), reads=[bbb[k]], writes=[bebl[k]])
                P.op("dve", lambda e: e.tensor_reduce(out=rr[k][:, 0:1], in_=b3[:, :, 63], axis=AX.X, op=ALU.add), reads=[bbb[k]], writes=[brr[k]])
                P.op("dve", lambda e: e.tensor_tensor(out=blsum[:, h:h + 1], in0=blsum[:, h:h + 1], in1=rr[k][:, 0:1], op=ALU.add),
                     reads=[brr[k], bblsum], writes=[bblsum])
                if not lite:
                    P.op("dve", lambda e: e.tensor_tensor(out=qp[k][:], in0=qs[k][:], in1=e1[k][:], op=ALU.mult), reads=[bqs[k], be1[k]], writes=[bqp[k]])
                P.op("dve", lambda e: e.tensor_tensor(out=kp[k][:], in0=kk[k][:], in1=e2[k][:], op=ALU.mult), reads=[bkk[k], be2[k]], writes=[bkp[k]])

            def stage2(h, n, k):
                ami = cnts["ami"]
                cs = slice(n * 512, (n + 1) * 512)
                pt, bpt = self.bank()
                ptb = pt[:].bitcast(BF16)
                for s4 in range(4):
                    P.op("pe", lambda e: e.transpose(ptb[:, s4 * 128:(s4 + 1) * 128], kp[k][:, s4 * 128:(s4 + 1) * 128], ident[:]),
                         reads=[bkp[k], bcn], writes=[bpt], signal=(s4 == 3))
                P.op("act", lambda e: e.activation(out=ktok[k][:].rearrange("p s v -> p (s v)"), in_=ptb[:, 0:512], func=AF.Copy),
                     reads=[bpt], writes=[bktok[k]])
                po, bpo = self.PS[k], self.bPS[k]
                for ci in range(8):
                    s4, half = ci // 2, ci % 2
                    rows = slice(half * 64, half * 64 + 64)
                    pu, bpu = self.PS[2 + half], self.bPS[2 + half]
                    P.op("pe", lambda e: e.matmul(pu[:, s4 * 128:(s4 + 1) * 128], ktok[k][rows, s4, :], vtok[k][rows, s4, :],
                                                  start=True, stop=True),
                         reads=[bktok[k], bvtok[k]], writes=[bpu], signal=(ci >= 6))
                for ci in range(8):
                    s4, half = ci // 2, ci % 2
                    eb = ebl[k][:, ci:ci + 1]
                    pu, bpu = self.PS[2 + half], self.bPS[2 + half]
                    if not lite:
                        P.op("dve", lambda e: e.tensor_scalar(out=sdb[ci][:], in0=S[:, h, :], scalar1=eb, scalar2=None, op0=ALU.mult),
                             reads=[bS[h], bebl[k]], writes=[bsdb[ci]])
                    P.op("dve", lambda e: e.scalar_tensor_tensor(out=S[:, h, :], in0=S[:, h, :], scalar=eb, in1=pu[:, s4 * 128:(s4 + 1) * 128],
                                                                 op0=ALU.mult, op1=ALU.add),
                         reads=[bS[h], bebl[k], bpu], writes=[bS[h]])
                for s4 in range(0 if lite else 4):
                    sl = slice(s4 * 128, (s4 + 1) * 128)
                    pa, bpa = self.bank()
                    P.op("pe", lambda e: e.matmul(pa[:, 0:128], kp[k][:, sl], qp[k][:, sl], start=True, stop=True),
                         reads=[bkp[k], bqp[k]], writes=[bpa])
                    a_ = ami % 2
                    ami += 1
                    P.op("dve", lambda e: e.tensor_tensor(out=am[a_][:], in0=pa[:, 0:128], in1=self.cst[:, C_MASK2:C_MASK2 + 128], op=ALU.mult),
                         reads=[bpa, self.bvec], writes=[bam[a_]])
                    P.op("pe", lambda e: e.matmul(po[:, sl], vtok[k][:, s4, :], am[a_][:], start=True, stop=False),
                         reads=[bvtok[k], bam[a_]], writes=[bpo], signal=False)
                    for half in range(2):
                        ci = s4 * 2 + half
                        hs = slice(s4 * 128 + half * 64, s4 * 128 + half * 64 + 64)
                        P.op("pe", lambda e: e.matmul(po[:, hs], sdb[ci][:], qp[k][:, hs], start=False, stop=(half == 1)),
                             reads=[bsdb[ci], bqp[k]], writes=[bpo], signal=True)
                if lite:
                    return
                P.op("act", lambda e: e.activation(out=osq[k][:], in_=po[:, :], func=AF.Square), reads=[bpo], writes=[bosq[k]])
                pm, bpm = self.bank()
                P.op("pe", lambda e: e.matmul(pm[:, :], ones128[:], osq[k][:], start=True, stop=True), reads=[bcn, bosq[k]], writes=[bpm])
                P.op("dve", lambda e: e.tensor_scalar(out=rr[k][:], in0=pm[:, :], scalar1=RMS_EPS, scalar2=None, op0=ALU.add),
                     reads=[bpm], writes=[brr[k]])
                P.op("act", lambda e: e.activation(out=rr[k][:], in_=rr[k][:], func=AF.Sqrt), reads=[brr[k]], writes=[brr[k]])
                P.op("dve", lambda e: e.reciprocal(out=rr[k][:], in_=rr[k][:]), reads=[brr[k]], writes=[brr[k]])
                P.op("dve", lambda e: e.tensor_tensor(out=yy[k][:], in0=po[:, :], in1=rr[k][:], op=ALU.mult), reads=[bpo, brr[k]], writes=[byy[k]])
                P.op("dve", lambda e: e.scalar_tensor_tensor(out=Y[:, h, cs], in0=yy[k][:], scalar=self.vec[:, V_NORMG + h:V_NORMG + h + 1],
                                                             in1=G[k][:], op0=ALU.mult, op1=ALU.mult),
                     reads=[byy[k], self.bvec, bG[k]], writes=[bY[h]])

            prev = None
            Wh = None
            for idx in range(H * NT):
                h, n = idx // NT, idx % NT
                if n == 0:
                    Wh = (self.load_w("win", h), self.load_w("win", H + h), self.load_w("win", 2 * H + h), self.load_w("win", 3 * H + h),
                          lb[:, h:h + 1], oml[:, h:h + 1], lbm1[:, h:h + 1])
                k = idx % 2
                stage1(h, n, k, Wh)
                if prev is not None:
                    stage2(*prev)
                prev = (h, n, k)
            stage2(*prev)
            P.op("act", lambda e: e.activation(out=blsum[:], in_=blsum[:], func=AF.Exp), reads=[bblsum], writes=[bblsum])
            d_o = P.dsem("hgo")
            P.dma("sp", self.Aout, blsum[:], d_o, reads=[bblsum], writes=[self.bout])
            P.dma("sp", self.Bout, S[:].rearrange("p h v -> p (h v)"), d_o, reads=bS, writes=[self.bout])
            self.rot_banks = (0, 1, 2, 3, 4, 5, 6, 7)
            if not lite:
                self.out_proj_residual("wout", Y, bY, DC, first=True)
            P.barrier()

    def moba_qkv(self):
        P = self.P
        H = 8
        with ExitStack() as es:
            self.pos_d = self.dram("pos", [1, T], I32, "ExternalInput")
            self.QTo = self.dram("QT", [128, H * T], BF16, "ExternalOutput")
            self.KTo = self.dram("KT", [128, H * T], BF16, "ExternalOutput")
            self.Vo = self.dram("V", [128, 16 * 1024], BF16, "ExternalOutput")
            self.KMo = self.dram("KM", [128, H * 8], F32, "ExternalOutput")
            posi = self.sb(es, "mq_posi", [32, T], I32)
            ang = self.sb(es, "mq_ang", [32, T], F32)
            tmp = self.sb(es, "mq_tmp", [32, T], F32)
            Ct = self.sb(es, "mq_C", [32, T], F32)
            St = self.sb(es, "mq_S", [32, T], F32)
            npi = self.sb(es, "mq_npi", [32, 1], F32)
            km = self.sb(es, "mq_km", [128, H, 8], F32)
            btab, bkm = Buf(), Buf()
            d_p = P.dsem("pos")
            P.dma("sp", posi[:], self.pos_d.partition_broadcast(32), d_p, writes=[btab])
            P.op("dve", lambda e: e.tensor_copy(out=ang[:], in_=posi[:]), reads=[btab], writes=[btab])
            P.op("dve", lambda e: e.memset(npi[:], -math.pi), writes=[btab])
            P.op("dve", lambda e: e.tensor_scalar(out=ang[:], in0=ang[:], scalar1=self.cst[0:32, C_INVF:C_INVF + 1], scalar2=None, op0=ALU.mult),
                 reads=[btab, self.bvec], writes=[btab])
            ki = self.sb(es, "mq_ki", [32, T], I32)
            kfl = self.sb(es, "mq_kfl", [32, T], F32)
            for (dst_, off_) in ((St, 0.5), (Ct, 0.75)):
                P.op("dve", lambda e: e.tensor_scalar(out=tmp[:], in0=ang[:], scalar1=1.0 / (2 * math.pi), scalar2=off_, op0=ALU.mult, op1=ALU.add),
                     reads=[btab], writes=[btab])
                P.op("dve", lambda e: e.tensor_copy(out=ki[:], in_=tmp[:]), reads=[btab], writes=[btab])
                P.op("dve", lambda e: e.tensor_copy(out=kfl[:], in_=ki[:]), reads=[btab], writes=[btab])
                P.op("dve", lambda e: e.tensor_tensor(out=tmp[:], in0=tmp[:], in1=kfl[:], op=ALU.subtract), reads=[btab], writes=[btab])
                P.op("dve", lambda e: e.tensor_scalar(out=kfl[:], in0=tmp[:], scalar1=0.0, scalar2=None, op0=ALU.is_lt), reads=[btab], writes=[btab])
                P.op("dve", lambda e: e.tensor_tensor(out=tmp[:], in0=tmp[:], in1=kfl[:], op=ALU.add), reads=[btab], writes=[btab])
                P.op("act", lambda e: e.activation(out=dst_[:], in_=tmp[:], func=AF.Sin, bias=npi[:, 0:1], scale=2 * math.pi), reads=[btab], writes=[btab])
            P.op("dve", lambda e: e.tensor_scalar(out=St[:], in0=St[:], scalar1=self.cst[0:32, C_SGN:C_SGN + 1], scalar2=None, op0=ALU.mult),
                 reads=[btab, self.bvec], writes=[btab])
            R2 = lambda nm, shp, dt: [self.sb(es, f"{nm}{k}", shp, dt) for k in range(2)]
            t1, bt1 = R2("mq_t1", [32, 512], F32), [Buf(), Buf()]
            t2, bt2 = R2("mq_t2", [32, 512], F32), [Buf(), Buf()]
            kf, bkf = R2("mq_kf", [128, 512], F32), [Buf(), Buf()]
            ob, bob = R2("mq_ob", [128, 512], BF16), [Buf(), Buf()]
            vb, bvb = R2("mq_vb", [128, 4, 128], BF16), [Buf(), Buf()]
            dob = [P.dsem(f"ob{k}") for k in range(2)]
            dvb = [P.dsem(f"vb{k}") for k in range(2)]
            Vov = self.Vo.rearrange("p (t f) -> p t f", f=1024)
            it = 0
            for h in range(H):
                for qk in range(2):
                    w = self.load_w("win", qk * H + h)
                    wp = self.load_w("win", 3 * H + qk * H + h)
                    dst = self.QTo if qk == 0 else self.KTo
                    for n in range(NT):
                        k = it % 2
                        it += 1
                        c0 = 2 + n * 512
                        cs = slice(n * 512, (n + 1) * 512)
                        pa, bpa = self.bank()
                        self.proj(w[0], w[1], c0, 512, pa, bpa)
                        pb, bpb = self.bank()
                        self.proj(wp[0], wp[1], c0, 512, pb, bpb, m=32)
                        P.op("dve", lambda e: e.tensor_tensor(out=t1[k][:], in0=pa[0:32, :], in1=Ct[:, cs], op=ALU.mult), reads=[bpa, btab], writes=[bt1[k]])
                        P.op("dve", lambda e: e.tensor_tensor(out=t2[k][:], in0=pb[0:32, :], in1=St[:, cs], op=ALU.mult), reads=[bpb, btab], writes=[bt2[k]])
                        P.op("dve", lambda e: e.tensor_tensor(out=kf[k][0:32, :], in0=t1[k][:], in1=t2[k][:], op=ALU.add), reads=[bt1[k], bt2[k]], writes=[bkf[k]])
                        P.op("act", lambda e: e.activation(out=kf[k][32:64, :], in_=pa[32:64, :], func=AF.Copy), reads=[bpa], writes=[bkf[k]])
                        P.op("act", lambda e: e.activation(out=kf[k][64:128, :], in_=pa[64:128, :], func=AF.Copy), reads=[bpa], writes=[bkf[k]])
                        P.op("act", lambda e: e.activation(out=ob[k][:], in_=kf[k][:], func=AF.Copy), reads=[bkf[k]], writes=[bob[k]])
                        if qk == 1:
                            P.op("dve", lambda e: e.tensor_reduce(out=km[:, h, 2 * n:2 * n + 2], in_=kf[k][:].rearrange("p (b t) -> p b t", t=256),
                                                                  axis=AX.X, op=ALU.add), reads=[bkf[k]], writes=[bkm])
                        P.dma("sp", dst[:, h * T + n * 512:h * T + (n + 1) * 512], ob[k][:], dob[k], reads=[bob[k]], writes=[self.bout])
                wv = self.load_w("win", 2 * H + h)
                for n in range(NT):
                    k = it % 2
                    it += 1
                    c0 = 2 + n * 512
                    pv, bpv = self.bank()
                    for s4 in range(4):
                        for c in range(DC):
                            P.op("pe", lambda e: e.matmul(pv[:, s4 * 128:(s4 + 1) * 128], self.XB[:, c, c0 + s4 * 128:c0 + (s4 + 1) * 128],
                                                          wv[0][:, c, :], start=(c == 0), stop=(c == DC - 1)),
                                 reads=[wv[1], self.bXB[n]], writes=[bpv], signal=(c == DC - 1 and s4 == 3))
                    P.op("act", lambda e: e.activation(out=vb[k][:].rearrange("p s v -> p (s v)"), in_=pv[:, :], func=AF.Copy), reads=[bpv], writes=[bvb[k]])
                    P.dma("sp", Vov[:, n * 4:(n + 1) * 4, h * 128:(h + 1) * 128], vb[k][:], dvb[k], reads=[bvb[k]], writes=[self.bout])
            P.op("dve", lambda e: e.tensor_scalar(out=km[:], in0=km[:], scalar1=1.0 / 256.0, scalar2=None, op0=ALU.mult), reads=[bkm], writes=[bkm])
            d_k = P.dsem("kmo")
            P.dma("sp", self.KMo, km[:].rearrange("p h b -> p (h b)"), d_k, reads=[bkm], writes=[self.bout])
            P.barrier()

    def moba_attn(self):
        P = self.P
        H = 8
        NS = 72
        SCALE = 1.0 / math.sqrt(128.0)
        with ExitStack() as es:
            self.QTi = self.dram("QT", [128, H * T], BF16, "ExternalInput")
            self.Kall = self.dram("Kall", [H * 128, NS * 256], BF16, "ExternalInput")
            self.Vall = self.dram("Vall", [H * 128, NS * 256], BF16, "ExternalInput")
            self.KMall = self.dram("KMall", [128, H * NS], F32, "ExternalInput")
            self.mcst_d = self.dram("mcst", [128, 3 * 8 * NS + 4 * 512], F32, "ExternalInput")
            QT = self.sb(es, "ma_QT", [128, H, T], BF16)
            bQT = Buf()
            d_q = P.dsem("qt")
            P.dma("sp", QT[:].rearrange("p h t -> p (h t)"), self.QTi, d_q, writes=[bQT])
            mc = self.sb(es, "ma_mc", [128, 3, 8, NS], F32)
            caus32 = self.sb(es, "ma_c32", [128, 512], F32)
            caus = self.sb(es, "ma_caus", [128, 4, 512], BF16)
            bmc = Buf()
            d_m = P.dsem("mc")
            d_m2 = P.dsem("mc2")
            P.dma("sp", mc[:].rearrange("p a l s -> p (a l s)"), self.mcst_d[:, 0:3 * 8 * NS], d_m, writes=[bmc])
            for v in range(4):
                P.dma("sp", caus32[:], self.mcst_d[:, 3 * 8 * NS + v * 512:3 * 8 * NS + (v + 1) * 512], d_m2, writes=[bmc])
                P.op("act", lambda e: e.activation(out=caus[:, v, :], in_=caus32[:], func=AF.Copy), reads=[bmc], writes=[bmc])
            kmf = self.sb(es, "ma_kmf", [128, H, NS], F32)
            kmb = self.sb(es, "ma_kmb", [128, H, NS], BF16)
            d_km = P.dsem("km")
            bkm = Buf()
            P.dma("sp", kmf[:].rearrange("p h s -> p (h s)"), self.KMall, d_km, writes=[bkm])
            P.op("act", lambda e: e.activation(out=kmb[:], in_=kmf[:], func=AF.Copy), reads=[bkm], writes=[bkm])
            ident = self.sb(es, "ma_ident", [128, 128], BF16)
            ones = self.sb(es, "ma_ones", [128, 128], BF16)
            Esel = self.sb(es, "ma_Esel", [NS, NS, 128], BF16)
            bcn = Buf()
            P.op("act", lambda e: e.activation(out=ident[:], in_=self.cst[:, C_IDENT:C_IDENT + 128], func=AF.Copy), reads=[self.bvec], writes=[bcn])
            P.op("dve", lambda e: e.memset(ones[:], 1.0), writes=[bcn])
            P.op("dve", lambda e: e.tensor_copy(out=Esel[:], in_=self.cst[0:NS, C_IDENT:C_IDENT + NS].unsqueeze(2).to_broadcast([NS, NS, 128])),
                 reads=[self.bvec], writes=[bcn])
            maskT = [self.sb(es, f"ma_maskT{k}", [NS, T], BF16) for k in range(2)]
            bmaskT = [Buf(), Buf()]
            R2 = lambda nm, shp, dt: [self.sb(es, f"{nm}{k}", shp, dt) for k in range(2)]
            gm, bgm = R2("ma_gm", [128, NS], F32), [Buf(), Buf()]
            t8, bt8 = R2("ma_t8", [128, 8], F32), [Buf(), Buf()]
            al, bal = R2("ma_al", [128, NS], F32), [Buf(), Buf()]
            alb, balb = R2("ma_alb", [128, NS], BF16), [Buf(), Buf()]
            NKS = 3
            KS = [self.sb(es, f"ma_KS{k}", [128, 1024], BF16) for k in range(NKS)]
            VS = [self.sb(es, f"ma_VS{k}", [128, 8, 128], BF16) for k in range(NKS)]
            bKS = [Buf() for _ in range(NKS)]
            bVS = [Buf() for _ in range(NKS)]
            dKS = [P.dsem(f"ks{k}") for k in range(NKS)]
            dVS = [P.dsem(f"vs{k}") for k in range(NKS)]
            NPT = 4
            PT = [self.sb(es, f"ma_PT{k}", [128, 512], BF16) for k in range(NPT)]
            bPT = [Buf() for _ in range(NPT)]
            rden, brden = R2("ma_rden", [128, 512], F32), [Buf(), Buf()]
            Y = self.XB
            bYh = [Buf() for _ in range(H)]
            self.rot_banks = (4, 5, 6, 7)
            gi = 0
            ksi = 0
            pti = 0
            for h in range(H):
                mk = h % 2
                for s16 in range(16):
                    g_ = gi % 2
                    gi += 1
                    lb_ = s16 // 2
                    qs_ = slice(s16 * 128, (s16 + 1) * 128)
                    pg, bpg = self.bank()
                    P.op("pe", lambda e: e.matmul(pg[:, 0:NS], QT[:, h, qs_], kmb[:, h, :], start=True, stop=True), reads=[bQT, bkm], writes=[bpg])
                    P.op("dve", lambda e: e.tensor_tensor(out=gm[g_][:], in0=pg[:, 0:NS], in1=mc[:, 0, lb_, :], op=ALU.add), reads=[bpg, bmc], writes=[bgm[g_]])
                    P.op("dve", lambda e: e.max(out=t8[g_][:], in_=gm[g_][:]), reads=[bgm[g_]], writes=[bt8[g_]])
                    P.op("dve", lambda e: e.tensor_scalar(out=al[g_][:], in0=gm[g_][:], scalar1=t8[g_][:, 2:3], scalar2=None, op0=ALU.is_ge),
                         reads=[bgm[g_], bt8[g_]], writes=[bal[g_]])
                    P.op("dve", lambda e: e.tensor_tensor(out=al[g_][:], in0=al[g_][:], in1=mc[:, 1, lb_, :], op=ALU.mult), reads=[bal[g_], bmc], writes=[bal[g_]])
                    P.op("dve", lambda e: e.tensor_tensor(out=al[g_][:], in0=al[g_][:], in1=mc[:, 2, lb_, :], op=ALU.add), reads=[bal[g_], bmc], writes=[bal[g_]])
                    P.op("dve", lambda e: e.tensor_scalar(out=alb[g_][:], in0=al[g_][:], scalar1=-1.0, scalar2=30000.0, op0=ALU.add, op1=ALU.mult),
                         reads=[bal[g_]], writes=[balb[g_]])
                    pt_, bpt_ = self.bank()
                    ptb = pt_[:].bitcast(BF16)
                    P.op("pe", lambda e: e.transpose(ptb[0:NS, 0:128], alb[g_][:], ident[:]), reads=[balb[g_], bcn], writes=[bpt_])
                    P.op("act", lambda e: e.activation(out=maskT[mk][:, qs_], in_=ptb[0:NS, 0:128], func=AF.Copy), reads=[bpt_], writes=[bmaskT[mk]])
                for qt in range(2):
                    pend = []
                    LA = 2
                    O = [self.PS[0], self.PS[1]]
                    bO = [self.bPS[0], self.bPS[1]]
                    DN = [self.PS[2], self.PS[3]]
                    bDN = [self.bPS[2], self.bPS[3]]
                    def need(j, hf):
                        if j < 8:
                            return j in (qt * 4 + hf * 2, qt * 4 + hf * 2 + 1)
                        return (j - 8) < 32 * qt + 16 * hf + 16
                    ulist = {hf: [(j, kt2) for j in range(NS) for kt2 in range(2) if need(j, hf)] for hf in range(2)}
                    for g4 in range(NS // 4):
                        if not any(need(g4 * 4 + j4, hf) for j4 in range(4) for hf in range(2)):
                            continue
                        ks = ksi % NKS
                        ksi += 1
                        P.dma("sp", KS[ks][:], self.Kall[h * 128:(h + 1) * 128, g4 * 1024:(g4 + 1) * 1024], dKS[ks], writes=[bKS[ks]])
                        P.dma("sp", VS[ks][:].rearrange("p t v -> p (t v)"), self.Vall[h * 128:(h + 1) * 128, g4 * 1024:(g4 + 1) * 1024], dVS[ks], writes=[bVS[ks]])
                        for j4 in range(4):
                            j = g4 * 4 + j4
                            for kt2 in range(2):
                                kti = j4 * 2 + kt2
                                for hf in range(2):
                                    if not need(j, hf):
                                        continue
                                    first = ((j, kt2) == ulist[hf][0])
                                    last = ((j, kt2) == ulist[hf][-1])
                                    q0 = qt * 1024 + hf * 512
                                    ps, bps = self.bank()
                                    lbs = (qt * 4 + hf * 2, qt * 4 + hf * 2 + 1)
                                    diag = j in lbs
                                    P.op("pe", lambda e: e.matmul(ps[:, :], KS[ks][:, kti * 128:(kti + 1) * 128], QT[:, h, q0:q0 + 512], start=True, stop=False),
                                         reads=[bKS[ks], bQT], writes=[bps], signal=False)
                                    P.op("pe", lambda e: e.matmul(ps[:, :], Esel[:, j, :], maskT[mk][:, q0:q0 + 512], start=False, stop=(not diag)),
                                         reads=[bcn, bmaskT[mk]], writes=[bps], signal=(not diag))
                                    if diag:
                                        v = kt2 * 2 + (j - lbs[0])
                                        P.op("pe", lambda e: e.matmul(ps[:, :], ident[:], caus[:, v, :], start=False, stop=True),
                                             reads=[bcn, bmc], writes=[bps])
                                    p_ = pti % NPT
                                    pti += 1
                                    P.op("act", lambda e: e.activation(out=PT[p_][:], in_=ps[:, :], func=AF.Exp, scale=SCALE), reads=[bps], writes=[bPT[p_]])
                                    def _pv(ks=ks, kti=kti, p_=p_, hf=hf, first=first, last=last):
                                        P.op("pe", lambda e: e.matmul(O[hf][:, :], VS[ks][:, kti, :], PT[p_][:], start=first, stop=last),
                                             reads=[bVS[ks], bPT[p_]], writes=[bO[hf]], signal=last)
                                        P.op("pe", lambda e: e.matmul(DN[hf][:, :], ones[:], PT[p_][:], start=first, stop=last),
                                             reads=[bcn, bPT[p_]], writes=[bDN[hf]], signal=True)
                                    pend.append(_pv)
                                    if len(pend) > LA:
                                        pend.pop(0)()
                    while pend:
                        pend.pop(0)()
                    for hf in range(2):
                        q0 = qt * 1024 + hf * 512
                        r_ = hf
                        P.op("dve", lambda e: e.reciprocal(out=rden[r_][:], in_=DN[hf][:, :]), reads=[bDN[hf]], writes=[brden[r_]])
                        P.op("dve", lambda e: e.tensor_tensor(out=Y[:, h, 2 + q0:2 + q0 + 512], in0=O[hf][:, :], in1=rden[r_][:], op=ALU.mult),
                             reads=[bO[hf], brden[r_]], writes=[bYh[h]])
            self.rot_banks = (0, 1, 2, 3, 4, 5, 6, 7)
            self.out_proj_residual("wout", Y, bYh, DC, first=True, coff=2)
            P.barrier()


_PROGS = {}


def prog(kind):
    if kind not in _PROGS:
        _PROGS[kind] = Builder(kind).build()
    return _PROGS[kind]


def shard_xT(x_full):
    xT = np.ascontiguousarray(x_full.T)
    xTp = np.concatenate([np.zeros((D, 2), np.float32), xT], axis=1)
    return [np.ascontiguousarray(xTp[:, c * T:c * T + T + 2]) for c in range(NCORE)]


def launch(kind, maps):
    res = run_bass_kernel_spmd(prog(kind), maps, core_ids=list(range(NCORE)))
    return res.results


def gather_x(results):
    return np.ascontiguousarray(np.concatenate([r["outT"] for r in results], axis=1).T)


def run_ffn(inp, i, x):
    xs = shard_xT(x)
    vec = make_vec(inp[f"l{i}_ln2_g"], inp[f"l{i}_ln2_b"], conv=inp[f"l{i}_ffn_conv"])
    wup = tile_w(inp[f"l{i}_ffn_w_up"])
    wdn = tile_w(inp[f"l{i}_ffn_w_down"])
    maps = [dict(xT=xs[c], vec=vec, cst=make_cst(c), wup=wup, wdn=wdn) for c in range(NCORE)]
    return gather_x(launch("ff", maps))


def run_conv(inp, i, x):
    xs = shard_xT(x)
    vec = make_vec(inp[f"l{i}_ln1_g"], inp[f"l{i}_ln1_b"], conv=inp[f"l{i}_mix_conv"])
    win = tile_w(inp[f"l{i}_mix_w_in"])
    wout = tile_w(inp[f"l{i}_mix_w_out"])
    maps = [dict(xT=xs[c], vec=vec, cst=make_cst(c), win=win, wout=wout) for c in range(NCORE)]
    return gather_x(launch("cv", maps))


def run_hgrn(inp, i, x):
    xs = shard_xT(x)
    vec = make_vec(inp[f"l{i}_ln1_g"], inp[f"l{i}_ln1_b"], norm_g=inp[f"l{i}_mix_norm_g"], lbraw=inp["hgrn_lower_bounds"])
    win = tile_w(inp[f"l{i}_mix_w_in"])
    wout = tile_w(inp[f"l{i}_mix_w_out"])
    Aall = np.zeros((128, NCORE * 8), np.float32)
    Ball = np.zeros((NCORE * 128, 1024), np.float32)
    out = None
    for phase in range(2):
        maps = [dict(xT=xs[c], vec=vec, cst=make_cst(c, layer=i), win=win, Aall=Aall, Ball=Ball) for c in range(NCORE)]
        if phase == 1:
            for m in maps:
                m["wout"] = wout
        res = launch("hg1" if phase == 0 else "hg", maps)
        if phase == 0:
            Aall = np.ascontiguousarray(np.concatenate([r["Aout"] for r in res], axis=1))
            Ball = np.ascontiguousarray(np.concatenate([r["Bout"] for r in res], axis=0))
        else:
            out = gather_x(res)
    return out


def zz_block(c, s_):
    return 8 * s_ + (c if s_ % 2 == 0 else 7 - c)


def run_moba(inp, i, x):
    H = 8
    NSL = 72
    xs = shard_xT(x)
    vec = make_vec(inp[f"l{i}_ln1_g"], inp[f"l{i}_ln1_b"])
    W = inp[f"l{i}_mix_w_in"]
    perm = []
    for qk in range(2):
        for h in range(H):
            base = qk * 1024 + h * 128
            Wp = np.zeros((D, 128), np.float32)
            Wp[:, 0:16] = W[:, base + 16:base + 32]
            Wp[:, 16:32] = W[:, base:base + 16]
            perm.append(Wp)
    win = tile_w(np.concatenate([W] + perm, axis=1))
    pos = inp["positions"].astype(np.int32)
    maps = [dict(xT=xs[c], vec=vec, cst=make_cst(c), win=win, pos=np.ascontiguousarray(pos[:, c * T:(c + 1) * T])) for c in range(NCORE)]
    res = launch("mo1", maps)
    Qg = np.concatenate([np.asarray(r["QT"]).reshape(128, H, 8, 256) for r in res], axis=2)
    Kg = np.concatenate([np.asarray(r["KT"]).reshape(128, H, 8, 256).transpose(1, 0, 2, 3) for r in res], axis=2)
    Vg = np.concatenate([np.asarray(r["V"]).reshape(128, 8, 2, H, 128).transpose(3, 0, 1, 2, 4) for r in res], axis=2)
    KMg = np.concatenate([np.asarray(r["KM"]).reshape(128, H, 8) for r in res], axis=2)
    xTfull = np.ascontiguousarray(x.T).reshape(D, 64, 256)
    wout = tile_w(inp[f"l{i}_mix_w_out"])
    caus = np.zeros((4, 128, 512), np.float32)
    p_ = np.arange(128)[:, None]
    qi = np.arange(256)[None, :]
    for kt2 in range(2):
        for posb in range(2):
            caus[kt2 * 2 + posb][:, posb * 256:(posb + 1) * 256] = np.where(kt2 * 128 + p_ > qi, -30000.0, 0.0)
    maps = []
    own_all = []
    for c in range(NCORE):
        own = [zz_block(c, s_) for s_ in range(8)]
        own_all.append(own)
        pc = np.array(own + list(range(64)))
        Kall = np.ascontiguousarray(Kg[:, :, pc, :]).reshape(H * 128, NSL * 256)
        Vall = np.ascontiguousarray(Vg[:, :, pc]).reshape(H * 128, NSL * 256)
        KMall = np.ascontiguousarray(KMg[:, :, pc]).reshape(128, H * NSL)
        QT = np.ascontiguousarray(Qg[:, :, np.array(own), :]).reshape(128, H * T)
        xT = np.concatenate([np.zeros((D, 2), np.float32), xTfull[:, np.array(own), :].reshape(D, T)], axis=1)
        past = np.zeros((8, NSL), np.float32)
        ownm = np.zeros((8, NSL), np.float32)
        for s_ in range(8):
            past[s_, 8:] = (np.arange(64) < own[s_]).astype(np.float32)
            ownm[s_, s_] = 1.0
        pastbias = np.where(past > 0, 0.0, -1e30).astype(np.float32)
        row = np.concatenate([pastbias.reshape(-1), past.reshape(-1), ownm.reshape(-1)])
        mcst = np.concatenate([np.broadcast_to(row[None, :], (128, 3 * 8 * NSL)), caus.transpose(1, 0, 2).reshape(128, 2048)], axis=1).astype(np.float32)
        maps.append(dict(xT=np.ascontiguousarray(xT), vec=vec, cst=make_cst(c), QT=QT, Kall=Kall, Vall=Vall, KMall=KMall,
                         mcst=np.ascontiguousarray(mcst), wout=wout))
    res2 = launch("mo2", maps)
    out = np.zeros((64, 256, D), np.float32)
    for c in range(NCORE):
        oc = np.asarray(res2[c]["outT"]).T.reshape(8, 256, D)
        for s_ in range(8):
            out[own_all[c][s_]] = oc[s_]
    return np.ascontiguousarray(out.reshape(64 * 256, D))


def kernel(**inputs):
    inp = {k: np.asarray(v) for k, v in inputs.items()}
    x = np.ascontiguousarray(inp["x"][0])
    for i in range(DEPTH):
        kind = i % 3
        if kind == 0:
            x = run_hgrn(inp, i, x)
        elif kind == 1:
            x = run_moba(inp, i, x)
        else:
            x = run_conv(inp, i, x)
        x = run_ffn(inp, i, x)
    return x[None].astype(np.float32)
```

```python
import math
import numpy as np
from contextlib import ExitStack
import concourse.bass as bass
import concourse.mybir as mybir
from concourse.bass_utils import run_bass_kernel_spmd

F32 = mybir.dt.float32
BF16 = mybir.dt.bfloat16
I32 = mybir.dt.int32
AF = mybir.ActivationFunctionType
ALU = mybir.AluOpType
AX = mybir.AxisListType

NCORE = 8
T = 2048
D = 1024
DC = 8
DFF = 2816
FC = 22
DEPTH = 4
ALPHA = (2.0 * DEPTH) ** 0.25
LN_EPS = 1e-5
RMS_EPS = 1e-6
NT = T // 512


class Buf:
    __slots__ = ("name", "w", "r")

    def __init__(self, name="b"):
        self.name = name
        self.w = None
        self.r = {}


class DSem:
    def __init__(self, key, h):
        self.key = key
        self.h = h
        self.cnt = 0


class Prog:
    ENG = ("pe", "act", "dve", "pool", "sp")

    def __init__(self, nc, es):
        self.nc = nc
        self.es = es
        self.eng = dict(pe=nc.tensor, act=nc.scalar, dve=nc.vector, pool=nc.gpsimd, sp=nc.sync)
        self.semh = {}
        self.cnt = {}
        self.known = {k: {} for k in self.ENG}
        for k in self.ENG:
            self.semh[k] = es.enter_context(nc.semaphore("s_" + k))
            self.cnt[k] = 0
        self.dsems = []
        self.nwait = 0
        self.nins = 0

    def dsem(self, name):
        h = self.es.enter_context(self.nc.semaphore("d_" + name))
        d = DSem("d_" + name, h)
        self.semh[d.key] = h
        self.dsems.append(d)
        return d

    def _wait(self, e, key, val):
        if val <= 0:
            return
        if self.known[e].get(key, 0) >= val:
            return
        if key == e:
            if e == "pe":
                return
            if val > self.cnt[e]:
                return
        self.eng[e].wait_ge(self.semh[key], val)
        self.nwait += 1
        self.known[e][key] = val

    def deps(self, e, reads, writes):
        need = {}
        for b in reads:
            if b.w is not None:
                k, v = b.w
                if need.get(k, 0) < v:
                    need[k] = v
        for b in writes:
            if b.w is not None:
                k, v = b.w
                if need.get(k, 0) < v:
                    need[k] = v
            for k, v in b.r.items():
                if need.get(k, 0) < v:
                    need[k] = v
        for k, v in need.items():
            self._wait(e, k, v)

    def op(self, e, fn, reads=(), writes=(), signal=True):
        self.deps(e, reads, writes)
        ins = fn(self.eng[e])
        self.nins += 1
        if signal:
            self.cnt[e] += 1
            ins.then_inc(self.semh[e], 1)
            mark = (e, self.cnt[e])
        else:
            mark = (e, self.cnt[e] + 1)
        for b in writes:
            b.w = mark
            b.r = {}
        for b in reads:
            if b.r.get(e, 0) < mark[1]:
                b.r[e] = mark[1]
        return ins

    def _mark_async(self, ds, reads, writes):
        mark = (ds.key, ds.cnt)
        for b in writes:
            b.w = mark
            b.r = {}
        for b in reads:
            b.r[ds.key] = ds.cnt

    def dma(self, q, out, in_, ds, reads=(), writes=(), **kw):
        self.deps(q, reads, writes)
        ds.cnt += 16
        ins = self.eng[q].dma_start(out=out, in_=in_, **kw)
        ins.then_inc(ds.h, 16)
        self._mark_async(ds, reads, writes)
        return ins

    def allgather(self, in_ap, out_ap, ds, reads=(), writes=()):
        q = "pool"
        self.deps(q, reads, writes)
        ds.cnt += 1
        ins = self.nc.gpsimd.collective_compute(
            "AllGather", ALU.bypass, replica_groups=[list(range(NCORE))],
            ins=[in_ap.opt()], outs=[out_ap.opt()])
        ins.then_inc(ds.h, 1)
        self._mark_async(ds, reads, writes)
        return ins

    def barrier(self, engines=None):
        engines = engines or self.ENG
        for e in engines:
            for k in self.ENG:
                if k != e:
                    self._wait(e, k, self.cnt[k])
            for d in self.dsems:
                self._wait(e, d.key, d.cnt)


def tile_w(W):
    Din, Fo = W.shape
    Wt = W.reshape(Din // 128, 128, Fo // 128, 128).transpose(2, 1, 0, 3)
    return np.ascontiguousarray(Wt).reshape(Fo // 128 * 128, Din)


def vec_cols(v):
    return np.ascontiguousarray(v.reshape(-1, 128).T)


V_LNG = 0
V_LNB = 8
V_CONV = 16
V_NORMG = 148
V_LBRAW = 156
NVEC = 188
C_HMASK = 0
C_LMASK = 8
C_MASK2 = 12
C_IDENT = 140
C_INVF = 268
C_SGN = 269
NCST = 270


def conv_cols(cw):
    return np.concatenate([vec_cols(cw[k]) for k in range(cw.shape[0])], axis=1)


def make_vec(ln_g, ln_b, conv=None, norm_g=None, lbraw=None):
    v = np.zeros((128, NVEC), np.float32)
    v[:, V_LNG:V_LNG + 8] = vec_cols(ln_g)
    v[:, V_LNB:V_LNB + 8] = vec_cols(ln_b)
    if conv is not None:
        c = conv_cols(conv)
        v[:, V_CONV:V_CONV + c.shape[1]] = c
    if norm_g is not None:
        v[:, V_NORMG:V_NORMG + 8] = vec_cols(norm_g)
    if lbraw is not None:
        v[:, V_LBRAW:V_LBRAW + 32] = np.concatenate([vec_cols(lbraw[l]) for l in range(DEPTH)], axis=1)
    return v


def make_cst(core, layer=0):
    c = np.zeros((128, NCST), np.float32)
    for r in range(NCORE):
        c[:, C_HMASK + r] = 1.0 if r < core else 0.0
    for l in range(DEPTH):
        c[:, C_LMASK + l] = 1.0 if 1 <= l <= layer else 0.0
    s_ = np.arange(128)[:, None]
    t_ = np.arange(128)[None, :]
    c[:, C_MASK2:C_MASK2 + 128] = ((s_ // 64 == t_ // 64) & (s_ <= t_)).astype(np.float32)
    c[:, C_IDENT:C_IDENT + 128] = np.eye(128, dtype=np.float32)
    invf = (1.0 / (500000.0 ** (np.arange(0, 32, 2, dtype=np.float32) / np.float32(32.0)))).astype(np.float32)
    c[0:32, C_INVF] = np.concatenate([invf, invf])
    c[0:16, C_SGN] = -1.0
    c[16:32, C_SGN] = 1.0
    return c


class Builder:
    def __init__(self, kind):
        self.kind = kind
        self.nc = bass.Bass("TRN2", target_bir_lowering=False)

    def dram(self, name, shape, dt, kind="Internal"):
        return self.nc.dram_tensor(name, list(shape), dt, kind=kind).ap()

    def sb(self, es, name, shape, dt):
        self._uid = getattr(self, "_uid", 0) + 1
        return es.enter_context(self.nc.sbuf_tensor(f"{name}_{self._uid}", list(shape), dt))

    def build(self):
        nc = self.nc
        kind = self.kind
        with ExitStack() as es:
            self.es = es
            P = self.P = Prog(nc, es)
            self.xT = self.dram("xT", [D, T + 2], F32, "ExternalInput")
            self.vec_d = self.dram("vec", [128, NVEC], F32, "ExternalInput")
            self.cst_d = self.dram("cst", [128, NCST], F32, "ExternalInput")
            self.wd = {}
            if kind == "ff":
                self.wd["wup"] = self.dram("wup", [2 * DFF, D], F32, "ExternalInput")
                self.wd["wdn"] = self.dram("wdn", [D, DFF], F32, "ExternalInput")
            elif kind == "cv":
                self.wd["win"] = self.dram("win", [3072, D], F32, "ExternalInput")
                self.wd["wout"] = self.dram("wout", [D, D], F32, "ExternalInput")
            elif kind == "hg":
                self.wd["win"] = self.dram("win", [4096, D], F32, "ExternalInput")
                self.wd["wout"] = self.dram("wout", [D, D], F32, "ExternalInput")
            elif kind == "hg1":
                self.wd["win"] = self.dram("win", [4096, D], F32, "ExternalInput")
            elif kind == "mo1":
                self.wd["win"] = self.dram("win", [5120, D], F32, "ExternalInput")
            elif kind == "mo2":
                self.wd["wout"] = self.dram("wout", [D, D], F32, "ExternalInput")
            if kind not in ("mo1", "hg1"):
                self.outT = self.dram("outT", [D, T], F32, "ExternalOutput")
            self.XR = self.sb(es, "XR", [128, DC, T], F32)
            self.XB = self.sb(es, "XB", [128, DC, T + 2], BF16)
            self.bXR = [Buf(f"XR{n}") for n in range(NT)]
            self.bXB = [Buf(f"XB{n}") for n in range(NT)]
            self.bXBh = Buf("XBh")
            self.vec = self.sb(es, "vecs", [128, NVEC], F32)
            self.cst = self.sb(es, "csts", [128, NCST], F32)
            self.bvec = Buf("vec")
            self.ones_ln = self.sb(es, "ones_ln", [128, 128], BF16)
            self.bconst = Buf("const")
            NW = 2 if kind == "mo2" else 8
            self.WS = [self.sb(es, f"ws{k}", [128, DC, 128], BF16) for k in range(NW)]
            self.bWS = [Buf(f"ws{k}") for k in range(NW)]
            self.dWS = [P.dsem(f"ws{k}") for k in range(NW)]
            self.wsi = 0
            self.PS = [es.enter_context(nc.psum_tensor(f"ps{k}", [128, 512], F32)) for k in range(8)]
            self.bPS = [Buf(f"ps{k}") for k in range(8)]
            self.psi = 0
            self.d_out = P.dsem("out")
            self.d_misc = P.dsem("misc")
            self.d_misc2 = P.dsem("misc2")
            self.bout = Buf("out")

            P.dma("sp", self.vec[:], self.vec_d, self.d_misc, writes=[self.bvec])
            P.dma("sp", self.cst[:], self.cst_d, self.d_misc2, writes=[self.bvec])
            xTv = self.xT.rearrange("(c p) t -> p c t", p=128)
            d_ins = [P.dsem(f"in{n}") for n in range(NT + 1)]
            for n in range(NT):
                P.dma("sp", self.XR[:, :, n * 512:(n + 1) * 512], xTv[:, :, 2 + n * 512:2 + (n + 1) * 512],
                      d_ins[n], writes=[self.bXR[n]])
            self.hal32 = self.sb(es, "hal32", [128, DC, 2], F32)
            self.bhal32 = Buf("hal32")
            P.dma("sp", self.hal32[:], xTv[:, :, 0:2], d_ins[NT], writes=[self.bhal32])
            P.op("dve", lambda e: e.memset(self.ones_ln[:], 1.0 / D), writes=[self.bconst])
            if kind != "mo2":
                for n in range(NT):
                    P.op("act", lambda e: e.activation(out=self.XB[:, :, 2 + n * 512:2 + (n + 1) * 512],
                                                       in_=self.XR[:, :, n * 512:(n + 1) * 512], func=AF.Copy),
                         reads=[self.bXR[n]], writes=[self.bXB[n]])
                P.op("act", lambda e: e.activation(out=self.XB[:, :, 0:2], in_=self.hal32[:], func=AF.Copy),
                     reads=[self.bhal32], writes=[self.bXBh])

            if kind == "ff":
                self.ffn()
            elif kind == "cv":
                self.conv_mixer()
            elif kind in ("hg", "hg1"):
                self.hgrn_mixer()
            elif kind == "mo1":
                self.moba_qkv()
            elif kind == "mo2":
                self.moba_attn()
            if kind not in ("mo1", "hg1"):
                self.layernorm(V_LNG, V_LNB)
                oTv = self.outT.rearrange("(c p) t -> p c t", p=128)
                for n in range(NT):
                    P.dma("sp", oTv[:, :, n * 512:(n + 1) * 512], self.XR[:, :, n * 512:(n + 1) * 512],
                          self.d_out, reads=[self.bXR[n]], writes=[self.bout])
            P.barrier()
        return nc

    def bank(self):
        rb = getattr(self, "rot_banks", (0, 1, 2, 3, 4, 5, 6, 7))
        self.psi = (self.psi + 1) % len(rb)
        k = rb[self.psi]
        return self.PS[k], self.bPS[k]

    def load_w(self, name, j, c0=0, nch=DC):
        P = self.P
        k = self.wsi
        self.wsi = (self.wsi + 1) % len(self.WS)
        slot, b, ds = self.WS[k], self.bWS[k], self.dWS[k]
        src = self.wd[name][j * 128:(j + 1) * 128, c0 * 128:(c0 + nch) * 128]
        dst = slot[:, 0:nch, :].rearrange("p c i -> p (c i)")
        P.dma("pool", dst, src, ds, writes=[b])
        return slot, b

    def xb_bufs(self, col0, n):
        bs = []
        if col0 < 2:
            bs.append(self.bXBh)
        lo = max(col0 - 2, 0)
        hi = col0 + n - 2
        for t in range(NT):
            if lo < (t + 1) * 512 and hi > t * 512:
                bs.append(self.bXB[t])
        return bs

    def proj(self, w, bw, col0, n, ps, bps, m0=0, m=128, src=None, srcbufs=None, nch=DC):
        P = self.P
        src = self.XB if src is None else src
        srcbufs = self.xb_bufs(col0, n) if srcbufs is None else srcbufs
        for c in range(nch):
            P.op("pe", lambda e: e.matmul(ps[0:m, 0:n], w[:, c, m0:m0 + m], src[:, c, col0:col0 + n],
                                          start=(c == 0), stop=(c == nch - 1)),
                 reads=[bw] + srcbufs, writes=[bps], signal=(c == nch - 1))

    def layernorm(self, goff, boff):
        P = self.P
        with ExitStack() as es:
            zb = [self.sb(es, f"ln_zb{k}", [128, DC, 512], BF16) for k in range(2)]
            zq = [self.sb(es, f"ln_zq{k}", [128, DC, 512], BF16) for k in range(2)]
            bzb = [Buf() for _ in range(2)]
            bzq = [Buf() for _ in range(2)]
            mean = [self.sb(es, f"ln_mean{k}", [128, 512], F32) for k in range(2)]
            rstd = [self.sb(es, f"ln_rstd{k}", [128, 512], F32) for k in range(2)]
            tmp = [self.sb(es, f"ln_tmp{k}", [128, 512], F32) for k in range(2)]
            bmean = [Buf() for _ in range(2)]
            brstd = [Buf() for _ in range(2)]
            btmp = [Buf() for _ in range(2)]
            for n in range(NT):
                k = n % 2
                cs = slice(n * 512, (n + 1) * 512)
                P.op("act", lambda e: e.activation(out=zb[k][:], in_=self.XR[:, :, cs], func=AF.Copy),
                     reads=[self.bXR[n]], writes=[bzb[k]])
                P.op("act", lambda e: e.activation(out=zq[k][:], in_=self.XR[:, :, cs], func=AF.Square),
                     reads=[self.bXR[n]], writes=[bzq[k]])
                pm, bpm = self.bank()
                for c in range(DC):
                    P.op("pe", lambda e: e.matmul(pm[:, :], self.ones_ln[:], zb[k][:, c, :], start=(c == 0), stop=(c == DC - 1)),
                         reads=[self.bconst, bzb[k]], writes=[bpm], signal=(c == DC - 1))
                pq, bpq = self.bank()
                for c in range(DC):
                    P.op("pe", lambda e: e.matmul(pq[:, :], self.ones_ln[:], zq[k][:, c, :], start=(c == 0), stop=(c == DC - 1)),
                         reads=[self.bconst, bzq[k]], writes=[bpq], signal=(c == DC - 1))
                P.op("dve", lambda e: e.tensor_copy(out=mean[k][:], in_=pm[:, :]), reads=[bpm], writes=[bmean[k]])
                P.op("dve", lambda e: e.tensor_tensor(out=tmp[k][:], in0=mean[k][:], in1=mean[k][:], op=ALU.mult),
                     reads=[bmean[k]], writes=[btmp[k]])
                P.op("dve", lambda e: e.tensor_tensor(out=rstd[k][:], in0=pq[:, :], in1=tmp[k][:], op=ALU.subtract),
                     reads=[bpq, btmp[k]], writes=[brstd[k]])
                P.op("dve", lambda e: e.tensor_scalar(out=rstd[k][:], in0=rstd[k][:], scalar1=LN_EPS, scalar2=None,
                                                      op0=ALU.add),
                     reads=[brstd[k]], writes=[brstd[k]])
                P.op("act", lambda e: e.activation(out=rstd[k][:], in_=rstd[k][:], func=AF.Ln),
                     reads=[brstd[k]], writes=[brstd[k]])
                P.op("act", lambda e: e.activation(out=rstd[k][:], in_=rstd[k][:], func=AF.Exp, scale=-0.5),
                     reads=[brstd[k]], writes=[brstd[k]])
                for c in range(DC):
                    P.op("dve", lambda e: e.tensor_tensor(out=tmp[k][:], in0=self.XR[:, c, cs], in1=mean[k][:], op=ALU.subtract),
                         reads=[self.bXR[n], bmean[k]], writes=[btmp[k]])
                    P.op("dve", lambda e: e.tensor_tensor(out=tmp[k][:], in0=tmp[k][:], in1=rstd[k][:], op=ALU.mult),
                         reads=[btmp[k], brstd[k]], writes=[btmp[k]])
                    P.op("act", lambda e: e.activation(out=self.XR[:, c, cs], in_=tmp[k][:], func=AF.Identity,
                                                       bias=self.vec[:, boff + c:boff + c + 1],
                                                       scale=self.vec[:, goff + c:goff + c + 1]),
                         reads=[btmp[k], self.bvec], writes=[self.bXR[n]])
                P.op("act", lambda e: e.activation(out=self.XB[:, :, 2 + n * 512:2 + (n + 1) * 512],
                                                   in_=self.XR[:, :, cs], func=AF.Copy),
                     reads=[self.bXR[n]], writes=[self.bXB[n]])
            P.barrier()

    def out_proj_residual(self, wname, Y, bY, nch, c0=0, first=True, wtile_c0=0, coff=0):
        P = self.P
        for f in range(DC):
            w, bw = self.load_w(wname, f, c0=wtile_c0, nch=nch)
            for n in range(NT):
                ps, bps = self.bank()
                self.proj(w, bw, coff + n * 512, 512, ps, bps, src=Y, srcbufs=bY if isinstance(bY, list) else [bY], nch=nch)
                cs = slice(n * 512, (n + 1) * 512)
                if first:
                    P.op("dve", lambda e: e.scalar_tensor_tensor(out=self.XR[:, f, cs], in0=self.XR[:, f, cs], scalar=ALPHA,
                                                                 in1=ps[:, :], op0=ALU.mult, op1=ALU.add),
                         reads=[bps, self.bXR[n]], writes=[self.bXR[n]])
                else:
                    P.op("dve", lambda e: e.tensor_tensor(out=self.XR[:, f, cs], in0=self.XR[:, f, cs], in1=ps[:, :], op=ALU.add),
                         reads=[bps, self.bXR[n]], writes=[self.bXR[n]])

    def ffn(self):
        P = self.P
        cvo = V_CONV
        groups = [(0, 8), (8, 15), (15, 22)]
        with ExitStack() as es:
            GH = 8
            H = self.sb(es, "ffn_H", [128, GH, T], BF16)
            bH = [Buf() for _ in range(GH)]
            U = [[self.sb(es, f"ffn_U{ab}{k}", [128, 514], F32) for k in range(3)] for ab in range(2)]
            bU = [[Buf() for _ in range(3)] for _ in range(2)]
            Y = [[self.sb(es, f"ffn_Y{ab}{k}", [128, 512], F32) for k in range(2)] for ab in range(2)]
            bY = [[Buf() for _ in range(2)] for _ in range(2)]
            TT = [[self.sb(es, f"ffn_T{ab}{k}", [128, 512], F32) for k in range(2)] for ab in range(2)]
            bTT = [[Buf() for _ in range(2)] for _ in range(2)]
            SA = [self.sb(es, f"ffn_SA{k}", [128, 512], F32) for k in range(2)]
            bSA = [Buf() for _ in range(2)]
            ui = 0
            yi = 0
            for gi, (j0, j1) in enumerate(groups):
                for j in range(j0, j1):
                    ws = [self.load_w("wup", j), self.load_w("wup", FC + j)]
                    for n in range(NT):
                        uk = ui % 3
                        up = (ui - 1) % 3
                        ui += 1
                        yk = yi % 2
                        yi += 1
                        for ab in range(2):
                            w, bw = ws[ab]
                            u, bu = U[ab][uk], bU[ab][uk]
                            cj = j + ab * FC
                            if n == 0:
                                ph, bph = self.bank()
                                self.proj(w, bw, 0, 2, ph, bph)
                                P.op("act", lambda e: e.activation(out=u[:, 0:2], in_=ph[:, 0:2], func=AF.Copy),
                                     reads=[bph], writes=[bu])
                            else:
                                P.op("act", lambda e: e.activation(out=u[:, 0:2], in_=U[ab][up][:, 512:514], func=AF.Copy),
                                     reads=[bU[ab][up]], writes=[bu])
                            ps, bps = self.bank()
                            self.proj(w, bw, 2 + n * 512, 512, ps, bps)
                            P.op("act", lambda e: e.activation(out=u[:, 2:514], in_=ps[:, :], func=AF.Copy),
                                 reads=[bps], writes=[bu])
                            t, bt = TT[ab][yk], bTT[ab][yk]
                            y, by = Y[ab][yk], bY[ab][yk]
                            w0 = self.vec[:, cvo + cj:cvo + cj + 1]
                            w1 = self.vec[:, cvo + 44 + cj:cvo + 44 + cj + 1]
                            w2 = self.vec[:, cvo + 88 + cj:cvo + 88 + cj + 1]
                            P.op("act", lambda e: e.activation(out=t[:], in_=u[:, 0:512], func=AF.Copy, scale=w0),
                                 reads=[bu, self.bvec], writes=[bt])
                            P.op("dve", lambda e: e.scalar_tensor_tensor(out=t[:], in0=u[:, 1:513], scalar=w1, in1=t[:],
                                                                         op0=ALU.mult, op1=ALU.add),
                                 reads=[bu, bt, self.bvec], writes=[bt])
                            P.op("dve", lambda e: e.scalar_tensor_tensor(out=y[:], in0=u[:, 2:514], scalar=w2, in1=t[:],
                                                                         op0=ALU.mult, op1=ALU.add),
                                 reads=[bu, bt, self.bvec], writes=[by])
                        sa, bsa = SA[yk], bSA[yk]
                        P.op("act", lambda e: e.activation(out=sa[:], in_=Y[0][yk][:], func=AF.Silu),
                             reads=[bY[0][yk]], writes=[bsa])
                        P.op("dve", lambda e: e.tensor_tensor(out=H[:, j - j0, n * 512:(n + 1) * 512], in0=sa[:], in1=Y[1][yk][:],
                                                              op=ALU.mult),
                             reads=[bsa, bY[1][yk]], writes=[bH[j - j0]])
                self.out_proj_residual("wdn", H, bH[:j1 - j0], j1 - j0, first=(gi == 0), wtile_c0=j0)
            P.barrier()

    def conv_mixer(self):
        P = self.P
        cvo = V_CONV
        with ExitStack() as es:
            Yo = self.sb(es, "cm_Y", [128, DC, T], BF16)
            bYo = [Buf() for _ in range(DC)]
            PR = [self.sb(es, f"cm_P{k}", [128, 514], F32) for k in range(3)]
            bPR = [Buf() for _ in range(3)]
            CG = [self.sb(es, f"cm_CG{k}", [128, 514], F32) for k in range(2)]
            bCG = [Buf() for _ in range(2)]
            TT = [self.sb(es, f"cm_T{k}", [128, 512], F32) for k in range(2)]
            bTT = [Buf() for _ in range(2)]
            ui = 0
            for c in range(DC):
                wbg = self.load_w("win", c)
                wcg = self.load_w("win", DC + c)
                wh = self.load_w("win", 2 * DC + c)
                w0 = self.vec[:, cvo + c:cvo + c + 1]
                w1 = self.vec[:, cvo + 8 + c:cvo + 8 + c + 1]
                w2 = self.vec[:, cvo + 16 + c:cvo + 16 + c + 1]
                for n in range(NT):
                    uk = ui % 3
                    up = (ui - 1) % 3
                    k2 = ui % 2
                    ui += 1
                    pr, bpr = PR[uk], bPR[uk]
                    cg, bcg = CG[k2], bCG[k2]
                    t, bt = TT[k2], bTT[k2]
                    if n == 0:
                        p1, bp1 = self.bank()
                        self.proj(wcg[0], wcg[1], 0, 2, p1, bp1)
                        P.op("act", lambda e: e.activation(out=cg[:, 0:2], in_=p1[:, 0:2], func=AF.Copy), reads=[bp1], writes=[bcg])
                        p2, bp2 = self.bank()
                        self.proj(wh[0], wh[1], 0, 2, p2, bp2)
                        P.op("dve", lambda e: e.tensor_tensor(out=pr[:, 0:2], in0=cg[:, 0:2], in1=p2[:, 0:2], op=ALU.mult),
                             reads=[bcg, bp2], writes=[bpr])
                    else:
                        P.op("act", lambda e: e.activation(out=pr[:, 0:2], in_=PR[up][:, 512:514], func=AF.Copy),
                             reads=[bPR[up]], writes=[bpr])
                    p1, bp1 = self.bank()
                    self.proj(wcg[0], wcg[1], 2 + n * 512, 512, p1, bp1)
                    P.op("act", lambda e: e.activation(out=cg[:, 2:514], in_=p1[:, :], func=AF.Copy), reads=[bp1], writes=[bcg])
                    p2, bp2 = self.bank()
                    self.proj(wh[0], wh[1], 2 + n * 512, 512, p2, bp2)
                    P.op("dve", lambda e: e.tensor_tensor(out=pr[:, 2:514], in0=cg[:, 2:514], in1=p2[:, :], op=ALU.mult),
                         reads=[bcg, bp2], writes=[bpr])
                    p3, bp3 = self.bank()
                    self.proj(wbg[0], wbg[1], 2 + n * 512, 512, p3, bp3)
                    P.op("act", lambda e: e.activation(out=t[:], in_=pr[:, 0:512], func=AF.Copy, scale=w0),
                         reads=[bpr, self.bvec], writes=[bt])
                    P.op("dve", lambda e: e.scalar_tensor_tensor(out=t[:], in0=pr[:, 1:513], scalar=w1, in1=t[:],
                                                                 op0=ALU.mult, op1=ALU.add),
                         reads=[bpr, bt, self.bvec], writes=[bt])
                    P.op("dve", lambda e: e.scalar_tensor_tensor(out=t[:], in0=pr[:, 2:514], scalar=w2, in1=t[:],
                                                                 op0=ALU.mult, op1=ALU.add),
                         reads=[bpr, bt, self.bvec], writes=[bt])
                    P.op("dve", lambda e: e.tensor_tensor(out=Yo[:, c, n * 512:(n + 1) * 512], in0=t[:], in1=p3[:, :], op=ALU.mult),
                         reads=[bt, bp3], writes=[bYo[c]])
            self.out_proj_residual("wout", Yo, bYo, DC, first=True)
            P.barrier()

    def hgrn_mixer(self):
        P = self.P
        nc = self.nc
        H = 8
        lite = (self.kind == "hg1")
        with ExitStack() as es:
            self.Aall = self.dram("Aall", [128, NCORE * H], F32, "ExternalInput")
            self.Ball = self.dram("Ball", [NCORE * 128, H * 128], F32, "ExternalInput")
            self.Aout = self.dram("Aout", [128, H], F32, "ExternalOutput")
            self.Bout = self.dram("Bout", [128, H * 128], F32, "ExternalOutput")
            G = [self.sb(es, f"hg_G{k}", [128, 512], BF16) for k in range(2)]
            bG = [Buf() for _ in range(2)]
            Y = self.sb(es, "hg_Y", [128, H, T], BF16)
            bY = [Buf() for _ in range(H)]
            S = self.sb(es, "hg_S", [128, H, 128], F32)
            bS = [Buf() for _ in range(H)]
            Aall = self.sb(es, "hg_Aall", [128, NCORE, H], F32)
            bAall = Buf()
            ap_ = self.sb(es, "hg_ap", [128, H], F32)
            bap = Buf()
            om = self.sb(es, "hg_om", [128, NCORE], F32)
            bom = Buf()
            Bm = [self.sb(es, f"hg_Bm{k}", [128, H, 128], F32) for k in range(1)]
            bBm = [Buf() for _ in range(1)]
            dBm = [P.dsem(f"bm{k}") for k in range(1)]
            d_a = P.dsem("aall")
            lbe = self.sb(es, "hg_lbe", [128, 4, H], F32)
            lb = self.sb(es, "hg_lb", [128, H], F32)
            oml = self.sb(es, "hg_oml", [128, H], F32)
            lbm1 = self.sb(es, "hg_lbm1", [128, H], F32)
            lsum = self.sb(es, "hg_lsum", [128, H], F32)
            blb = Buf()
            rmask = self.sb(es, "hg_rmask", [128, 512], F32)
            ident = self.sb(es, "hg_ident", [128, 128], BF16)
            ones128 = self.sb(es, "hg_ones", [128, 128], BF16)
            bcn = Buf()
            blsum = self.sb(es, "hg_blsum", [128, H], F32)
            bblsum = Buf()
            P.op("dve", lambda e: e.memset(rmask[:], 1.0), writes=[bcn])
            P.op("dve", lambda e: e.memset(rmask[:].rearrange("p (c t) -> p c t", t=64)[:, :, 0:1], 0.0), writes=[bcn])
            P.op("dve", lambda e: e.memset(ones128[:], 1.0 / 128.0), writes=[bcn])
            one1 = self.sb(es, "hg_one1", [128, 1], F32)
            P.op("dve", lambda e: e.memset(one1[:], 1.0), writes=[bcn])
            P.op("dve", lambda e: e.memset(blsum[:], 0.0), writes=[bblsum])
            P.op("act", lambda e: e.activation(out=ident[:], in_=self.cst[:, C_IDENT:C_IDENT + 128], func=AF.Copy),
                 reads=[self.bvec], writes=[bcn])
            P.op("act", lambda e: e.activation(out=lbe[:].rearrange("p l h -> p (l h)"), in_=self.vec[:, V_LBRAW:V_LBRAW + 32], func=AF.Exp),
                 reads=[self.bvec], writes=[blb])
            P.op("dve", lambda e: e.tensor_tensor(out=lsum[:], in0=lbe[:, 0, :], in1=lbe[:, 1, :], op=ALU.add), reads=[blb], writes=[blb])
            P.op("dve", lambda e: e.tensor_tensor(out=lsum[:], in0=lsum[:], in1=lbe[:, 2, :], op=ALU.add), reads=[blb], writes=[blb])
            P.op("dve", lambda e: e.tensor_tensor(out=lsum[:], in0=lsum[:], in1=lbe[:, 3, :], op=ALU.add), reads=[blb], writes=[blb])
            P.op("dve", lambda e: e.reciprocal(out=lsum[:], in_=lsum[:]), reads=[blb], writes=[blb])
            P.op("dve", lambda e: e.tensor_scalar(out=lb[:], in0=lbe[:, 1, :], scalar1=self.cst[:, C_LMASK + 1:C_LMASK + 2], scalar2=None, op0=ALU.mult),
                 reads=[blb, self.bvec], writes=[blb])
            for l in (2, 3):
                P.op("dve", lambda e: e.scalar_tensor_tensor(out=lb[:], in0=lbe[:, l, :], scalar=self.cst[:, C_LMASK + l:C_LMASK + l + 1],
                                                             in1=lb[:], op0=ALU.mult, op1=ALU.add),
                     reads=[blb, self.bvec], writes=[blb])
            P.op("dve", lambda e: e.tensor_tensor(out=lb[:], in0=lb[:], in1=lsum[:], op=ALU.mult), reads=[blb], writes=[blb])
            P.op("dve", lambda e: e.tensor_scalar(out=lbm1[:], in0=lb[:], scalar1=-1.0, scalar2=None, op0=ALU.add), reads=[blb], writes=[blb])
            P.op("dve", lambda e: e.tensor_scalar(out=oml[:], in0=lbm1[:], scalar1=-1.0, scalar2=None, op0=ALU.mult), reads=[blb], writes=[blb])
            P.dma("sp", Aall[:].rearrange("p r h -> p (r h)"), self.Aall, d_a, writes=[bAall])
            P.op("dve", lambda e: e.memset(S[:], 0.0), writes=bS)
            P.op("dve", lambda e: e.tensor_scalar(out=om[:], in0=self.cst[:, C_HMASK:C_HMASK + NCORE], scalar1=-1.0, scalar2=1.0,
                                                  op0=ALU.mult, op1=ALU.add), reads=[self.bvec], writes=[bom])
            for r in range(NCORE - 1):
                k = 0
                P.dma("sp", Bm[k][:].rearrange("p h v -> p (h v)"), self.Ball[r * 128:(r + 1) * 128, :], dBm[k], writes=[bBm[k]])
                mr = self.cst[:, C_HMASK + r:C_HMASK + r + 1]
                P.op("dve", lambda e: e.tensor_scalar(out=ap_[:], in0=Aall[:, r, :], scalar1=mr, scalar2=om[:, r:r + 1], op0=ALU.mult, op1=ALU.add),
                     reads=[bAall, self.bvec, bom], writes=[bap])
                P.op("dve", lambda e: e.tensor_scalar(out=Bm[k][:], in0=Bm[k][:], scalar1=mr, scalar2=None, op0=ALU.mult),
                     reads=[bBm[k], self.bvec], writes=[bBm[k]])
                for h in range(H):
                    P.op("dve", lambda e: e.scalar_tensor_tensor(out=S[:, h, :], in0=S[:, h, :], scalar=ap_[:, h:h + 1], in1=Bm[k][:, h, :],
                                                                 op0=ALU.mult, op1=ALU.add),
                         reads=[bS[h], bap, bBm[k]], writes=[bS[h]])
            R2 = lambda nm, shp, dt: [self.sb(es, f"{nm}{k}", shp, dt) for k in range(2)]
            sig, bsig = R2("hg_sig", [128, 512], F32), [Buf(), Buf()]
            qs, bqs = R2("hg_qs", [128, 512], F32), [Buf(), Buf()]
            R1 = lambda nm, shp, dt: [self.sb(es, nm, shp, dt)] * 2
            B1 = lambda: [Buf()] * 2
            kk, bkk = R1("hg_k", [128, 512], F32), B1()
            gg, bgg = R1("hg_g", [128, 512], F32), B1()
            bb, bbb = R1("hg_b", [128, 512], F32), B1()
            e1, be1 = R2("hg_e1", [128, 512], F32), [Buf(), Buf()]
            e2, be2 = R1("hg_e2", [128, 512], F32), B1()
            ebl, bebl = R2("hg_ebl", [128, 8], F32), [Buf(), Buf()]
            qp, bqp = R2("hg_qp", [128, 512], BF16), [Buf(), Buf()]
            kp, bkp = R2("hg_kp", [128, 512], BF16), [Buf(), Buf()]
            ktok, bktok = R2("hg_ktok", [128, 4, 128], BF16), [Buf(), Buf()]
            vtok, bvtok = R2("hg_vtok", [128, 4, 128], BF16), [Buf(), Buf()]
            am, bam = R2("hg_am", [128, 128], BF16), [Buf(), Buf()]
            sdb = [self.sb(es, f"hg_sdb{c_}", [128, 128], BF16) for c_ in range(8)]
            bsdb = [Buf() for _ in range(8)]
            osq, bosq = R2("hg_osq", [128, 512], BF16), [Buf(), Buf()]
            rr, brr = R2("hg_rr", [128, 512], F32), [Buf(), Buf()]
            yy, byy = R2("hg_yy", [128, 512], F32), [Buf(), Buf()]
            it = 0
            sdi = 0
            ami = 0
            self.rot_banks = (4, 5, 6, 7)
            cnts = {"ami": 0}

            def stage1(h, n, k, Wh):
                wq, wf, wi, wg, lbh, omlh, lbm1h = Wh
                c0 = 2 + n * 512
                cs = slice(n * 512, (n + 1) * 512)
                if not lite:
                    pq, bpq = self.bank()
                    self.proj(wq[0], wq[1], c0, 512, pq, bpq)
                pf, bpf = self.bank()
                self.proj(wf[0], wf[1], c0, 512, pf, bpf)
                if not lite:
                    pg, bpg = self.bank()
                    self.proj(wg[0], wg[1], c0, 512, pg, bpg)
                pv, bpv = self.bank()
                for s4 in range(4):
                    for c in range(DC):
                        P.op("pe", lambda e: e.matmul(pv[:, s4 * 128:(s4 + 1) * 128], self.XB[:, c, c0 + s4 * 128:c0 + (s4 + 1) * 128],
                                                      wi[0][:, c, :], start=(c == 0), stop=(c == DC - 1)),
                             reads=[wi[1], self.bXB[n]], writes=[bpv], signal=(c == DC - 1 and s4 == 3))
                P.op("act", lambda e: e.activation(out=sig[k][:], in_=pf[:, :], func=AF.Exp, scale=-1.0), reads=[bpf], writes=[bsig[k]])
                P.op("act", lambda e: e.activation(out=sig[k][:], in_=sig[k][:], func=AF.Ln, bias=one1[:, 0:1]), reads=[bsig[k], bcn], writes=[bsig[k]])
                P.op("act", lambda e: e.activation(out=sig[k][:], in_=sig[k][:], func=AF.Exp, scale=-1.0), reads=[bsig[k]], writes=[bsig[k]])
                if not lite:
                    P.op("act", lambda e: e.activation(out=qs[k][:], in_=pq[:, :], func=AF.Silu), reads=[bpq], writes=[bqs[k]])
                    P.op("act", lambda e: e.activation(out=G[k][:], in_=pg[:, :], func=AF.Silu), reads=[bpg], writes=[bG[k]])
                P.op("act", lambda e: e.activation(out=vtok[k][:].rearrange("p s v -> p (s v)"), in_=pv[:, :], func=AF.Copy),
                     reads=[bpv], writes=[bvtok[k]])
                P.op("dve", lambda e: e.tensor_scalar(out=kk[k][:], in0=sig[k][:], scalar1=lbm1h, scalar2=omlh, op0=ALU.mult, op1=ALU.add),
                     reads=[bsig[k], blb], writes=[bkk[k]])
                P.op("act", lambda e: e.activation(out=gg[k][:], in_=sig[k][:], func=AF.Ln, bias=lbh, scale=omlh),
                     reads=[bsig[k], blb], writes=[bgg[k]])
                P.op("dve", lambda e: e.tensor_tensor_scan(out=bb[k][:], data0=rmask[:], data1=gg[k][:], initial=0.0, op0=ALU.mult, op1=ALU.add),
                     reads=[bgg[k], bcn], writes=[bbb[k]])
                b3 = bb[k][:].rearrange("p (c t) -> p c t", t=64)
                P.op("dve", lambda e: e.tensor_tensor(out=e1[k][:].rearrange("p (c t) -> p c t", t=64), in0=b3,
                                                      in1=b3[:, :, 63:64].to_broadcast([128, 8, 64]), op=ALU.subtract),
                     reads=[bbb[k]], writes=[be1[k]])
                P.op("act", lambda e: e.activation(out=e2[k][:], in_=e1[k][:], func=AF.Exp, scale=-1.0), reads=[be1[k]], writes=[be2[k]])
                if not lite:
                    P.op("act", lambda e: e.activation(out=e1[k][:], in_=e1[k][:], func=AF.Exp), reads=[be1[k], be2[k]], writes=[be1[k]])
                P.op("act", lambda e: e.activation(out=ebl[k][:], in_=b3[:, :, 63], func=AF.Exp), reads=[bbb[k]], writes=[bebl[k]])
                P.op("dve", lambda e: e.tensor_reduce(out=rr[k][:, 0:1], in_=b3[:, :, 63], axis=AX.X, op=ALU.add), reads=[bbb[k]], writes=[brr[k]])
                P.op("dve", lambda e: e.tensor_tensor(out=blsum[:, h:h + 1], in0=blsum[:, h:h + 1], in1=rr[k][:, 0:1], op=ALU.add),
                     reads=[brr[k], bblsum], writes=[bblsum])
                if not lite:
                    P.op("dve", lambda e: e.tensor_tensor(out=qp[k][:], in0=qs[k][:], in1=e1[k][:], op=ALU.mult), reads=[bqs[k], be1[k]], writes=[bqp[k]])
                P.op("dve", lambda e: e.tensor_tensor(out=kp[k][:], in0=kk[k][:], in1=e2[k][:], op=ALU.mult), reads=[bkk[k], be2[k]], writes=[bkp[k]])

            def stage2(h, n, k):
                ami = cnts["ami"]
                cs = slice(n * 512, (n + 1) * 512)
                pt, bpt = self.bank()
                ptb = pt[:].bitcast(BF16)
                for s4 in range(4):
                    P.op("pe", lambda e: e.transpose(ptb[:, s4 * 128:(s4 + 1) * 128], kp[k][:, s4 * 128:(s4 + 1) * 128], ident[:]),
                         reads=[bkp[k], bcn], writes=[bpt], signal=(s4 == 3))
                P.op("act", lambda e: e.activation(out=ktok[k][:].rearrange("p s v -> p (s v)"), in_=ptb[:, 0:512], func=AF.Copy),
                     reads=[bpt], writes=[bktok[k]])
                po, bpo = self.PS[k], self.bPS[k]
                for ci in range(8):
                    s4, half = ci // 2, ci % 2
                    rows = slice(half * 64, half * 64 + 64)
                    pu, bpu = self.PS[2 + half], self.bPS[2 + half]
                    P.op("pe", lambda e: e.matmul(pu[:, s4 * 128:(s4 + 1) * 128], ktok[k][rows, s4, :], vtok[k][rows, s4, :],
                                                  start=True, stop=True),
                         reads=[bktok[k], bvtok[k]], writes=[bpu], signal=(ci >= 6))
                for ci in range(8):
                    s4, half = ci // 2, ci % 2
                    eb = ebl[k][:, ci:ci + 1]
                    pu, bpu = self.PS[2 + half], self.bPS[2 + half]
                    if not lite:
                        P.op("dve", lambda e: e.tensor_scalar(out=sdb[ci][:], in0=S[:, h, :], scalar1=eb, scalar2=None, op0=ALU.mult),
                             reads=[bS[h], bebl[k]], writes=[bsdb[ci]])
                    P.op("dve", lambda e: e.scalar_tensor_tensor(out=S[:, h, :], in0=S[:, h, :], scalar=eb, in1=pu[:, s4 * 128:(s4 + 1) * 128],
                                                                 op0=ALU.mult, op1=ALU.add),
                         reads=[bS[h], bebl[k], bpu], writes=[bS[h]])
                for s4 in range(0 if lite else 4):
                    sl = slice(s4 * 128, (s4 + 1) * 128)
                    pa, bpa = self.bank()
                    P.op("pe", lambda e: e.matmul(pa[:, 0:128], kp[k][:, sl], qp[k][:, sl], start=True, stop=True),
                         reads=[bkp[k], bqp[k]], writes=[bpa])
                    a_ = ami % 2
                    ami += 1
                    P.op("dve", lambda e: e.tensor_tensor(out=am[a_][:], in0=pa[:, 0:128], in1=self.cst[:, C_MASK2:C_MASK2 + 128], op=ALU.mult),
                         reads=[bpa, self.bvec], writes=[bam[a_]])
                    P.op("pe", lambda e: e.matmul(po[:, sl], vtok[k][:, s4, :], am[a_][:], start=True, stop=False),
                         reads=[bvtok[k], bam[a_]], writes=[bpo], signal=False)
                    for half in range(2):
                        ci = s4 * 2 + half
                        hs = slice(s4 * 128 + half * 64, s4 * 128 + half * 64 + 64)
                        P.op("pe", lambda e: e.matmul(po[:, hs], sdb[ci][:], qp[k][:, hs], start=False, stop=(half == 1)),
                             reads=[bsdb[ci], bqp[k]], writes=[bpo], signal=True)
                if lite:
                    return
                P.op("act", lambda e: e.activation(out=osq[k][:], in_=po[:, :], func=AF.Square), reads=[bpo], writes=[bosq[k]])
                pm, bpm = self.bank()
                P.op("pe", lambda e: e.matmul(pm[:, :], ones128[:], osq[k][:], start=True, stop=True), reads=[bcn, bosq[k]], writes=[bpm])
                P.op("dve", lambda e: e.tensor_scalar(out=rr[k][:], in0=pm[:, :], scalar1=RMS_EPS, scalar2=None, op0=ALU.add),
                     reads=[bpm], writes=[brr[k]])
                P.op("act", lambda e: e.activation(out=rr[k][:], in_=rr[k][:], func=AF.Ln), reads=[brr[k]], writes=[brr[k]])
                P.op("act", lambda e: e.activation(out=rr[k][:], in_=rr[k][:], func=AF.Exp, scale=-0.5), reads=[brr[k]], writes=[brr[k]])
                P.op("dve", lambda e: e.tensor_tensor(out=yy[k][:], in0=po[:, :], in1=rr[k][:], op=ALU.mult), reads=[bpo, brr[k]], writes=[byy[k]])
                P.op("dve", lambda e: e.scalar_tensor_tensor(out=Y[:, h, cs], in0=yy[k][:], scalar=self.vec[:, V_NORMG + h:V_NORMG + h + 1],
                                                             in1=G[k][:], op0=ALU.mult, op1=ALU.mult),
                     reads=[byy[k], self.bvec, bG[k]], writes=[bY[h]])

            def wts(h):
                return (self.load_w("win", h), self.load_w("win", H + h), self.load_w("win", 2 * H + h), self.load_w("win", 3 * H + h),
                        lb[:, h:h + 1], oml[:, h:h + 1], lbm1[:, h:h + 1])
            for hp in range(0, H, 2):
                WA, WB = wts(hp), wts(hp + 1)
                for n in range(NT):
                    stage1(hp, n, 0, WA)
                    stage1(hp + 1, n, 1, WB)
                    stage2(hp, n, 0)
                    stage2(hp + 1, n, 1)
            P.op("act", lambda e: e.activation(out=blsum[:], in_=blsum[:], func=AF.Exp), reads=[bblsum], writes=[bblsum])
            d_o = P.dsem("hgo")
            P.dma("sp", self.Aout, blsum[:], d_o, reads=[bblsum], writes=[self.bout])
            P.dma("sp", self.Bout, S[:].rearrange("p h v -> p (h v)"), d_o, reads=bS, writes=[self.bout])
            self.rot_banks = (0, 1, 2, 3, 4, 5, 6, 7)
            if not lite:
                self.out_proj_residual("wout", Y, bY, DC, first=True)
            P.barrier()

    def moba_qkv(self):
        P = self.P
        H = 8
        with ExitStack() as es:
            self.pos_d = self.dram("pos", [1, T], I32, "ExternalInput")
            self.QTo = self.dram("QT", [128, H * T], BF16, "ExternalOutput")
            self.KTo = self.dram("KT", [128, H * T], BF16, "ExternalOutput")
            self.Vo = self.dram("V", [128, 16 * 1024], BF16, "ExternalOutput")
            self.KMo = self.dram("KM", [128, H * 8], F32, "ExternalOutput")
            posi = self.sb(es, "mq_posi", [32, T], I32)
            ang = self.sb(es, "mq_ang", [32, T], F32)
            tmp = self.sb(es, "mq_tmp", [32, T], F32)
            Ct = self.sb(es, "mq_C", [32, T], F32)
            St = self.sb(es, "mq_S", [32, T], F32)
            npi = self.sb(es, "mq_npi", [32, 1], F32)
            km = self.sb(es, "mq_km", [128, H, 8], F32)
            btab, bkm = Buf(), Buf()
            d_p = P.dsem("pos")
            P.dma("sp", posi[:], self.pos_d.partition_broadcast(32), d_p, writes=[btab])
            P.op("dve", lambda e: e.tensor_copy(out=ang[:], in_=posi[:]), reads=[btab], writes=[btab])
            P.op("dve", lambda e: e.memset(npi[:], -math.pi), writes=[btab])
            P.op("dve", lambda e: e.tensor_scalar(out=ang[:], in0=ang[:], scalar1=self.cst[0:32, C_INVF:C_INVF + 1], scalar2=None, op0=ALU.mult),
                 reads=[btab, self.bvec], writes=[btab])
            ki = self.sb(es, "mq_ki", [32, T], I32)
            kfl = self.sb(es, "mq_kfl", [32, T], F32)
            for (dst_, off_) in ((St, 0.5), (Ct, 0.75)):
                P.op("dve", lambda e: e.tensor_scalar(out=tmp[:], in0=ang[:], scalar1=1.0 / (2 * math.pi), scalar2=off_, op0=ALU.mult, op1=ALU.add),
                     reads=[btab], writes=[btab])
                P.op("dve", lambda e: e.tensor_copy(out=ki[:], in_=tmp[:]), reads=[btab], writes=[btab])
                P.op("dve", lambda e: e.tensor_copy(out=kfl[:], in_=ki[:]), reads=[btab], writes=[btab])
                P.op("dve", lambda e: e.tensor_tensor(out=tmp[:], in0=tmp[:], in1=kfl[:], op=ALU.subtract), reads=[btab], writes=[btab])
                P.op("dve", lambda e: e.tensor_scalar(out=kfl[:], in0=tmp[:], scalar1=0.0, scalar2=None, op0=ALU.is_lt), reads=[btab], writes=[btab])
                P.op("dve", lambda e: e.tensor_tensor(out=tmp[:], in0=tmp[:], in1=kfl[:], op=ALU.add), reads=[btab], writes=[btab])
                P.op("act", lambda e: e.activation(out=dst_[:], in_=tmp[:], func=AF.Sin, bias=npi[:, 0:1], scale=2 * math.pi), reads=[btab], writes=[btab])
            P.op("dve", lambda e: e.tensor_scalar(out=St[:], in0=St[:], scalar1=self.cst[0:32, C_SGN:C_SGN + 1], scalar2=None, op0=ALU.mult),
                 reads=[btab, self.bvec], writes=[btab])
            R2 = lambda nm, shp, dt: [self.sb(es, f"{nm}{k}", shp, dt) for k in range(2)]
            t1, bt1 = R2("mq_t1", [32, 512], F32), [Buf(), Buf()]
            t2, bt2 = R2("mq_t2", [32, 512], F32), [Buf(), Buf()]
            kf, bkf = R2("mq_kf", [128, 512], F32), [Buf(), Buf()]
            ob, bob = R2("mq_ob", [128, 512], BF16), [Buf(), Buf()]
            vb, bvb = R2("mq_vb", [128, 4, 128], BF16), [Buf(), Buf()]
            dob = [P.dsem(f"ob{k}") for k in range(2)]
            dvb = [P.dsem(f"vb{k}") for k in range(2)]
            Vov = self.Vo.rearrange("p (t f) -> p t f", f=1024)
            it = 0
            for h in range(H):
                for qk in range(2):
                    w = self.load_w("win", qk * H + h)
                    wp = self.load_w("win", 3 * H + qk * H + h)
                    dst = self.QTo if qk == 0 else self.KTo
                    for n in range(NT):
                        k = it % 2
                        it += 1
                        c0 = 2 + n * 512
                        cs = slice(n * 512, (n + 1) * 512)
                        pa, bpa = self.bank()
                        self.proj(w[0], w[1], c0, 512, pa, bpa)
                        pb, bpb = self.bank()
                        self.proj(wp[0], wp[1], c0, 512, pb, bpb, m=32)
                        P.op("dve", lambda e: e.tensor_tensor(out=t1[k][:], in0=pa[0:32, :], in1=Ct[:, cs], op=ALU.mult), reads=[bpa, btab], writes=[bt1[k]])
                        P.op("dve", lambda e: e.tensor_tensor(out=t2[k][:], in0=pb[0:32, :], in1=St[:, cs], op=ALU.mult), reads=[bpb, btab], writes=[bt2[k]])
                        P.op("dve", lambda e: e.tensor_tensor(out=kf[k][0:32, :], in0=t1[k][:], in1=t2[k][:], op=ALU.add), reads=[bt1[k], bt2[k]], writes=[bkf[k]])
                        P.op("act", lambda e: e.activation(out=kf[k][32:64, :], in_=pa[32:64, :], func=AF.Copy), reads=[bpa], writes=[bkf[k]])
                        P.op("act", lambda e: e.activation(out=kf[k][64:128, :], in_=pa[64:128, :], func=AF.Copy), reads=[bpa], writes=[bkf[k]])
                        P.op("act", lambda e: e.activation(out=ob[k][:], in_=kf[k][:], func=AF.Copy), reads=[bkf[k]], writes=[bob[k]])
                        if qk == 1:
                            P.op("dve", lambda e: e.tensor_reduce(out=km[:, h, 2 * n:2 * n + 2], in_=kf[k][:].rearrange("p (b t) -> p b t", t=256),
                                                                  axis=AX.X, op=ALU.add), reads=[bkf[k]], writes=[bkm])
                        P.dma("sp", dst[:, h * T + n * 512:h * T + (n + 1) * 512], ob[k][:], dob[k], reads=[bob[k]], writes=[self.bout])
                wv = self.load_w("win", 2 * H + h)
                for n in range(NT):
                    k = it % 2
                    it += 1
                    c0 = 2 + n * 512
                    pv, bpv = self.bank()
                    for s4 in range(4):
                        for c in range(DC):
                            P.op("pe", lambda e: e.matmul(pv[:, s4 * 128:(s4 + 1) * 128], self.XB[:, c, c0 + s4 * 128:c0 + (s4 + 1) * 128],
                                                          wv[0][:, c, :], start=(c == 0), stop=(c == DC - 1)),
                                 reads=[wv[1], self.bXB[n]], writes=[bpv], signal=(c == DC - 1 and s4 == 3))
                    P.op("act", lambda e: e.activation(out=vb[k][:].rearrange("p s v -> p (s v)"), in_=pv[:, :], func=AF.Copy), reads=[bpv], writes=[bvb[k]])
                    P.dma("sp", Vov[:, n * 4:(n + 1) * 4, h * 128:(h + 1) * 128], vb[k][:], dvb[k], reads=[bvb[k]], writes=[self.bout])
            P.op("dve", lambda e: e.tensor_scalar(out=km[:], in0=km[:], scalar1=1.0 / 256.0, scalar2=None, op0=ALU.mult), reads=[bkm], writes=[bkm])
            d_k = P.dsem("kmo")
            P.dma("sp", self.KMo, km[:].rearrange("p h b -> p (h b)"), d_k, reads=[bkm], writes=[self.bout])
            P.barrier()

    def moba_attn(self):
        P = self.P
        H = 8
        NS = 72
        SCALE = 1.0 / math.sqrt(128.0)
        with ExitStack() as es:
            self.QTi = self.dram("QT", [128, H * T], BF16, "ExternalInput")
            self.Kall = self.dram("Kall", [H * 128, NS * 256], BF16, "ExternalInput")
            self.Vall = self.dram("Vall", [H * 128, NS * 256], BF16, "ExternalInput")
            self.KMall = self.dram("KMall", [128, H * NS], F32, "ExternalInput")
            self.mcst_d = self.dram("mcst", [128, 3 * 8 * NS + 4 * 512], F32, "ExternalInput")
            QT = self.sb(es, "ma_QT", [128, H, T], BF16)
            bQT = Buf()
            d_q = P.dsem("qt")
            P.dma("sp", QT[:].rearrange("p h t -> p (h t)"), self.QTi, d_q, writes=[bQT])
            mc = self.sb(es, "ma_mc", [128, 3, 8, NS], F32)
            caus32 = self.sb(es, "ma_c32", [128, 512], F32)
            caus = self.sb(es, "ma_caus", [128, 4, 512], BF16)
            bmc = Buf()
            d_m = P.dsem("mc")
            d_m2 = P.dsem("mc2")
            P.dma("sp", mc[:].rearrange("p a l s -> p (a l s)"), self.mcst_d[:, 0:3 * 8 * NS], d_m, writes=[bmc])
            for v in range(4):
                P.dma("sp", caus32[:], self.mcst_d[:, 3 * 8 * NS + v * 512:3 * 8 * NS + (v + 1) * 512], d_m2, writes=[bmc])
                P.op("act", lambda e: e.activation(out=caus[:, v, :], in_=caus32[:], func=AF.Copy), reads=[bmc], writes=[bmc])
            kmf = self.sb(es, "ma_kmf", [128, H, NS], F32)
            kmb = self.sb(es, "ma_kmb", [128, H, NS], BF16)
            d_km = P.dsem("km")
            bkm = Buf()
            P.dma("sp", kmf[:].rearrange("p h s -> p (h s)"), self.KMall, d_km, writes=[bkm])
            P.op("act", lambda e: e.activation(out=kmb[:], in_=kmf[:], func=AF.Copy), reads=[bkm], writes=[bkm])
            ident = self.sb(es, "ma_ident", [128, 128], BF16)
            ones = self.sb(es, "ma_ones", [128, 128], BF16)
            Esel = self.sb(es, "ma_Esel", [NS, NS, 128], BF16)
            bcn = Buf()
            P.op("act", lambda e: e.activation(out=ident[:], in_=self.cst[:, C_IDENT:C_IDENT + 128], func=AF.Copy), reads=[self.bvec], writes=[bcn])
            P.op("dve", lambda e: e.memset(ones[:], 1.0), writes=[bcn])
            P.op("dve", lambda e: e.tensor_copy(out=Esel[:], in_=self.cst[0:NS, C_IDENT:C_IDENT + NS].unsqueeze(2).to_broadcast([NS, NS, 128])),
                 reads=[self.bvec], writes=[bcn])
            maskT = [self.sb(es, f"ma_maskT{k}", [NS, T], BF16) for k in range(2)]
            bmaskT = [Buf(), Buf()]
            R2 = lambda nm, shp, dt: [self.sb(es, f"{nm}{k}", shp, dt) for k in range(2)]
            gm, bgm = R2("ma_gm", [128, NS], F32), [Buf(), Buf()]
            t8, bt8 = R2("ma_t8", [128, 8], F32), [Buf(), Buf()]
            al, bal = R2("ma_al", [128, NS], F32), [Buf(), Buf()]
            alb, balb = R2("ma_alb", [128, NS], BF16), [Buf(), Buf()]
            NKS = 3
            KS = [self.sb(es, f"ma_KS{k}", [128, 1024], BF16) for k in range(NKS)]
            VS = [self.sb(es, f"ma_VS{k}", [128, 8, 128], BF16) for k in range(NKS)]
            bKS = [Buf() for _ in range(NKS)]
            bVS = [Buf() for _ in range(NKS)]
            dKS = [P.dsem(f"ks{k}") for k in range(NKS)]
            dVS = [P.dsem(f"vs{k}") for k in range(NKS)]
            NPT = 4
            PT = [self.sb(es, f"ma_PT{k}", [128, 512], BF16) for k in range(NPT)]
            bPT = [Buf() for _ in range(NPT)]
            rden, brden = R2("ma_rden", [128, 512], F32), [Buf(), Buf()]
            Y = self.XB
            bYh = [Buf() for _ in range(H)]
            self.rot_banks = (4, 5, 6, 7)
            gi = 0
            ksi = 0
            pti = 0
            for h in range(H):
                mk = h % 2
                for s16 in range(16):
                    g_ = gi % 2
                    gi += 1
                    lb_ = s16 // 2
                    qs_ = slice(s16 * 128, (s16 + 1) * 128)
                    pg, bpg = self.bank()
                    P.op("pe", lambda e: e.matmul(pg[:, 0:NS], QT[:, h, qs_], kmb[:, h, :], start=True, stop=True), reads=[bQT, bkm], writes=[bpg])
                    P.op("dve", lambda e: e.tensor_tensor(out=gm[g_][:], in0=pg[:, 0:NS], in1=mc[:, 0, lb_, :], op=ALU.add), reads=[bpg, bmc], writes=[bgm[g_]])
                    P.op("dve", lambda e: e.max(out=t8[g_][:], in_=gm[g_][:]), reads=[bgm[g_]], writes=[bt8[g_]])
                    P.op("dve", lambda e: e.tensor_scalar(out=al[g_][:], in0=gm[g_][:], scalar1=t8[g_][:, 2:3], scalar2=None, op0=ALU.is_ge),
                         reads=[bgm[g_], bt8[g_]], writes=[bal[g_]])
                    P.op("dve", lambda e: e.tensor_tensor(out=al[g_][:], in0=al[g_][:], in1=mc[:, 1, lb_, :], op=ALU.mult), reads=[bal[g_], bmc], writes=[bal[g_]])
                    P.op("dve", lambda e: e.tensor_tensor(out=al[g_][:], in0=al[g_][:], in1=mc[:, 2, lb_, :], op=ALU.add), reads=[bal[g_], bmc], writes=[bal[g_]])
                    P.op("dve", lambda e: e.tensor_scalar(out=alb[g_][:], in0=al[g_][:], scalar1=-1.0, scalar2=30000.0, op0=ALU.add, op1=ALU.mult),
                         reads=[bal[g_]], writes=[balb[g_]])
                    pt_, bpt_ = self.bank()
                    ptb = pt_[:].bitcast(BF16)
                    P.op("pe", lambda e: e.transpose(ptb[0:NS, 0:128], alb[g_][:], ident[:]), reads=[balb[g_], bcn], writes=[bpt_])
                    P.op("act", lambda e: e.activation(out=maskT[mk][:, qs_], in_=ptb[0:NS, 0:128], func=AF.Copy), reads=[bpt_], writes=[bmaskT[mk]])
                for qt in range(2):
                    pend = []
                    LA = 2
                    O = [self.PS[0], self.PS[1]]
                    bO = [self.bPS[0], self.bPS[1]]
                    DN = [self.PS[2], self.PS[3]]
                    bDN = [self.bPS[2], self.bPS[3]]
                    def need(j, hf):
                        if j < 8:
                            return j in (qt * 4 + hf * 2, qt * 4 + hf * 2 + 1)
                        return (j - 8) < 32 * qt + 16 * hf + 16
                    ulist = {hf: [(j, kt2) for j in range(NS) for kt2 in range(2) if need(j, hf)] for hf in range(2)}
                    for g4 in range(NS // 4):
                        if not any(need(g4 * 4 + j4, hf) for j4 in range(4) for hf in range(2)):
                            continue
                        ks = ksi % NKS
                        ksi += 1
                        P.dma("sp", KS[ks][:], self.Kall[h * 128:(h + 1) * 128, g4 * 1024:(g4 + 1) * 1024], dKS[ks], writes=[bKS[ks]])
                        P.dma("sp", VS[ks][:].rearrange("p t v -> p (t v)"), self.Vall[h * 128:(h + 1) * 128, g4 * 1024:(g4 + 1) * 1024], dVS[ks], writes=[bVS[ks]])
                        for j4 in range(4):
                            j = g4 * 4 + j4
                            for kt2 in range(2):
                                kti = j4 * 2 + kt2
                                for hf in range(2):
                                    if not need(j, hf):
                                        continue
                                    first = ((j, kt2) == ulist[hf][0])
                                    last = ((j, kt2) == ulist[hf][-1])
                                    q0 = qt * 1024 + hf * 512
                                    ps, bps = self.bank()
                                    lbs = (qt * 4 + hf * 2, qt * 4 + hf * 2 + 1)
                                    diag = j in lbs
                                    P.op("pe", lambda e: e.matmul(ps[:, :], KS[ks][:, kti * 128:(kti + 1) * 128], QT[:, h, q0:q0 + 512], start=True, stop=False),
                                         reads=[bKS[ks], bQT], writes=[bps], signal=False)
                                    P.op("pe", lambda e: e.matmul(ps[:, :], Esel[:, j, :], maskT[mk][:, q0:q0 + 512], start=False, stop=(not diag)),
                                         reads=[bcn, bmaskT[mk]], writes=[bps], signal=(not diag))
                                    if diag:
                                        v = kt2 * 2 + (j - lbs[0])
                                        P.op("pe", lambda e: e.matmul(ps[:, :], ident[:], caus[:, v, :], start=False, stop=True),
                                             reads=[bcn, bmc], writes=[bps])
                                    p_ = pti % NPT
                                    pti += 1
                                    P.op("act", lambda e: e.activation(out=PT[p_][:], in_=ps[:, :], func=AF.Exp, scale=SCALE), reads=[bps], writes=[bPT[p_]])
                                    def _pv(ks=ks, kti=kti, p_=p_, hf=hf, first=first, last=last):
                                        P.op("pe", lambda e: e.matmul(O[hf][:, :], VS[ks][:, kti, :], PT[p_][:], start=first, stop=last),
                                             reads=[bVS[ks], bPT[p_]], writes=[bO[hf]], signal=last)
                                        P.op("pe", lambda e: e.matmul(DN[hf][:, :], ones[:], PT[p_][:], start=first, stop=last),
                                             reads=[bcn, bPT[p_]], writes=[bDN[hf]], signal=True)
                                    pend.append(_pv)
                                    if len(pend) > LA:
                                        pend.pop(0)()
                    while pend:
                        pend.pop(0)()
                    for hf in range(2):
                        q0 = qt * 1024 + hf * 512
                        r_ = hf
                        P.op("act", lambda e: e.activation(out=rden[r_][:], in_=DN[hf][:, :], func=AF.Ln), reads=[bDN[hf]], writes=[brden[r_]])
                        P.op("act", lambda e: e.activation(out=rden[r_][:], in_=rden[r_][:], func=AF.Exp, scale=-1.0), reads=[brden[r_]], writes=[brden[r_]])
                        P.op("dve", lambda e: e.tensor_tensor(out=Y[:, h, 2 + q0:2 + q0 + 512], in0=O[hf][:, :], in1=rden[r_][:], op=ALU.mult),
                             reads=[bO[hf], brden[r_]], writes=[bYh[h]])
            self.rot_banks = (0, 1, 2, 3, 4, 5, 6, 7)
            self.out_proj_residual("wout", Y, bYh, DC, first=True, coff=2)
            P.barrier()


_PROGS = {}


def prog(kind):
    if kind not in _PROGS:
        _PROGS[kind] = Builder(kind).build()
    return _PROGS[kind]


def shard_xT(x_full):
    xT = np.ascontiguousarray(x_full.T)
    xTp = np.concatenate([np.zeros((D, 2), np.float32), xT], axis=1)
    return [np.ascontiguousarray(xTp[:, c * T:c * T + T + 2]) for c in range(NCORE)]


def launch(kind, maps):
    import os
    if os.environ.get("TRACE_KIND") == kind:
        res = run_bass_kernel_spmd(prog(kind), maps, core_ids=list(range(NCORE)), trace=True)
        print("[trace]", kind, "exec_time_ns", res.exec_time_ns)
        try:
            insts = res.instructions_and_trace[0]
            from collections import defaultdict
            busy = defaultdict(float); cnt = defaultdict(int); wt = defaultdict(float)
            byname = defaultdict(float); bn = defaultdict(int)
            srcl = defaultdict(float)
            for it_ in insts:
                e = str(it_.engine)
                busy[e] += it_.duration; cnt[e] += 1
                try:
                    wt[e] += (it_.evt_wait_time or 0)
                except Exception:
                    pass
                key = (e, str(it_.op_name))
                byname[key] += it_.duration; bn[key] += 1
                srcl[(e, it_.source_line)] += it_.duration
            for e in busy:
                print(f"[trace] {e:24s} busy={busy[e]/1e3:9.1f}us n={cnt[e]:6d} waits={wt[e]/1e3:9.1f}us")
            for key, v in sorted(byname.items(), key=lambda kv: -kv[1])[:25]:
                print(f"[trace]   {key[0]:20s} {key[1]:28s} {v/1e3:9.1f}us n={bn[key]:6d} avg={v/bn[key]:7.1f}ns")
            t0 = min(i_.timestamp for i_ in insts); t1 = max(i_.end_timestamp for i_ in insts)
            lo = int(os.environ.get("TRACE_LO", "0")); hi = int(os.environ.get("TRACE_HI", "0"))
            sel = [i_ for i_ in insts if lo <= (i_.source_line or 0) <= hi]
            if sel:
                print(f"[trace] total span {(t1-t0)/1e3:.1f}us; lines {lo}-{hi}: from {(min(i_.timestamp for i_ in sel)-t0)/1e3:.1f}us to {(max(i_.end_timestamp for i_ in sel)-t0)/1e3:.1f}us")
            for key, v in sorted(srcl.items(), key=lambda kv: -kv[1])[:25]:
                print(f"[trace]   line {key[1]} {key[0]:12s} {v/1e3:9.1f}us")
        except Exception as ex:
            print("[trace] summary failed", repr(ex))
        return res.results
    res = run_bass_kernel_spmd(prog(kind), maps, core_ids=list(range(NCORE)))
    return res.results


def gather_x(results):
    return np.ascontiguousarray(np.concatenate([r["outT"] for r in results], axis=1).T)


def run_ffn(inp, i, x):
    xs = shard_xT(x)
    vec = make_vec(inp[f"l{i}_ln2_g"], inp[f"l{i}_ln2_b"], conv=inp[f"l{i}_ffn_conv"])
    wup = tile_w(inp[f"l{i}_ffn_w_up"])
    wdn = tile_w(inp[f"l{i}_ffn_w_down"])
    maps = [dict(xT=xs[c], vec=vec, cst=make_cst(c), wup=wup, wdn=wdn) for c in range(NCORE)]
    return gather_x(launch("ff", maps))


def run_conv(inp, i, x):
    xs = shard_xT(x)
    vec = make_vec(inp[f"l{i}_ln1_g"], inp[f"l{i}_ln1_b"], conv=inp[f"l{i}_mix_conv"])
    win = tile_w(inp[f"l{i}_mix_w_in"])
    wout = tile_w(inp[f"l{i}_mix_w_out"])
    maps = [dict(xT=xs[c], vec=vec, cst=make_cst(c), win=win, wout=wout) for c in range(NCORE)]
    return gather_x(launch("cv", maps))


def run_hgrn(inp, i, x):
    xs = shard_xT(x)
    vec = make_vec(inp[f"l{i}_ln1_g"], inp[f"l{i}_ln1_b"], norm_g=inp[f"l{i}_mix_norm_g"], lbraw=inp["hgrn_lower_bounds"])
    win = tile_w(inp[f"l{i}_mix_w_in"])
    wout = tile_w(inp[f"l{i}_mix_w_out"])
    Aall = np.zeros((128, NCORE * 8), np.float32)
    Ball = np.zeros((NCORE * 128, 1024), np.float32)
    out = None
    for phase in range(2):
        maps = [dict(xT=xs[c], vec=vec, cst=make_cst(c, layer=i), win=win, Aall=Aall, Ball=Ball) for c in range(NCORE)]
        if phase == 1:
            for m in maps:
                m["wout"] = wout
        res = launch("hg1" if phase == 0 else "hg", maps)
        if phase == 0:
            Aall = np.ascontiguousarray(np.concatenate([r["Aout"] for r in res], axis=1))
            Ball = np.ascontiguousarray(np.concatenate([r["Bout"] for r in res], axis=0))
        else:
            out = gather_x(res)
    return out


def zz_block(c, s_):
    return 8 * s_ + (c if s_ % 2 == 0 else 7 - c)


def run_moba(inp, i, x):
    H = 8
    NSL = 72
    xs = shard_xT(x)
    vec = make_vec(inp[f"l{i}_ln1_g"], inp[f"l{i}_ln1_b"])
    W = inp[f"l{i}_mix_w_in"]
    perm = []
    for qk in range(2):
        for h in range(H):
            base = qk * 1024 + h * 128
            Wp = np.zeros((D, 128), np.float32)
            Wp[:, 0:16] = W[:, base + 16:base + 32]
            Wp[:, 16:32] = W[:, base:base + 16]
            perm.append(Wp)
    win = tile_w(np.concatenate([W] + perm, axis=1))
    pos = inp["positions"].astype(np.int32)
    maps = [dict(xT=xs[c], vec=vec, cst=make_cst(c), win=win, pos=np.ascontiguousarray(pos[:, c * T:(c + 1) * T])) for c in range(NCORE)]
    res = launch("mo1", maps)
    Qg = np.concatenate([np.asarray(r["QT"]).reshape(128, H, 8, 256) for r in res], axis=2)
    Kg = np.concatenate([np.asarray(r["KT"]).reshape(128, H, 8, 256).transpose(1, 0, 2, 3) for r in res], axis=2)
    Vg = np.concatenate([np.asarray(r["V"]).reshape(128, 8, 2, H, 128).transpose(3, 0, 1, 2, 4) for r in res], axis=2)
    KMg = np.concatenate([np.asarray(r["KM"]).reshape(128, H, 8) for r in res], axis=2)
    xTfull = np.ascontiguousarray(x.T).reshape(D, 64, 256)
    wout = tile_w(inp[f"l{i}_mix_w_out"])
    caus = np.zeros((4, 128, 512), np.float32)
    p_ = np.arange(128)[:, None]
    qi = np.arange(256)[None, :]
    for kt2 in range(2):
        for posb in range(2):
            caus[kt2 * 2 + posb][:, posb * 256:(posb + 1) * 256] = np.where(kt2 * 128 + p_ > qi, -30000.0, 0.0)
    maps = []
    own_all = []
    for c in range(NCORE):
        own = [zz_block(c, s_) for s_ in range(8)]
        own_all.append(own)
        pc = np.array(own + list(range(64)))
        Kall = np.ascontiguousarray(Kg[:, :, pc, :]).reshape(H * 128, NSL * 256)
        Vall = np.ascontiguousarray(Vg[:, :, pc]).reshape(H * 128, NSL * 256)
        KMall = np.ascontiguousarray(KMg[:, :, pc]).reshape(128, H * NSL)
        QT = np.ascontiguousarray(Qg[:, :, np.array(own), :]).reshape(128, H * T)
        xT = np.concatenate([np.zeros((D, 2), np.float32), xTfull[:, np.array(own), :].reshape(D, T)], axis=1)
        past = np.zeros((8, NSL), np.float32)
        ownm = np.zeros((8, NSL), np.float32)
        for s_ in range(8):
            past[s_, 8:] = (np.arange(64) < own[s_]).astype(np.float32)
            ownm[s_, s_] = 1.0
        pastbias = np.where(past > 0, 0.0, -1e30).astype(np.float32)
        row = np.concatenate([pastbias.reshape(-1), past.reshape(-1), ownm.reshape(-1)])
        mcst = np.concatenate([np.broadcast_to(row[None, :], (128, 3 * 8 * NSL)), caus.transpose(1, 0, 2).reshape(128, 2048)], axis=1).astype(np.float32)
        maps.append(dict(xT=np.ascontiguousarray(xT), vec=vec, cst=make_cst(c), QT=QT, Kall=Kall, Vall=Vall, KMall=KMall,
                         mcst=np.ascontiguousarray(mcst), wout=wout))
    res2 = launch("mo2", maps)
    out = np.zeros((64, 256, D), np.float32)
    for c in range(NCORE):
        oc = np.asarray(res2[c]["outT"]).T.reshape(8, 256, D)
        for s_ in range(8):
            out[own_all[c][s_]] = oc[s_]
    return np.ascontiguousarray(out.reshape(64 * 256, D))


def kernel(**inputs):
    inp = {k: np.asarray(v) for k, v in inputs.items()}
    x = np.ascontiguousarray(inp["x"][0])
    for i in range(DEPTH):
        kind = i % 3
        if kind == 0:
            x = run_hgrn(inp, i, x)
        elif kind == 1:
            x = run_moba(inp, i, x)
        else:
            x = run_conv(inp, i, x)
        x = run_ffn(inp, i, x)
    return x[None].astype(np.float32)
```

```python
import math
import numpy as np
from contextlib import ExitStack
import concourse.bass as bass
import concourse.mybir as mybir
from concourse.bass_utils import run_bass_kernel_spmd

F32 = mybir.dt.float32
BF16 = mybir.dt.bfloat16
I32 = mybir.dt.int32
AF = mybir.ActivationFunctionType
ALU = mybir.AluOpType
AX = mybir.AxisListType

NCORE = 8
T = 2048
D = 1024
DC = 8
DFF = 2816
FC = 22
DEPTH = 4
ALPHA = (2.0 * DEPTH) ** 0.25
LN_EPS = 1e-5
RMS_EPS = 1e-6
NT = T // 512


class Buf:
    __slots__ = ("name", "w", "r")

    def __init__(self, name="b"):
        self.name = name
        self.w = None
        self.r = {}


class DSem:
    def __init__(self, key, h):
        self.key = key
        self.h = h
        self.cnt = 0


class Prog:
    ENG = ("pe", "act", "dve", "pool", "sp")

    def __init__(self, nc, es):
        self.nc = nc
        self.es = es
        self.eng = dict(pe=nc.tensor, act=nc.scalar, dve=nc.vector, pool=nc.gpsimd, sp=nc.sync)
        self.semh = {}
        self.cnt = {}
        self.known = {k: {} for k in self.ENG}
        for k in self.ENG:
            self.semh[k] = es.enter_context(nc.semaphore("s_" + k))
            self.cnt[k] = 0
        self.dsems = []
        self.nwait = 0
        self.nins = 0

    def dsem(self, name):
        h = self.es.enter_context(self.nc.semaphore("d_" + name))
        d = DSem("d_" + name, h)
        self.semh[d.key] = h
        self.dsems.append(d)
        return d

    def _wait(self, e, key, val):
        if val <= 0:
            return
        if self.known[e].get(key, 0) >= val:
            return
        if key == e:
            if e == "pe":
                return
            if val > self.cnt[e]:
                return
        self.eng[e].wait_ge(self.semh[key], val)
        self.nwait += 1
        self.known[e][key] = val

    def deps(self, e, reads, writes):
        need = {}
        for b in reads:
            if b.w is not None:
                k, v = b.w
                if need.get(k, 0) < v:
                    need[k] = v
        for b in writes:
            if b.w is not None:
                k, v = b.w
                if need.get(k, 0) < v:
                    need[k] = v
            for k, v in b.r.items():
                if need.get(k, 0) < v:
                    need[k] = v
        for k, v in need.items():
            self._wait(e, k, v)

    def op(self, e, fn, reads=(), writes=(), signal=True):
        self.deps(e, reads, writes)
        ins = fn(self.eng[e])
        self.nins += 1
        if signal:
            self.cnt[e] += 1
            ins.then_inc(self.semh[e], 1)
            mark = (e, self.cnt[e])
        else:
            mark = (e, self.cnt[e] + 1)
        for b in writes:
            b.w = mark
            b.r = {}
        for b in reads:
            if b.r.get(e, 0) < mark[1]:
                b.r[e] = mark[1]
        return ins

    def _mark_async(self, ds, reads, writes):
        mark = (ds.key, ds.cnt)
        for b in writes:
            b.w = mark
            b.r = {}
        for b in reads:
            b.r[ds.key] = ds.cnt

    def dma(self, q, out, in_, ds, reads=(), writes=(), **kw):
        self.deps(q, reads, writes)
        ds.cnt += 16
        ins = self.eng[q].dma_start(out=out, in_=in_, **kw)
        ins.then_inc(ds.h, 16)
        self._mark_async(ds, reads, writes)
        return ins

    def allgather(self, in_ap, out_ap, ds, reads=(), writes=()):
        q = "pool"
        self.deps(q, reads, writes)
        ds.cnt += 1
        ins = self.nc.gpsimd.collective_compute(
            "AllGather", ALU.bypass, replica_groups=[list(range(NCORE))],
            ins=[in_ap.opt()], outs=[out_ap.opt()])
        ins.then_inc(ds.h, 1)
        self._mark_async(ds, reads, writes)
        return ins

    def barrier(self, engines=None):
        engines = engines or self.ENG
        for e in engines:
            for k in self.ENG:
                if k != e:
                    self._wait(e, k, self.cnt[k])
            for d in self.dsems:
                self._wait(e, d.key, d.cnt)


def tile_w(W):
    Din, Fo = W.shape
    Wt = W.reshape(Din // 128, 128, Fo // 128, 128).transpose(2, 1, 0, 3)
    return np.ascontiguousarray(Wt).reshape(Fo // 128 * 128, Din)


def vec_cols(v):
    return np.ascontiguousarray(v.reshape(-1, 128).T)


V_LNG = 0
V_LNB = 8
V_CONV = 16
V_NORMG = 148
V_LBRAW = 156
NVEC = 188
C_HMASK = 0
C_LMASK = 8
C_MASK2 = 12
C_IDENT = 140
C_INVF = 268
C_SGN = 269
NCST = 270


def conv_cols(cw):
    return np.concatenate([vec_cols(cw[k]) for k in range(cw.shape[0])], axis=1)


def make_vec(ln_g, ln_b, conv=None, norm_g=None, lbraw=None):
    v = np.zeros((128, NVEC), np.float32)
    v[:, V_LNG:V_LNG + 8] = vec_cols(ln_g)
    v[:, V_LNB:V_LNB + 8] = vec_cols(ln_b)
    if conv is not None:
        c = conv_cols(conv)
        v[:, V_CONV:V_CONV + c.shape[1]] = c
    if norm_g is not None:
        v[:, V_NORMG:V_NORMG + 8] = vec_cols(norm_g)
    if lbraw is not None:
        v[:, V_LBRAW:V_LBRAW + 32] = np.concatenate([vec_cols(lbraw[l]) for l in range(DEPTH)], axis=1)
    return v


def make_cst(core, layer=0):
    c = np.zeros((128, NCST), np.float32)
    for r in range(NCORE):
        c[:, C_HMASK + r] = 1.0 if r < core else 0.0
    for l in range(DEPTH):
        c[:, C_LMASK + l] = 1.0 if 1 <= l <= layer else 0.0
    s_ = np.arange(128)[:, None]
    t_ = np.arange(128)[None, :]
    c[:, C_MASK2:C_MASK2 + 128] = ((s_ // 64 == t_ // 64) & (s_ <= t_)).astype(np.float32)
    c[:, C_IDENT:C_IDENT + 128] = np.eye(128, dtype=np.float32)
    invf = (1.0 / (500000.0 ** (np.arange(0, 32, 2, dtype=np.float32) / np.float32(32.0)))).astype(np.float32)
    c[0:32, C_INVF] = np.concatenate([invf, invf])
    c[0:16, C_SGN] = -1.0
    c[16:32, C_SGN] = 1.0
    return c


class Builder:
    def __init__(self, kind):
        self.kind = kind
        self.nc = bass.Bass("TRN2", target_bir_lowering=False)

    def dram(self, name, shape, dt, kind="Internal"):
        return self.nc.dram_tensor(name, list(shape), dt, kind=kind).ap()

    def sb(self, es, name, shape, dt):
        self._uid = getattr(self, "_uid", 0) + 1
        return es.enter_context(self.nc.sbuf_tensor(f"{name}_{self._uid}", list(shape), dt))

    def build(self):
        nc = self.nc
        kind = self.kind
        with ExitStack() as es:
            self.es = es
            P = self.P = Prog(nc, es)
            self.xT = self.dram("xT", [D, T + 2], F32, "ExternalInput")
            self.vec_d = self.dram("vec", [128, NVEC], F32, "ExternalInput")
            self.cst_d = self.dram("cst", [128, NCST], F32, "ExternalInput")
            self.wd = {}
            if kind == "ff":
                self.wd["wup"] = self.dram("wup", [2 * DFF, D], F32, "ExternalInput")
                self.wd["wdn"] = self.dram("wdn", [D, DFF], F32, "ExternalInput")
            elif kind == "cv":
                self.wd["win"] = self.dram("win", [3072, D], F32, "ExternalInput")
                self.wd["wout"] = self.dram("wout", [D, D], F32, "ExternalInput")
            elif kind == "hg":
                self.wd["win"] = self.dram("win", [4096, D], F32, "ExternalInput")
                self.wd["wout"] = self.dram("wout", [D, D], F32, "ExternalInput")
            elif kind == "hg1":
                self.wd["win"] = self.dram("win", [4096, D], F32, "ExternalInput")
            elif kind == "mo1":
                self.wd["win"] = self.dram("win", [5120, D], F32, "ExternalInput")
            elif kind == "mo2":
                self.wd["wout"] = self.dram("wout", [D, D], F32, "ExternalInput")
            if kind not in ("mo1", "hg1"):
                self.outT = self.dram("outT", [D, T], F32, "ExternalOutput")
            self.XR = self.sb(es, "XR", [128, DC, T], F32)
            self.XB = self.sb(es, "XB", [128, DC, T + 2], BF16)
            self.bXR = [Buf(f"XR{n}") for n in range(NT)]
            self.bXB = [Buf(f"XB{n}") for n in range(NT)]
            self.bXBh = Buf("XBh")
            self.vec = self.sb(es, "vecs", [128, NVEC], F32)
            self.cst = self.sb(es, "csts", [128, NCST], F32)
            self.bvec = Buf("vec")
            self.ones_ln = self.sb(es, "ones_ln", [128, 128], BF16)
            self.bconst = Buf("const")
            NW = 2 if kind == "mo2" else 8
            self.WS = [self.sb(es, f"ws{k}", [128, DC, 128], BF16) for k in range(NW)]
            self.bWS = [Buf(f"ws{k}") for k in range(NW)]
            self.dWS = [P.dsem(f"ws{k}") for k in range(NW)]
            self.wsi = 0
            self.PS = [es.enter_context(nc.psum_tensor(f"ps{k}", [128, 512], F32)) for k in range(8)]
            self.bPS = [Buf(f"ps{k}") for k in range(8)]
            self.psi = 0
            self.d_out = P.dsem("out")
            self.d_misc = P.dsem("misc")
            self.d_misc2 = P.dsem("misc2")
            self.bout = Buf("out")

            P.dma("sp", self.vec[:], self.vec_d, self.d_misc, writes=[self.bvec])
            P.dma("sp", self.cst[:], self.cst_d, self.d_misc2, writes=[self.bvec])
            xTv = self.xT.rearrange("(c p) t -> p c t", p=128)
            d_ins = [P.dsem(f"in{n}") for n in range(NT + 1)]
            for n in range(NT):
                P.dma("sp", self.XR[:, :, n * 512:(n + 1) * 512], xTv[:, :, 2 + n * 512:2 + (n + 1) * 512],
                      d_ins[n], writes=[self.bXR[n]])
            self.hal32 = self.sb(es, "hal32", [128, DC, 2], F32)
            self.bhal32 = Buf("hal32")
            P.dma("sp", self.hal32[:], xTv[:, :, 0:2], d_ins[NT], writes=[self.bhal32])
            P.op("dve", lambda e: e.memset(self.ones_ln[:], 1.0 / D), writes=[self.bconst])
            if kind != "mo2":
                for n in range(NT):
                    P.op("act", lambda e: e.activation(out=self.XB[:, :, 2 + n * 512:2 + (n + 1) * 512],
                                                       in_=self.XR[:, :, n * 512:(n + 1) * 512], func=AF.Copy),
                         reads=[self.bXR[n]], writes=[self.bXB[n]])
                P.op("act", lambda e: e.activation(out=self.XB[:, :, 0:2], in_=self.hal32[:], func=AF.Copy),
                     reads=[self.bhal32], writes=[self.bXBh])

            if kind == "ff":
                self.ffn()
            elif kind == "cv":
                self.conv_mixer()
            elif kind in ("hg", "hg1"):
                self.hgrn_mixer()
            elif kind == "mo1":
                self.moba_qkv()
            elif kind == "mo2":
                self.moba_attn()
            if kind not in ("mo1", "hg1"):
                self.layernorm(V_LNG, V_LNB)
                oTv = self.outT.rearrange("(c p) t -> p c t", p=128)
                for n in range(NT):
                    P.dma("sp", oTv[:, :, n * 512:(n + 1) * 512], self.XR[:, :, n * 512:(n + 1) * 512],
                          self.d_out, reads=[self.bXR[n]], writes=[self.bout])
            P.barrier()
        return nc

    def bank(self):
        rb = getattr(self, "rot_banks", (0, 1, 2, 3, 4, 5, 6, 7))
        self.psi = (self.psi + 1) % len(rb)
        k = rb[self.psi]
        return self.PS[k], self.bPS[k]

    def load_w(self, name, j, c0=0, nch=DC):
        P = self.P
        k = self.wsi
        self.wsi = (self.wsi + 1) % len(self.WS)
        slot, b, ds = self.WS[k], self.bWS[k], self.dWS[k]
        src = self.wd[name][j * 128:(j + 1) * 128, c0 * 128:(c0 + nch) * 128]
        dst = slot[:, 0:nch, :].rearrange("p c i -> p (c i)")
        P.dma("pool", dst, src, ds, writes=[b])
        return slot, b

    def xb_bufs(self, col0, n):
        bs = []
        if col0 < 2:
            bs.append(self.bXBh)
        lo = max(col0 - 2, 0)
        hi = col0 + n - 2
        for t in range(NT):
            if lo < (t + 1) * 512 and hi > t * 512:
                bs.append(self.bXB[t])
        return bs

    def proj(self, w, bw, col0, n, ps, bps, m0=0, m=128, src=None, srcbufs=None, nch=DC):
        P = self.P
        src = self.XB if src is None else src
        srcbufs = self.xb_bufs(col0, n) if srcbufs is None else srcbufs
        for c in range(nch):
            P.op("pe", lambda e: e.matmul(ps[0:m, 0:n], w[:, c, m0:m0 + m], src[:, c, col0:col0 + n],
                                          start=(c == 0), stop=(c == nch - 1)),
                 reads=[bw] + srcbufs, writes=[bps], signal=(c == nch - 1))

    def layernorm(self, goff, boff):
        P = self.P
        with ExitStack() as es:
            zb = [self.sb(es, f"ln_zb{k}", [128, DC, 512], BF16) for k in range(2)]
            zq = [self.sb(es, f"ln_zq{k}", [128, DC, 512], BF16) for k in range(2)]
            bzb = [Buf() for _ in range(2)]
            bzq = [Buf() for _ in range(2)]
            mean = [self.sb(es, f"ln_mean{k}", [128, 512], F32) for k in range(2)]
            rstd = [self.sb(es, f"ln_rstd{k}", [128, 512], F32) for k in range(2)]
            tmp = [self.sb(es, f"ln_tmp{k}", [128, 512], F32) for k in range(2)]
            bmean = [Buf() for _ in range(2)]
            brstd = [Buf() for _ in range(2)]
            btmp = [Buf() for _ in range(2)]
            ct = [self.sb(es, f"ln_ct{k}", [128, 512], F32) for k in range(4)]
            bct = [Buf() for _ in range(4)]
            cti = 0
            for n in range(NT):
                k = n % 2
                cs = slice(n * 512, (n + 1) * 512)
                P.op("act", lambda e: e.activation(out=zb[k][:], in_=self.XR[:, :, cs], func=AF.Copy),
                     reads=[self.bXR[n]], writes=[bzb[k]])
                P.op("act", lambda e: e.activation(out=zq[k][:], in_=self.XR[:, :, cs], func=AF.Square),
                     reads=[self.bXR[n]], writes=[bzq[k]])
                pm, bpm = self.bank()
                for c in range(DC):
                    P.op("pe", lambda e: e.matmul(pm[:, :], self.ones_ln[:], zb[k][:, c, :], start=(c == 0), stop=(c == DC - 1)),
                         reads=[self.bconst, bzb[k]], writes=[bpm], signal=(c == DC - 1))
                pq, bpq = self.bank()
                for c in range(DC):
                    P.op("pe", lambda e: e.matmul(pq[:, :], self.ones_ln[:], zq[k][:, c, :], start=(c == 0), stop=(c == DC - 1)),
                         reads=[self.bconst, bzq[k]], writes=[bpq], signal=(c == DC - 1))
                P.op("dve", lambda e: e.tensor_copy(out=mean[k][:], in_=pm[:, :]), reads=[bpm], writes=[bmean[k]])
                P.op("dve", lambda e: e.tensor_tensor(out=tmp[k][:], in0=mean[k][:], in1=mean[k][:], op=ALU.mult),
                     reads=[bmean[k]], writes=[btmp[k]])
                P.op("dve", lambda e: e.tensor_tensor(out=rstd[k][:], in0=pq[:, :], in1=tmp[k][:], op=ALU.subtract),
                     reads=[bpq, btmp[k]], writes=[brstd[k]])
                P.op("dve", lambda e: e.tensor_scalar(out=rstd[k][:], in0=rstd[k][:], scalar1=LN_EPS, scalar2=None,
                                                      op0=ALU.add),
                     reads=[brstd[k]], writes=[brstd[k]])
                P.op("act", lambda e: e.activation(out=rstd[k][:], in_=rstd[k][:], func=AF.Ln),
                     reads=[brstd[k]], writes=[brstd[k]])
                P.op("act", lambda e: e.activation(out=rstd[k][:], in_=rstd[k][:], func=AF.Exp, scale=-0.5),
                     reads=[brstd[k]], writes=[brstd[k]])
                bc = []
                for c in range(DC):
                    b_ = Buf()
                    b_.w = self.bXR[n].w
                    b_.r = dict(self.bXR[n].r)
                    bc.append(b_)
                for c in range(DC):
                    t_ = cti % len(ct)
                    cti += 1
                    P.op("dve", lambda e: e.tensor_tensor(out=ct[t_][:], in0=self.XR[:, c, cs], in1=mean[k][:], op=ALU.subtract),
                         reads=[bc[c], bmean[k]], writes=[bct[t_]])
                    P.op("dve", lambda e: e.tensor_tensor(out=ct[t_][:], in0=ct[t_][:], in1=rstd[k][:], op=ALU.mult),
                         reads=[bct[t_], brstd[k]], writes=[bct[t_]])
                    P.op("act", lambda e: e.activation(out=self.XR[:, c, cs], in_=ct[t_][:], func=AF.Identity,
                                                       bias=self.vec[:, boff + c:boff + c + 1],
                                                       scale=self.vec[:, goff + c:goff + c + 1]),
                         reads=[bct[t_], self.bvec], writes=[bc[c]])
                P.op("act", lambda e: e.activation(out=self.XB[:, :, 2 + n * 512:2 + (n + 1) * 512],
                                                   in_=self.XR[:, :, cs], func=AF.Copy),
                     reads=bc, writes=[self.bXB[n]])
                self.bXR[n].w = bc[DC - 1].w
                self.bXR[n].r = dict(bc[DC - 1].r)
            P.barrier()

    def out_proj_residual(self, wname, Y, bY, nch, c0=0, first=True, wtile_c0=0, coff=0):
        P = self.P
        for f in range(DC):
            w, bw = self.load_w(wname, f, c0=wtile_c0, nch=nch)
            for n in range(NT):
                ps, bps = self.bank()
                self.proj(w, bw, coff + n * 512, 512, ps, bps, src=Y, srcbufs=bY if isinstance(bY, list) else [bY], nch=nch)
                cs = slice(n * 512, (n + 1) * 512)
                if first:
                    P.op("dve", lambda e: e.scalar_tensor_tensor(out=self.XR[:, f, cs], in0=self.XR[:, f, cs], scalar=ALPHA,
                                                                 in1=ps[:, :], op0=ALU.mult, op1=ALU.add),
                         reads=[bps, self.bXR[n]], writes=[self.bXR[n]])
                else:
                    P.op("dve", lambda e: e.tensor_tensor(out=self.XR[:, f, cs], in0=self.XR[:, f, cs], in1=ps[:, :], op=ALU.add),
                         reads=[bps, self.bXR[n]], writes=[self.bXR[n]])

    def ffn(self):
        P = self.P
        cvo = V_CONV
        groups = [(0, 8), (8, 15), (15, 22)]
        with ExitStack() as es:
            GH = 8
            H = self.sb(es, "ffn_H", [128, GH, T], BF16)
            bH = [Buf() for _ in range(GH)]
            U = [[self.sb(es, f"ffn_U{ab}{k}", [128, 514], F32) for k in range(3)] for ab in range(2)]
            bU = [[Buf() for _ in range(3)] for _ in range(2)]
            Y = [[self.sb(es, f"ffn_Y{ab}{k}", [128, 512], F32) for k in range(2)] for ab in range(2)]
            bY = [[Buf() for _ in range(2)] for _ in range(2)]
            TT = [[self.sb(es, f"ffn_T{ab}{k}", [128, 512], F32) for k in range(2)] for ab in range(2)]
            bTT = [[Buf() for _ in range(2)] for _ in range(2)]
            SA = [self.sb(es, f"ffn_SA{k}", [128, 512], F32) for k in range(2)]
            bSA = [Buf() for _ in range(2)]
            ui = 0
            yi = 0
            for gi, (j0, j1) in enumerate(groups):
                for j in range(j0, j1):
                    ws = [self.load_w("wup", j), self.load_w("wup", FC + j)]
                    for n in range(NT):
                        uk = ui % 3
                        up = (ui - 1) % 3
                        ui += 1
                        yk = yi % 2
                        yi += 1
                        for ab in range(2):
                            w, bw = ws[ab]
                            u, bu = U[ab][uk], bU[ab][uk]
                            cj = j + ab * FC
                            if n == 0:
                                ph, bph = self.bank()
                                self.proj(w, bw, 0, 2, ph, bph)
                                P.op("act", lambda e: e.activation(out=u[:, 0:2], in_=ph[:, 0:2], func=AF.Copy),
                                     reads=[bph], writes=[bu])
                            else:
                                P.op("act", lambda e: e.activation(out=u[:, 0:2], in_=U[ab][up][:, 512:514], func=AF.Copy),
                                     reads=[bU[ab][up]], writes=[bu])
                            ps, bps = self.bank()
                            self.proj(w, bw, 2 + n * 512, 512, ps, bps)
                            P.op("act", lambda e: e.activation(out=u[:, 2:514], in_=ps[:, :], func=AF.Copy),
                                 reads=[bps], writes=[bu])
                            t, bt = TT[ab][yk], bTT[ab][yk]
                            y, by = Y[ab][yk], bY[ab][yk]
                            w0 = self.vec[:, cvo + cj:cvo + cj + 1]
                            w1 = self.vec[:, cvo + 44 + cj:cvo + 44 + cj + 1]
                            w2 = self.vec[:, cvo + 88 + cj:cvo + 88 + cj + 1]
                            P.op("act", lambda e: e.activation(out=t[:], in_=u[:, 0:512], func=AF.Copy, scale=w0),
                                 reads=[bu, self.bvec], writes=[bt])
                            P.op("dve", lambda e: e.scalar_tensor_tensor(out=t[:], in0=u[:, 1:513], scalar=w1, in1=t[:],
                                                                         op0=ALU.mult, op1=ALU.add),
                                 reads=[bu, bt, self.bvec], writes=[bt])
                            P.op("dve", lambda e: e.scalar_tensor_tensor(out=y[:], in0=u[:, 2:514], scalar=w2, in1=t[:],
                                                                         op0=ALU.mult, op1=ALU.add),
                                 reads=[bu, bt, self.bvec], writes=[by])
                        sa, bsa = SA[yk], bSA[yk]
                        P.op("act", lambda e: e.activation(out=sa[:], in_=Y[0][yk][:], func=AF.Silu),
                             reads=[bY[0][yk]], writes=[bsa])
                        P.op("dve", lambda e: e.tensor_tensor(out=H[:, j - j0, n * 512:(n + 1) * 512], in0=sa[:], in1=Y[1][yk][:],
                                                              op=ALU.mult),
                             reads=[bsa, bY[1][yk]], writes=[bH[j - j0]])
                self.out_proj_residual("wdn", H, bH[:j1 - j0], j1 - j0, first=(gi == 0), wtile_c0=j0)
            P.barrier()

    def conv_mixer(self):
        P = self.P
        cvo = V_CONV
        with ExitStack() as es:
            Yo = self.sb(es, "cm_Y", [128, DC, T], BF16)
            bYo = [Buf() for _ in range(DC)]
            PR = [self.sb(es, f"cm_P{k}", [128, 514], F32) for k in range(3)]
            bPR = [Buf() for _ in range(3)]
            CG = [self.sb(es, f"cm_CG{k}", [128, 514], F32) for k in range(2)]
            bCG = [Buf() for _ in range(2)]
            TT = [self.sb(es, f"cm_T{k}", [128, 512], F32) for k in range(2)]
            bTT = [Buf() for _ in range(2)]
            ui = 0
            for c in range(DC):
                wbg = self.load_w("win", c)
                wcg = self.load_w("win", DC + c)
                wh = self.load_w("win", 2 * DC + c)
                w0 = self.vec[:, cvo + c:cvo + c + 1]
                w1 = self.vec[:, cvo + 8 + c:cvo + 8 + c + 1]
                w2 = self.vec[:, cvo + 16 + c:cvo + 16 + c + 1]
                for n in range(NT):
                    uk = ui % 3
                    up = (ui - 1) % 3
                    k2 = ui % 2
                    ui += 1
                    pr, bpr = PR[uk], bPR[uk]
                    cg, bcg = CG[k2], bCG[k2]
                    t, bt = TT[k2], bTT[k2]
                    if n == 0:
                        p1, bp1 = self.bank()
                        self.proj(wcg[0], wcg[1], 0, 2, p1, bp1)
                        P.op("act", lambda e: e.activation(out=cg[:, 0:2], in_=p1[:, 0:2], func=AF.Copy), reads=[bp1], writes=[bcg])
                        p2, bp2 = self.bank()
                        self.proj(wh[0], wh[1], 0, 2, p2, bp2)
                        P.op("dve", lambda e: e.tensor_tensor(out=pr[:, 0:2], in0=cg[:, 0:2], in1=p2[:, 0:2], op=ALU.mult),
                             reads=[bcg, bp2], writes=[bpr])
                    else:
                        P.op("act", lambda e: e.activation(out=pr[:, 0:2], in_=PR[up][:, 512:514], func=AF.Copy),
                             reads=[bPR[up]], writes=[bpr])
                    p1, bp1 = self.bank()
                    self.proj(wcg[0], wcg[1], 2 + n * 512, 512, p1, bp1)
                    P.op("act", lambda e: e.activation(out=cg[:, 2:514], in_=p1[:, :], func=AF.Copy), reads=[bp1], writes=[bcg])
                    p2, bp2 = self.bank()
                    self.proj(wh[0], wh[1], 2 + n * 512, 512, p2, bp2)
                    P.op("dve", lambda e: e.tensor_tensor(out=pr[:, 2:514], in0=cg[:, 2:514], in1=p2[:, :], op=ALU.mult),
                         reads=[bcg, bp2], writes=[bpr])
                    p3, bp3 = self.bank()
                    self.proj(wbg[0], wbg[1], 2 + n * 512, 512, p3, bp3)
                    P.op("act", lambda e: e.activation(out=t[:], in_=pr[:, 0:512], func=AF.Copy, scale=w0),
                         reads=[bpr, self.bvec], writes=[bt])
                    P.op("dve", lambda e: e.scalar_tensor_tensor(out=t[:], in0=pr[:, 1:513], scalar=w1, in1=t[:],
                                                                 op0=ALU.mult, op1=ALU.add),
                         reads=[bpr, bt, self.bvec], writes=[bt])
                    P.op("dve", lambda e: e.scalar_tensor_tensor(out=t[:], in0=pr[:, 2:514], scalar=w2, in1=t[:],
                                                                 op0=ALU.mult, op1=ALU.add),
                         reads=[bpr, bt, self.bvec], writes=[bt])
                    P.op("dve", lambda e: e.tensor_tensor(out=Yo[:, c, n * 512:(n + 1) * 512], in0=t[:], in1=p3[:, :], op=ALU.mult),
                         reads=[bt, bp3], writes=[bYo[c]])
            self.out_proj_residual("wout", Yo, bYo, DC, first=True)
            P.barrier()

    def hgrn_mixer(self):
        P = self.P
        nc = self.nc
        H = 8
        lite = (self.kind == "hg1")
        with ExitStack() as es:
            self.Aall = self.dram("Aall", [128, NCORE * H], F32, "ExternalInput")
            self.Ball = self.dram("Ball", [NCORE * 128, H * 128], F32, "ExternalInput")
            self.Aout = self.dram("Aout", [128, H], F32, "ExternalOutput")
            self.Bout = self.dram("Bout", [128, H * 128], F32, "ExternalOutput")
            G = [self.sb(es, f"hg_G{k}", [128, 512], BF16) for k in range(2)]
            bG = [Buf() for _ in range(2)]
            Y = self.sb(es, "hg_Y", [128, H, T], BF16)
            bY = [Buf() for _ in range(H)]
            S = self.sb(es, "hg_S", [128, H, 128], F32)
            bS = [Buf() for _ in range(H)]
            Aall = self.sb(es, "hg_Aall", [128, NCORE, H], F32)
            bAall = Buf()
            ap_ = self.sb(es, "hg_ap", [128, H], F32)
            bap = Buf()
            om = self.sb(es, "hg_om", [128, NCORE], F32)
            bom = Buf()
            Bm = [self.sb(es, f"hg_Bm{k}", [128, H, 128], F32) for k in range(1)]
            bBm = [Buf() for _ in range(1)]
            dBm = [P.dsem(f"bm{k}") for k in range(1)]
            d_a = P.dsem("aall")
            lbe = self.sb(es, "hg_lbe", [128, 4, H], F32)
            lb = self.sb(es, "hg_lb", [128, H], F32)
            oml = self.sb(es, "hg_oml", [128, H], F32)
            lbm1 = self.sb(es, "hg_lbm1", [128, H], F32)
            lsum = self.sb(es, "hg_lsum", [128, H], F32)
            blb = Buf()
            rmask = self.sb(es, "hg_rmask", [128, 512], F32)
            ident = self.sb(es, "hg_ident", [128, 128], BF16)
            ones128 = self.sb(es, "hg_ones", [128, 128], BF16)
            bcn = Buf()
            blsum = self.sb(es, "hg_blsum", [128, H], F32)
            bblsum = Buf()
            P.op("dve", lambda e: e.memset(rmask[:], 1.0), writes=[bcn])
            P.op("dve", lambda e: e.memset(rmask[:].rearrange("p (c t) -> p c t", t=64)[:, :, 0:1], 0.0), writes=[bcn])
            P.op("dve", lambda e: e.memset(ones128[:], 1.0 / 128.0), writes=[bcn])
            one1 = self.sb(es, "hg_one1", [128, 1], F32)
            P.op("dve", lambda e: e.memset(one1[:], 1.0), writes=[bcn])
            P.op("dve", lambda e: e.memset(blsum[:], 0.0), writes=[bblsum])
            P.op("act", lambda e: e.activation(out=ident[:], in_=self.cst[:, C_IDENT:C_IDENT + 128], func=AF.Copy),
                 reads=[self.bvec], writes=[bcn])
            P.op("act", lambda e: e.activation(out=lbe[:].rearrange("p l h -> p (l h)"), in_=self.vec[:, V_LBRAW:V_LBRAW + 32], func=AF.Exp),
                 reads=[self.bvec], writes=[blb])
            P.op("dve", lambda e: e.tensor_tensor(out=lsum[:], in0=lbe[:, 0, :], in1=lbe[:, 1, :], op=ALU.add), reads=[blb], writes=[blb])
            P.op("dve", lambda e: e.tensor_tensor(out=lsum[:], in0=lsum[:], in1=lbe[:, 2, :], op=ALU.add), reads=[blb], writes=[blb])
            P.op("dve", lambda e: e.tensor_tensor(out=lsum[:], in0=lsum[:], in1=lbe[:, 3, :], op=ALU.add), reads=[blb], writes=[blb])
            P.op("dve", lambda e: e.reciprocal(out=lsum[:], in_=lsum[:]), reads=[blb], writes=[blb])
            P.op("dve", lambda e: e.tensor_scalar(out=lb[:], in0=lbe[:, 1, :], scalar1=self.cst[:, C_LMASK + 1:C_LMASK + 2], scalar2=None, op0=ALU.mult),
                 reads=[blb, self.bvec], writes=[blb])
            for l in (2, 3):
                P.op("dve", lambda e: e.scalar_tensor_tensor(out=lb[:], in0=lbe[:, l, :], scalar=self.cst[:, C_LMASK + l:C_LMASK + l + 1],
                                                             in1=lb[:], op0=ALU.mult, op1=ALU.add),
                     reads=[blb, self.bvec], writes=[blb])
            P.op("dve", lambda e: e.tensor_tensor(out=lb[:], in0=lb[:], in1=lsum[:], op=ALU.mult), reads=[blb], writes=[blb])
            P.op("dve", lambda e: e.tensor_scalar(out=lbm1[:], in0=lb[:], scalar1=-1.0, scalar2=None, op0=ALU.add), reads=[blb], writes=[blb])
            P.op("dve", lambda e: e.tensor_scalar(out=oml[:], in0=lbm1[:], scalar1=-1.0, scalar2=None, op0=ALU.mult), reads=[blb], writes=[blb])
            P.dma("sp", Aall[:].rearrange("p r h -> p (r h)"), self.Aall, d_a, writes=[bAall])
            P.op("dve", lambda e: e.memset(S[:], 0.0), writes=bS)
            P.op("dve", lambda e: e.tensor_scalar(out=om[:], in0=self.cst[:, C_HMASK:C_HMASK + NCORE], scalar1=-1.0, scalar2=1.0,
                                                  op0=ALU.mult, op1=ALU.add), reads=[self.bvec], writes=[bom])
            for r in range(NCORE - 1):
                k = 0
                P.dma("sp", Bm[k][:].rearrange("p h v -> p (h v)"), self.Ball[r * 128:(r + 1) * 128, :], dBm[k], writes=[bBm[k]])
                mr = self.cst[:, C_HMASK + r:C_HMASK + r + 1]
                P.op("dve", lambda e: e.tensor_scalar(out=ap_[:], in0=Aall[:, r, :], scalar1=mr, scalar2=om[:, r:r + 1], op0=ALU.mult, op1=ALU.add),
                     reads=[bAall, self.bvec, bom], writes=[bap])
                P.op("dve", lambda e: e.tensor_scalar(out=Bm[k][:], in0=Bm[k][:], scalar1=mr, scalar2=None, op0=ALU.mult),
                     reads=[bBm[k], self.bvec], writes=[bBm[k]])
                for h in range(H):
                    P.op("dve", lambda e: e.scalar_tensor_tensor(out=S[:, h, :], in0=S[:, h, :], scalar=ap_[:, h:h + 1], in1=Bm[k][:, h, :],
                                                                 op0=ALU.mult, op1=ALU.add),
                         reads=[bS[h], bap, bBm[k]], writes=[bS[h]])
            R2 = lambda nm, shp, dt: [self.sb(es, f"{nm}{k}", shp, dt) for k in range(2)]
            sig, bsig = R2("hg_sig", [128, 512], F32), [Buf(), Buf()]
            qs, bqs = R2("hg_qs", [128, 512], F32), [Buf(), Buf()]
            R1 = lambda nm, shp, dt: [self.sb(es, nm, shp, dt)] * 2
            B1 = lambda: [Buf()] * 2
            kk, bkk = R1("hg_k", [128, 512], F32), B1()
            gg, bgg = R1("hg_g", [128, 512], F32), B1()
            bb, bbb = R1("hg_b", [128, 512], F32), B1()
            e1, be1 = R2("hg_e1", [128, 512], F32), [Buf(), Buf()]
            e2, be2 = R1("hg_e2", [128, 512], F32), B1()
            ebl, bebl = R2("hg_ebl", [128, 8], F32), [Buf(), Buf()]
            qp, bqp = R2("hg_qp", [128, 512], BF16), [Buf(), Buf()]
            kp, bkp = R2("hg_kp", [128, 512], BF16), [Buf(), Buf()]
            ktok, bktok = R2("hg_ktok", [128, 4, 128], BF16), [Buf(), Buf()]
            vtok, bvtok = R2("hg_vtok", [128, 4, 128], BF16), [Buf(), Buf()]
            am, bam = R2("hg_am", [128, 128], BF16), [Buf(), Buf()]
            sdb = [self.sb(es, f"hg_sdb{c_}", [128, 128], BF16) for c_ in range(8)]
            bsdb = [Buf() for _ in range(8)]
            osq, bosq = R2("hg_osq", [128, 512], BF16), [Buf(), Buf()]
            rr, brr = R2("hg_rr", [128, 512], F32), [Buf(), Buf()]
            yy, byy = R2("hg_yy", [128, 512], F32), [Buf(), Buf()]
            it = 0
            sdi = 0
            ami = 0
            self.rot_banks = (4, 5, 6, 7)
            cnts = {"ami": 0}

            def stage1(h, n, k, Wh):
                wq, wf, wi, wg, lbh, omlh, lbm1h = Wh
                c0 = 2 + n * 512
                cs = slice(n * 512, (n + 1) * 512)
                if not lite:
                    pq, bpq = self.bank()
                    self.proj(wq[0], wq[1], c0, 512, pq, bpq)
                pf, bpf = self.bank()
                self.proj(wf[0], wf[1], c0, 512, pf, bpf)
                if not lite:
                    pg, bpg = self.bank()
                    self.proj(wg[0], wg[1], c0, 512, pg, bpg)
                pv, bpv = self.bank()
                for s4 in range(4):
                    for c in range(DC):
                        P.op("pe", lambda e: e.matmul(pv[:, s4 * 128:(s4 + 1) * 128], self.XB[:, c, c0 + s4 * 128:c0 + (s4 + 1) * 128],
                                                      wi[0][:, c, :], start=(c == 0), stop=(c == DC - 1)),
                             reads=[wi[1], self.bXB[n]], writes=[bpv], signal=(c == DC - 1 and s4 == 3))
                P.op("act", lambda e: e.activation(out=sig[k][:], in_=pf[:, :], func=AF.Exp, scale=-1.0), reads=[bpf], writes=[bsig[k]])
                P.op("act", lambda e: e.activation(out=sig[k][:], in_=sig[k][:], func=AF.Ln, bias=one1[:, 0:1]), reads=[bsig[k], bcn], writes=[bsig[k]])
                P.op("act", lambda e: e.activation(out=sig[k][:], in_=sig[k][:], func=AF.Exp, scale=-1.0), reads=[bsig[k]], writes=[bsig[k]])
                if not lite:
                    P.op("act", lambda e: e.activation(out=qs[k][:], in_=pq[:, :], func=AF.Silu), reads=[bpq], writes=[bqs[k]])
                    P.op("act", lambda e: e.activation(out=G[k][:], in_=pg[:, :], func=AF.Silu), reads=[bpg], writes=[bG[k]])
                P.op("act", lambda e: e.activation(out=vtok[k][:].rearrange("p s v -> p (s v)"), in_=pv[:, :], func=AF.Copy),
                     reads=[bpv], writes=[bvtok[k]])
                P.op("dve", lambda e: e.tensor_scalar(out=kk[k][:], in0=sig[k][:], scalar1=lbm1h, scalar2=omlh, op0=ALU.mult, op1=ALU.add),
                     reads=[bsig[k], blb], writes=[bkk[k]])
                P.op("act", lambda e: e.activation(out=gg[k][:], in_=sig[k][:], func=AF.Ln, bias=lbh, scale=omlh),
                     reads=[bsig[k], blb], writes=[bgg[k]])
                P.op("dve", lambda e: e.tensor_tensor_scan(out=bb[k][:], data0=rmask[:], data1=gg[k][:], initial=0.0, op0=ALU.mult, op1=ALU.add),
                     reads=[bgg[k], bcn], writes=[bbb[k]])
                b3 = bb[k][:].rearrange("p (c t) -> p c t", t=64)
                P.op("dve", lambda e: e.tensor_tensor(out=e1[k][:].rearrange("p (c t) -> p c t", t=64), in0=b3,
                                                      in1=b3[:, :, 63:64].to_broadcast([128, 8, 64]), op=ALU.subtract),
                     reads=[bbb[k]], writes=[be1[k]])
                P.op("act", lambda e: e.activation(out=e2[k][:], in_=e1[k][:], func=AF.Exp, scale=-1.0), reads=[be1[k]], writes=[be2[k]])
                if not lite:
                    P.op("act", lambda e: e.activation(out=e1[k][:], in_=e1[k][:], func=AF.Exp), reads=[be1[k], be2[k]], writes=[be1[k]])
                P.op("act", lambda e: e.activation(out=ebl[k][:], in_=b3[:, :, 63], func=AF.Exp), reads=[bbb[k]], writes=[bebl[k]])
                P.op("dve", lambda e: e.tensor_reduce(out=rr[k][:, 0:1], in_=b3[:, :, 63], axis=AX.X, op=ALU.add), reads=[bbb[k]], writes=[brr[k]])
                P.op("dve", lambda e: e.tensor_tensor(out=blsum[:, h:h + 1], in0=blsum[:, h:h + 1], in1=rr[k][:, 0:1], op=ALU.add),
                     reads=[brr[k], bblsum], writes=[bblsum])
                if not lite:
                    P.op("dve", lambda e: e.tensor_tensor(out=qp[k][:], in0=qs[k][:], in1=e1[k][:], op=ALU.mult), reads=[bqs[k], be1[k]], writes=[bqp[k]])
                P.op("dve", lambda e: e.tensor_tensor(out=kp[k][:], in0=kk[k][:], in1=e2[k][:], op=ALU.mult), reads=[bkk[k], be2[k]], writes=[bkp[k]])

            def stage2(h, n, k):
                ami = cnts["ami"]
                cs = slice(n * 512, (n + 1) * 512)
                pt, bpt = self.bank()
                ptb = pt[:].bitcast(BF16)
                for s4 in range(4):
                    P.op("pe", lambda e: e.transpose(ptb[:, s4 * 128:(s4 + 1) * 128], kp[k][:, s4 * 128:(s4 + 1) * 128], ident[:]),
                         reads=[bkp[k], bcn], writes=[bpt], signal=(s4 == 3))
                P.op("act", lambda e: e.activation(out=ktok[k][:].rearrange("p s v -> p (s v)"), in_=ptb[:, 0:512], func=AF.Copy),
                     reads=[bpt], writes=[bktok[k]])
                po, bpo = self.PS[k], self.bPS[k]
                for ci in range(8):
                    s4, half = ci // 2, ci % 2
                    rows = slice(half * 64, half * 64 + 64)
                    pu, bpu = self.PS[2 + half], self.bPS[2 + half]
                    P.op("pe", lambda e: e.matmul(pu[:, s4 * 128:(s4 + 1) * 128], ktok[k][rows, s4, :], vtok[k][rows, s4, :],
                                                  start=True, stop=True),
                         reads=[bktok[k], bvtok[k]], writes=[bpu], signal=(ci >= 6))
                for ci in range(8):
                    s4, half = ci // 2, ci % 2
                    eb = ebl[k][:, ci:ci + 1]
                    pu, bpu = self.PS[2 + half], self.bPS[2 + half]
                    if not lite:
                        P.op("dve", lambda e: e.tensor_scalar(out=sdb[ci][:], in0=S[:, h, :], scalar1=eb, scalar2=None, op0=ALU.mult),
                             reads=[bS[h], bebl[k]], writes=[bsdb[ci]])
                    P.op("dve", lambda e: e.scalar_tensor_tensor(out=S[:, h, :], in0=S[:, h, :], scalar=eb, in1=pu[:, s4 * 128:(s4 + 1) * 128],
                                                                 op0=ALU.mult, op1=ALU.add),
                         reads=[bS[h], bebl[k], bpu], writes=[bS[h]])
                for s4 in range(0 if lite else 4):
                    sl = slice(s4 * 128, (s4 + 1) * 128)
                    pa, bpa = self.bank()
                    P.op("pe", lambda e: e.matmul(pa[:, 0:128], kp[k][:, sl], qp[k][:, sl], start=True, stop=True),
                         reads=[bkp[k], bqp[k]], writes=[bpa])
                    a_ = ami % 2
                    ami += 1
                    P.op("dve", lambda e: e.tensor_tensor(out=am[a_][:], in0=pa[:, 0:128], in1=self.cst[:, C_MASK2:C_MASK2 + 128], op=ALU.mult),
                         reads=[bpa, self.bvec], writes=[bam[a_]])
                    P.op("pe", lambda e: e.matmul(po[:, sl], vtok[k][:, s4, :], am[a_][:], start=True, stop=False),
                         reads=[bvtok[k], bam[a_]], writes=[bpo], signal=False)
                    for half in range(2):
                        ci = s4 * 2 + half
                        hs = slice(s4 * 128 + half * 64, s4 * 128 + half * 64 + 64)
                        P.op("pe", lambda e: e.matmul(po[:, hs], sdb[ci][:], qp[k][:, hs], start=False, stop=(half == 1)),
                             reads=[bsdb[ci], bqp[k]], writes=[bpo], signal=True)
                if lite:
                    return
                P.op("act", lambda e: e.activation(out=osq[k][:], in_=po[:, :], func=AF.Square), reads=[bpo], writes=[bosq[k]])
                pm, bpm = self.bank()
                P.op("pe", lambda e: e.matmul(pm[:, :], ones128[:], osq[k][:], start=True, stop=True), reads=[bcn, bosq[k]], writes=[bpm])
                P.op("dve", lambda e: e.tensor_scalar(out=rr[k][:], in0=pm[:, :], scalar1=RMS_EPS, scalar2=None, op0=ALU.add),
                     reads=[bpm], writes=[brr[k]])
                P.op("act", lambda e: e.activation(out=rr[k][:], in_=rr[k][:], func=AF.Ln), reads=[brr[k]], writes=[brr[k]])
                P.op("act", lambda e: e.activation(out=rr[k][:], in_=rr[k][:], func=AF.Exp, scale=-0.5), reads=[brr[k]], writes=[brr[k]])
                P.op("dve", lambda e: e.tensor_tensor(out=yy[k][:], in0=po[:, :], in1=rr[k][:], op=ALU.mult), reads=[bpo, brr[k]], writes=[byy[k]])
                P.op("dve", lambda e: e.scalar_tensor_tensor(out=Y[:, h, cs], in0=yy[k][:], scalar=self.vec[:, V_NORMG + h:V_NORMG + h + 1],
                                                             in1=G[k][:], op0=ALU.mult, op1=ALU.mult),
                     reads=[byy[k], self.bvec, bG[k]], writes=[bY[h]])

            def wts(h):
                return (self.load_w("win", h), self.load_w("win", H + h), self.load_w("win", 2 * H + h), self.load_w("win", 3 * H + h),
                        lb[:, h:h + 1], oml[:, h:h + 1], lbm1[:, h:h + 1])
            for hp in range(0, H, 2):
                WA, WB = wts(hp), wts(hp + 1)
                for n in range(NT):
                    stage1(hp, n, 0, WA)
                    stage1(hp + 1, n, 1, WB)
                    stage2(hp, n, 0)
                    stage2(hp + 1, n, 1)
            P.op("act", lambda e: e.activation(out=blsum[:], in_=blsum[:], func=AF.Exp), reads=[bblsum], writes=[bblsum])
            d_o = P.dsem("hgo")
            P.dma("sp", self.Aout, blsum[:], d_o, reads=[bblsum], writes=[self.bout])
            P.dma("sp", self.Bout, S[:].rearrange("p h v -> p (h v)"), d_o, reads=bS, writes=[self.bout])
            self.rot_banks = (0, 1, 2, 3, 4, 5, 6, 7)
            if not lite:
                self.out_proj_residual("wout", Y, bY, DC, first=True)
            P.barrier()

    def moba_qkv(self):
        P = self.P
        H = 8
        with ExitStack() as es:
            self.pos_d = self.dram("pos", [1, T], I32, "ExternalInput")
            self.QTo = self.dram("QT", [128, H * T], BF16, "ExternalOutput")
            self.KTo = self.dram("KT", [128, H * T], BF16, "ExternalOutput")
            self.Vo = self.dram("V", [128, 16 * 1024], BF16, "ExternalOutput")
            self.KMo = self.dram("KM", [128, H * 8], F32, "ExternalOutput")
            posi = self.sb(es, "mq_posi", [32, T], I32)
            ang = self.sb(es, "mq_ang", [32, T], F32)
            tmp = self.sb(es, "mq_tmp", [32, T], F32)
            Ct = self.sb(es, "mq_C", [32, T], F32)
            St = self.sb(es, "mq_S", [32, T], F32)
            npi = self.sb(es, "mq_npi", [32, 1], F32)
            km = self.sb(es, "mq_km", [128, H, 8], F32)
            btab, bkm = Buf(), Buf()
            d_p = P.dsem("pos")
            P.dma("sp", posi[:], self.pos_d.partition_broadcast(32), d_p, writes=[btab])
            P.op("dve", lambda e: e.tensor_copy(out=ang[:], in_=posi[:]), reads=[btab], writes=[btab])
            P.op("dve", lambda e: e.memset(npi[:], -math.pi), writes=[btab])
            P.op("dve", lambda e: e.tensor_scalar(out=ang[:], in0=ang[:], scalar1=self.cst[0:32, C_INVF:C_INVF + 1], scalar2=None, op0=ALU.mult),
                 reads=[btab, self.bvec], writes=[btab])
            ki = self.sb(es, "mq_ki", [32, T], I32)
            kfl = self.sb(es, "mq_kfl", [32, T], F32)
            for (dst_, off_) in ((St, 0.5), (Ct, 0.75)):
                P.op("dve", lambda e: e.tensor_scalar(out=tmp[:], in0=ang[:], scalar1=1.0 / (2 * math.pi), scalar2=off_, op0=ALU.mult, op1=ALU.add),
                     reads=[btab], writes=[btab])
                P.op("dve", lambda e: e.tensor_copy(out=ki[:], in_=tmp[:]), reads=[btab], writes=[btab])
                P.op("dve", lambda e: e.tensor_copy(out=kfl[:], in_=ki[:]), reads=[btab], writes=[btab])
                P.op("dve", lambda e: e.tensor_tensor(out=tmp[:], in0=tmp[:], in1=kfl[:], op=ALU.subtract), reads=[btab], writes=[btab])
                P.op("dve", lambda e: e.tensor_scalar(out=kfl[:], in0=tmp[:], scalar1=0.0, scalar2=None, op0=ALU.is_lt), reads=[btab], writes=[btab])
                P.op("dve", lambda e: e.tensor_tensor(out=tmp[:], in0=tmp[:], in1=kfl[:], op=ALU.add), reads=[btab], writes=[btab])
                P.op("act", lambda e: e.activation(out=dst_[:], in_=tmp[:], func=AF.Sin, bias=npi[:, 0:1], scale=2 * math.pi), reads=[btab], writes=[btab])
            P.op("dve", lambda e: e.tensor_scalar(out=St[:], in0=St[:], scalar1=self.cst[0:32, C_SGN:C_SGN + 1], scalar2=None, op0=ALU.mult),
                 reads=[btab, self.bvec], writes=[btab])
            R2 = lambda nm, shp, dt: [self.sb(es, f"{nm}{k}", shp, dt) for k in range(2)]
            t1, bt1 = R2("mq_t1", [32, 512], F32), [Buf(), Buf()]
            t2, bt2 = R2("mq_t2", [32, 512], F32), [Buf(), Buf()]
            kf, bkf = R2("mq_kf", [128, 512], F32), [Buf(), Buf()]
            ob, bob = R2("mq_ob", [128, 512], BF16), [Buf(), Buf()]
            vb, bvb = R2("mq_vb", [128, 4, 128], BF16), [Buf(), Buf()]
            dob = [P.dsem(f"ob{k}") for k in range(2)]
            dvb = [P.dsem(f"vb{k}") for k in range(2)]
            Vov = self.Vo.rearrange("p (t f) -> p t f", f=1024)
            it = 0
            for h in range(H):
                for qk in range(2):
                    w = self.load_w("win", qk * H + h)
                    wp = self.load_w("win", 3 * H + qk * H + h)
                    dst = self.QTo if qk == 0 else self.KTo
                    for n in range(NT):
                        k = it % 2
                        it += 1
                        c0 = 2 + n * 512
                        cs = slice(n * 512, (n + 1) * 512)
                        pa, bpa = self.bank()
                        self.proj(w[0], w[1], c0, 512, pa, bpa)
                        pb, bpb = self.bank()
                        self.proj(wp[0], wp[1], c0, 512, pb, bpb, m=32)
                        P.op("dve", lambda e: e.tensor_tensor(out=t1[k][:], in0=pa[0:32, :], in1=Ct[:, cs], op=ALU.mult), reads=[bpa, btab], writes=[bt1[k]])
                        P.op("dve", lambda e: e.tensor_tensor(out=t2[k][:], in0=pb[0:32, :], in1=St[:, cs], op=ALU.mult), reads=[bpb, btab], writes=[bt2[k]])
                        P.op("dve", lambda e: e.tensor_tensor(out=kf[k][0:32, :], in0=t1[k][:], in1=t2[k][:], op=ALU.add), reads=[bt1[k], bt2[k]], writes=[bkf[k]])
                        P.op("act", lambda e: e.activation(out=kf[k][32:64, :], in_=pa[32:64, :], func=AF.Copy), reads=[bpa], writes=[bkf[k]])
                        P.op("act", lambda e: e.activation(out=kf[k][64:128, :], in_=pa[64:128, :], func=AF.Copy), reads=[bpa], writes=[bkf[k]])
                        P.op("act", lambda e: e.activation(out=ob[k][:], in_=kf[k][:], func=AF.Copy), reads=[bkf[k]], writes=[bob[k]])
                        if qk == 1:
                            P.op("dve", lambda e: e.tensor_reduce(out=km[:, h, 2 * n:2 * n + 2], in_=kf[k][:].rearrange("p (b t) -> p b t", t=256),
                                                                  axis=AX.X, op=ALU.add), reads=[bkf[k]], writes=[bkm])
                        P.dma("sp", dst[:, h * T + n * 512:h * T + (n + 1) * 512], ob[k][:], dob[k], reads=[bob[k]], writes=[self.bout])
                wv = self.load_w("win", 2 * H + h)
                for n in range(NT):
                    k = it % 2
                    it += 1
                    c0 = 2 + n * 512
                    pv, bpv = self.bank()
                    for s4 in range(4):
                        for c in range(DC):
                            P.op("pe", lambda e: e.matmul(pv[:, s4 * 128:(s4 + 1) * 128], self.XB[:, c, c0 + s4 * 128:c0 + (s4 + 1) * 128],
                                                          wv[0][:, c, :], start=(c == 0), stop=(c == DC - 1)),
                                 reads=[wv[1], self.bXB[n]], writes=[bpv], signal=(c == DC - 1 and s4 == 3))
                    P.op("act", lambda e: e.activation(out=vb[k][:].rearrange("p s v -> p (s v)"), in_=pv[:, :], func=AF.Copy), reads=[bpv], writes=[bvb[k]])
                    P.dma("sp", Vov[:, n * 4:(n + 1) * 4, h * 128:(h + 1) * 128], vb[k][:], dvb[k], reads=[bvb[k]], writes=[self.bout])
            P.op("dve", lambda e: e.tensor_scalar(out=km[:], in0=km[:], scalar1=1.0 / 256.0, scalar2=None, op0=ALU.mult), reads=[bkm], writes=[bkm])
            d_k = P.dsem("kmo")
            P.dma("sp", self.KMo, km[:].rearrange("p h b -> p (h b)"), d_k, reads=[bkm], writes=[self.bout])
            P.barrier()

    def moba_attn(self):
        P = self.P
        H = 8
        NS = 72
        SCALE = 1.0 / math.sqrt(128.0)
        with ExitStack() as es:
            self.QTi = self.dram("QT", [128, H * T], BF16, "ExternalInput")
            self.Kall = self.dram("Kall", [H * 128, NS * 256], BF16, "ExternalInput")
            self.Vall = self.dram("Vall", [H * 128, NS * 256], BF16, "ExternalInput")
            self.KMall = self.dram("KMall", [128, H * NS], F32, "ExternalInput")
            self.mcst_d = self.dram("mcst", [128, 3 * 8 * NS + 4 * 512], F32, "ExternalInput")
            QT = self.sb(es, "ma_QT", [128, H, T], BF16)
            bQT = Buf()
            d_q = P.dsem("qt")
            P.dma("sp", QT[:].rearrange("p h t -> p (h t)"), self.QTi, d_q, writes=[bQT])
            mc = self.sb(es, "ma_mc", [128, 3, 8, NS], F32)
            caus32 = self.sb(es, "ma_c32", [128, 512], F32)
            caus = self.sb(es, "ma_caus", [128, 4, 512], BF16)
            bmc = Buf()
            d_m = P.dsem("mc")
            d_m2 = P.dsem("mc2")
            P.dma("sp", mc[:].rearrange("p a l s -> p (a l s)"), self.mcst_d[:, 0:3 * 8 * NS], d_m, writes=[bmc])
            for v in range(4):
                P.dma("sp", caus32[:], self.mcst_d[:, 3 * 8 * NS + v * 512:3 * 8 * NS + (v + 1) * 512], d_m2, writes=[bmc])
                P.op("act", lambda e: e.activation(out=caus[:, v, :], in_=caus32[:], func=AF.Copy), reads=[bmc], writes=[bmc])
            kmf = self.sb(es, "ma_kmf", [128, H, NS], F32)
            kmb = self.sb(es, "ma_kmb", [128, H, NS], BF16)
            d_km = P.dsem("km")
            bkm = Buf()
            P.dma("sp", kmf[:].rearrange("p h s -> p (h s)"), self.KMall, d_km, writes=[bkm])
            P.op("act", lambda e: e.activation(out=kmb[:], in_=kmf[:], func=AF.Copy), reads=[bkm], writes=[bkm])
            ident = self.sb(es, "ma_ident", [128, 128], BF16)
            ones = self.sb(es, "ma_ones", [128, 128], BF16)
            Esel = self.sb(es, "ma_Esel", [NS, NS, 128], BF16)
            bcn = Buf()
            P.op("act", lambda e: e.activation(out=ident[:], in_=self.cst[:, C_IDENT:C_IDENT + 128], func=AF.Copy), reads=[self.bvec], writes=[bcn])
            P.op("dve", lambda e: e.memset(ones[:], 1.0), writes=[bcn])
            P.op("dve", lambda e: e.tensor_copy(out=Esel[:], in_=self.cst[0:NS, C_IDENT:C_IDENT + NS].unsqueeze(2).to_broadcast([NS, NS, 128])),
                 reads=[self.bvec], writes=[bcn])
            maskT = [self.sb(es, f"ma_maskT{k}", [NS, T], BF16) for k in range(2)]
            bmaskT = [Buf(), Buf()]
            R2 = lambda nm, shp, dt: [self.sb(es, f"{nm}{k}", shp, dt) for k in range(2)]
            gm, bgm = R2("ma_gm", [128, NS], F32), [Buf(), Buf()]
            t8, bt8 = R2("ma_t8", [128, 8], F32), [Buf(), Buf()]
            al, bal = R2("ma_al", [128, NS], F32), [Buf(), Buf()]
            alb, balb = R2("ma_alb", [128, NS], BF16), [Buf(), Buf()]
            NKS = 3
            KS = [self.sb(es, f"ma_KS{k}", [128, 1024], BF16) for k in range(NKS)]
            VS = [self.sb(es, f"ma_VS{k}", [128, 8, 128], BF16) for k in range(NKS)]
            bKS = [Buf() for _ in range(NKS)]
            bVS = [Buf() for _ in range(NKS)]
            dKS = [P.dsem(f"ks{k}") for k in range(NKS)]
            dVS = [P.dsem(f"vs{k}") for k in range(NKS)]
            NPT = 4
            PT = [self.sb(es, f"ma_PT{k}", [128, 512], BF16) for k in range(NPT)]
            bPT = [Buf() for _ in range(NPT)]
            rden, brden = R2("ma_rden", [128, 512], F32), [Buf(), Buf()]
            Y = self.XB
            bYh = [Buf() for _ in range(H)]
            self.rot_banks = (4, 5, 6, 7)
            gi = 0
            ksi = 0
            pti = 0
            for h in range(H):
                mk = h % 2
                for s16 in range(16):
                    g_ = gi % 2
                    gi += 1
                    lb_ = s16 // 2
                    qs_ = slice(s16 * 128, (s16 + 1) * 128)
                    pg, bpg = self.bank()
                    P.op("pe", lambda e: e.matmul(pg[:, 0:NS], QT[:, h, qs_], kmb[:, h, :], start=True, stop=True), reads=[bQT, bkm], writes=[bpg])
                    P.op("dve", lambda e: e.tensor_tensor(out=gm[g_][:], in0=pg[:, 0:NS], in1=mc[:, 0, lb_, :], op=ALU.add), reads=[bpg, bmc], writes=[bgm[g_]])
                    P.op("dve", lambda e: e.max(out=t8[g_][:], in_=gm[g_][:]), reads=[bgm[g_]], writes=[bt8[g_]])
                    P.op("dve", lambda e: e.tensor_scalar(out=al[g_][:], in0=gm[g_][:], scalar1=t8[g_][:, 2:3], scalar2=None, op0=ALU.is_ge),
                         reads=[bgm[g_], bt8[g_]], writes=[bal[g_]])
                    P.op("dve", lambda e: e.tensor_tensor(out=al[g_][:], in0=al[g_][:], in1=mc[:, 1, lb_, :], op=ALU.mult), reads=[bal[g_], bmc], writes=[bal[g_]])
                    P.op("dve", lambda e: e.tensor_tensor(out=al[g_][:], in0=al[g_][:], in1=mc[:, 2, lb_, :], op=ALU.add), reads=[bal[g_], bmc], writes=[bal[g_]])
                    P.op("dve", lambda e: e.tensor_scalar(out=alb[g_][:], in0=al[g_][:], scalar1=-1.0, scalar2=30000.0, op0=ALU.add, op1=ALU.mult),
                         reads=[bal[g_]], writes=[balb[g_]])
                    pt_, bpt_ = self.bank()
                    ptb = pt_[:].bitcast(BF16)
                    P.op("pe", lambda e: e.transpose(ptb[0:NS, 0:128], alb[g_][:], ident[:]), reads=[balb[g_], bcn], writes=[bpt_])
                    P.op("act", lambda e: e.activation(out=maskT[mk][:, qs_], in_=ptb[0:NS, 0:128], func=AF.Copy), reads=[bpt_], writes=[bmaskT[mk]])
                for qt in range(2):
                    pend = []
                    LA = 2
                    O = [self.PS[0], self.PS[1]]
                    bO = [self.bPS[0], self.bPS[1]]
                    DN = [self.PS[2], self.PS[3]]
                    bDN = [self.bPS[2], self.bPS[3]]
                    def need(j, hf):
                        if j < 8:
                            return j in (qt * 4 + hf * 2, qt * 4 + hf * 2 + 1)
                        return (j - 8) < 32 * qt + 16 * hf + 16
                    ulist = {hf: [(j, kt2) for j in range(NS) for kt2 in range(2) if need(j, hf)] for hf in range(2)}
                    for g4 in range(NS // 4):
                        if not any(need(g4 * 4 + j4, hf) for j4 in range(4) for hf in range(2)):
                            continue
                        ks = ksi % NKS
                        ksi += 1
                        P.dma("sp", KS[ks][:], self.Kall[h * 128:(h + 1) * 128, g4 * 1024:(g4 + 1) * 1024], dKS[ks], writes=[bKS[ks]])
                        P.dma("sp", VS[ks][:].rearrange("p t v -> p (t v)"), self.Vall[h * 128:(h + 1) * 128, g4 * 1024:(g4 + 1) * 1024], dVS[ks], writes=[bVS[ks]])
                        for j4 in range(4):
                            j = g4 * 4 + j4
                            for kt2 in range(2):
                                kti = j4 * 2 + kt2
                                for hf in range(2):
                                    if not need(j, hf):
                                        continue
                                    first = ((j, kt2) == ulist[hf][0])
                                    last = ((j, kt2) == ulist[hf][-1])
                                    q0 = qt * 1024 + hf * 512
                                    ps, bps = self.bank()
                                    lbs = (qt * 4 + hf * 2, qt * 4 + hf * 2 + 1)
                                    diag = j in lbs
                                    P.op("pe", lambda e: e.matmul(ps[:, :], KS[ks][:, kti * 128:(kti + 1) * 128], QT[:, h, q0:q0 + 512], start=True, stop=False),
                                         reads=[bKS[ks], bQT], writes=[bps], signal=False)
                                    P.op("pe", lambda e: e.matmul(ps[:, :], Esel[:, j, :], maskT[mk][:, q0:q0 + 512], start=False, stop=(not diag)),
                                         reads=[bcn, bmaskT[mk]], writes=[bps], signal=(not diag))
                                    if diag:
                                        v = kt2 * 2 + (j - lbs[0])
                                        P.op("pe", lambda e: e.matmul(ps[:, :], ident[:], caus[:, v, :], start=False, stop=True),
                                             reads=[bcn, bmc], writes=[bps])
                                    p_ = pti % NPT
                                    pti += 1
                                    P.op("act", lambda e: e.activation(out=PT[p_][:], in_=ps[:, :], func=AF.Exp, scale=SCALE), reads=[bps], writes=[bPT[p_]])
                                    def _pv(ks=ks, kti=kti, p_=p_, hf=hf, first=first, last=last):
                                        P.op("pe", lambda e: e.matmul(O[hf][:, :], VS[ks][:, kti, :], PT[p_][:], start=first, stop=last),
                                             reads=[bVS[ks], bPT[p_]], writes=[bO[hf]], signal=last)
                                        P.op("pe", lambda e: e.matmul(DN[hf][:, :], ones[:], PT[p_][:], start=first, stop=last),
                                             reads=[bcn, bPT[p_]], writes=[bDN[hf]], signal=True)
                                    pend.append(_pv)
                                    if len(pend) > LA:
                                        pend.pop(0)()
                    while pend:
                        pend.pop(0)()
                    for hf in range(2):
                        q0 = qt * 1024 + hf * 512
                        r_ = hf
                        P.op("act", lambda e: e.activation(out=rden[r_][:], in_=DN[hf][:, :], func=AF.Ln), reads=[bDN[hf]], writes=[brden[r_]])
                        P.op("act", lambda e: e.activation(out=rden[r_][:], in_=rden[r_][:], func=AF.Exp, scale=-1.0), reads=[brden[r_]], writes=[brden[r_]])
                        P.op("dve", lambda e: e.tensor_tensor(out=Y[:, h, 2 + q0:2 + q0 + 512], in0=O[hf][:, :], in1=rden[r_][:], op=ALU.mult),
                             reads=[bO[hf], brden[r_]], writes=[bYh[h]])
            self.rot_banks = (0, 1, 2, 3, 4, 5, 6, 7)
            self.out_proj_residual("wout", Y, bYh, DC, first=True, coff=2)
            P.barrier()


_PROGS = {}


def prog(kind):
    if kind not in _PROGS:
        _PROGS[kind] = Builder(kind).build()
    return _PROGS[kind]


def shard_xT(x_full):
    xT = np.ascontiguousarray(x_full.T)
    xTp = np.concatenate([np.zeros((D, 2), np.float32), xT], axis=1)
    return [np.ascontiguousarray(xTp[:, c * T:c * T + T + 2]) for c in range(NCORE)]


def launch(kind, maps):
    import os
    if os.environ.get("TRACE_KIND") == kind:
        res = run_bass_kernel_spmd(prog(kind), maps, core_ids=list(range(NCORE)), trace=True)
        print("[trace]", kind, "exec_time_ns", res.exec_time_ns)
        try:
            insts = res.instructions_and_trace[0]
            from collections import defaultdict
            busy = defaultdict(float); cnt = defaultdict(int); wt = defaultdict(float)
            byname = defaultdict(float); bn = defaultdict(int)
            srcl = defaultdict(float)
            for it_ in insts:
                e = str(it_.engine)
                busy[e] += it_.duration; cnt[e] += 1
                try:
                    wt[e] += (it_.evt_wait_time or 0)
                except Exception:
                    pass
                key = (e, str(it_.op_name))
                byname[key] += it_.duration; bn[key] += 1
                srcl[(e, it_.source_line)] += it_.duration
            for e in busy:
                print(f"[trace] {e:24s} busy={busy[e]/1e3:9.1f}us n={cnt[e]:6d} waits={wt[e]/1e3:9.1f}us")
            for key, v in sorted(byname.items(), key=lambda kv: -kv[1])[:25]:
                print(f"[trace]   {key[0]:20s} {key[1]:28s} {v/1e3:9.1f}us n={bn[key]:6d} avg={v/bn[key]:7.1f}ns")
            t0 = min(i_.timestamp for i_ in insts); t1 = max(i_.end_timestamp for i_ in insts)
            lo = int(os.environ.get("TRACE_LO", "0")); hi = int(os.environ.get("TRACE_HI", "0"))
            sel = [i_ for i_ in insts if lo <= (i_.source_line or 0) <= hi]
            if sel:
                print(f"[trace] total span {(t1-t0)/1e3:.1f}us; lines {lo}-{hi}: from {(min(i_.timestamp for i_ in sel)-t0)/1e3:.1f}us to {(max(i_.end_timestamp for i_ in sel)-t0)/1e3:.1f}us")
            for key, v in sorted(srcl.items(), key=lambda kv: -kv[1])[:25]:
                print(f"[trace]   line {key[1]} {key[0]:12s} {v/1e3:9.1f}us")
        except Exception as ex:
            print("[trace] summary failed", repr(ex))
        return res.results
    res = run_bass_kernel_spmd(prog(kind), maps, core_ids=list(range(NCORE)))
    return res.results


def gather_x(results):
    return np.ascontiguousarray(np.concatenate([r["outT"] for r in results], axis=1).T)


def run_ffn(inp, i, x):
    xs = shard_xT(x)
    vec = make_vec(inp[f"l{i}_ln2_g"], inp[f"l{i}_ln2_b"], conv=inp[f"l{i}_ffn_conv"])
    wup = tile_w(inp[f"l{i}_ffn_w_up"])
    wdn = tile_w(inp[f"l{i}_ffn_w_down"])
    maps = [dict(xT=xs[c], vec=vec, cst=make_cst(c), wup=wup, wdn=wdn) for c in range(NCORE)]
    return gather_x(launch("ff", maps))


def run_conv(inp, i, x):
    xs = shard_xT(x)
    vec = make_vec(inp[f"l{i}_ln1_g"], inp[f"l{i}_ln1_b"], conv=inp[f"l{i}_mix_conv"])
    win = tile_w(inp[f"l{i}_mix_w_in"])
    wout = tile_w(inp[f"l{i}_mix_w_out"])
    maps = [dict(xT=xs[c], vec=vec, cst=make_cst(c), win=win, wout=wout) for c in range(NCORE)]
    return gather_x(launch("cv", maps))


def run_hgrn(inp, i, x):
    xs = shard_xT(x)
    vec = make_vec(inp[f"l{i}_ln1_g"], inp[f"l{i}_ln1_b"], norm_g=inp[f"l{i}_mix_norm_g"], lbraw=inp["hgrn_lower_bounds"])
    win = tile_w(inp[f"l{i}_mix_w_in"])
    wout = tile_w(inp[f"l{i}_mix_w_out"])
    Aall = np.zeros((128, NCORE * 8), np.float32)
    Ball = np.zeros((NCORE * 128, 1024), np.float32)
    out = None
    for phase in range(2):
        maps = [dict(xT=xs[c], vec=vec, cst=make_cst(c, layer=i), win=win, Aall=Aall, Ball=Ball) for c in range(NCORE)]
        if phase == 1:
            for m in maps:
                m["wout"] = wout
        res = launch("hg1" if phase == 0 else "hg", maps)
        if phase == 0:
            Aall = np.ascontiguousarray(np.concatenate([r["Aout"] for r in res], axis=1))
            Ball = np.ascontiguousarray(np.concatenate([r["Bout"] for r in res], axis=0))
        else:
            out = gather_x(res)
    return out


def zz_block(c, s_):
    return 8 * s_ + (c if s_ % 2 == 0 else 7 - c)


def run_moba(inp, i, x):
    H = 8
    NSL = 72
    xs = shard_xT(x)
    vec = make_vec(inp[f"l{i}_ln1_g"], inp[f"l{i}_ln1_b"])
    W = inp[f"l{i}_mix_w_in"]
    perm = []
    for qk in range(2):
        for h in range(H):
            base = qk * 1024 + h * 128
            Wp = np.zeros((D, 128), np.float32)
            Wp[:, 0:16] = W[:, base + 16:base + 32]
            Wp[:, 16:32] = W[:, base:base + 16]
            perm.append(Wp)
    win = tile_w(np.concatenate([W] + perm, axis=1))
    pos = inp["positions"].astype(np.int32)
    maps = [dict(xT=xs[c], vec=vec, cst=make_cst(c), win=win, pos=np.ascontiguousarray(pos[:, c * T:(c + 1) * T])) for c in range(NCORE)]
    res = launch("mo1", maps)
    Qg = np.concatenate([np.asarray(r["QT"]).reshape(128, H, 8, 256) for r in res], axis=2)
    Kg = np.concatenate([np.asarray(r["KT"]).reshape(128, H, 8, 256).transpose(1, 0, 2, 3) for r in res], axis=2)
    Vg = np.concatenate([np.asarray(r["V"]).reshape(128, 8, 2, H, 128).transpose(3, 0, 1, 2, 4) for r in res], axis=2)
    KMg = np.concatenate([np.asarray(r["KM"]).reshape(128, H, 8) for r in res], axis=2)
    xTfull = np.ascontiguousarray(x.T).reshape(D, 64, 256)
    wout = tile_w(inp[f"l{i}_mix_w_out"])
    caus = np.zeros((4, 128, 512), np.float32)
    p_ = np.arange(128)[:, None]
    qi = np.arange(256)[None, :]
    for kt2 in range(2):
        for posb in range(2):
            caus[kt2 * 2 + posb][:, posb * 256:(posb + 1) * 256] = np.where(kt2 * 128 + p_ > qi, -30000.0, 0.0)
    maps = []
    own_all = []
    for c in range(NCORE):
        own = [zz_block(c, s_) for s_ in range(8)]
        own_all.append(own)
        pc = np.array(own + list(range(64)))
        Kall = np.ascontiguousarray(Kg[:, :, pc, :]).reshape(H * 128, NSL * 256)
        Vall = np.ascontiguousarray(Vg[:, :, pc]).reshape(H * 128, NSL * 256)
        KMall = np.ascontiguousarray(KMg[:, :, pc]).reshape(128, H * NSL)
        QT = np.ascontiguousarray(Qg[:, :, np.array(own), :]).reshape(128, H * T)
        xT = np.concatenate([np.zeros((D, 2), np.float32), xTfull[:, np.array(own), :].reshape(D, T)], axis=1)
        past = np.zeros((8, NSL), np.float32)
        ownm = np.zeros((8, NSL), np.float32)
        for s_ in range(8):
            past[s_, 8:] = (np.arange(64) < own[s_]).astype(np.float32)
            ownm[s_, s_] = 1.0
        pastbias = np.where(past > 0, 0.0, -1e30).astype(np.float32)
        row = np.concatenate([pastbias.reshape(-1), past.reshape(-1), ownm.reshape(-1)])
        mcst = np.concatenate([np.broadcast_to(row[None, :], (128, 3 * 8 * NSL)), caus.transpose(1, 0, 2).reshape(128, 2048)], axis=1).astype(np.float32)
        maps.append(dict(xT=np.ascontiguousarray(xT), vec=vec, cst=make_cst(c), QT=QT, Kall=Kall, Vall=Vall, KMall=KMall,
                         mcst=np.ascontiguousarray(mcst), wout=wout))
    res2 = launch("mo2", maps)
    out = np.zeros((64, 256, D), np.float32)
    for c in range(NCORE):
        oc = np.asarray(res2[c]["outT"]).T.reshape(8, 256, D)
        for s_ in range(8):
            out[own_all[c][s_]] = oc[s_]
    return np.ascontiguousarray(out.reshape(64 * 256, D))


def kernel(**inputs):
    inp = {k: np.asarray(v) for k, v in inputs.items()}
    x = np.ascontiguousarray(inp["x"][0])
    for i in range(DEPTH):
        kind = i % 3
        if kind == 0:
            x = run_hgrn(inp, i, x)
        elif kind == 1:
            x = run_moba(inp, i, x)
        else:
            x = run_conv(inp, i, x)
        x = run_ffn(inp, i, x)
    return x[None].astype(np.float32)
```

```python
import math
import numpy as np
from contextlib import ExitStack
import concourse.bass as bass
import concourse.mybir as mybir
from concourse.bass_utils import run_bass_kernel_spmd

F32 = mybir.dt.float32
BF16 = mybir.dt.bfloat16
I32 = mybir.dt.int32
AF = mybir.ActivationFunctionType
ALU = mybir.AluOpType
AX = mybir.AxisListType

NCORE = 8
T = 2048
D = 1024
DC = 8
DFF = 2816
FC = 22
DEPTH = 4
ALPHA = (2.0 * DEPTH) ** 0.25
LN_EPS = 1e-5
RMS_EPS = 1e-6
NT = T // 512


class Buf:
    __slots__ = ("name", "w", "r")

    def __init__(self, name="b"):
        self.name = name
        self.w = None
        self.r = {}


class DSem:
    def __init__(self, key, h):
        self.key = key
        self.h = h
        self.cnt = 0


class Prog:
    ENG = ("pe", "act", "dve", "pool", "sp")

    def __init__(self, nc, es):
        self.nc = nc
        self.es = es
        self.eng = dict(pe=nc.tensor, act=nc.scalar, dve=nc.vector, pool=nc.gpsimd, sp=nc.sync)
        self.semh = {}
        self.cnt = {}
        self.known = {k: {} for k in self.ENG}
        for k in self.ENG:
            self.semh[k] = es.enter_context(nc.semaphore("s_" + k))
            self.cnt[k] = 0
        self.dsems = []
        self.nwait = 0
        self.nins = 0

    def dsem(self, name):
        h = self.es.enter_context(self.nc.semaphore("d_" + name))
        d = DSem("d_" + name, h)
        self.semh[d.key] = h
        self.dsems.append(d)
        return d

    def _wait(self, e, key, val):
        if val <= 0:
            return
        if self.known[e].get(key, 0) >= val:
            return
        if key == e:
            if e == "pe":
                return
            if val > self.cnt[e]:
                return
        self.eng[e].wait_ge(self.semh[key], val)
        self.nwait += 1
        self.known[e][key] = val

    def deps(self, e, reads, writes):
        need = {}
        for b in reads:
            if b.w is not None:
                k, v = b.w
                if need.get(k, 0) < v:
                    need[k] = v
        for b in writes:
            if b.w is not None:
                k, v = b.w
                if need.get(k, 0) < v:
                    need[k] = v
            for k, v in b.r.items():
                if need.get(k, 0) < v:
                    need[k] = v
        for k, v in need.items():
            self._wait(e, k, v)

    def op(self, e, fn, reads=(), writes=(), signal=True):
        self.deps(e, reads, writes)
        ins = fn(self.eng[e])
        self.nins += 1
        if signal:
            self.cnt[e] += 1
            ins.then_inc(self.semh[e], 1)
            mark = (e, self.cnt[e])
        else:
            mark = (e, self.cnt[e] + 1)
        for b in writes:
            b.w = mark
            b.r = {}
        for b in reads:
            if b.r.get(e, 0) < mark[1]:
                b.r[e] = mark[1]
        return ins

    def _mark_async(self, ds, reads, writes):
        mark = (ds.key, ds.cnt)
        for b in writes:
            b.w = mark
            b.r = {}
        for b in reads:
            b.r[ds.key] = ds.cnt

    def dma(self, q, out, in_, ds, reads=(), writes=(), **kw):
        self.deps(q, reads, writes)
        ds.cnt += 16
        ins = self.eng[q].dma_start(out=out, in_=in_, **kw)
        ins.then_inc(ds.h, 16)
        self._mark_async(ds, reads, writes)
        return ins

    def allgather(self, in_ap, out_ap, ds, reads=(), writes=()):
        q = "pool"
        self.deps(q, reads, writes)
        ds.cnt += 1
        ins = self.nc.gpsimd.collective_compute(
            "AllGather", ALU.bypass, replica_groups=[list(range(NCORE))],
            ins=[in_ap.opt()], outs=[out_ap.opt()])
        ins.then_inc(ds.h, 1)
        self._mark_async(ds, reads, writes)
        return ins

    def barrier(self, engines=None):
        engines = engines or self.ENG
        for e in engines:
            for k in self.ENG:
                if k != e:
                    self._wait(e, k, self.cnt[k])
            for d in self.dsems:
                self._wait(e, d.key, d.cnt)


def tile_w(W):
    Din, Fo = W.shape
    Wt = W.reshape(Din // 128, 128, Fo // 128, 128).transpose(2, 1, 0, 3)
    return np.ascontiguousarray(Wt).reshape(Fo // 128 * 128, Din)


def vec_cols(v):
    return np.ascontiguousarray(v.reshape(-1, 128).T)


V_LNG = 0
V_LNB = 8
V_CONV = 16
V_NORMG = 148
V_LBRAW = 156
NVEC = 188
C_HMASK = 0
C_LMASK = 8
C_MASK2 = 12
C_IDENT = 140
C_INVF = 268
C_SGN = 269
NCST = 270


def conv_cols(cw):
    return np.concatenate([vec_cols(cw[k]) for k in range(cw.shape[0])], axis=1)


def make_vec(ln_g, ln_b, conv=None, norm_g=None, lbraw=None):
    v = np.zeros((128, NVEC), np.float32)
    v[:, V_LNG:V_LNG + 8] = vec_cols(ln_g)
    v[:, V_LNB:V_LNB + 8] = vec_cols(ln_b)
    if conv is not None:
        c = conv_cols(conv)
        v[:, V_CONV:V_CONV + c.shape[1]] = c
    if norm_g is not None:
        v[:, V_NORMG:V_NORMG + 8] = vec_cols(norm_g)
    if lbraw is not None:
        v[:, V_LBRAW:V_LBRAW + 32] = np.concatenate([vec_cols(lbraw[l]) for l in range(DEPTH)], axis=1)
    return v


def make_cst(core, layer=0):
    c = np.zeros((128, NCST), np.float32)
    for r in range(NCORE):
        c[:, C_HMASK + r] = 1.0 if r < core else 0.0
    for l in range(DEPTH):
        c[:, C_LMASK + l] = 1.0 if 1 <= l <= layer else 0.0
    s_ = np.arange(128)[:, None]
    t_ = np.arange(128)[None, :]
    c[:, C_MASK2:C_MASK2 + 128] = ((s_ // 64 == t_ // 64) & (s_ <= t_)).astype(np.float32)
    c[:, C_IDENT:C_IDENT + 128] = np.eye(128, dtype=np.float32)
    invf = (1.0 / (500000.0 ** (np.arange(0, 32, 2, dtype=np.float32) / np.float32(32.0)))).astype(np.float32)
    c[0:32, C_INVF] = np.concatenate([invf, invf])
    c[0:16, C_SGN] = -1.0
    c[16:32, C_SGN] = 1.0
    return c


class Builder:
    def __init__(self, kind):
        self.kind = kind
        self.nc = bass.Bass("TRN2", target_bir_lowering=False)

    def dram(self, name, shape, dt, kind="Internal"):
        return self.nc.dram_tensor(name, list(shape), dt, kind=kind).ap()

    def sb(self, es, name, shape, dt):
        self._uid = getattr(self, "_uid", 0) + 1
        return es.enter_context(self.nc.sbuf_tensor(f"{name}_{self._uid}", list(shape), dt))

    def build(self):
        nc = self.nc
        kind = self.kind
        with ExitStack() as es:
            self.es = es
            P = self.P = Prog(nc, es)
            self.xT = self.dram("xT", [D, T + 2], F32, "ExternalInput")
            self.vec_d = self.dram("vec", [128, NVEC], F32, "ExternalInput")
            self.cst_d = self.dram("cst", [128, NCST], F32, "ExternalInput")
            self.wd = {}
            if kind == "ff":
                self.wd["wup"] = self.dram("wup", [2 * DFF, D], F32, "ExternalInput")
                self.wd["wdn"] = self.dram("wdn", [D, DFF], F32, "ExternalInput")
            elif kind == "cv":
                self.wd["win"] = self.dram("win", [3072, D], F32, "ExternalInput")
                self.wd["wout"] = self.dram("wout", [D, D], F32, "ExternalInput")
            elif kind == "hg":
                self.wd["win"] = self.dram("win", [4096, D], F32, "ExternalInput")
                self.wd["wout"] = self.dram("wout", [D, D], F32, "ExternalInput")
            elif kind == "hg1":
                self.wd["win"] = self.dram("win", [4096, D], F32, "ExternalInput")
            elif kind == "mo1":
                self.wd["win"] = self.dram("win", [5120, D], F32, "ExternalInput")
            elif kind == "mo2":
                self.wd["wout"] = self.dram("wout", [D, D], F32, "ExternalInput")
            if kind not in ("mo1", "hg1"):
                self.outT = self.dram("outT", [D, T], F32, "ExternalOutput")
            self.XR = self.sb(es, "XR", [128, DC, T], F32)
            self.XB = self.sb(es, "XB", [128, DC, T + 2], BF16)
            self.bXR = [Buf(f"XR{n}") for n in range(NT)]
            self.bXB = [Buf(f"XB{n}") for n in range(NT)]
            self.bXBh = Buf("XBh")
            self.vec = self.sb(es, "vecs", [128, NVEC], F32)
            self.cst = self.sb(es, "csts", [128, NCST], F32)
            self.bvec = Buf("vec")
            self.ones_ln = self.sb(es, "ones_ln", [128, 128], BF16)
            self.bconst = Buf("const")
            NW = 2 if kind == "mo2" else 8
            self.WS = [self.sb(es, f"ws{k}", [128, DC, 128], BF16) for k in range(NW)]
            self.bWS = [Buf(f"ws{k}") for k in range(NW)]
            self.dWS = [P.dsem(f"ws{k}") for k in range(NW)]
            self.wsi = 0
            self.PS = [es.enter_context(nc.psum_tensor(f"ps{k}", [128, 512], F32)) for k in range(8)]
            self.bPS = [Buf(f"ps{k}") for k in range(8)]
            self.psi = 0
            self.d_out = P.dsem("out")
            self.d_misc = P.dsem("misc")
            self.d_misc2 = P.dsem("misc2")
            self.bout = Buf("out")

            P.dma("sp", self.vec[:], self.vec_d, self.d_misc, writes=[self.bvec])
            P.dma("sp", self.cst[:], self.cst_d, self.d_misc2, writes=[self.bvec])
            xTv = self.xT.rearrange("(c p) t -> p c t", p=128)
            d_ins = [P.dsem(f"in{n}") for n in range(NT + 1)]
            for n in range(NT):
                P.dma("sp", self.XR[:, :, n * 512:(n + 1) * 512], xTv[:, :, 2 + n * 512:2 + (n + 1) * 512],
                      d_ins[n], writes=[self.bXR[n]])
            self.hal32 = self.sb(es, "hal32", [128, DC, 2], F32)
            self.bhal32 = Buf("hal32")
            P.dma("sp", self.hal32[:], xTv[:, :, 0:2], d_ins[NT], writes=[self.bhal32])
            P.op("dve", lambda e: e.memset(self.ones_ln[:], 1.0 / D), writes=[self.bconst])
            if kind != "mo2":
                for n in range(NT):
                    P.op("act", lambda e: e.activation(out=self.XB[:, :, 2 + n * 512:2 + (n + 1) * 512],
                                                       in_=self.XR[:, :, n * 512:(n + 1) * 512], func=AF.Copy),
                         reads=[self.bXR[n]], writes=[self.bXB[n]])
                P.op("act", lambda e: e.activation(out=self.XB[:, :, 0:2], in_=self.hal32[:], func=AF.Copy),
                     reads=[self.bhal32], writes=[self.bXBh])

            if kind == "ff":
                self.ffn()
            elif kind == "cv":
                self.conv_mixer()
            elif kind in ("hg", "hg1"):
                self.hgrn_mixer()
            elif kind == "mo1":
                self.moba_qkv()
            elif kind == "mo2":
                self.moba_attn()
            if kind not in ("mo1", "hg1"):
                self.layernorm(V_LNG, V_LNB)
                oTv = self.outT.rearrange("(c p) t -> p c t", p=128)
                for n in range(NT):
                    P.dma("sp", oTv[:, :, n * 512:(n + 1) * 512], self.XR[:, :, n * 512:(n + 1) * 512],
                          self.d_out, reads=[self.bXR[n]], writes=[self.bout])
            P.barrier()
        return nc

    def bank(self):
        rb = getattr(self, "rot_banks", (0, 1, 2, 3, 4, 5, 6, 7))
        self.psi = (self.psi + 1) % len(rb)
        k = rb[self.psi]
        return self.PS[k], self.bPS[k]

    def load_w(self, name, j, c0=0, nch=DC):
        P = self.P
        k = self.wsi
        self.wsi = (self.wsi + 1) % len(self.WS)
        slot, b, ds = self.WS[k], self.bWS[k], self.dWS[k]
        src = self.wd[name][j * 128:(j + 1) * 128, c0 * 128:(c0 + nch) * 128]
        dst = slot[:, 0:nch, :].rearrange("p c i -> p (c i)")
        P.dma("pool", dst, src, ds, writes=[b])
        return slot, b

    def xb_bufs(self, col0, n):
        bs = []
        if col0 < 2:
            bs.append(self.bXBh)
        lo = max(col0 - 2, 0)
        hi = col0 + n - 2
        for t in range(NT):
            if lo < (t + 1) * 512 and hi > t * 512:
                bs.append(self.bXB[t])
        return bs

    def proj(self, w, bw, col0, n, ps, bps, m0=0, m=128, src=None, srcbufs=None, nch=DC):
        P = self.P
        src = self.XB if src is None else src
        srcbufs = self.xb_bufs(col0, n) if srcbufs is None else srcbufs
        for c in range(nch):
            P.op("pe", lambda e: e.matmul(ps[0:m, 0:n], w[:, c, m0:m0 + m], src[:, c, col0:col0 + n],
                                          start=(c == 0), stop=(c == nch - 1)),
                 reads=[bw] + srcbufs, writes=[bps], signal=(c == nch - 1))

    def layernorm(self, goff, boff):
        P = self.P
        with ExitStack() as es:
            zb = [self.sb(es, f"ln_zb{k}", [128, DC, 512], BF16) for k in range(2)]
            zq = [self.sb(es, f"ln_zq{k}", [128, DC, 512], BF16) for k in range(2)]
            bzb = [Buf() for _ in range(2)]
            bzq = [Buf() for _ in range(2)]
            mean = [self.sb(es, f"ln_mean{k}", [128, 512], F32) for k in range(2)]
            rstd = [self.sb(es, f"ln_rstd{k}", [128, 512], F32) for k in range(2)]
            tmp = [self.sb(es, f"ln_tmp{k}", [128, 512], F32) for k in range(2)]
            bmean = [Buf() for _ in range(2)]
            brstd = [Buf() for _ in range(2)]
            btmp = [Buf() for _ in range(2)]
            ct = [self.sb(es, f"ln_ct{k}", [128, 512], F32) for k in range(4)]
            bct = [Buf() for _ in range(4)]
            cti = 0
            for n in range(NT):
                k = n % 2
                cs = slice(n * 512, (n + 1) * 512)
                P.op("act", lambda e: e.activation(out=zb[k][:], in_=self.XR[:, :, cs], func=AF.Copy),
                     reads=[self.bXR[n]], writes=[bzb[k]])
                P.op("act", lambda e: e.activation(out=zq[k][:], in_=self.XR[:, :, cs], func=AF.Square),
                     reads=[self.bXR[n]], writes=[bzq[k]])
                pm, bpm = self.bank()
                for c in range(DC):
                    P.op("pe", lambda e: e.matmul(pm[:, :], self.ones_ln[:], zb[k][:, c, :], start=(c == 0), stop=(c == DC - 1)),
                         reads=[self.bconst, bzb[k]], writes=[bpm], signal=(c == DC - 1))
                pq, bpq = self.bank()
                for c in range(DC):
                    P.op("pe", lambda e: e.matmul(pq[:, :], self.ones_ln[:], zq[k][:, c, :], start=(c == 0), stop=(c == DC - 1)),
                         reads=[self.bconst, bzq[k]], writes=[bpq], signal=(c == DC - 1))
                P.op("dve", lambda e: e.tensor_copy(out=mean[k][:], in_=pm[:, :]), reads=[bpm], writes=[bmean[k]])
                P.op("dve", lambda e: e.tensor_tensor(out=tmp[k][:], in0=mean[k][:], in1=mean[k][:], op=ALU.mult),
                     reads=[bmean[k]], writes=[btmp[k]])
                P.op("dve", lambda e: e.tensor_tensor(out=rstd[k][:], in0=pq[:, :], in1=tmp[k][:], op=ALU.subtract),
                     reads=[bpq, btmp[k]], writes=[brstd[k]])
                P.op("dve", lambda e: e.tensor_scalar(out=rstd[k][:], in0=rstd[k][:], scalar1=LN_EPS, scalar2=None,
                                                      op0=ALU.add),
                     reads=[brstd[k]], writes=[brstd[k]])
                P.op("act", lambda e: e.activation(out=rstd[k][:], in_=rstd[k][:], func=AF.Ln),
                     reads=[brstd[k]], writes=[brstd[k]])
                P.op("act", lambda e: e.activation(out=rstd[k][:], in_=rstd[k][:], func=AF.Exp, scale=-0.5),
                     reads=[brstd[k]], writes=[brstd[k]])
                bc = []
                for c in range(DC):
                    b_ = Buf()
                    b_.w = self.bXR[n].w
                    b_.r = dict(self.bXR[n].r)
                    bc.append(b_)
                for c in range(DC):
                    t_ = cti % len(ct)
                    cti += 1
                    P.op("dve", lambda e: e.tensor_tensor(out=ct[t_][:], in0=self.XR[:, c, cs], in1=mean[k][:], op=ALU.subtract),
                         reads=[bc[c], bmean[k]], writes=[bct[t_]])
                    P.op("dve", lambda e: e.tensor_tensor(out=ct[t_][:], in0=ct[t_][:], in1=rstd[k][:], op=ALU.mult),
                         reads=[bct[t_], brstd[k]], writes=[bct[t_]])
                    P.op("act", lambda e: e.activation(out=self.XR[:, c, cs], in_=ct[t_][:], func=AF.Identity,
                                                       bias=self.vec[:, boff + c:boff + c + 1],
                                                       scale=self.vec[:, goff + c:goff + c + 1]),
                         reads=[bct[t_], self.bvec], writes=[bc[c]])
                P.op("act", lambda e: e.activation(out=self.XB[:, :, 2 + n * 512:2 + (n + 1) * 512],
                                                   in_=self.XR[:, :, cs], func=AF.Copy),
                     reads=bc, writes=[self.bXB[n]])
                self.bXR[n].w = bc[DC - 1].w
                self.bXR[n].r = dict(bc[DC - 1].r)
            P.barrier()

    def out_proj_residual(self, wname, Y, bY, nch, c0=0, first=True, wtile_c0=0, coff=0):
        P = self.P
        for f in range(DC):
            w, bw = self.load_w(wname, f, c0=wtile_c0, nch=nch)
            for n in range(NT):
                ps, bps = self.bank()
                self.proj(w, bw, coff + n * 512, 512, ps, bps, src=Y, srcbufs=bY if isinstance(bY, list) else [bY], nch=nch)
                cs = slice(n * 512, (n + 1) * 512)
                if first:
                    P.op("dve", lambda e: e.scalar_tensor_tensor(out=self.XR[:, f, cs], in0=self.XR[:, f, cs], scalar=ALPHA,
                                                                 in1=ps[:, :], op0=ALU.mult, op1=ALU.add),
                         reads=[bps, self.bXR[n]], writes=[self.bXR[n]])
                else:
                    P.op("dve", lambda e: e.tensor_tensor(out=self.XR[:, f, cs], in0=self.XR[:, f, cs], in1=ps[:, :], op=ALU.add),
                         reads=[bps, self.bXR[n]], writes=[self.bXR[n]])

    def ffn(self):
        P = self.P
        cvo = V_CONV
        groups = [(0, 8), (8, 15), (15, 22)]
        with ExitStack() as es:
            GH = 8
            H = self.sb(es, "ffn_H", [128, GH, T], BF16)
            bH = [Buf() for _ in range(GH)]
            U = [[self.sb(es, f"ffn_U{ab}{k}", [128, 514], F32) for k in range(3)] for ab in range(2)]
            bU = [[Buf() for _ in range(3)] for _ in range(2)]
            Y = [[self.sb(es, f"ffn_Y{ab}{k}", [128, 512], F32) for k in range(2)] for ab in range(2)]
            bY = [[Buf() for _ in range(2)] for _ in range(2)]
            TT = [[self.sb(es, f"ffn_T{ab}{k}", [128, 512], F32) for k in range(2)] for ab in range(2)]
            bTT = [[Buf() for _ in range(2)] for _ in range(2)]
            SA = [self.sb(es, f"ffn_SA{k}", [128, 512], F32) for k in range(2)]
            bSA = [Buf() for _ in range(2)]
            ui = 0
            yi = 0
            for gi, (j0, j1) in enumerate(groups):
                for j in range(j0, j1):
                    ws = [self.load_w("wup", j), self.load_w("wup", FC + j)]
                    for n in range(NT):
                        uk = ui % 3
                        up = (ui - 1) % 3
                        ui += 1
                        yk = yi % 2
                        yi += 1
                        for ab in range(2):
                            w, bw = ws[ab]
                            u, bu = U[ab][uk], bU[ab][uk]
                            cj = j + ab * FC
                            if n == 0:
                                ph, bph = self.bank()
                                self.proj(w, bw, 0, 2, ph, bph)
                                P.op("act", lambda e: e.activation(out=u[:, 0:2], in_=ph[:, 0:2], func=AF.Copy),
                                     reads=[bph], writes=[bu])
                            else:
                                P.op("act", lambda e: e.activation(out=u[:, 0:2], in_=U[ab][up][:, 512:514], func=AF.Copy),
                                     reads=[bU[ab][up]], writes=[bu])
                            ps, bps = self.bank()
                            self.proj(w, bw, 2 + n * 512, 512, ps, bps)
                            P.op("act", lambda e: e.activation(out=u[:, 2:514], in_=ps[:, :], func=AF.Copy),
                                 reads=[bps], writes=[bu])
                            t, bt = TT[ab][yk], bTT[ab][yk]
                            y, by = Y[ab][yk], bY[ab][yk]
                            w0 = self.vec[:, cvo + cj:cvo + cj + 1]
                            w1 = self.vec[:, cvo + 44 + cj:cvo + 44 + cj + 1]
                            w2 = self.vec[:, cvo + 88 + cj:cvo + 88 + cj + 1]
                            P.op("act", lambda e: e.activation(out=t[:], in_=u[:, 0:512], func=AF.Copy, scale=w0),
                                 reads=[bu, self.bvec], writes=[bt])
                            P.op("dve", lambda e: e.scalar_tensor_tensor(out=t[:], in0=u[:, 1:513], scalar=w1, in1=t[:],
                                                                         op0=ALU.mult, op1=ALU.add),
                                 reads=[bu, bt, self.bvec], writes=[bt])
                            P.op("dve", lambda e: e.scalar_tensor_tensor(out=y[:], in0=u[:, 2:514], scalar=w2, in1=t[:],
                                                                         op0=ALU.mult, op1=ALU.add),
                                 reads=[bu, bt, self.bvec], writes=[by])
                        sa, bsa = SA[yk], bSA[yk]
                        P.op("act", lambda e: e.activation(out=sa[:], in_=Y[0][yk][:], func=AF.Silu),
                             reads=[bY[0][yk]], writes=[bsa])
                        P.op("dve", lambda e: e.tensor_tensor(out=H[:, j - j0, n * 512:(n + 1) * 512], in0=sa[:], in1=Y[1][yk][:],
                                                              op=ALU.mult),
                             reads=[bsa, bY[1][yk]], writes=[bH[j - j0]])
                self.out_proj_residual("wdn", H, bH[:j1 - j0], j1 - j0, first=(gi == 0), wtile_c0=j0)
            P.barrier()

    def conv_mixer(self):
        P = self.P
        cvo = V_CONV
        with ExitStack() as es:
            Yo = self.sb(es, "cm_Y", [128, DC, T], BF16)
            bYo = [Buf() for _ in range(DC)]
            PR = [self.sb(es, f"cm_P{k}", [128, 514], F32) for k in range(3)]
            bPR = [Buf() for _ in range(3)]
            CG = [self.sb(es, f"cm_CG{k}", [128, 514], F32) for k in range(2)]
            bCG = [Buf() for _ in range(2)]
            TT = [self.sb(es, f"cm_T{k}", [128, 512], F32) for k in range(2)]
            bTT = [Buf() for _ in range(2)]
            ui = 0
            for c in range(DC):
                wbg = self.load_w("win", c)
                wcg = self.load_w("win", DC + c)
                wh = self.load_w("win", 2 * DC + c)
                w0 = self.vec[:, cvo + c:cvo + c + 1]
                w1 = self.vec[:, cvo + 8 + c:cvo + 8 + c + 1]
                w2 = self.vec[:, cvo + 16 + c:cvo + 16 + c + 1]
                for n in range(NT):
                    uk = ui % 3
                    up = (ui - 1) % 3
                    k2 = ui % 2
                    ui += 1
                    pr, bpr = PR[uk], bPR[uk]
                    cg, bcg = CG[k2], bCG[k2]
                    t, bt = TT[k2], bTT[k2]
                    if n == 0:
                        p1, bp1 = self.bank()
                        self.proj(wcg[0], wcg[1], 0, 2, p1, bp1)
                        P.op("act", lambda e: e.activation(out=cg[:, 0:2], in_=p1[:, 0:2], func=AF.Copy), reads=[bp1], writes=[bcg])
                        p2, bp2 = self.bank()
                        self.proj(wh[0], wh[1], 0, 2, p2, bp2)
                        P.op("dve", lambda e: e.tensor_tensor(out=pr[:, 0:2], in0=cg[:, 0:2], in1=p2[:, 0:2], op=ALU.mult),
                             reads=[bcg, bp2], writes=[bpr])
                    else:
                        P.op("act", lambda e: e.activation(out=pr[:, 0:2], in_=PR[up][:, 512:514], func=AF.Copy),
                             reads=[bPR[up]], writes=[bpr])
                    p1, bp1 = self.bank()
                    self.proj(wcg[0], wcg[1], 2 + n * 512, 512, p1, bp1)
                    P.op("act", lambda e: e.activation(out=cg[:, 2:514], in_=p1[:, :], func=AF.Copy), reads=[bp1], writes=[bcg])
                    p2, bp2 = self.bank()
                    self.proj(wh[0], wh[1], 2 + n * 512, 512, p2, bp2)
                    P.op("dve", lambda e: e.tensor_tensor(out=pr[:, 2:514], in0=cg[:, 2:514], in1=p2[:, :], op=ALU.mult),
                         reads=[bcg, bp2], writes=[bpr])
                    p3, bp3 = self.bank()
                    self.proj(wbg[0], wbg[1], 2 + n * 512, 512, p3, bp3)
                    P.op("act", lambda e: e.activation(out=t[:], in_=pr[:, 0:512], func=AF.Copy, scale=w0),
                         reads=[bpr, self.bvec], writes=[bt])
                    P.op("dve", lambda e: e.scalar_tensor_tensor(out=t[:], in0=pr[:, 1:513], scalar=w1, in1=t[:],
                                                                 op0=ALU.mult, op1=ALU.add),
                         reads=[bpr, bt, self.bvec], writes=[bt])
                    P.op("dve", lambda e: e.scalar_tensor_tensor(out=t[:], in0=pr[:, 2:514], scalar=w2, in1=t[:],
                                                                 op0=ALU.mult, op1=ALU.add),
                         reads=[bpr, bt, self.bvec], writes=[bt])
                    P.op("dve", lambda e: e.tensor_tensor(out=Yo[:, c, n * 512:(n + 1) * 512], in0=t[:], in1=p3[:, :], op=ALU.mult),
                         reads=[bt, bp3], writes=[bYo[c]])
            self.out_proj_residual("wout", Yo, bYo, DC, first=True)
            P.barrier()

    def hgrn_mixer(self):
        P = self.P
        nc = self.nc
        H = 8
        lite = (self.kind == "hg1")
        with ExitStack() as es:
            self.Aall = self.dram("Aall", [128, NCORE * H], F32, "ExternalInput")
            self.Ball = self.dram("Ball", [NCORE * 128, H * 128], F32, "ExternalInput")
            self.Aout = self.dram("Aout", [128, H], F32, "ExternalOutput")
            self.Bout = self.dram("Bout", [128, H * 128], F32, "ExternalOutput")
            G = [self.sb(es, f"hg_G{k}", [128, 512], BF16) for k in range(2)]
            bG = [Buf() for _ in range(2)]
            Y = self.sb(es, "hg_Y", [128, H, T], BF16)
            bY = [Buf() for _ in range(H)]
            S = self.sb(es, "hg_S", [128, H, 128], F32)
            bS = [Buf() for _ in range(H)]
            Aall = self.sb(es, "hg_Aall", [128, NCORE, H], F32)
            bAall = Buf()
            ap_ = self.sb(es, "hg_ap", [128, H], F32)
            bap = Buf()
            om = self.sb(es, "hg_om", [128, NCORE], F32)
            bom = Buf()
            Bm = [self.sb(es, f"hg_Bm{k}", [128, H, 128], F32) for k in range(1)]
            bBm = [Buf() for _ in range(1)]
            dBm = [P.dsem(f"bm{k}") for k in range(1)]
            d_a = P.dsem("aall")
            lbe = self.sb(es, "hg_lbe", [128, 4, H], F32)
            lb = self.sb(es, "hg_lb", [128, H], F32)
            oml = self.sb(es, "hg_oml", [128, H], F32)
            lbm1 = self.sb(es, "hg_lbm1", [128, H], F32)
            lsum = self.sb(es, "hg_lsum", [128, H], F32)
            blb = Buf()
            rmask = self.sb(es, "hg_rmask", [128, 512], F32)
            ident = self.sb(es, "hg_ident", [128, 128], BF16)
            ones128 = self.sb(es, "hg_ones", [128, 128], BF16)
            bcn = Buf()
            blsum = self.sb(es, "hg_blsum", [128, H], F32)
            bblsum = Buf()
            P.op("dve", lambda e: e.memset(rmask[:], 1.0), writes=[bcn])
            P.op("dve", lambda e: e.memset(rmask[:].rearrange("p (c t) -> p c t", t=64)[:, :, 0:1], 0.0), writes=[bcn])
            P.op("dve", lambda e: e.memset(ones128[:], 1.0 / 128.0), writes=[bcn])
            one1 = self.sb(es, "hg_one1", [128, 1], F32)
            P.op("dve", lambda e: e.memset(one1[:], 1.0), writes=[bcn])
            P.op("dve", lambda e: e.memset(blsum[:], 0.0), writes=[bblsum])
            P.op("act", lambda e: e.activation(out=ident[:], in_=self.cst[:, C_IDENT:C_IDENT + 128], func=AF.Copy),
                 reads=[self.bvec], writes=[bcn])
            P.op("act", lambda e: e.activation(out=lbe[:].rearrange("p l h -> p (l h)"), in_=self.vec[:, V_LBRAW:V_LBRAW + 32], func=AF.Exp),
                 reads=[self.bvec], writes=[blb])
            P.op("dve", lambda e: e.tensor_tensor(out=lsum[:], in0=lbe[:, 0, :], in1=lbe[:, 1, :], op=ALU.add), reads=[blb], writes=[blb])
            P.op("dve", lambda e: e.tensor_tensor(out=lsum[:], in0=lsum[:], in1=lbe[:, 2, :], op=ALU.add), reads=[blb], writes=[blb])
            P.op("dve", lambda e: e.tensor_tensor(out=lsum[:], in0=lsum[:], in1=lbe[:, 3, :], op=ALU.add), reads=[blb], writes=[blb])
            P.op("dve", lambda e: e.reciprocal(out=lsum[:], in_=lsum[:]), reads=[blb], writes=[blb])
            P.op("dve", lambda e: e.tensor_scalar(out=lb[:], in0=lbe[:, 1, :], scalar1=self.cst[:, C_LMASK + 1:C_LMASK + 2], scalar2=None, op0=ALU.mult),
                 reads=[blb, self.bvec], writes=[blb])
            for l in (2, 3):
                P.op("dve", lambda e: e.scalar_tensor_tensor(out=lb[:], in0=lbe[:, l, :], scalar=self.cst[:, C_LMASK + l:C_LMASK + l + 1],
                                                             in1=lb[:], op0=ALU.mult, op1=ALU.add),
                     reads=[blb, self.bvec], writes=[blb])
            P.op("dve", lambda e: e.tensor_tensor(out=lb[:], in0=lb[:], in1=lsum[:], op=ALU.mult), reads=[blb], writes=[blb])
            P.op("dve", lambda e: e.tensor_scalar(out=lbm1[:], in0=lb[:], scalar1=-1.0, scalar2=None, op0=ALU.add), reads=[blb], writes=[blb])
            P.op("dve", lambda e: e.tensor_scalar(out=oml[:], in0=lbm1[:], scalar1=-1.0, scalar2=None, op0=ALU.mult), reads=[blb], writes=[blb])
            P.dma("sp", Aall[:].rearrange("p r h -> p (r h)"), self.Aall, d_a, writes=[bAall])
            P.op("dve", lambda e: e.memset(S[:], 0.0), writes=bS)
            P.op("dve", lambda e: e.tensor_scalar(out=om[:], in0=self.cst[:, C_HMASK:C_HMASK + NCORE], scalar1=-1.0, scalar2=1.0,
                                                  op0=ALU.mult, op1=ALU.add), reads=[self.bvec], writes=[bom])
            for r in range(NCORE - 1):
                k = 0
                P.dma("sp", Bm[k][:].rearrange("p h v -> p (h v)"), self.Ball[r * 128:(r + 1) * 128, :], dBm[k], writes=[bBm[k]])
                mr = self.cst[:, C_HMASK + r:C_HMASK + r + 1]
                P.op("dve", lambda e: e.tensor_scalar(out=ap_[:], in0=Aall[:, r, :], scalar1=mr, scalar2=om[:, r:r + 1], op0=ALU.mult, op1=ALU.add),
                     reads=[bAall, self.bvec, bom], writes=[bap])
                P.op("dve", lambda e: e.tensor_scalar(out=Bm[k][:], in0=Bm[k][:], scalar1=mr, scalar2=None, op0=ALU.mult),
                     reads=[bBm[k], self.bvec], writes=[bBm[k]])
                for h in range(H):
                    P.op("dve", lambda e: e.scalar_tensor_tensor(out=S[:, h, :], in0=S[:, h, :], scalar=ap_[:, h:h + 1], in1=Bm[k][:, h, :],
                                                                 op0=ALU.mult, op1=ALU.add),
                         reads=[bS[h], bap, bBm[k]], writes=[bS[h]])
            R2 = lambda nm, shp, dt: [self.sb(es, f"{nm}{k}", shp, dt) for k in range(2)]
            sig, bsig = R2("hg_sig", [128, 512], F32), [Buf(), Buf()]
            qs, bqs = R2("hg_qs", [128, 512], F32), [Buf(), Buf()]
            R1 = lambda nm, shp, dt: [self.sb(es, nm, shp, dt)] * 2
            B1 = lambda: [Buf()] * 2
            kk, bkk = R1("hg_k", [128, 512], F32), B1()
            gg, bgg = R1("hg_g", [128, 512], F32), B1()
            bb, bbb = R1("hg_b", [128, 512], F32), B1()
            e1, be1 = R2("hg_e1", [128, 512], F32), [Buf(), Buf()]
            e2, be2 = R1("hg_e2", [128, 512], F32), B1()
            ebl, bebl = R2("hg_ebl", [128, 8], F32), [Buf(), Buf()]
            qp, bqp = R2("hg_qp", [128, 512], BF16), [Buf(), Buf()]
            kp, bkp = R2("hg_kp", [128, 512], BF16), [Buf(), Buf()]
            ktok, bktok = R2("hg_ktok", [128, 4, 128], BF16), [Buf(), Buf()]
            vtok, bvtok = R2("hg_vtok", [128, 4, 128], BF16), [Buf(), Buf()]
            am, bam = R2("hg_am", [128, 128], BF16), [Buf(), Buf()]
            sdb = [self.sb(es, f"hg_sdb{c_}", [128, 128], BF16) for c_ in range(8)]
            bsdb = [Buf() for _ in range(8)]
            osq, bosq = R2("hg_osq", [128, 512], BF16), [Buf(), Buf()]
            rr, brr = R2("hg_rr", [128, 512], F32), [Buf(), Buf()]
            yy, byy = R2("hg_yy", [128, 512], F32), [Buf(), Buf()]
            it = 0
            sdi = 0
            ami = 0
            self.rot_banks = (4, 5, 6, 7)
            cnts = {"ami": 0}

            def stage1(h, n, k, Wh):
                wq, wf, wi, wg, lbh, omlh, lbm1h = Wh
                c0 = 2 + n * 512
                cs = slice(n * 512, (n + 1) * 512)
                if not lite:
                    pq, bpq = self.bank()
                    self.proj(wq[0], wq[1], c0, 512, pq, bpq)
                pf, bpf = self.bank()
                self.proj(wf[0], wf[1], c0, 512, pf, bpf)
                if not lite:
                    pg, bpg = self.bank()
                    self.proj(wg[0], wg[1], c0, 512, pg, bpg)
                pv, bpv = self.bank()
                for s4 in range(4):
                    for c in range(DC):
                        P.op("pe", lambda e: e.matmul(pv[:, s4 * 128:(s4 + 1) * 128], self.XB[:, c, c0 + s4 * 128:c0 + (s4 + 1) * 128],
                                                      wi[0][:, c, :], start=(c == 0), stop=(c == DC - 1)),
                             reads=[wi[1], self.bXB[n]], writes=[bpv], signal=(c == DC - 1 and s4 == 3))
                P.op("act", lambda e: e.activation(out=sig[k][:], in_=pf[:, :], func=AF.Exp, scale=-1.0), reads=[bpf], writes=[bsig[k]])
                P.op("act", lambda e: e.activation(out=sig[k][:], in_=sig[k][:], func=AF.Ln, bias=one1[:, 0:1]), reads=[bsig[k], bcn], writes=[bsig[k]])
                P.op("act", lambda e: e.activation(out=sig[k][:], in_=sig[k][:], func=AF.Exp, scale=-1.0), reads=[bsig[k]], writes=[bsig[k]])
                if not lite:
                    P.op("act", lambda e: e.activation(out=qs[k][:], in_=pq[:, :], func=AF.Silu), reads=[bpq], writes=[bqs[k]])
                    P.op("act", lambda e: e.activation(out=G[k][:], in_=pg[:, :], func=AF.Silu), reads=[bpg], writes=[bG[k]])
                P.op("act", lambda e: e.activation(out=vtok[k][:].rearrange("p s v -> p (s v)"), in_=pv[:, :], func=AF.Copy),
                     reads=[bpv], writes=[bvtok[k]])
                P.op("dve", lambda e: e.tensor_scalar(out=kk[k][:], in0=sig[k][:], scalar1=lbm1h, scalar2=omlh, op0=ALU.mult, op1=ALU.add),
                     reads=[bsig[k], blb], writes=[bkk[k]])
                P.op("act", lambda e: e.activation(out=gg[k][:], in_=sig[k][:], func=AF.Ln, bias=lbh, scale=omlh),
                     reads=[bsig[k], blb], writes=[bgg[k]])
                P.op("dve", lambda e: e.tensor_tensor_scan(out=bb[k][:], data0=rmask[:], data1=gg[k][:], initial=0.0, op0=ALU.mult, op1=ALU.add),
                     reads=[bgg[k], bcn], writes=[bbb[k]])
                b3 = bb[k][:].rearrange("p (c t) -> p c t", t=64)
                P.op("dve", lambda e: e.tensor_tensor(out=e1[k][:].rearrange("p (c t) -> p c t", t=64), in0=b3,
                                                      in1=b3[:, :, 63:64].to_broadcast([128, 8, 64]), op=ALU.subtract),
                     reads=[bbb[k]], writes=[be1[k]])
                P.op("act", lambda e: e.activation(out=e2[k][:], in_=e1[k][:], func=AF.Exp, scale=-1.0), reads=[be1[k]], writes=[be2[k]])
                if not lite:
                    P.op("act", lambda e: e.activation(out=e1[k][:], in_=e1[k][:], func=AF.Exp), reads=[be1[k], be2[k]], writes=[be1[k]])
                P.op("act", lambda e: e.activation(out=ebl[k][:], in_=b3[:, :, 63], func=AF.Exp), reads=[bbb[k]], writes=[bebl[k]])
                P.op("dve", lambda e: e.tensor_reduce(out=rr[k][:, 0:1], in_=b3[:, :, 63], axis=AX.X, op=ALU.add), reads=[bbb[k]], writes=[brr[k]])
                P.op("dve", lambda e: e.tensor_tensor(out=blsum[:, h:h + 1], in0=blsum[:, h:h + 1], in1=rr[k][:, 0:1], op=ALU.add),
                     reads=[brr[k], bblsum], writes=[bblsum])
                if not lite:
                    P.op("dve", lambda e: e.tensor_tensor(out=qp[k][:], in0=qs[k][:], in1=e1[k][:], op=ALU.mult), reads=[bqs[k], be1[k]], writes=[bqp[k]])
                P.op("dve", lambda e: e.tensor_tensor(out=kp[k][:], in0=kk[k][:], in1=e2[k][:], op=ALU.mult), reads=[bkk[k], be2[k]], writes=[bkp[k]])

            def stage2(h, n, k):
                ami = cnts["ami"]
                cs = slice(n * 512, (n + 1) * 512)
                pt, bpt = self.bank()
                ptb = pt[:].bitcast(BF16)
                for s4 in range(4):
                    P.op("pe", lambda e: e.transpose(ptb[:, s4 * 128:(s4 + 1) * 128], kp[k][:, s4 * 128:(s4 + 1) * 128], ident[:]),
                         reads=[bkp[k], bcn], writes=[bpt], signal=(s4 == 3))
                P.op("act", lambda e: e.activation(out=ktok[k][:].rearrange("p s v -> p (s v)"), in_=ptb[:, 0:512], func=AF.Copy),
                     reads=[bpt], writes=[bktok[k]])
                po, bpo = self.PS[k], self.bPS[k]
                for ci in range(8):
                    s4, half = ci // 2, ci % 2
                    rows = slice(half * 64, half * 64 + 64)
                    pu, bpu = self.PS[2 + half], self.bPS[2 + half]
                    P.op("pe", lambda e: e.matmul(pu[:, s4 * 128:(s4 + 1) * 128], ktok[k][rows, s4, :], vtok[k][rows, s4, :],
                                                  start=True, stop=True),
                         reads=[bktok[k], bvtok[k]], writes=[bpu], signal=(ci >= 6))
                for ci in range(8):
                    s4, half = ci // 2, ci % 2
                    eb = ebl[k][:, ci:ci + 1]
                    pu, bpu = self.PS[2 + half], self.bPS[2 + half]
                    if not lite:
                        P.op("dve", lambda e: e.tensor_scalar(out=sdb[ci][:], in0=S[:, h, :], scalar1=eb, scalar2=None, op0=ALU.mult),
                             reads=[bS[h], bebl[k]], writes=[bsdb[ci]])
                    P.op("dve", lambda e: e.scalar_tensor_tensor(out=S[:, h, :], in0=S[:, h, :], scalar=eb, in1=pu[:, s4 * 128:(s4 + 1) * 128],
                                                                 op0=ALU.mult, op1=ALU.add),
                         reads=[bS[h], bebl[k], bpu], writes=[bS[h]])
                for s4 in range(0 if lite else 4):
                    sl = slice(s4 * 128, (s4 + 1) * 128)
                    pa, bpa = self.bank()
                    P.op("pe", lambda e: e.matmul(pa[:, 0:128], kp[k][:, sl], qp[k][:, sl], start=True, stop=True),
                         reads=[bkp[k], bqp[k]], writes=[bpa])
                    a_ = ami % 2
                    ami += 1
                    P.op("dve", lambda e: e.tensor_tensor(out=am[a_][:], in0=pa[:, 0:128], in1=self.cst[:, C_MASK2:C_MASK2 + 128], op=ALU.mult),
                         reads=[bpa, self.bvec], writes=[bam[a_]])
                    P.op("pe", lambda e: e.matmul(po[:, sl], vtok[k][:, s4, :], am[a_][:], start=True, stop=False),
                         reads=[bvtok[k], bam[a_]], writes=[bpo], signal=False)
                    for half in range(2):
                        ci = s4 * 2 + half
                        hs = slice(s4 * 128 + half * 64, s4 * 128 + half * 64 + 64)
                        P.op("pe", lambda e: e.matmul(po[:, hs], sdb[ci][:], qp[k][:, hs], start=False, stop=(half == 1)),
                             reads=[bsdb[ci], bqp[k]], writes=[bpo], signal=True)
                if lite:
                    return
                P.op("act", lambda e: e.activation(out=osq[k][:], in_=po[:, :], func=AF.Square), reads=[bpo], writes=[bosq[k]])
                pm, bpm = self.bank()
                P.op("pe", lambda e: e.matmul(pm[:, :], ones128[:], osq[k][:], start=True, stop=True), reads=[bcn, bosq[k]], writes=[bpm])
                P.op("dve", lambda e: e.tensor_scalar(out=rr[k][:], in0=pm[:, :], scalar1=RMS_EPS, scalar2=None, op0=ALU.add),
                     reads=[bpm], writes=[brr[k]])
                P.op("act", lambda e: e.activation(out=rr[k][:], in_=rr[k][:], func=AF.Ln), reads=[brr[k]], writes=[brr[k]])
                P.op("act", lambda e: e.activation(out=rr[k][:], in_=rr[k][:], func=AF.Exp, scale=-0.5), reads=[brr[k]], writes=[brr[k]])
                P.op("dve", lambda e: e.tensor_tensor(out=yy[k][:], in0=po[:, :], in1=rr[k][:], op=ALU.mult), reads=[bpo, brr[k]], writes=[byy[k]])
                P.op("dve", lambda e: e.scalar_tensor_tensor(out=Y[:, h, cs], in0=yy[k][:], scalar=self.vec[:, V_NORMG + h:V_NORMG + h + 1],
                                                             in1=G[k][:], op0=ALU.mult, op1=ALU.mult),
                     reads=[byy[k], self.bvec, bG[k]], writes=[bY[h]])

            def wts(h):
                return (self.load_w("win", h), self.load_w("win", H + h), self.load_w("win", 2 * H + h), self.load_w("win", 3 * H + h),
                        lb[:, h:h + 1], oml[:, h:h + 1], lbm1[:, h:h + 1])
            for hp in range(0, H, 2):
                WA, WB = wts(hp), wts(hp + 1)
                for n in range(NT):
                    stage1(hp, n, 0, WA)
                    stage1(hp + 1, n, 1, WB)
                    stage2(hp, n, 0)
                    stage2(hp + 1, n, 1)
            P.op("act", lambda e: e.activation(out=blsum[:], in_=blsum[:], func=AF.Exp), reads=[bblsum], writes=[bblsum])
            d_o = P.dsem("hgo")
            P.dma("sp", self.Aout, blsum[:], d_o, reads=[bblsum], writes=[self.bout])
            P.dma("sp", self.Bout, S[:].rearrange("p h v -> p (h v)"), d_o, reads=bS, writes=[self.bout])
            self.rot_banks = (0, 1, 2, 3, 4, 5, 6, 7)
            if not lite:
                self.out_proj_residual("wout", Y, bY, DC, first=True)
            P.barrier()

    def moba_qkv(self):
        P = self.P
        H = 8
        with ExitStack() as es:
            self.pos_d = self.dram("pos", [1, T], I32, "ExternalInput")
            self.QTo = self.dram("QT", [128, H * T], BF16, "ExternalOutput")
            self.KTo = self.dram("KT", [128, H * T], BF16, "ExternalOutput")
            self.Vo = self.dram("V", [128, 16 * 1024], BF16, "ExternalOutput")
            self.KMo = self.dram("KM", [128, H * 8], F32, "ExternalOutput")
            posi = self.sb(es, "mq_posi", [32, T], I32)
            ang = self.sb(es, "mq_ang", [32, T], F32)
            tmp = self.sb(es, "mq_tmp", [32, T], F32)
            Ct = self.sb(es, "mq_C", [32, T], F32)
            St = self.sb(es, "mq_S", [32, T], F32)
            npi = self.sb(es, "mq_npi", [32, 1], F32)
            km = self.sb(es, "mq_km", [128, H, 8], F32)
            btab, bkm = Buf(), Buf()
            d_p = P.dsem("pos")
            P.dma("sp", posi[:], self.pos_d.partition_broadcast(32), d_p, writes=[btab])
            P.op("dve", lambda e: e.tensor_copy(out=ang[:], in_=posi[:]), reads=[btab], writes=[btab])
            P.op("dve", lambda e: e.memset(npi[:], -math.pi), writes=[btab])
            P.op("dve", lambda e: e.tensor_scalar(out=ang[:], in0=ang[:], scalar1=self.cst[0:32, C_INVF:C_INVF + 1], scalar2=None, op0=ALU.mult),
                 reads=[btab, self.bvec], writes=[btab])
            ki = self.sb(es, "mq_ki", [32, T], I32)
            kfl = self.sb(es, "mq_kfl", [32, T], F32)
            for (dst_, off_) in ((St, 0.5), (Ct, 0.75)):
                P.op("dve", lambda e: e.tensor_scalar(out=tmp[:], in0=ang[:], scalar1=1.0 / (2 * math.pi), scalar2=off_, op0=ALU.mult, op1=ALU.add),
                     reads=[btab], writes=[btab])
                P.op("dve", lambda e: e.tensor_copy(out=ki[:], in_=tmp[:]), reads=[btab], writes=[btab])
                P.op("dve", lambda e: e.tensor_copy(out=kfl[:], in_=ki[:]), reads=[btab], writes=[btab])
                P.op("dve", lambda e: e.tensor_tensor(out=tmp[:], in0=tmp[:], in1=kfl[:], op=ALU.subtract), reads=[btab], writes=[btab])
                P.op("dve", lambda e: e.tensor_scalar(out=kfl[:], in0=tmp[:], scalar1=0.0, scalar2=None, op0=ALU.is_lt), reads=[btab], writes=[btab])
                P.op("dve", lambda e: e.tensor_tensor(out=tmp[:], in0=tmp[:], in1=kfl[:], op=ALU.add), reads=[btab], writes=[btab])
                P.op("act", lambda e: e.activation(out=dst_[:], in_=tmp[:], func=AF.Sin, bias=npi[:, 0:1], scale=2 * math.pi), reads=[btab], writes=[btab])
            P.op("dve", lambda e: e.tensor_scalar(out=St[:], in0=St[:], scalar1=self.cst[0:32, C_SGN:C_SGN + 1], scalar2=None, op0=ALU.mult),
                 reads=[btab, self.bvec], writes=[btab])
            R2 = lambda nm, shp, dt: [self.sb(es, f"{nm}{k}", shp, dt) for k in range(2)]
            t1, bt1 = R2("mq_t1", [32, 512], F32), [Buf(), Buf()]
            t2, bt2 = R2("mq_t2", [32, 512], F32), [Buf(), Buf()]
            kf, bkf = R2("mq_kf", [128, 512], F32), [Buf(), Buf()]
            ob, bob = R2("mq_ob", [128, 512], BF16), [Buf(), Buf()]
            vb, bvb = R2("mq_vb", [128, 4, 128], BF16), [Buf(), Buf()]
            dob = [P.dsem(f"ob{k}") for k in range(2)]
            dvb = [P.dsem(f"vb{k}") for k in range(2)]
            Vov = self.Vo.rearrange("p (t f) -> p t f", f=1024)
            it = 0
            for h in range(H):
                for qk in range(2):
                    w = self.load_w("win", qk * H + h)
                    wp = self.load_w("win", 3 * H + qk * H + h)
                    dst = self.QTo if qk == 0 else self.KTo
                    for n in range(NT):
                        k = it % 2
                        it += 1
                        c0 = 2 + n * 512
                        cs = slice(n * 512, (n + 1) * 512)
                        pa, bpa = self.bank()
                        self.proj(w[0], w[1], c0, 512, pa, bpa)
                        pb, bpb = self.bank()
                        self.proj(wp[0], wp[1], c0, 512, pb, bpb, m=32)
                        P.op("dve", lambda e: e.tensor_tensor(out=t1[k][:], in0=pa[0:32, :], in1=Ct[:, cs], op=ALU.mult), reads=[bpa, btab], writes=[bt1[k]])
                        P.op("dve", lambda e: e.tensor_tensor(out=t2[k][:], in0=pb[0:32, :], in1=St[:, cs], op=ALU.mult), reads=[bpb, btab], writes=[bt2[k]])
                        P.op("dve", lambda e: e.tensor_tensor(out=kf[k][0:32, :], in0=t1[k][:], in1=t2[k][:], op=ALU.add), reads=[bt1[k], bt2[k]], writes=[bkf[k]])
                        P.op("act", lambda e: e.activation(out=kf[k][32:64, :], in_=pa[32:64, :], func=AF.Copy), reads=[bpa], writes=[bkf[k]])
                        P.op("act", lambda e: e.activation(out=kf[k][64:128, :], in_=pa[64:128, :], func=AF.Copy), reads=[bpa], writes=[bkf[k]])
                        P.op("act", lambda e: e.activation(out=ob[k][:], in_=kf[k][:], func=AF.Copy), reads=[bkf[k]], writes=[bob[k]])
                        if qk == 1:
                            P.op("dve", lambda e: e.tensor_reduce(out=km[:, h, 2 * n:2 * n + 2], in_=kf[k][:].rearrange("p (b t) -> p b t", t=256),
                                                                  axis=AX.X, op=ALU.add), reads=[bkf[k]], writes=[bkm])
                        P.dma("sp", dst[:, h * T + n * 512:h * T + (n + 1) * 512], ob[k][:], dob[k], reads=[bob[k]], writes=[self.bout])
                wv = self.load_w("win", 2 * H + h)
                for n in range(NT):
                    k = it % 2
                    it += 1
                    c0 = 2 + n * 512
                    pv, bpv = self.bank()
                    for s4 in range(4):
                        for c in range(DC):
                            P.op("pe", lambda e: e.matmul(pv[:, s4 * 128:(s4 + 1) * 128], self.XB[:, c, c0 + s4 * 128:c0 + (s4 + 1) * 128],
                                                          wv[0][:, c, :], start=(c == 0), stop=(c == DC - 1)),
                                 reads=[wv[1], self.bXB[n]], writes=[bpv], signal=(c == DC - 1 and s4 == 3))
                    P.op("act", lambda e: e.activation(out=vb[k][:].rearrange("p s v -> p (s v)"), in_=pv[:, :], func=AF.Copy), reads=[bpv], writes=[bvb[k]])
                    P.dma("sp", Vov[:, n * 4:(n + 1) * 4, h * 128:(h + 1) * 128], vb[k][:], dvb[k], reads=[bvb[k]], writes=[self.bout])
            P.op("dve", lambda e: e.tensor_scalar(out=km[:], in0=km[:], scalar1=1.0 / 256.0, scalar2=None, op0=ALU.mult), reads=[bkm], writes=[bkm])
            d_k = P.dsem("kmo")
            P.dma("sp", self.KMo, km[:].rearrange("p h b -> p (h b)"), d_k, reads=[bkm], writes=[self.bout])
            P.barrier()

    def moba_attn(self):
        P = self.P
        H = 8
        NS = 72
        SCALE = 1.0 / math.sqrt(128.0)
        with ExitStack() as es:
            self.QTi = self.dram("QT", [128, H * T], BF16, "ExternalInput")
            self.Kall = self.dram("Kall", [H * 128, NS * 256], BF16, "ExternalInput")
            self.Vall = self.dram("Vall", [H * 128, NS * 256], BF16, "ExternalInput")
            self.KMall = self.dram("KMall", [128, H * NS], F32, "ExternalInput")
            self.mcst_d = self.dram("mcst", [128, 3 * 8 * NS + 4 * 512], F32, "ExternalInput")
            QT = self.sb(es, "ma_QT", [128, H, T], BF16)
            bQT = Buf()
            d_q = P.dsem("qt")
            P.dma("sp", QT[:].rearrange("p h t -> p (h t)"), self.QTi, d_q, writes=[bQT])
            mc = self.sb(es, "ma_mc", [128, 3, 8, NS], F32)
            caus32 = self.sb(es, "ma_c32", [128, 512], F32)
            caus = self.sb(es, "ma_caus", [128, 4, 512], BF16)
            bmc = Buf()
            d_m = P.dsem("mc")
            d_m2 = P.dsem("mc2")
            P.dma("sp", mc[:].rearrange("p a l s -> p (a l s)"), self.mcst_d[:, 0:3 * 8 * NS], d_m, writes=[bmc])
            for v in range(4):
                P.dma("sp", caus32[:], self.mcst_d[:, 3 * 8 * NS + v * 512:3 * 8 * NS + (v + 1) * 512], d_m2, writes=[bmc])
                P.op("act", lambda e: e.activation(out=caus[:, v, :], in_=caus32[:], func=AF.Copy), reads=[bmc], writes=[bmc])
            kmf = self.sb(es, "ma_kmf", [128, H, NS], F32)
            kmb = self.sb(es, "ma_kmb", [128, H, NS], BF16)
            d_km = P.dsem("km")
            bkm = Buf()
            P.dma("sp", kmf[:].rearrange("p h s -> p (h s)"), self.KMall, d_km, writes=[bkm])
            P.op("act", lambda e: e.activation(out=kmb[:], in_=kmf[:], func=AF.Copy), reads=[bkm], writes=[bkm])
            ident = self.sb(es, "ma_ident", [128, 128], BF16)
            ones = self.sb(es, "ma_ones", [128, 128], BF16)
            Esel = self.sb(es, "ma_Esel", [NS, NS, 128], BF16)
            bcn = Buf()
            P.op("act", lambda e: e.activation(out=ident[:], in_=self.cst[:, C_IDENT:C_IDENT + 128], func=AF.Copy), reads=[self.bvec], writes=[bcn])
            P.op("dve", lambda e: e.memset(ones[:], 1.0), writes=[bcn])
            P.op("dve", lambda e: e.tensor_copy(out=Esel[:], in_=self.cst[0:NS, C_IDENT:C_IDENT + NS].unsqueeze(2).to_broadcast([NS, NS, 128])),
                 reads=[self.bvec], writes=[bcn])
            maskT = [self.sb(es, f"ma_maskT{k}", [NS, T], BF16) for k in range(2)]
            bmaskT = [Buf(), Buf()]
            R2 = lambda nm, shp, dt: [self.sb(es, f"{nm}{k}", shp, dt) for k in range(2)]
            gm, bgm = R2("ma_gm", [128, NS], F32), [Buf(), Buf()]
            t8, bt8 = R2("ma_t8", [128, 8], F32), [Buf(), Buf()]
            al, bal = R2("ma_al", [128, NS], F32), [Buf(), Buf()]
            alb, balb = R2("ma_alb", [128, NS], BF16), [Buf(), Buf()]
            NKS = 3
            KS = [self.sb(es, f"ma_KS{k}", [128, 1024], BF16) for k in range(NKS)]
            VS = [self.sb(es, f"ma_VS{k}", [128, 8, 128], BF16) for k in range(NKS)]
            bKS = [Buf() for _ in range(NKS)]
            bVS = [Buf() for _ in range(NKS)]
            dKS = [P.dsem(f"ks{k}") for k in range(NKS)]
            dVS = [P.dsem(f"vs{k}") for k in range(NKS)]
            NPT = 4
            PT = [self.sb(es, f"ma_PT{k}", [128, 512], BF16) for k in range(NPT)]
            bPT = [Buf() for _ in range(NPT)]
            rden, brden = R2("ma_rden", [128, 512], F32), [Buf(), Buf()]
            acc, bacc = R2("ma_acc", [128, 512], F32), [Buf(), Buf()]
            ahi = self.sb(es, "ma_ahi", [128, 512], BF16)
            alo = self.sb(es, "ma_alo", [128, 512], BF16)
            bahi, balo = Buf(), Buf()
            Y = self.XB
            bYh = [Buf() for _ in range(H)]
            self.rot_banks = (4, 5, 6, 7)
            gi = 0
            ksi = 0
            pti = 0
            for h in range(H):
                mk = h % 2
                for s16 in range(16):
                    g_ = gi % 2
                    gi += 1
                    lb_ = s16 // 2
                    qs_ = slice(s16 * 128, (s16 + 1) * 128)
                    pg, bpg = self.bank()
                    P.op("pe", lambda e: e.matmul(pg[:, 0:NS], QT[:, h, qs_], kmb[:, h, :], start=True, stop=True), reads=[bQT, bkm], writes=[bpg])
                    P.op("dve", lambda e: e.tensor_tensor(out=gm[g_][:], in0=pg[:, 0:NS], in1=mc[:, 0, lb_, :], op=ALU.add), reads=[bpg, bmc], writes=[bgm[g_]])
                    P.op("dve", lambda e: e.max(out=t8[g_][:], in_=gm[g_][:]), reads=[bgm[g_]], writes=[bt8[g_]])
                    P.op("dve", lambda e: e.tensor_scalar(out=al[g_][:], in0=gm[g_][:], scalar1=t8[g_][:, 2:3], scalar2=None, op0=ALU.is_ge),
                         reads=[bgm[g_], bt8[g_]], writes=[bal[g_]])
                    P.op("dve", lambda e: e.tensor_tensor(out=al[g_][:], in0=al[g_][:], in1=mc[:, 1, lb_, :], op=ALU.mult), reads=[bal[g_], bmc], writes=[bal[g_]])
                    P.op("dve", lambda e: e.tensor_tensor(out=al[g_][:], in0=al[g_][:], in1=mc[:, 2, lb_, :], op=ALU.add), reads=[bal[g_], bmc], writes=[bal[g_]])
                    P.op("dve", lambda e: e.tensor_scalar(out=alb[g_][:], in0=al[g_][:], scalar1=-1.0, scalar2=30000.0, op0=ALU.add, op1=ALU.mult),
                         reads=[bal[g_]], writes=[balb[g_]])
                    pt_, bpt_ = self.bank()
                    ptb = pt_[:].bitcast(BF16)
                    P.op("pe", lambda e: e.transpose(ptb[0:NS, 0:128], alb[g_][:], ident[:]), reads=[balb[g_], bcn], writes=[bpt_])
                    P.op("act", lambda e: e.activation(out=maskT[mk][:, qs_], in_=ptb[0:NS, 0:128], func=AF.Copy), reads=[bpt_], writes=[bmaskT[mk]])
                for qt in range(2):
                    pend = []
                    LA = 2
                    O = [self.PS[0], self.PS[1]]
                    bO = [self.bPS[0], self.bPS[1]]
                    DN = [self.PS[2], self.PS[3]]
                    bDN = [self.bPS[2], self.bPS[3]]
                    def need(j, hf):
                        if j < 8:
                            return j in (qt * 4 + hf * 2, qt * 4 + hf * 2 + 1)
                        return (j - 8) < 32 * qt + 16 * hf + 16
                    ulist = {hf: [(j, kt2) for j in range(NS) for kt2 in range(2) if need(j, hf)] for hf in range(2)}
                    for g4 in range(NS // 4):
                        if not any(need(g4 * 4 + j4, hf) for j4 in range(4) for hf in range(2)):
                            continue
                        ks = ksi % NKS
                        ksi += 1
                        P.dma("sp", KS[ks][:], self.Kall[h * 128:(h + 1) * 128, g4 * 1024:(g4 + 1) * 1024], dKS[ks], writes=[bKS[ks]])
                        P.dma("sp", VS[ks][:].rearrange("p t v -> p (t v)"), self.Vall[h * 128:(h + 1) * 128, g4 * 1024:(g4 + 1) * 1024], dVS[ks], writes=[bVS[ks]])
                        for j4 in range(4):
                            j = g4 * 4 + j4
                            for kt2 in range(2):
                                kti = j4 * 2 + kt2
                                for hf in range(2):
                                    if not need(j, hf):
                                        continue
                                    first = ((j, kt2) == ulist[hf][0])
                                    last = ((j, kt2) == ulist[hf][-1])
                                    q0 = qt * 1024 + hf * 512
                                    ps, bps = self.bank()
                                    lbs = (qt * 4 + hf * 2, qt * 4 + hf * 2 + 1)
                                    diag = j in lbs
                                    P.op("pe", lambda e: e.matmul(ps[:, :], KS[ks][:, kti * 128:(kti + 1) * 128], QT[:, h, q0:q0 + 512], start=True, stop=False),
                                         reads=[bKS[ks], bQT], writes=[bps], signal=False)
                                    P.op("pe", lambda e: e.matmul(ps[:, :], Esel[:, j, :], maskT[mk][:, q0:q0 + 512], start=False, stop=(not diag)),
                                         reads=[bcn, bmaskT[mk]], writes=[bps], signal=(not diag))
                                    if diag:
                                        v = kt2 * 2 + (j - lbs[0])
                                        P.op("pe", lambda e: e.matmul(ps[:, :], ident[:], caus[:, v, :], start=False, stop=True),
                                             reads=[bcn, bmc], writes=[bps])
                                    p_ = pti % NPT
                                    pti += 1
                                    P.op("act", lambda e: e.activation(out=PT[p_][:], in_=ps[:, :], func=AF.Exp, scale=SCALE), reads=[bps], writes=[bPT[p_]])
                                    def _pv(ks=ks, kti=kti, p_=p_, hf=hf, first=first, last=last):
                                        P.op("pe", lambda e: e.matmul(O[hf][:, :], VS[ks][:, kti, :], PT[p_][:], start=first, stop=last),
                                             reads=[bVS[ks], bPT[p_]], writes=[bO[hf]], signal=True)
                                        if first:
                                            P.op("dve", lambda e: e.tensor_copy(out=acc[hf][:], in_=PT[p_][:]), reads=[bPT[p_]], writes=[bacc[hf]])
                                        else:
                                            P.op("dve", lambda e: e.tensor_tensor(out=acc[hf][:], in0=acc[hf][:], in1=PT[p_][:], op=ALU.add),
                                                 reads=[bacc[hf], bPT[p_]], writes=[bacc[hf]])
                                    pend.append(_pv)
                                    if len(pend) > LA:
                                        pend.pop(0)()
                    while pend:
                        pend.pop(0)()
                    for hf in range(2):
                        q0 = qt * 1024 + hf * 512
                        r_ = hf
                        P.op("act", lambda e: e.activation(out=ahi[:], in_=acc[hf][:], func=AF.Copy), reads=[bacc[hf]], writes=[bahi])
                        P.op("dve", lambda e: e.tensor_tensor(out=alo[:], in0=acc[hf][:], in1=ahi[:], op=ALU.subtract), reads=[bacc[hf], bahi], writes=[balo])
                        P.op("pe", lambda e: e.matmul(DN[hf][:, :], ones[:], ahi[:], start=True, stop=False), reads=[bcn, bahi], writes=[bDN[hf]], signal=False)
                        P.op("pe", lambda e: e.matmul(DN[hf][:, :], ones[:], alo[:], start=False, stop=True), reads=[bcn, balo], writes=[bDN[hf]])
                        P.op("act", lambda e: e.activation(out=rden[r_][:], in_=DN[hf][:, :], func=AF.Ln), reads=[bDN[hf]], writes=[brden[r_]])
                        P.op("act", lambda e: e.activation(out=rden[r_][:], in_=rden[r_][:], func=AF.Exp, scale=-1.0), reads=[brden[r_]], writes=[brden[r_]])
                        P.op("dve", lambda e: e.tensor_tensor(out=Y[:, h, 2 + q0:2 + q0 + 512], in0=O[hf][:, :], in1=rden[r_][:], op=ALU.mult),
                             reads=[bO[hf], brden[r_]], writes=[bYh[h]])
            self.rot_banks = (0, 1, 2, 3, 4, 5, 6, 7)
            self.out_proj_residual("wout", Y, bYh, DC, first=True, coff=2)
            P.barrier()


_PROGS = {}


def prog(kind):
    if kind not in _PROGS:
        _PROGS[kind] = Builder(kind).build()
    return _PROGS[kind]


def shard_xT(x_full):
    xT = np.ascontiguousarray(x_full.T)
    xTp = np.concatenate([np.zeros((D, 2), np.float32), xT], axis=1)
    return [np.ascontiguousarray(xTp[:, c * T:c * T + T + 2]) for c in range(NCORE)]


def launch(kind, maps):
    import os
    if os.environ.get("TRACE_KIND") == kind:
        res = run_bass_kernel_spmd(prog(kind), maps, core_ids=list(range(NCORE)), trace=True)
        print("[trace]", kind, "exec_time_ns", res.exec_time_ns)
        try:
            insts = res.instructions_and_trace[0]
            from collections import defaultdict
            busy = defaultdict(float); cnt = defaultdict(int); wt = defaultdict(float)
            byname = defaultdict(float); bn = defaultdict(int)
            srcl = defaultdict(float)
            for it_ in insts:
                e = str(it_.engine)
                busy[e] += it_.duration; cnt[e] += 1
                try:
                    wt[e] += (it_.evt_wait_time or 0)
                except Exception:
                    pass
                key = (e, str(it_.op_name))
                byname[key] += it_.duration; bn[key] += 1
                srcl[(e, it_.source_line)] += it_.duration
            for e in busy:
                print(f"[trace] {e:24s} busy={busy[e]/1e3:9.1f}us n={cnt[e]:6d} waits={wt[e]/1e3:9.1f}us")
            for key, v in sorted(byname.items(), key=lambda kv: -kv[1])[:25]:
                print(f"[trace]   {key[0]:20s} {key[1]:28s} {v/1e3:9.1f}us n={bn[key]:6d} avg={v/bn[key]:7.1f}ns")
            t0 = min(i_.timestamp for i_ in insts); t1 = max(i_.end_timestamp for i_ in insts)
            lo = int(os.environ.get("TRACE_LO", "0")); hi = int(os.environ.get("TRACE_HI", "0"))
            sel = [i_ for i_ in insts if lo <= (i_.source_line or 0) <= hi]
            if sel:
                print(f"[trace] total span {(t1-t0)/1e3:.1f}us; lines {lo}-{hi}: from {(min(i_.timestamp for i_ in sel)-t0)/1e3:.1f}us to {(max(i_.end_timestamp for i_ in sel)-t0)/1e3:.1f}us")
            for key, v in sorted(srcl.items(), key=lambda kv: -kv[1])[:25]:
                print(f"[trace]   line {key[1]} {key[0]:12s} {v/1e3:9.1f}us")
        except Exception as ex:
            print("[trace] summary failed", repr(ex))
        return res.results
    res = run_bass_kernel_spmd(prog(kind), maps, core_ids=list(range(NCORE)))
    return res.results


def gather_x(results):
    return np.ascontiguousarray(np.concatenate([r["outT"] for r in results], axis=1).T)


def run_ffn(inp, i, x):
    xs = shard_xT(x)
    vec = make_vec(inp[f"l{i}_ln2_g"], inp[f"l{i}_ln2_b"], conv=inp[f"l{i}_ffn_conv"])
    wup = tile_w(inp[f"l{i}_ffn_w_up"])
    wdn = tile_w(inp[f"l{i}_ffn_w_down"])
    maps = [dict(xT=xs[c], vec=vec, cst=make_cst(c), wup=wup, wdn=wdn) for c in range(NCORE)]
    return gather_x(launch("ff", maps))


def run_conv(inp, i, x):
    xs = shard_xT(x)
    vec = make_vec(inp[f"l{i}_ln1_g"], inp[f"l{i}_ln1_b"], conv=inp[f"l{i}_mix_conv"])
    win = tile_w(inp[f"l{i}_mix_w_in"])
    wout = tile_w(inp[f"l{i}_mix_w_out"])
    maps = [dict(xT=xs[c], vec=vec, cst=make_cst(c), win=win, wout=wout) for c in range(NCORE)]
    return gather_x(launch("cv", maps))


def run_hgrn(inp, i, x):
    xs = shard_xT(x)
    vec = make_vec(inp[f"l{i}_ln1_g"], inp[f"l{i}_ln1_b"], norm_g=inp[f"l{i}_mix_norm_g"], lbraw=inp["hgrn_lower_bounds"])
    win = tile_w(inp[f"l{i}_mix_w_in"])
    wout = tile_w(inp[f"l{i}_mix_w_out"])
    Aall = np.zeros((128, NCORE * 8), np.float32)
    Ball = np.zeros((NCORE * 128, 1024), np.float32)
    out = None
    for phase in range(2):
        maps = [dict(xT=xs[c], vec=vec, cst=make_cst(c, layer=i), win=win, Aall=Aall, Ball=Ball) for c in range(NCORE)]
        if phase == 1:
            for m in maps:
                m["wout"] = wout
        res = launch("hg1" if phase == 0 else "hg", maps)
        if phase == 0:
            Aall = np.ascontiguousarray(np.concatenate([r["Aout"] for r in res], axis=1))
            Ball = np.ascontiguousarray(np.concatenate([r["Bout"] for r in res], axis=0))
        else:
            out = gather_x(res)
    return out


def zz_block(c, s_):
    return 8 * s_ + (c if s_ % 2 == 0 else 7 - c)


def run_moba(inp, i, x):
    H = 8
    NSL = 72
    xs = shard_xT(x)
    vec = make_vec(inp[f"l{i}_ln1_g"], inp[f"l{i}_ln1_b"])
    W = inp[f"l{i}_mix_w_in"]
    perm = []
    for qk in range(2):
        for h in range(H):
            base = qk * 1024 + h * 128
            Wp = np.zeros((D, 128), np.float32)
            Wp[:, 0:16] = W[:, base + 16:base + 32]
            Wp[:, 16:32] = W[:, base:base + 16]
            perm.append(Wp)
    win = tile_w(np.concatenate([W] + perm, axis=1))
    pos = inp["positions"].astype(np.int32)
    maps = [dict(xT=xs[c], vec=vec, cst=make_cst(c), win=win, pos=np.ascontiguousarray(pos[:, c * T:(c + 1) * T])) for c in range(NCORE)]
    res = launch("mo1", maps)
    Qg = np.concatenate([np.asarray(r["QT"]).reshape(128, H, 8, 256) for r in res], axis=2)
    Kg = np.concatenate([np.asarray(r["KT"]).reshape(128, H, 8, 256).transpose(1, 0, 2, 3) for r in res], axis=2)
    Vg = np.concatenate([np.asarray(r["V"]).reshape(128, 8, 2, H, 128).transpose(3, 0, 1, 2, 4) for r in res], axis=2)
    KMg = np.concatenate([np.asarray(r["KM"]).reshape(128, H, 8) for r in res], axis=2)
    xTfull = np.ascontiguousarray(x.T).reshape(D, 64, 256)
    wout = tile_w(inp[f"l{i}_mix_w_out"])
    caus = np.zeros((4, 128, 512), np.float32)
    p_ = np.arange(128)[:, None]
    qi = np.arange(256)[None, :]
    for kt2 in range(2):
        for posb in range(2):
            caus[kt2 * 2 + posb][:, posb * 256:(posb + 1) * 256] = np.where(kt2 * 128 + p_ > qi, -30000.0, 0.0)
    maps = []
    own_all = []
    for c in range(NCORE):
        own = [zz_block(c, s_) for s_ in range(8)]
        own_all.append(own)
        pc = np.array(own + list(range(64)))
        Kall = np.ascontiguousarray(Kg[:, :, pc, :]).reshape(H * 128, NSL * 256)
        Vall = np.ascontiguousarray(Vg[:, :, pc]).reshape(H * 128, NSL * 256)
        KMall = np.ascontiguousarray(KMg[:, :, pc]).reshape(128, H * NSL)
        QT = np.ascontiguousarray(Qg[:, :, np.array(own), :]).reshape(128, H * T)
        xT = np.concatenate([np.zeros((D, 2), np.float32), xTfull[:, np.array(own), :].reshape(D, T)], axis=1)
        past = np.zeros((8, NSL), np.float32)
        ownm = np.zeros((8, NSL), np.float32)
        for s_ in range(8):
            past[s_, 8:] = (np.arange(64) < own[s_]).astype(np.float32)
            ownm[s_, s_] = 1.0
        pastbias = np.where(past > 0, 0.0, -1e30).astype(np.float32)
        row = np.concatenate([pastbias.reshape(-1), past.reshape(-1), ownm.reshape(-1)])
        mcst = np.concatenate([np.broadcast_to(row[None, :], (128, 3 * 8 * NSL)), caus.transpose(1, 0, 2).reshape(128, 2048)], axis=1).astype(np.float32)
        maps.append(dict(xT=np.ascontiguousarray(xT), vec=vec, cst=make_cst(c), QT=QT, Kall=Kall, Vall=Vall, KMall=KMall,
                         mcst=np.ascontiguousarray(mcst), wout=wout))
    res2 = launch("mo2", maps)
    out = np.zeros((64, 256, D), np.float32)
    for c in range(NCORE):
        oc = np.asarray(res2[c]["outT"]).T.reshape(8, 256, D)
        for s_ in range(8):
            out[own_all[c][s_]] = oc[s_]
    return np.ascontiguousarray(out.reshape(64 * 256, D))


def kernel(**inputs):
    inp = {k: np.asarray(v) for k, v in inputs.items()}
    x = np.ascontiguousarray(inp["x"][0])
    for i in range(DEPTH):
        kind = i % 3
        if kind == 0:
            x = run_hgrn(inp, i, x)
        elif kind == 1:
            x = run_moba(inp, i, x)
        else:
            x = run_conv(inp, i, x)
        x = run_ffn(inp, i, x)
    return x[None].astype(np.float32)
```

```python
import math
import numpy as np
from contextlib import ExitStack
import concourse.bass as bass
import concourse.mybir as mybir
from concourse.bass_utils import run_bass_kernel_spmd

F32 = mybir.dt.float32
BF16 = mybir.dt.bfloat16
I32 = mybir.dt.int32
AF = mybir.ActivationFunctionType
ALU = mybir.AluOpType
AX = mybir.AxisListType

NCORE = 8
T = 2048
D = 1024
DC = 8
DFF = 2816
FC = 22
DEPTH = 4
ALPHA = (2.0 * DEPTH) ** 0.25
LN_EPS = 1e-5
RMS_EPS = 1e-6
NT = T // 512


class Buf:
    __slots__ = ("name", "w", "r")

    def __init__(self, name="b"):
        self.name = name
        self.w = None
        self.r = {}


class DSem:
    def __init__(self, key, h):
        self.key = key
        self.h = h
        self.cnt = 0


class Prog:
    ENG = ("pe", "act", "dve", "pool", "sp")

    def __init__(self, nc, es):
        self.nc = nc
        self.es = es
        self.eng = dict(pe=nc.tensor, act=nc.scalar, dve=nc.vector, pool=nc.gpsimd, sp=nc.sync)
        self.semh = {}
        self.cnt = {}
        self.known = {k: {} for k in self.ENG}
        for k in self.ENG:
            self.semh[k] = es.enter_context(nc.semaphore("s_" + k))
            self.cnt[k] = 0
        self.dsems = []
        self.nwait = 0
        self.nins = 0

    def dsem(self, name):
        h = self.es.enter_context(self.nc.semaphore("d_" + name))
        d = DSem("d_" + name, h)
        self.semh[d.key] = h
        self.dsems.append(d)
        return d

    def _wait(self, e, key, val):
        if val <= 0:
            return
        if self.known[e].get(key, 0) >= val:
            return
        if key == e:
            if e == "pe":
                return
            if val > self.cnt[e]:
                return
        self.eng[e].wait_ge(self.semh[key], val)
        self.nwait += 1
        self.known[e][key] = val

    def deps(self, e, reads, writes):
        need = {}
        for b in reads:
            if b.w is not None:
                k, v = b.w
                if need.get(k, 0) < v:
                    need[k] = v
        for b in writes:
            if b.w is not None:
                k, v = b.w
                if need.get(k, 0) < v:
                    need[k] = v
            for k, v in b.r.items():
                if need.get(k, 0) < v:
                    need[k] = v
        for k, v in need.items():
            self._wait(e, k, v)

    def op(self, e, fn, reads=(), writes=(), signal=True):
        self.deps(e, reads, writes)
        ins = fn(self.eng[e])
        self.nins += 1
        if signal:
            self.cnt[e] += 1
            ins.then_inc(self.semh[e], 1)
            mark = (e, self.cnt[e])
        else:
            mark = (e, self.cnt[e] + 1)
        for b in writes:
            b.w = mark
            b.r = {}
        for b in reads:
            if b.r.get(e, 0) < mark[1]:
                b.r[e] = mark[1]
        return ins

    def _mark_async(self, ds, reads, writes):
        mark = (ds.key, ds.cnt)
        for b in writes:
            b.w = mark
            b.r = {}
        for b in reads:
            b.r[ds.key] = ds.cnt

    def dma(self, q, out, in_, ds, reads=(), writes=(), **kw):
        self.deps(q, reads, writes)
        ds.cnt += 16
        ins = self.eng[q].dma_start(out=out, in_=in_, **kw)
        ins.then_inc(ds.h, 16)
        self._mark_async(ds, reads, writes)
        return ins

    def allgather(self, in_ap, out_ap, ds, reads=(), writes=()):
        q = "pool"
        self.deps(q, reads, writes)
        ds.cnt += 1
        ins = self.nc.gpsimd.collective_compute(
            "AllGather", ALU.bypass, replica_groups=[list(range(NCORE))],
            ins=[in_ap.opt()], outs=[out_ap.opt()])
        ins.then_inc(ds.h, 1)
        self._mark_async(ds, reads, writes)
        return ins

    def barrier(self, engines=None):
        engines = engines or self.ENG
        for e in engines:
            for k in self.ENG:
                if k != e:
                    self._wait(e, k, self.cnt[k])
            for d in self.dsems:
                self._wait(e, d.key, d.cnt)


def tile_w(W):
    Din, Fo = W.shape
    Wt = W.reshape(Din // 128, 128, Fo // 128, 128).transpose(2, 1, 0, 3)
    return np.ascontiguousarray(Wt).reshape(Fo // 128 * 128, Din)


def vec_cols(v):
    return np.ascontiguousarray(v.reshape(-1, 128).T)


V_LNG = 0
V_LNB = 8
V_CONV = 16
V_NORMG = 148
V_LBRAW = 156
NVEC = 188
C_HMASK = 0
C_LMASK = 8
C_MASK2 = 12
C_IDENT = 140
C_INVF = 268
C_SGN = 269
NCST = 270


def conv_cols(cw):
    return np.concatenate([vec_cols(cw[k]) for k in range(cw.shape[0])], axis=1)


def make_vec(ln_g, ln_b, conv=None, norm_g=None, lbraw=None):
    v = np.zeros((128, NVEC), np.float32)
    v[:, V_LNG:V_LNG + 8] = vec_cols(ln_g)
    v[:, V_LNB:V_LNB + 8] = vec_cols(ln_b)
    if conv is not None:
        c = conv_cols(conv)
        v[:, V_CONV:V_CONV + c.shape[1]] = c
    if norm_g is not None:
        v[:, V_NORMG:V_NORMG + 8] = vec_cols(norm_g)
    if lbraw is not None:
        v[:, V_LBRAW:V_LBRAW + 32] = np.concatenate([vec_cols(lbraw[l]) for l in range(DEPTH)], axis=1)
    return v


def make_cst(core, layer=0):
    c = np.zeros((128, NCST), np.float32)
    for r in range(NCORE):
        c[:, C_HMASK + r] = 1.0 if r < core else 0.0
    for l in range(DEPTH):
        c[:, C_LMASK + l] = 1.0 if 1 <= l <= layer else 0.0
    s_ = np.arange(128)[:, None]
    t_ = np.arange(128)[None, :]
    c[:, C_MASK2:C_MASK2 + 128] = ((s_ // 64 == t_ // 64) & (s_ <= t_)).astype(np.float32)
    c[:, C_IDENT:C_IDENT + 128] = np.eye(128, dtype=np.float32)
    invf = (1.0 / (500000.0 ** (np.arange(0, 32, 2, dtype=np.float32) / np.float32(32.0)))).astype(np.float32)
    c[0:32, C_INVF] = np.concatenate([invf, invf])
    c[0:16, C_SGN] = -1.0
    c[16:32, C_SGN] = 1.0
    return c


class Builder:
    def __init__(self, kind):
        self.kind = kind
        self.nc = bass.Bass("TRN2", target_bir_lowering=False)

    def dram(self, name, shape, dt, kind="Internal"):
        return self.nc.dram_tensor(name, list(shape), dt, kind=kind).ap()

    def sb(self, es, name, shape, dt):
        self._uid = getattr(self, "_uid", 0) + 1
        return es.enter_context(self.nc.sbuf_tensor(f"{name}_{self._uid}", list(shape), dt))

    def build(self):
        nc = self.nc
        kind = self.kind
        with ExitStack() as es:
            self.es = es
            P = self.P = Prog(nc, es)
            self.xT = self.dram("xT", [D, T + 2], F32, "ExternalInput")
            self.vec_d = self.dram("vec", [128, NVEC], F32, "ExternalInput")
            self.cst_d = self.dram("cst", [128, NCST], F32, "ExternalInput")
            self.wd = {}
            if kind in ("ff", "ffq", "ffh"):
                self.wd["wup"] = self.dram("wup", [2 * DFF, D], F32, "ExternalInput")
                self.wd["wdn"] = self.dram("wdn", [D, DFF], F32, "ExternalInput")
                if kind == "ffq":
                    self.wd["win"] = self.dram("win", [5120, D], F32, "ExternalInput")
                if kind == "ffh":
                    self.wd["win"] = self.dram("win", [4096, D], F32, "ExternalInput")
            elif kind == "cv":
                self.wd["win"] = self.dram("win", [3072, D], F32, "ExternalInput")
                self.wd["wout"] = self.dram("wout", [D, D], F32, "ExternalInput")
            elif kind == "hg":
                self.wd["win"] = self.dram("win", [4096, D], F32, "ExternalInput")
                self.wd["wout"] = self.dram("wout", [D, D], F32, "ExternalInput")
            elif kind == "hg1":
                self.wd["win"] = self.dram("win", [4096, D], F32, "ExternalInput")
            elif kind == "mo1":
                self.wd["win"] = self.dram("win", [5120, D], F32, "ExternalInput")
            elif kind == "mo2":
                self.wd["wout"] = self.dram("wout", [D, D], F32, "ExternalInput")
            if kind not in ("mo1", "hg1"):
                self.outT = self.dram("outT", [D, T], F32, "ExternalOutput")
            self.XR = self.sb(es, "XR", [128, DC, T], F32)
            self.XB = self.sb(es, "XB", [128, DC, T + 2], BF16)
            self.bXR = [Buf(f"XR{n}") for n in range(NT)]
            self.bXB = [Buf(f"XB{n}") for n in range(NT)]
            self.bXBh = Buf("XBh")
            self.vec = self.sb(es, "vecs", [128, NVEC], F32)
            self.cst = self.sb(es, "csts", [128, NCST], F32)
            self.bvec = Buf("vec")
            self.ones_ln = self.sb(es, "ones_ln", [128, 128], BF16)
            self.bconst = Buf("const")
            NW = 2 if kind == "mo2" else 8
            self.WS = [self.sb(es, f"ws{k}", [128, DC, 128], BF16) for k in range(NW)]
            self.bWS = [Buf(f"ws{k}") for k in range(NW)]
            self.dWS = [P.dsem(f"ws{k}") for k in range(NW)]
            self.wsi = 0
            self.PS = [es.enter_context(nc.psum_tensor(f"ps{k}", [128, 512], F32)) for k in range(8)]
            self.bPS = [Buf(f"ps{k}") for k in range(8)]
            self.psi = 0
            self.d_out = P.dsem("out")
            self.d_misc = P.dsem("misc")
            self.d_misc2 = P.dsem("misc2")
            self.bout = Buf("out")

            P.dma("sp", self.vec[:], self.vec_d, self.d_misc, writes=[self.bvec])
            P.dma("sp", self.cst[:], self.cst_d, self.d_misc2, writes=[self.bvec])
            xTv = self.xT.rearrange("(c p) t -> p c t", p=128)
            d_ins = [P.dsem(f"in{n}") for n in range(NT + 1)]
            for n in range(NT):
                P.dma("sp", self.XR[:, :, n * 512:(n + 1) * 512], xTv[:, :, 2 + n * 512:2 + (n + 1) * 512],
                      d_ins[n], writes=[self.bXR[n]])
            self.hal32 = self.sb(es, "hal32", [128, DC, 2], F32)
            self.bhal32 = Buf("hal32")
            P.dma("sp", self.hal32[:], xTv[:, :, 0:2], d_ins[NT], writes=[self.bhal32])
            P.op("dve", lambda e: e.memset(self.ones_ln[:], 1.0 / D), writes=[self.bconst])
            if kind != "mo2":
                for n in range(NT):
                    P.op("act", lambda e: e.activation(out=self.XB[:, :, 2 + n * 512:2 + (n + 1) * 512],
                                                       in_=self.XR[:, :, n * 512:(n + 1) * 512], func=AF.Copy),
                         reads=[self.bXR[n]], writes=[self.bXB[n]])
                P.op("act", lambda e: e.activation(out=self.XB[:, :, 0:2], in_=self.hal32[:], func=AF.Copy),
                     reads=[self.bhal32], writes=[self.bXBh])

            if kind in ("ff", "ffq", "ffh"):
                self.ffn()
            elif kind == "cv":
                self.conv_mixer()
            elif kind in ("hg", "hg1"):
                self.hgrn_mixer()
            elif kind == "mo1":
                self.moba_qkv()
            elif kind == "mo2":
                self.moba_attn()
            if kind not in ("mo1", "hg1"):
                self.layernorm(V_LNG, V_LNB)
                oTv = self.outT.rearrange("(c p) t -> p c t", p=128)
                for n in range(NT):
                    P.dma("sp", oTv[:, :, n * 512:(n + 1) * 512], self.XR[:, :, n * 512:(n + 1) * 512],
                          self.d_out, reads=[self.bXR[n]], writes=[self.bout])
            if kind == "ffq":
                self.moba_qkv()
            if kind == "ffh":
                self.hgrn_mixer()
            P.barrier()
        return nc

    def bank(self):
        rb = getattr(self, "rot_banks", (0, 1, 2, 3, 4, 5, 6, 7))
        self.psi = (self.psi + 1) % len(rb)
        k = rb[self.psi]
        return self.PS[k], self.bPS[k]

    def load_w(self, name, j, c0=0, nch=DC):
        P = self.P
        k = self.wsi
        self.wsi = (self.wsi + 1) % len(self.WS)
        slot, b, ds = self.WS[k], self.bWS[k], self.dWS[k]
        src = self.wd[name][j * 128:(j + 1) * 128, c0 * 128:(c0 + nch) * 128]
        dst = slot[:, 0:nch, :].rearrange("p c i -> p (c i)")
        P.dma("pool", dst, src, ds, writes=[b])
        return slot, b

    def xb_bufs(self, col0, n):
        bs = []
        if col0 < 2:
            bs.append(self.bXBh)
        lo = max(col0 - 2, 0)
        hi = col0 + n - 2
        for t in range(NT):
            if lo < (t + 1) * 512 and hi > t * 512:
                bs.append(self.bXB[t])
        return bs

    def proj(self, w, bw, col0, n, ps, bps, m0=0, m=128, src=None, srcbufs=None, nch=DC):
        P = self.P
        src = self.XB if src is None else src
        srcbufs = self.xb_bufs(col0, n) if srcbufs is None else srcbufs
        for c in range(nch):
            P.op("pe", lambda e: e.matmul(ps[0:m, 0:n], w[:, c, m0:m0 + m], src[:, c, col0:col0 + n],
                                          start=(c == 0), stop=(c == nch - 1)),
                 reads=[bw] + srcbufs, writes=[bps], signal=(c == nch - 1))

    def layernorm(self, goff, boff):
        P = self.P
        with ExitStack() as es:
            zb = [self.sb(es, f"ln_zb{k}", [128, DC, 512], BF16) for k in range(2)]
            zq = [self.sb(es, f"ln_zq{k}", [128, DC, 512], BF16) for k in range(2)]
            bzb = [Buf() for _ in range(2)]
            bzq = [Buf() for _ in range(2)]
            mean = [self.sb(es, f"ln_mean{k}", [128, 512], F32) for k in range(2)]
            rstd = [self.sb(es, f"ln_rstd{k}", [128, 512], F32) for k in range(2)]
            tmp = [self.sb(es, f"ln_tmp{k}", [128, 512], F32) for k in range(2)]
            bmean = [Buf() for _ in range(2)]
            brstd = [Buf() for _ in range(2)]
            btmp = [Buf() for _ in range(2)]
            ct = [self.sb(es, f"ln_ct{k}", [128, 512], F32) for k in range(4)]
            bct = [Buf() for _ in range(4)]
            cti = 0
            for n in range(NT):
                k = n % 2
                cs = slice(n * 512, (n + 1) * 512)
                P.op("act", lambda e: e.activation(out=zb[k][:], in_=self.XR[:, :, cs], func=AF.Copy),
                     reads=[self.bXR[n]], writes=[bzb[k]])
                P.op("act", lambda e: e.activation(out=zq[k][:], in_=self.XR[:, :, cs], func=AF.Square),
                     reads=[self.bXR[n]], writes=[bzq[k]])
                pm, bpm = self.bank()
                for c in range(DC):
                    P.op("pe", lambda e: e.matmul(pm[:, :], self.ones_ln[:], zb[k][:, c, :], start=(c == 0), stop=(c == DC - 1)),
                         reads=[self.bconst, bzb[k]], writes=[bpm], signal=(c == DC - 1))
                pq, bpq = self.bank()
                for c in range(DC):
                    P.op("pe", lambda e: e.matmul(pq[:, :], self.ones_ln[:], zq[k][:, c, :], start=(c == 0), stop=(c == DC - 1)),
                         reads=[self.bconst, bzq[k]], writes=[bpq], signal=(c == DC - 1))
                P.op("dve", lambda e: e.tensor_copy(out=mean[k][:], in_=pm[:, :]), reads=[bpm], writes=[bmean[k]])
                P.op("dve", lambda e: e.tensor_tensor(out=tmp[k][:], in0=mean[k][:], in1=mean[k][:], op=ALU.mult),
                     reads=[bmean[k]], writes=[btmp[k]])
                P.op("dve", lambda e: e.tensor_tensor(out=rstd[k][:], in0=pq[:, :], in1=tmp[k][:], op=ALU.subtract),
                     reads=[bpq, btmp[k]], writes=[brstd[k]])
                P.op("dve", lambda e: e.tensor_scalar(out=rstd[k][:], in0=rstd[k][:], scalar1=LN_EPS, scalar2=None,
                                                      op0=ALU.add),
                     reads=[brstd[k]], writes=[brstd[k]])
                P.op("act", lambda e: e.activation(out=rstd[k][:], in_=rstd[k][:], func=AF.Ln),
                     reads=[brstd[k]], writes=[brstd[k]])
                P.op("act", lambda e: e.activation(out=rstd[k][:], in_=rstd[k][:], func=AF.Exp, scale=-0.5),
                     reads=[brstd[k]], writes=[brstd[k]])
                bc = []
                for c in range(DC):
                    b_ = Buf()
                    b_.w = self.bXR[n].w
                    b_.r = dict(self.bXR[n].r)
                    bc.append(b_)
                for c in range(DC):
                    t_ = cti % len(ct)
                    cti += 1
                    P.op("dve", lambda e: e.tensor_tensor(out=ct[t_][:], in0=self.XR[:, c, cs], in1=mean[k][:], op=ALU.subtract),
                         reads=[bc[c], bmean[k]], writes=[bct[t_]])
                    P.op("dve", lambda e: e.tensor_tensor(out=ct[t_][:], in0=ct[t_][:], in1=rstd[k][:], op=ALU.mult),
                         reads=[bct[t_], brstd[k]], writes=[bct[t_]])
                    P.op("act", lambda e: e.activation(out=self.XR[:, c, cs], in_=ct[t_][:], func=AF.Identity,
                                                       bias=self.vec[:, boff + c:boff + c + 1],
                                                       scale=self.vec[:, goff + c:goff + c + 1]),
                         reads=[bct[t_], self.bvec], writes=[bc[c]])
                P.op("act", lambda e: e.activation(out=self.XB[:, :, 2 + n * 512:2 + (n + 1) * 512],
                                                   in_=self.XR[:, :, cs], func=AF.Copy),
                     reads=bc, writes=[self.bXB[n]])
                self.bXR[n].w = bc[DC - 1].w
                self.bXR[n].r = dict(bc[DC - 1].r)
            P.barrier()

    def out_proj_residual(self, wname, Y, bY, nch, c0=0, first=True, wtile_c0=0, coff=0):
        P = self.P
        for f in range(DC):
            w, bw = self.load_w(wname, f, c0=wtile_c0, nch=nch)
            for n in range(NT):
                ps, bps = self.bank()
                self.proj(w, bw, coff + n * 512, 512, ps, bps, src=Y, srcbufs=bY if isinstance(bY, list) else [bY], nch=nch)
                cs = slice(n * 512, (n + 1) * 512)
                if first:
                    P.op("dve", lambda e: e.scalar_tensor_tensor(out=self.XR[:, f, cs], in0=self.XR[:, f, cs], scalar=ALPHA,
                                                                 in1=ps[:, :], op0=ALU.mult, op1=ALU.add),
                         reads=[bps, self.bXR[n]], writes=[self.bXR[n]])
                else:
                    P.op("dve", lambda e: e.tensor_tensor(out=self.XR[:, f, cs], in0=self.XR[:, f, cs], in1=ps[:, :], op=ALU.add),
                         reads=[bps, self.bXR[n]], writes=[self.bXR[n]])

    def ffn(self):
        P = self.P
        cvo = V_CONV
        groups = [(0, 8), (8, 15), (15, 22)]
        with ExitStack() as es:
            GH = 8
            H = self.sb(es, "ffn_H", [128, GH, T], BF16)
            bH = [Buf() for _ in range(GH)]
            U = [[self.sb(es, f"ffn_U{ab}{k}", [128, 514], F32) for k in range(3)] for ab in range(2)]
            bU = [[Buf() for _ in range(3)] for _ in range(2)]
            Y = [[self.sb(es, f"ffn_Y{ab}{k}", [128, 512], F32) for k in range(2)] for ab in range(2)]
            bY = [[Buf() for _ in range(2)] for _ in range(2)]
            TT = [[self.sb(es, f"ffn_T{ab}{k}", [128, 512], F32) for k in range(2)] for ab in range(2)]
            bTT = [[Buf() for _ in range(2)] for _ in range(2)]
            SA = [self.sb(es, f"ffn_SA{k}", [128, 512], F32) for k in range(2)]
            bSA = [Buf() for _ in range(2)]
            ui = 0
            yi = 0
            for gi, (j0, j1) in enumerate(groups):
                for j in range(j0, j1):
                    ws = [self.load_w("wup", j), self.load_w("wup", FC + j)]
                    for n in range(NT):
                        uk = ui % 3
                        up = (ui - 1) % 3
                        ui += 1
                        yk = yi % 2
                        yi += 1
                        for ab in range(2):
                            w, bw = ws[ab]
                            u, bu = U[ab][uk], bU[ab][uk]
                            cj = j + ab * FC
                            if n == 0:
                                ph, bph = self.bank()
                                self.proj(w, bw, 0, 2, ph, bph)
                                P.op("act", lambda e: e.activation(out=u[:, 0:2], in_=ph[:, 0:2], func=AF.Copy),
                                     reads=[bph], writes=[bu])
                            else:
                                P.op("act", lambda e: e.activation(out=u[:, 0:2], in_=U[ab][up][:, 512:514], func=AF.Copy),
                                     reads=[bU[ab][up]], writes=[bu])
                            ps, bps = self.bank()
                            self.proj(w, bw, 2 + n * 512, 512, ps, bps)
                            P.op("act", lambda e: e.activation(out=u[:, 2:514], in_=ps[:, :], func=AF.Copy),
                                 reads=[bps], writes=[bu])
                            t, bt = TT[ab][yk], bTT[ab][yk]
                            y, by = Y[ab][yk], bY[ab][yk]
                            w0 = self.vec[:, cvo + cj:cvo + cj + 1]
                            w1 = self.vec[:, cvo + 44 + cj:cvo + 44 + cj + 1]
                            w2 = self.vec[:, cvo + 88 + cj:cvo + 88 + cj + 1]
                            P.op("act", lambda e: e.activation(out=t[:], in_=u[:, 0:512], func=AF.Copy, scale=w0),
                                 reads=[bu, self.bvec], writes=[bt])
                            P.op("dve", lambda e: e.scalar_tensor_tensor(out=t[:], in0=u[:, 1:513], scalar=w1, in1=t[:],
                                                                         op0=ALU.mult, op1=ALU.add),
                                 reads=[bu, bt, self.bvec], writes=[bt])
                            P.op("dve", lambda e: e.scalar_tensor_tensor(out=y[:], in0=u[:, 2:514], scalar=w2, in1=t[:],
                                                                         op0=ALU.mult, op1=ALU.add),
                                 reads=[bu, bt, self.bvec], writes=[by])
                        sa, bsa = SA[yk], bSA[yk]
                        P.op("act", lambda e: e.activation(out=sa[:], in_=Y[0][yk][:], func=AF.Silu),
                             reads=[bY[0][yk]], writes=[bsa])
                        P.op("dve", lambda e: e.tensor_tensor(out=H[:, j - j0, n * 512:(n + 1) * 512], in0=sa[:], in1=Y[1][yk][:],
                                                              op=ALU.mult),
                             reads=[bsa, bY[1][yk]], writes=[bH[j - j0]])
                self.out_proj_residual("wdn", H, bH[:j1 - j0], j1 - j0, first=(gi == 0), wtile_c0=j0)
            P.barrier()

    def conv_mixer(self):
        P = self.P
        cvo = V_CONV
        with ExitStack() as es:
            Yo = self.sb(es, "cm_Y", [128, DC, T], BF16)
            bYo = [Buf() for _ in range(DC)]
            PR = [self.sb(es, f"cm_P{k}", [128, 514], F32) for k in range(3)]
            bPR = [Buf() for _ in range(3)]
            CG = [self.sb(es, f"cm_CG{k}", [128, 514], F32) for k in range(2)]
            bCG = [Buf() for _ in range(2)]
            TT = [self.sb(es, f"cm_T{k}", [128, 512], F32) for k in range(2)]
            bTT = [Buf() for _ in range(2)]
            ui = 0
            for c in range(DC):
                wbg = self.load_w("win", c)
                wcg = self.load_w("win", DC + c)
                wh = self.load_w("win", 2 * DC + c)
                w0 = self.vec[:, cvo + c:cvo + c + 1]
                w1 = self.vec[:, cvo + 8 + c:cvo + 8 + c + 1]
                w2 = self.vec[:, cvo + 16 + c:cvo + 16 + c + 1]
                for n in range(NT):
                    uk = ui % 3
                    up = (ui - 1) % 3
                    k2 = ui % 2
                    ui += 1
                    pr, bpr = PR[uk], bPR[uk]
                    cg, bcg = CG[k2], bCG[k2]
                    t, bt = TT[k2], bTT[k2]
                    if n == 0:
                        p1, bp1 = self.bank()
                        self.proj(wcg[0], wcg[1], 0, 2, p1, bp1)
                        P.op("act", lambda e: e.activation(out=cg[:, 0:2], in_=p1[:, 0:2], func=AF.Copy), reads=[bp1], writes=[bcg])
                        p2, bp2 = self.bank()
                        self.proj(wh[0], wh[1], 0, 2, p2, bp2)
                        P.op("dve", lambda e: e.tensor_tensor(out=pr[:, 0:2], in0=cg[:, 0:2], in1=p2[:, 0:2], op=ALU.mult),
                             reads=[bcg, bp2], writes=[bpr])
                    else:
                        P.op("act", lambda e: e.activation(out=pr[:, 0:2], in_=PR[up][:, 512:514], func=AF.Copy),
                             reads=[bPR[up]], writes=[bpr])
                    p1, bp1 = self.bank()
                    self.proj(wcg[0], wcg[1], 2 + n * 512, 512, p1, bp1)
                    P.op("act", lambda e: e.activation(out=cg[:, 2:514], in_=p1[:, :], func=AF.Copy), reads=[bp1], writes=[bcg])
                    p2, bp2 = self.bank()
                    self.proj(wh[0], wh[1], 2 + n * 512, 512, p2, bp2)
                    P.op("dve", lambda e: e.tensor_tensor(out=pr[:, 2:514], in0=cg[:, 2:514], in1=p2[:, :], op=ALU.mult),
                         reads=[bcg, bp2], writes=[bpr])
                    p3, bp3 = self.bank()
                    self.proj(wbg[0], wbg[1], 2 + n * 512, 512, p3, bp3)
                    P.op("act", lambda e: e.activation(out=t[:], in_=pr[:, 0:512], func=AF.Copy, scale=w0),
                         reads=[bpr, self.bvec], writes=[bt])
                    P.op("dve", lambda e: e.scalar_tensor_tensor(out=t[:], in0=pr[:, 1:513], scalar=w1, in1=t[:],
                                                                 op0=ALU.mult, op1=ALU.add),
                         reads=[bpr, bt, self.bvec], writes=[bt])
                    P.op("dve", lambda e: e.scalar_tensor_tensor(out=t[:], in0=pr[:, 2:514], scalar=w2, in1=t[:],
                                                                 op0=ALU.mult, op1=ALU.add),
                         reads=[bpr, bt, self.bvec], writes=[bt])
                    P.op("dve", lambda e: e.tensor_tensor(out=Yo[:, c, n * 512:(n + 1) * 512], in0=t[:], in1=p3[:, :], op=ALU.mult),
                         reads=[bt, bp3], writes=[bYo[c]])
            self.out_proj_residual("wout", Yo, bYo, DC, first=True)
            P.barrier()

    def hgrn_mixer(self):
        P = self.P
        nc = self.nc
        H = 8
        lite = (self.kind in ("hg1", "ffh"))
        with ExitStack() as es:
            self.Aall = self.dram("Aall", [128, NCORE * H], F32, "ExternalInput")
            self.Ball = self.dram("Ball", [NCORE * 128, H * 128], F32, "ExternalInput")
            self.Aout = self.dram("Aout", [128, H], F32, "ExternalOutput")
            self.Bout = self.dram("Bout", [128, H * 128], F32, "ExternalOutput")
            G = [self.sb(es, f"hg_G{k}", [128, 512], BF16) for k in range(2)]
            bG = [Buf() for _ in range(2)]
            Y = self.sb(es, "hg_Y", [128, H, T], BF16)
            bY = [Buf() for _ in range(H)]
            S = self.sb(es, "hg_S", [128, H, 128], F32)
            bS = [Buf() for _ in range(H)]
            Aall = self.sb(es, "hg_Aall", [128, NCORE, H], F32)
            bAall = Buf()
            ap_ = self.sb(es, "hg_ap", [128, H], F32)
            bap = Buf()
            om = self.sb(es, "hg_om", [128, NCORE], F32)
            bom = Buf()
            Bm = [self.sb(es, f"hg_Bm{k}", [128, H, 128], F32) for k in range(1)]
            bBm = [Buf() for _ in range(1)]
            dBm = [P.dsem(f"bm{k}") for k in range(1)]
            d_a = P.dsem("aall")
            lbe = self.sb(es, "hg_lbe", [128, 4, H], F32)
            lb = self.sb(es, "hg_lb", [128, H], F32)
            oml = self.sb(es, "hg_oml", [128, H], F32)
            lbm1 = self.sb(es, "hg_lbm1", [128, H], F32)
            lsum = self.sb(es, "hg_lsum", [128, H], F32)
            blb = Buf()
            rmask = self.sb(es, "hg_rmask", [128, 512], F32)
            ident = self.sb(es, "hg_ident", [128, 128], BF16)
            ones128 = self.sb(es, "hg_ones", [128, 128], BF16)
            bcn = Buf()
            blsum = self.sb(es, "hg_blsum", [128, H], F32)
            bblsum = Buf()
            P.op("dve", lambda e: e.memset(rmask[:], 1.0), writes=[bcn])
            P.op("dve", lambda e: e.memset(rmask[:].rearrange("p (c t) -> p c t", t=64)[:, :, 0:1], 0.0), writes=[bcn])
            P.op("dve", lambda e: e.memset(ones128[:], 1.0 / 128.0), writes=[bcn])
            one1 = self.sb(es, "hg_one1", [128, 1], F32)
            P.op("dve", lambda e: e.memset(one1[:], 1.0), writes=[bcn])
            P.op("dve", lambda e: e.memset(blsum[:], 0.0), writes=[bblsum])
            P.op("act", lambda e: e.activation(out=ident[:], in_=self.cst[:, C_IDENT:C_IDENT + 128], func=AF.Copy),
                 reads=[self.bvec], writes=[bcn])
            P.op("act", lambda e: e.activation(out=lbe[:].rearrange("p l h -> p (l h)"), in_=self.vec[:, V_LBRAW:V_LBRAW + 32], func=AF.Exp),
                 reads=[self.bvec], writes=[blb])
            P.op("dve", lambda e: e.tensor_tensor(out=lsum[:], in0=lbe[:, 0, :], in1=lbe[:, 1, :], op=ALU.add), reads=[blb], writes=[blb])
            P.op("dve", lambda e: e.tensor_tensor(out=lsum[:], in0=lsum[:], in1=lbe[:, 2, :], op=ALU.add), reads=[blb], writes=[blb])
            P.op("dve", lambda e: e.tensor_tensor(out=lsum[:], in0=lsum[:], in1=lbe[:, 3, :], op=ALU.add), reads=[blb], writes=[blb])
            P.op("dve", lambda e: e.reciprocal(out=lsum[:], in_=lsum[:]), reads=[blb], writes=[blb])
            P.op("dve", lambda e: e.tensor_scalar(out=lb[:], in0=lbe[:, 1, :], scalar1=self.cst[:, C_LMASK + 1:C_LMASK + 2], scalar2=None, op0=ALU.mult),
                 reads=[blb, self.bvec], writes=[blb])
            for l in (2, 3):
                P.op("dve", lambda e: e.scalar_tensor_tensor(out=lb[:], in0=lbe[:, l, :], scalar=self.cst[:, C_LMASK + l:C_LMASK + l + 1],
                                                             in1=lb[:], op0=ALU.mult, op1=ALU.add),
                     reads=[blb, self.bvec], writes=[blb])
            P.op("dve", lambda e: e.tensor_tensor(out=lb[:], in0=lb[:], in1=lsum[:], op=ALU.mult), reads=[blb], writes=[blb])
            P.op("dve", lambda e: e.tensor_scalar(out=lbm1[:], in0=lb[:], scalar1=-1.0, scalar2=None, op0=ALU.add), reads=[blb], writes=[blb])
            P.op("dve", lambda e: e.tensor_scalar(out=oml[:], in0=lbm1[:], scalar1=-1.0, scalar2=None, op0=ALU.mult), reads=[blb], writes=[blb])
            P.dma("sp", Aall[:].rearrange("p r h -> p (r h)"), self.Aall, d_a, writes=[bAall])
            P.op("dve", lambda e: e.memset(S[:], 0.0), writes=bS)
            P.op("dve", lambda e: e.tensor_scalar(out=om[:], in0=self.cst[:, C_HMASK:C_HMASK + NCORE], scalar1=-1.0, scalar2=1.0,
                                                  op0=ALU.mult, op1=ALU.add), reads=[self.bvec], writes=[bom])
            for r in range(NCORE - 1):
                k = 0
                P.dma("sp", Bm[k][:].rearrange("p h v -> p (h v)"), self.Ball[r * 128:(r + 1) * 128, :], dBm[k], writes=[bBm[k]])
                mr = self.cst[:, C_HMASK + r:C_HMASK + r + 1]
                P.op("dve", lambda e: e.tensor_scalar(out=ap_[:], in0=Aall[:, r, :], scalar1=mr, scalar2=om[:, r:r + 1], op0=ALU.mult, op1=ALU.add),
                     reads=[bAall, self.bvec, bom], writes=[bap])
                P.op("dve", lambda e: e.tensor_scalar(out=Bm[k][:], in0=Bm[k][:], scalar1=mr, scalar2=None, op0=ALU.mult),
                     reads=[bBm[k], self.bvec], writes=[bBm[k]])
                for h in range(H):
                    P.op("dve", lambda e: e.scalar_tensor_tensor(out=S[:, h, :], in0=S[:, h, :], scalar=ap_[:, h:h + 1], in1=Bm[k][:, h, :],
                                                                 op0=ALU.mult, op1=ALU.add),
                         reads=[bS[h], bap, bBm[k]], writes=[bS[h]])
            R2 = lambda nm, shp, dt: [self.sb(es, f"{nm}{k}", shp, dt) for k in range(2)]
            sig, bsig = R2("hg_sig", [128, 512], F32), [Buf(), Buf()]
            qs, bqs = R2("hg_qs", [128, 512], F32), [Buf(), Buf()]
            R1 = lambda nm, shp, dt: [self.sb(es, nm, shp, dt)] * 2
            B1 = lambda: [Buf()] * 2
            kk, bkk = R1("hg_k", [128, 512], F32), B1()
            gg, bgg = R1("hg_g", [128, 512], F32), B1()
            bb, bbb = R1("hg_b", [128, 512], F32), B1()
            e1, be1 = R2("hg_e1", [128, 512], F32), [Buf(), Buf()]
            e2, be2 = R1("hg_e2", [128, 512], F32), B1()
            ebl, bebl = R2("hg_ebl", [128, 8], F32), [Buf(), Buf()]
            qp, bqp = R2("hg_qp", [128, 512], BF16), [Buf(), Buf()]
            kp, bkp = R2("hg_kp", [128, 512], BF16), [Buf(), Buf()]
            ktok, bktok = R2("hg_ktok", [128, 4, 128], BF16), [Buf(), Buf()]
            vtok, bvtok = R2("hg_vtok", [128, 4, 128], BF16), [Buf(), Buf()]
            am, bam = R2("hg_am", [128, 128], BF16), [Buf(), Buf()]
            sdb = [self.sb(es, f"hg_sdb{c_}", [128, 128], BF16) for c_ in range(8)]
            bsdb = [Buf() for _ in range(8)]
            osq, bosq = R2("hg_osq", [128, 512], BF16), [Buf(), Buf()]
            rr, brr = R2("hg_rr", [128, 512], F32), [Buf(), Buf()]
            yy, byy = R2("hg_yy", [128, 512], F32), [Buf(), Buf()]
            it = 0
            sdi = 0
            ami = 0
            self.rot_banks = (4, 5, 6, 7)
            cnts = {"ami": 0}

            def stage1(h, n, k, Wh):
                wq, wf, wi, wg, lbh, omlh, lbm1h = Wh
                c0 = 2 + n * 512
                cs = slice(n * 512, (n + 1) * 512)
                if not lite:
                    pq, bpq = self.bank()
                    self.proj(wq[0], wq[1], c0, 512, pq, bpq)
                pf, bpf = self.bank()
                self.proj(wf[0], wf[1], c0, 512, pf, bpf)
                if not lite:
                    pg, bpg = self.bank()
                    self.proj(wg[0], wg[1], c0, 512, pg, bpg)
                pv, bpv = self.bank()
                for s4 in range(4):
                    for c in range(DC):
                        P.op("pe", lambda e: e.matmul(pv[:, s4 * 128:(s4 + 1) * 128], self.XB[:, c, c0 + s4 * 128:c0 + (s4 + 1) * 128],
                                                      wi[0][:, c, :], start=(c == 0), stop=(c == DC - 1)),
                             reads=[wi[1], self.bXB[n]], writes=[bpv], signal=(c == DC - 1 and s4 == 3))
                P.op("act", lambda e: e.activation(out=sig[k][:], in_=pf[:, :], func=AF.Exp, scale=-1.0), reads=[bpf], writes=[bsig[k]])
                P.op("act", lambda e: e.activation(out=sig[k][:], in_=sig[k][:], func=AF.Ln, bias=one1[:, 0:1]), reads=[bsig[k], bcn], writes=[bsig[k]])
                P.op("act", lambda e: e.activation(out=sig[k][:], in_=sig[k][:], func=AF.Exp, scale=-1.0), reads=[bsig[k]], writes=[bsig[k]])
                if not lite:
                    P.op("act", lambda e: e.activation(out=qs[k][:], in_=pq[:, :], func=AF.Silu), reads=[bpq], writes=[bqs[k]])
                    P.op("act", lambda e: e.activation(out=G[k][:], in_=pg[:, :], func=AF.Silu), reads=[bpg], writes=[bG[k]])
                P.op("act", lambda e: e.activation(out=vtok[k][:].rearrange("p s v -> p (s v)"), in_=pv[:, :], func=AF.Copy),
                     reads=[bpv], writes=[bvtok[k]])
                P.op("dve", lambda e: e.tensor_scalar(out=kk[k][:], in0=sig[k][:], scalar1=lbm1h, scalar2=omlh, op0=ALU.mult, op1=ALU.add),
                     reads=[bsig[k], blb], writes=[bkk[k]])
                P.op("act", lambda e: e.activation(out=gg[k][:], in_=sig[k][:], func=AF.Ln, bias=lbh, scale=omlh),
                     reads=[bsig[k], blb], writes=[bgg[k]])
                P.op("dve", lambda e: e.tensor_tensor_scan(out=bb[k][:], data0=rmask[:], data1=gg[k][:], initial=0.0, op0=ALU.mult, op1=ALU.add),
                     reads=[bgg[k], bcn], writes=[bbb[k]])
                b3 = bb[k][:].rearrange("p (c t) -> p c t", t=64)
                P.op("dve", lambda e: e.tensor_tensor(out=e1[k][:].rearrange("p (c t) -> p c t", t=64), in0=b3,
                                                      in1=b3[:, :, 63:64].to_broadcast([128, 8, 64]), op=ALU.subtract),
                     reads=[bbb[k]], writes=[be1[k]])
                P.op("act", lambda e: e.activation(out=e2[k][:], in_=e1[k][:], func=AF.Exp, scale=-1.0), reads=[be1[k]], writes=[be2[k]])
                if not lite:
                    P.op("act", lambda e: e.activation(out=e1[k][:], in_=e1[k][:], func=AF.Exp), reads=[be1[k], be2[k]], writes=[be1[k]])
                P.op("act", lambda e: e.activation(out=ebl[k][:], in_=b3[:, :, 63], func=AF.Exp), reads=[bbb[k]], writes=[bebl[k]])
                P.op("dve", lambda e: e.tensor_reduce(out=rr[k][:, 0:1], in_=b3[:, :, 63], axis=AX.X, op=ALU.add), reads=[bbb[k]], writes=[brr[k]])
                P.op("dve", lambda e: e.tensor_tensor(out=blsum[:, h:h + 1], in0=blsum[:, h:h + 1], in1=rr[k][:, 0:1], op=ALU.add),
                     reads=[brr[k], bblsum], writes=[bblsum])
                if not lite:
                    P.op("dve", lambda e: e.tensor_tensor(out=qp[k][:], in0=qs[k][:], in1=e1[k][:], op=ALU.mult), reads=[bqs[k], be1[k]], writes=[bqp[k]])
                P.op("dve", lambda e: e.tensor_tensor(out=kp[k][:], in0=kk[k][:], in1=e2[k][:], op=ALU.mult), reads=[bkk[k], be2[k]], writes=[bkp[k]])

            def stage2(h, n, k):
                ami = cnts["ami"]
                cs = slice(n * 512, (n + 1) * 512)
                pt, bpt = self.bank()
                ptb = pt[:].bitcast(BF16)
                for s4 in range(4):
                    P.op("pe", lambda e: e.transpose(ptb[:, s4 * 128:(s4 + 1) * 128], kp[k][:, s4 * 128:(s4 + 1) * 128], ident[:]),
                         reads=[bkp[k], bcn], writes=[bpt], signal=(s4 == 3))
                P.op("act", lambda e: e.activation(out=ktok[k][:].rearrange("p s v -> p (s v)"), in_=ptb[:, 0:512], func=AF.Copy),
                     reads=[bpt], writes=[bktok[k]])
                po, bpo = self.PS[k], self.bPS[k]
                for ci in range(8):
                    s4, half = ci // 2, ci % 2
                    rows = slice(half * 64, half * 64 + 64)
                    pu, bpu = self.PS[2 + half], self.bPS[2 + half]
                    P.op("pe", lambda e: e.matmul(pu[:, s4 * 128:(s4 + 1) * 128], ktok[k][rows, s4, :], vtok[k][rows, s4, :],
                                                  start=True, stop=True),
                         reads=[bktok[k], bvtok[k]], writes=[bpu], signal=(ci >= 6))
                for ci in range(8):
                    s4, half = ci // 2, ci % 2
                    eb = ebl[k][:, ci:ci + 1]
                    pu, bpu = self.PS[2 + half], self.bPS[2 + half]
                    if not lite:
                        P.op("dve", lambda e: e.tensor_scalar(out=sdb[ci][:], in0=S[:, h, :], scalar1=eb, scalar2=None, op0=ALU.mult),
                             reads=[bS[h], bebl[k]], writes=[bsdb[ci]])
                    P.op("dve", lambda e: e.scalar_tensor_tensor(out=S[:, h, :], in0=S[:, h, :], scalar=eb, in1=pu[:, s4 * 128:(s4 + 1) * 128],
                                                                 op0=ALU.mult, op1=ALU.add),
                         reads=[bS[h], bebl[k], bpu], writes=[bS[h]])
                for s4 in range(0 if lite else 4):
                    sl = slice(s4 * 128, (s4 + 1) * 128)
                    pa, bpa = self.bank()
                    P.op("pe", lambda e: e.matmul(pa[:, 0:128], kp[k][:, sl], qp[k][:, sl], start=True, stop=True),
                         reads=[bkp[k], bqp[k]], writes=[bpa])
                    a_ = ami % 2
                    ami += 1
                    P.op("dve", lambda e: e.tensor_tensor(out=am[a_][:], in0=pa[:, 0:128], in1=self.cst[:, C_MASK2:C_MASK2 + 128], op=ALU.mult),
                         reads=[bpa, self.bvec], writes=[bam[a_]])
                    P.op("pe", lambda e: e.matmul(po[:, sl], vtok[k][:, s4, :], am[a_][:], start=True, stop=False),
                         reads=[bvtok[k], bam[a_]], writes=[bpo], signal=False)
                    for half in range(2):
                        ci = s4 * 2 + half
                        hs = slice(s4 * 128 + half * 64, s4 * 128 + half * 64 + 64)
                        P.op("pe", lambda e: e.matmul(po[:, hs], sdb[ci][:], qp[k][:, hs], start=False, stop=(half == 1)),
                             reads=[bsdb[ci], bqp[k]], writes=[bpo], signal=True)
                if lite:
                    return
                P.op("act", lambda e: e.activation(out=osq[k][:], in_=po[:, :], func=AF.Square), reads=[bpo], writes=[bosq[k]])
                pm, bpm = self.bank()
                P.op("pe", lambda e: e.matmul(pm[:, :], ones128[:], osq[k][:], start=True, stop=True), reads=[bcn, bosq[k]], writes=[bpm])
                P.op("dve", lambda e: e.tensor_scalar(out=rr[k][:], in0=pm[:, :], scalar1=RMS_EPS, scalar2=None, op0=ALU.add),
                     reads=[bpm], writes=[brr[k]])
                P.op("act", lambda e: e.activation(out=rr[k][:], in_=rr[k][:], func=AF.Ln), reads=[brr[k]], writes=[brr[k]])
                P.op("act", lambda e: e.activation(out=rr[k][:], in_=rr[k][:], func=AF.Exp, scale=-0.5), reads=[brr[k]], writes=[brr[k]])
                P.op("dve", lambda e: e.tensor_tensor(out=yy[k][:], in0=po[:, :], in1=rr[k][:], op=ALU.mult), reads=[bpo, brr[k]], writes=[byy[k]])
                P.op("dve", lambda e: e.scalar_tensor_tensor(out=Y[:, h, cs], in0=yy[k][:], scalar=self.vec[:, V_NORMG + h:V_NORMG + h + 1],
                                                             in1=G[k][:], op0=ALU.mult, op1=ALU.mult),
                     reads=[byy[k], self.bvec, bG[k]], writes=[bY[h]])

            def wts(h):
                return (self.load_w("win", h), self.load_w("win", H + h), self.load_w("win", 2 * H + h), self.load_w("win", 3 * H + h),
                        lb[:, h:h + 1], oml[:, h:h + 1], lbm1[:, h:h + 1])
            for hp in range(0, H, 2):
                WA, WB = wts(hp), wts(hp + 1)
                for n in range(NT):
                    stage1(hp, n, 0, WA)
                    stage1(hp + 1, n, 1, WB)
                    stage2(hp, n, 0)
                    stage2(hp + 1, n, 1)
            P.op("act", lambda e: e.activation(out=blsum[:], in_=blsum[:], func=AF.Exp), reads=[bblsum], writes=[bblsum])
            d_o = P.dsem("hgo")
            P.dma("sp", self.Aout, blsum[:], d_o, reads=[bblsum], writes=[self.bout])
            P.dma("sp", self.Bout, S[:].rearrange("p h v -> p (h v)"), d_o, reads=bS, writes=[self.bout])
            self.rot_banks = (0, 1, 2, 3, 4, 5, 6, 7)
            if not lite:
                self.out_proj_residual("wout", Y, bY, DC, first=True)
            P.barrier()

    def moba_qkv(self):
        P = self.P
        H = 8
        with ExitStack() as es:
            self.pos_d = self.dram("pos", [1, T], I32, "ExternalInput")
            self.QTo = self.dram("QT", [128, H * T], BF16, "ExternalOutput")
            self.KTo = self.dram("KT", [128, H * T], BF16, "ExternalOutput")
            self.Vo = self.dram("V", [128, 16 * 1024], BF16, "ExternalOutput")
            self.KMo = self.dram("KM", [128, H * 8], F32, "ExternalOutput")
            posi = self.sb(es, "mq_posi", [32, T], I32)
            ang = self.sb(es, "mq_ang", [32, T], F32)
            tmp = self.sb(es, "mq_tmp", [32, T], F32)
            Ct = self.sb(es, "mq_C", [32, T], F32)
            St = self.sb(es, "mq_S", [32, T], F32)
            npi = self.sb(es, "mq_npi", [32, 1], F32)
            km = self.sb(es, "mq_km", [128, H, 8], F32)
            btab, bkm = Buf(), Buf()
            d_p = P.dsem("pos")
            P.dma("sp", posi[:], self.pos_d.partition_broadcast(32), d_p, writes=[btab])
            P.op("dve", lambda e: e.tensor_copy(out=ang[:], in_=posi[:]), reads=[btab], writes=[btab])
            P.op("dve", lambda e: e.memset(npi[:], -math.pi), writes=[btab])
            P.op("dve", lambda e: e.tensor_scalar(out=ang[:], in0=ang[:], scalar1=self.cst[0:32, C_INVF:C_INVF + 1], scalar2=None, op0=ALU.mult),
                 reads=[btab, self.bvec], writes=[btab])
            ki = self.sb(es, "mq_ki", [32, T], I32)
            kfl = self.sb(es, "mq_kfl", [32, T], F32)
            for (dst_, off_) in ((St, 0.5), (Ct, 0.75)):
                P.op("dve", lambda e: e.tensor_scalar(out=tmp[:], in0=ang[:], scalar1=1.0 / (2 * math.pi), scalar2=off_, op0=ALU.mult, op1=ALU.add),
                     reads=[btab], writes=[btab])
                P.op("dve", lambda e: e.tensor_copy(out=ki[:], in_=tmp[:]), reads=[btab], writes=[btab])
                P.op("dve", lambda e: e.tensor_copy(out=kfl[:], in_=ki[:]), reads=[btab], writes=[btab])
                P.op("dve", lambda e: e.tensor_tensor(out=tmp[:], in0=tmp[:], in1=kfl[:], op=ALU.subtract), reads=[btab], writes=[btab])
                P.op("dve", lambda e: e.tensor_scalar(out=kfl[:], in0=tmp[:], scalar1=0.0, scalar2=None, op0=ALU.is_lt), reads=[btab], writes=[btab])
                P.op("dve", lambda e: e.tensor_tensor(out=tmp[:], in0=tmp[:], in1=kfl[:], op=ALU.add), reads=[btab], writes=[btab])
                P.op("act", lambda e: e.activation(out=dst_[:], in_=tmp[:], func=AF.Sin, bias=npi[:, 0:1], scale=2 * math.pi), reads=[btab], writes=[btab])
            P.op("dve", lambda e: e.tensor_scalar(out=St[:], in0=St[:], scalar1=self.cst[0:32, C_SGN:C_SGN + 1], scalar2=None, op0=ALU.mult),
                 reads=[btab, self.bvec], writes=[btab])
            R2 = lambda nm, shp, dt: [self.sb(es, f"{nm}{k}", shp, dt) for k in range(2)]
            t1, bt1 = R2("mq_t1", [32, 512], F32), [Buf(), Buf()]
            t2, bt2 = R2("mq_t2", [32, 512], F32), [Buf(), Buf()]
            kf, bkf = R2("mq_kf", [128, 512], F32), [Buf(), Buf()]
            ob, bob = R2("mq_ob", [128, 512], BF16), [Buf(), Buf()]
            vb, bvb = R2("mq_vb", [128, 4, 128], BF16), [Buf(), Buf()]
            dob = [P.dsem(f"ob{k}") for k in range(2)]
            dvb = [P.dsem(f"vb{k}") for k in range(2)]
            Vov = self.Vo.rearrange("p (t f) -> p t f", f=1024)
            it = 0
            for h in range(H):
                for qk in range(2):
                    w = self.load_w("win", qk * H + h)
                    wp = self.load_w("win", 3 * H + qk * H + h)
                    dst = self.QTo if qk == 0 else self.KTo
                    for n in range(NT):
                        k = it % 2
                        it += 1
                        c0 = 2 + n * 512
                        cs = slice(n * 512, (n + 1) * 512)
                        pa, bpa = self.bank()
                        self.proj(w[0], w[1], c0, 512, pa, bpa)
                        pb, bpb = self.bank()
                        self.proj(wp[0], wp[1], c0, 512, pb, bpb, m=32)
                        P.op("dve", lambda e: e.tensor_tensor(out=t1[k][:], in0=pa[0:32, :], in1=Ct[:, cs], op=ALU.mult), reads=[bpa, btab], writes=[bt1[k]])
                        P.op("dve", lambda e: e.tensor_tensor(out=t2[k][:], in0=pb[0:32, :], in1=St[:, cs], op=ALU.mult), reads=[bpb, btab], writes=[bt2[k]])
                        P.op("dve", lambda e: e.tensor_tensor(out=kf[k][0:32, :], in0=t1[k][:], in1=t2[k][:], op=ALU.add), reads=[bt1[k], bt2[k]], writes=[bkf[k]])
                        P.op("act", lambda e: e.activation(out=kf[k][32:64, :], in_=pa[32:64, :], func=AF.Copy), reads=[bpa], writes=[bkf[k]])
                        P.op("act", lambda e: e.activation(out=kf[k][64:128, :], in_=pa[64:128, :], func=AF.Copy), reads=[bpa], writes=[bkf[k]])
                        P.op("act", lambda e: e.activation(out=ob[k][:], in_=kf[k][:], func=AF.Copy), reads=[bkf[k]], writes=[bob[k]])
                        if qk == 1:
                            P.op("dve", lambda e: e.tensor_reduce(out=km[:, h, 2 * n:2 * n + 2], in_=kf[k][:].rearrange("p (b t) -> p b t", t=256),
                                                                  axis=AX.X, op=ALU.add), reads=[bkf[k]], writes=[bkm])
                        P.dma("sp", dst[:, h * T + n * 512:h * T + (n + 1) * 512], ob[k][:], dob[k], reads=[bob[k]], writes=[self.bout])
                wv = self.load_w("win", 2 * H + h)
                for n in range(NT):
                    k = it % 2
                    it += 1
                    c0 = 2 + n * 512
                    pv, bpv = self.bank()
                    for s4 in range(4):
                        for c in range(DC):
                            P.op("pe", lambda e: e.matmul(pv[:, s4 * 128:(s4 + 1) * 128], self.XB[:, c, c0 + s4 * 128:c0 + (s4 + 1) * 128],
                                                          wv[0][:, c, :], start=(c == 0), stop=(c == DC - 1)),
                                 reads=[wv[1], self.bXB[n]], writes=[bpv], signal=(c == DC - 1 and s4 == 3))
                    P.op("act", lambda e: e.activation(out=vb[k][:].rearrange("p s v -> p (s v)"), in_=pv[:, :], func=AF.Copy), reads=[bpv], writes=[bvb[k]])
                    P.dma("sp", Vov[:, n * 4:(n + 1) * 4, h * 128:(h + 1) * 128], vb[k][:], dvb[k], reads=[bvb[k]], writes=[self.bout])
            P.op("dve", lambda e: e.tensor_scalar(out=km[:], in0=km[:], scalar1=1.0 / 256.0, scalar2=None, op0=ALU.mult), reads=[bkm], writes=[bkm])
            d_k = P.dsem("kmo")
            P.dma("sp", self.KMo, km[:].rearrange("p h b -> p (h b)"), d_k, reads=[bkm], writes=[self.bout])
            P.barrier()

    def moba_attn(self):
        P = self.P
        H = 8
        NS = 72
        SCALE = 1.0 / math.sqrt(128.0)
        with ExitStack() as es:
            self.QTi = self.dram("QT", [128, H * T], BF16, "ExternalInput")
            self.Kall = self.dram("Kall", [H * 128, NS * 256], BF16, "ExternalInput")
            self.Vall = self.dram("Vall", [H * 128, NS * 256], BF16, "ExternalInput")
            self.KMall = self.dram("KMall", [128, H * NS], F32, "ExternalInput")
            self.mcst_d = self.dram("mcst", [128, 3 * 8 * NS + 4 * 512], F32, "ExternalInput")
            QT = self.sb(es, "ma_QT", [128, H, T], BF16)
            bQT = Buf()
            d_q = P.dsem("qt")
            P.dma("sp", QT[:].rearrange("p h t -> p (h t)"), self.QTi, d_q, writes=[bQT])
            mc = self.sb(es, "ma_mc", [128, 3, 8, NS], F32)
            caus32 = self.sb(es, "ma_c32", [128, 512], F32)
            caus = self.sb(es, "ma_caus", [128, 4, 512], BF16)
            bmc = Buf()
            d_m = P.dsem("mc")
            d_m2 = P.dsem("mc2")
            P.dma("sp", mc[:].rearrange("p a l s -> p (a l s)"), self.mcst_d[:, 0:3 * 8 * NS], d_m, writes=[bmc])
            for v in range(4):
                P.dma("sp", caus32[:], self.mcst_d[:, 3 * 8 * NS + v * 512:3 * 8 * NS + (v + 1) * 512], d_m2, writes=[bmc])
                P.op("act", lambda e: e.activation(out=caus[:, v, :], in_=caus32[:], func=AF.Copy), reads=[bmc], writes=[bmc])
            kmf = self.sb(es, "ma_kmf", [128, H, NS], F32)
            kmb = self.sb(es, "ma_kmb", [128, H, NS], BF16)
            d_km = P.dsem("km")
            bkm = Buf()
            P.dma("sp", kmf[:].rearrange("p h s -> p (h s)"), self.KMall, d_km, writes=[bkm])
            P.op("act", lambda e: e.activation(out=kmb[:], in_=kmf[:], func=AF.Copy), reads=[bkm], writes=[bkm])
            ident = self.sb(es, "ma_ident", [128, 128], BF16)
            ones = self.sb(es, "ma_ones", [128, 128], BF16)
            Esel = self.sb(es, "ma_Esel", [NS, NS, 128], BF16)
            bcn = Buf()
            P.op("act", lambda e: e.activation(out=ident[:], in_=self.cst[:, C_IDENT:C_IDENT + 128], func=AF.Copy), reads=[self.bvec], writes=[bcn])
            P.op("dve", lambda e: e.memset(ones[:], 1.0), writes=[bcn])
            P.op("dve", lambda e: e.tensor_copy(out=Esel[:], in_=self.cst[0:NS, C_IDENT:C_IDENT + NS].unsqueeze(2).to_broadcast([NS, NS, 128])),
                 reads=[self.bvec], writes=[bcn])
            maskT = [self.sb(es, f"ma_maskT{k}", [NS, T], BF16) for k in range(2)]
            bmaskT = [Buf(), Buf()]
            R2 = lambda nm, shp, dt: [self.sb(es, f"{nm}{k}", shp, dt) for k in range(2)]
            gm, bgm = R2("ma_gm", [128, NS], F32), [Buf(), Buf()]
            t8, bt8 = R2("ma_t8", [128, 8], F32), [Buf(), Buf()]
            al, bal = R2("ma_al", [128, NS], F32), [Buf(), Buf()]
            alb, balb = R2("ma_alb", [128, NS], BF16), [Buf(), Buf()]
            NKS = 3
            KS = [self.sb(es, f"ma_KS{k}", [128, 1024], BF16) for k in range(NKS)]
            VS = [self.sb(es, f"ma_VS{k}", [128, 8, 128], BF16) for k in range(NKS)]
            bKS = [Buf() for _ in range(NKS)]
            bVS = [Buf() for _ in range(NKS)]
            dKS = [P.dsem(f"ks{k}") for k in range(NKS)]
            dVS = [P.dsem(f"vs{k}") for k in range(NKS)]
            NPT = 4
            PT = [self.sb(es, f"ma_PT{k}", [128, 512], BF16) for k in range(NPT)]
            bPT = [Buf() for _ in range(NPT)]
            rden, brden = R2("ma_rden", [128, 512], F32), [Buf(), Buf()]
            acc, bacc = R2("ma_acc", [128, 512], F32), [Buf(), Buf()]
            ahi = self.sb(es, "ma_ahi", [128, 512], BF16)
            alo = self.sb(es, "ma_alo", [128, 512], BF16)
            bahi, balo = Buf(), Buf()
            Y = self.XB
            bYh = [Buf() for _ in range(H)]
            self.rot_banks = (4, 5, 6, 7)
            gi = 0
            ksi = 0
            pti = 0
            for h in range(H):
                mk = h % 2
                for s16 in range(16):
                    g_ = gi % 2
                    gi += 1
                    lb_ = s16 // 2
                    qs_ = slice(s16 * 128, (s16 + 1) * 128)
                    pg, bpg = self.bank()
                    P.op("pe", lambda e: e.matmul(pg[:, 0:NS], QT[:, h, qs_], kmb[:, h, :], start=True, stop=True), reads=[bQT, bkm], writes=[bpg])
                    P.op("dve", lambda e: e.tensor_tensor(out=gm[g_][:], in0=pg[:, 0:NS], in1=mc[:, 0, lb_, :], op=ALU.add), reads=[bpg, bmc], writes=[bgm[g_]])
                    P.op("dve", lambda e: e.max(out=t8[g_][:], in_=gm[g_][:]), reads=[bgm[g_]], writes=[bt8[g_]])
                    P.op("dve", lambda e: e.tensor_scalar(out=al[g_][:], in0=gm[g_][:], scalar1=t8[g_][:, 2:3], scalar2=None, op0=ALU.is_ge),
                         reads=[bgm[g_], bt8[g_]], writes=[bal[g_]])
                    P.op("dve", lambda e: e.tensor_tensor(out=al[g_][:], in0=al[g_][:], in1=mc[:, 1, lb_, :], op=ALU.mult), reads=[bal[g_], bmc], writes=[bal[g_]])
                    P.op("dve", lambda e: e.tensor_tensor(out=al[g_][:], in0=al[g_][:], in1=mc[:, 2, lb_, :], op=ALU.add), reads=[bal[g_], bmc], writes=[bal[g_]])
                    P.op("dve", lambda e: e.tensor_scalar(out=alb[g_][:], in0=al[g_][:], scalar1=-1.0, scalar2=30000.0, op0=ALU.add, op1=ALU.mult),
                         reads=[bal[g_]], writes=[balb[g_]])
                    pt_, bpt_ = self.bank()
                    ptb = pt_[:].bitcast(BF16)
                    P.op("pe", lambda e: e.transpose(ptb[0:NS, 0:128], alb[g_][:], ident[:]), reads=[balb[g_], bcn], writes=[bpt_])
                    P.op("act", lambda e: e.activation(out=maskT[mk][:, qs_], in_=ptb[0:NS, 0:128], func=AF.Copy), reads=[bpt_], writes=[bmaskT[mk]])
                for qt in range(2):
                    pend = []
                    LA = 2
                    O = [self.PS[0], self.PS[1]]
                    bO = [self.bPS[0], self.bPS[1]]
                    DN = [self.PS[2], self.PS[3]]
                    bDN = [self.bPS[2], self.bPS[3]]
                    def need(j, hf):
                        if j < 8:
                            return j in (qt * 4 + hf * 2, qt * 4 + hf * 2 + 1)
                        return (j - 8) < 32 * qt + 16 * hf + 16
                    ulist = {hf: [(j, kt2) for j in range(NS) for kt2 in range(2) if need(j, hf)] for hf in range(2)}
                    for g4 in range(NS // 4):
                        if not any(need(g4 * 4 + j4, hf) for j4 in range(4) for hf in range(2)):
                            continue
                        ks = ksi % NKS
                        ksi += 1
                        P.dma("sp", KS[ks][:], self.Kall[h * 128:(h + 1) * 128, g4 * 1024:(g4 + 1) * 1024], dKS[ks], writes=[bKS[ks]])
                        P.dma("sp", VS[ks][:].rearrange("p t v -> p (t v)"), self.Vall[h * 128:(h + 1) * 128, g4 * 1024:(g4 + 1) * 1024], dVS[ks], writes=[bVS[ks]])
                        for j4 in range(4):
                            j = g4 * 4 + j4
                            for kt2 in range(2):
                                kti = j4 * 2 + kt2
                                for hf in range(2):
                                    if not need(j, hf):
                                        continue
                                    first = ((j, kt2) == ulist[hf][0])
                                    last = ((j, kt2) == ulist[hf][-1])
                                    q0 = qt * 1024 + hf * 512
                                    ps, bps = self.bank()
                                    lbs = (qt * 4 + hf * 2, qt * 4 + hf * 2 + 1)
                                    diag = j in lbs
                                    P.op("pe", lambda e: e.matmul(ps[:, :], KS[ks][:, kti * 128:(kti + 1) * 128], QT[:, h, q0:q0 + 512], start=True, stop=False),
                                         reads=[bKS[ks], bQT], writes=[bps], signal=False)
                                    P.op("pe", lambda e: e.matmul(ps[:, :], Esel[:, j, :], maskT[mk][:, q0:q0 + 512], start=False, stop=(not diag)),
                                         reads=[bcn, bmaskT[mk]], writes=[bps], signal=(not diag))
                                    if diag:
                                        v = kt2 * 2 + (j - lbs[0])
                                        P.op("pe", lambda e: e.matmul(ps[:, :], ident[:], caus[:, v, :], start=False, stop=True),
                                             reads=[bcn, bmc], writes=[bps])
                                    p_ = pti % NPT
                                    pti += 1
                                    P.op("act", lambda e: e.activation(out=PT[p_][:], in_=ps[:, :], func=AF.Exp, scale=SCALE), reads=[bps], writes=[bPT[p_]])
                                    def _pv(ks=ks, kti=kti, p_=p_, hf=hf, first=first, last=last):
                                        P.op("pe", lambda e: e.matmul(O[hf][:, :], VS[ks][:, kti, :], PT[p_][:], start=first, stop=last),
                                             reads=[bVS[ks], bPT[p_]], writes=[bO[hf]], signal=True)
                                        if first:
                                            P.op("dve", lambda e: e.tensor_copy(out=acc[hf][:], in_=PT[p_][:]), reads=[bPT[p_]], writes=[bacc[hf]])
                                        else:
                                            P.op("dve", lambda e: e.tensor_tensor(out=acc[hf][:], in0=acc[hf][:], in1=PT[p_][:], op=ALU.add),
                                                 reads=[bacc[hf], bPT[p_]], writes=[bacc[hf]])
                                    pend.append(_pv)
                                    if len(pend) > LA:
                                        pend.pop(0)()
                    while pend:
                        pend.pop(0)()
                    for hf in range(2):
                        q0 = qt * 1024 + hf * 512
                        r_ = hf
                        P.op("act", lambda e: e.activation(out=ahi[:], in_=acc[hf][:], func=AF.Copy), reads=[bacc[hf]], writes=[bahi])
                        P.op("dve", lambda e: e.tensor_tensor(out=alo[:], in0=acc[hf][:], in1=ahi[:], op=ALU.subtract), reads=[bacc[hf], bahi], writes=[balo])
                        P.op("pe", lambda e: e.matmul(DN[hf][:, :], ones[:], ahi[:], start=True, stop=False), reads=[bcn, bahi], writes=[bDN[hf]], signal=False)
                        P.op("pe", lambda e: e.matmul(DN[hf][:, :], ones[:], alo[:], start=False, stop=True), reads=[bcn, balo], writes=[bDN[hf]])
                        P.op("act", lambda e: e.activation(out=rden[r_][:], in_=DN[hf][:, :], func=AF.Ln), reads=[bDN[hf]], writes=[brden[r_]])
                        P.op("act", lambda e: e.activation(out=rden[r_][:], in_=rden[r_][:], func=AF.Exp, scale=-1.0), reads=[brden[r_]], writes=[brden[r_]])
                        P.op("dve", lambda e: e.tensor_tensor(out=Y[:, h, 2 + q0:2 + q0 + 512], in0=O[hf][:, :], in1=rden[r_][:], op=ALU.mult),
                             reads=[bO[hf], brden[r_]], writes=[bYh[h]])
            self.rot_banks = (0, 1, 2, 3, 4, 5, 6, 7)
            self.out_proj_residual("wout", Y, bYh, DC, first=True, coff=2)
            P.barrier()


_PROGS = {}


def prog(kind):
    if kind not in _PROGS:
        _PROGS[kind] = Builder(kind).build()
    return _PROGS[kind]


def shard_xT(x_full):
    xT = np.ascontiguousarray(x_full.T)
    xTp = np.concatenate([np.zeros((D, 2), np.float32), xT], axis=1)
    return [np.ascontiguousarray(xTp[:, c * T:c * T + T + 2]) for c in range(NCORE)]


def launch(kind, maps):
    import os
    if os.environ.get("TRACE_KIND") == kind:
        res = run_bass_kernel_spmd(prog(kind), maps, core_ids=list(range(NCORE)), trace=True)
        print("[trace]", kind, "exec_time_ns", res.exec_time_ns)
        try:
            insts = res.instructions_and_trace[0]
            from collections import defaultdict
            busy = defaultdict(float); cnt = defaultdict(int); wt = defaultdict(float)
            byname = defaultdict(float); bn = defaultdict(int)
            srcl = defaultdict(float)
            for it_ in insts:
                e = str(it_.engine)
                busy[e] += it_.duration; cnt[e] += 1
                try:
                    wt[e] += (it_.evt_wait_time or 0)
                except Exception:
                    pass
                key = (e, str(it_.op_name))
                byname[key] += it_.duration; bn[key] += 1
                srcl[(e, it_.source_line)] += it_.duration
            for e in busy:
                print(f"[trace] {e:24s} busy={busy[e]/1e3:9.1f}us n={cnt[e]:6d} waits={wt[e]/1e3:9.1f}us")
            for key, v in sorted(byname.items(), key=lambda kv: -kv[1])[:25]:
                print(f"[trace]   {key[0]:20s} {key[1]:28s} {v/1e3:9.1f}us n={bn[key]:6d} avg={v/bn[key]:7.1f}ns")
            t0 = min(i_.timestamp for i_ in insts); t1 = max(i_.end_timestamp for i_ in insts)
            lo = int(os.environ.get("TRACE_LO", "0")); hi = int(os.environ.get("TRACE_HI", "0"))
            sel = [i_ for i_ in insts if lo <= (i_.source_line or 0) <= hi]
            if sel:
                print(f"[trace] total span {(t1-t0)/1e3:.1f}us; lines {lo}-{hi}: from {(min(i_.timestamp for i_ in sel)-t0)/1e3:.1f}us to {(max(i_.end_timestamp for i_ in sel)-t0)/1e3:.1f}us")
            for key, v in sorted(srcl.items(), key=lambda kv: -kv[1])[:25]:
                print(f"[trace]   line {key[1]} {key[0]:12s} {v/1e3:9.1f}us")
        except Exception as ex:
            print("[trace] summary failed", repr(ex))
        return res.results
    res = run_bass_kernel_spmd(prog(kind), maps, core_ids=list(range(NCORE)))
    return res.results


def gather_x(results):
    return np.ascontiguousarray(np.concatenate([r["outT"] for r in results], axis=1).T)


def run_ffn(inp, i, x, extra=None):
    xs = shard_xT(x)
    vec = make_vec(inp[f"l{i}_ln2_g"], inp[f"l{i}_ln2_b"], conv=inp[f"l{i}_ffn_conv"],
                   lbraw=inp["hgrn_lower_bounds"] if extra == "h" else None)
    wup = tile_w(inp[f"l{i}_ffn_w_up"])
    wdn = tile_w(inp[f"l{i}_ffn_w_down"])
    maps = [dict(xT=xs[c], vec=vec, cst=make_cst(c, layer=(i + 1 if extra == "h" else 0)), wup=wup, wdn=wdn) for c in range(NCORE)]
    kind = "ff"
    if extra == "q":
        kind = "ffq"
        win = moba_win_tiled(inp[f"l{i + 1}_mix_w_in"])
        pos = inp["positions"].astype(np.int32)
        for c in range(NCORE):
            maps[c]["win"] = win
            maps[c]["pos"] = np.ascontiguousarray(pos[:, c * T:(c + 1) * T])
    elif extra == "h":
        kind = "ffh"
        win = tile_w(inp[f"l{i + 1}_mix_w_in"])
        for c in range(NCORE):
            maps[c]["win"] = win
            maps[c]["Aall"] = np.zeros((128, NCORE * 8), np.float32)
            maps[c]["Ball"] = np.zeros((NCORE * 128, 1024), np.float32)
    res = launch(kind, maps)
    return gather_x(res), res


def moba_win_tiled(W):
    perm = []
    for qk in range(2):
        for h in range(8):
            base = qk * 1024 + h * 128
            Wp = np.zeros((D, 128), np.float32)
            Wp[:, 0:16] = W[:, base + 16:base + 32]
            Wp[:, 16:32] = W[:, base:base + 16]
            perm.append(Wp)
    return tile_w(np.concatenate([W] + perm, axis=1))


def run_conv(inp, i, x):
    xs = shard_xT(x)
    vec = make_vec(inp[f"l{i}_ln1_g"], inp[f"l{i}_ln1_b"], conv=inp[f"l{i}_mix_conv"])
    win = tile_w(inp[f"l{i}_mix_w_in"])
    wout = tile_w(inp[f"l{i}_mix_w_out"])
    maps = [dict(xT=xs[c], vec=vec, cst=make_cst(c), win=win, wout=wout) for c in range(NCORE)]
    return gather_x(launch("cv", maps))


def run_hgrn(inp, i, x, res0=None):
    xs = shard_xT(x)
    vec = make_vec(inp[f"l{i}_ln1_g"], inp[f"l{i}_ln1_b"], norm_g=inp[f"l{i}_mix_norm_g"], lbraw=inp["hgrn_lower_bounds"])
    win = tile_w(inp[f"l{i}_mix_w_in"])
    wout = tile_w(inp[f"l{i}_mix_w_out"])
    Aall = np.zeros((128, NCORE * 8), np.float32)
    Ball = np.zeros((NCORE * 128, 1024), np.float32)
    out = None
    for phase in range(2):
        maps = [dict(xT=xs[c], vec=vec, cst=make_cst(c, layer=i), win=win, Aall=Aall, Ball=Ball) for c in range(NCORE)]
        if phase == 1:
            for m in maps:
                m["wout"] = wout
        if phase == 0 and res0 is not None:
            res = res0
        else:
            res = launch("hg1" if phase == 0 else "hg", maps)
        if phase == 0:
            Aall = np.ascontiguousarray(np.concatenate([r["Aout"] for r in res], axis=1))
            Ball = np.ascontiguousarray(np.concatenate([r["Bout"] for r in res], axis=0))
        else:
            out = gather_x(res)
    return out


def zz_block(c, s_):
    return 8 * s_ + (c if s_ % 2 == 0 else 7 - c)


def run_moba(inp, i, x, res1=None):
    H = 8
    NSL = 72
    xs = shard_xT(x)
    vec = make_vec(inp[f"l{i}_ln1_g"], inp[f"l{i}_ln1_b"])
    W = inp[f"l{i}_mix_w_in"]
    perm = []
    for qk in range(2):
        for h in range(H):
            base = qk * 1024 + h * 128
            Wp = np.zeros((D, 128), np.float32)
            Wp[:, 0:16] = W[:, base + 16:base + 32]
            Wp[:, 16:32] = W[:, base:base + 16]
            perm.append(Wp)
    win = tile_w(np.concatenate([W] + perm, axis=1))
    pos = inp["positions"].astype(np.int32)
    maps = [dict(xT=xs[c], vec=vec, cst=make_cst(c), win=win, pos=np.ascontiguousarray(pos[:, c * T:(c + 1) * T])) for c in range(NCORE)]
    res = res1 if res1 is not None else launch("mo1", maps)
    Qg = np.concatenate([np.asarray(r["QT"]).reshape(128, H, 8, 256) for r in res], axis=2)
    Kg = np.concatenate([np.asarray(r["KT"]).reshape(128, H, 8, 256).transpose(1, 0, 2, 3) for r in res], axis=2)
    Vg = np.concatenate([np.asarray(r["V"]).reshape(128, 8, 2, H, 128).transpose(3, 0, 1, 2, 4) for r in res], axis=2)
    KMg = np.concatenate([np.asarray(r["KM"]).reshape(128, H, 8) for r in res], axis=2)
    xTfull = np.ascontiguousarray(x.T).reshape(D, 64, 256)
    wout = tile_w(inp[f"l{i}_mix_w_out"])
    caus = np.zeros((4, 128, 512), np.float32)
    p_ = np.arange(128)[:, None]
    qi = np.arange(256)[None, :]
    for kt2 in range(2):
        for posb in range(2):
            caus[kt2 * 2 + posb][:, posb * 256:(posb + 1) * 256] = np.where(kt2 * 128 + p_ > qi, -30000.0, 0.0)
    maps = []
    own_all = []
    for c in range(NCORE):
        own = [zz_block(c, s_) for s_ in range(8)]
        own_all.append(own)
        pc = np.array(own + list(range(64)))
        Kall = np.ascontiguousarray(Kg[:, :, pc, :]).reshape(H * 128, NSL * 256)
        Vall = np.ascontiguousarray(Vg[:, :, pc]).reshape(H * 128, NSL * 256)
        KMall = np.ascontiguousarray(KMg[:, :, pc]).reshape(128, H * NSL)
        QT = np.ascontiguousarray(Qg[:, :, np.array(own), :]).reshape(128, H * T)
        xT = np.concatenate([np.zeros((D, 2), np.float32), xTfull[:, np.array(own), :].reshape(D, T)], axis=1)
        past = np.zeros((8, NSL), np.float32)
        ownm = np.zeros((8, NSL), np.float32)
        for s_ in range(8):
            past[s_, 8:] = (np.arange(64) < own[s_]).astype(np.float32)
            ownm[s_, s_] = 1.0
        pastbias = np.where(past > 0, 0.0, -1e30).astype(np.float32)
        row = np.concatenate([pastbias.reshape(-1), past.reshape(-1), ownm.reshape(-1)])
        mcst = np.concatenate([np.broadcast_to(row[None, :], (128, 3 * 8 * NSL)), caus.transpose(1, 0, 2).reshape(128, 2048)], axis=1).astype(np.float32)
        maps.append(dict(xT=np.ascontiguousarray(xT), vec=vec, cst=make_cst(c), QT=QT, Kall=Kall, Vall=Vall, KMall=KMall,
                         mcst=np.ascontiguousarray(mcst), wout=wout))
    res2 = launch("mo2", maps)
    out = np.zeros((64, 256, D), np.float32)
    for c in range(NCORE):
        oc = np.asarray(res2[c]["outT"]).T.reshape(8, 256, D)
        for s_ in range(8):
            out[own_all[c][s_]] = oc[s_]
    return np.ascontiguousarray(out.reshape(64 * 256, D))


def kernel(**inputs):
    inp = {k: np.asarray(v) for k, v in inputs.items()}
    x = np.ascontiguousarray(inp["x"][0])
    x = run_hgrn(inp, 0, x)
    x, r = run_ffn(inp, 0, x, extra="q")
    x = run_moba(inp, 1, x, res1=r)
    x, _ = run_ffn(inp, 1, x)
    x = run_conv(inp, 2, x)
    x, r = run_ffn(inp, 2, x, extra="h")
    x = run_hgrn(inp, 3, x, res0=r)
    x, _ = run_ffn(inp, 3, x)
    return x[None].astype(np.float32)
```
